# Optimizing a Trainium2 kernel written in Bass

```python
import jax, jax.numpy as jnp
from jax import lax
import numpy as np

D_MODEL = 2048
BATCH = 2
SEQ = 8192
DEPTH = 2

GRID_W = 64
CTX_LEN = 256
N_MOD = 9
EPS = 1e-6
D_FF = 5632
ATTN_HEADS = 8
ATTN_KV_HEADS = 2
HEAD_DIM = 128
ATTN_WIDTH = ATTN_HEADS * HEAD_DIM
KV_WIDTH = ATTN_KV_HEADS * HEAD_DIM
WINDOW = 128
BLOCK = 128
ROPE_BASE = 10000.0
NEG_INF = -1e30
SCONV_WIDTH = 1024
SCONV_K = 3
SCONV_LEFT = 1
AB_IN_WIDTH = ATTN_WIDTH + 2 * KV_WIDTH + 3 * SCONV_WIDTH
AB_MIX_WIDTH = ATTN_WIDTH + SCONV_WIDTH
D_RNN = 2048
RNN_HEADS = 16
RNN_BLOCK = D_RNN // RNN_HEADS
RG_CONV_K = 4
RG_CONV_LEFT = 2
RG_C = 8.0
N_EVEN = (DEPTH + 1) // 2
N_ODD = DEPTH // 2

kernel_name = "hybrid_swa_shortconv_rglru_macaron_dit"


def rms_norm(x, g):
    xf = x.astype(jnp.float32)
    y = xf * lax.rsqrt(jnp.mean(xf * xf, axis=-1, keepdims=True) + EPS)
    return y.astype(x.dtype) * g


def modulate(x, g, shift, scale):
    return rms_norm(x, g) * (1 + scale) + shift


def swiglu(h, w_in, w_out):
    gate, up = jnp.split(h @ w_in, 2, axis=-1)
    return (jax.nn.silu(gate) * up) @ w_out


def depthwise_conv(x, w, left):
    k_width = w.shape[0]
    t = x.shape[1]
    xp = jnp.pad(x, ((0, 0), (left, k_width - 1 - left), (0, 0)))
    return sum(xp[:, k:k + t] * w[k] for k in range(k_width))


def axial_rope_tables(seq, dtype):
    rows = seq // GRID_W
    row = jnp.repeat(jnp.arange(rows), GRID_W)
    col = jnp.tile(jnp.arange(GRID_W), rows)
    n_freq = HEAD_DIM // 4
    inv_freq = ROPE_BASE ** (-jnp.arange(n_freq, dtype=jnp.float32) / n_freq)
    ang = jnp.stack([row, col], axis=-1).astype(jnp.float32)[..., None] * inv_freq
    return jnp.cos(ang).astype(dtype), jnp.sin(ang).astype(dtype)


def apply_axial_rope(x, cos, sin):
    xs = x.reshape(x.shape[:-1] + (2, 2, HEAD_DIM // 4))
    x1, x2 = xs[..., 0, :], xs[..., 1, :]
    c, s = cos[None, :, None], sin[None, :, None]
    out = jnp.stack([x1 * c - x2 * s, x2 * c + x1 * s], axis=-2)
    return out.reshape(x.shape)


def windowed_attention(q, k, v, kc, vc, sink):
    bsz, seq = q.shape[:2]
    nb = seq // BLOCK
    g = ATTN_HEADS // ATTN_KV_HEADS
    scale = HEAD_DIM ** -0.5
    qb = q.reshape(bsz, nb, BLOCK, ATTN_KV_HEADS, g, HEAD_DIM)

    def band(t):
        tp = jnp.pad(t, ((0, 0), (BLOCK, BLOCK), (0, 0), (0, 0)))
        tp = tp.reshape(bsz, nb + 2, BLOCK, ATTN_KV_HEADS, HEAD_DIM)
        return jnp.concatenate([tp[:, :-2], tp[:, 1:-1], tp[:, 2:]], axis=2)

    kb, vb = band(k), band(v)
    s_loc = jnp.einsum('bnqkgd,bnjkd->bnkgqj', qb, kb).astype(jnp.float32) * scale
    q_pos = jnp.arange(nb)[:, None] * BLOCK + jnp.arange(BLOCK)[None, :]
    k_pos = (jnp.arange(nb)[:, None] - 1) * BLOCK + jnp.arange(3 * BLOCK)[None, :]
    kp = k_pos[:, None, :]
    valid = (jnp.abs(q_pos[:, :, None] - kp) <= WINDOW) & (kp >= 0) & (kp < seq)
    s_loc = jnp.where(valid[None, :, None, None], s_loc, NEG_INF)
    s_ctx = jnp.einsum('bnqkgd,bjkd->bnkgqj', qb, kc).astype(jnp.float32) * scale
    sink_col = jnp.broadcast_to(
        sink.astype(jnp.float32).reshape(ATTN_KV_HEADS, g)[None, None, :, :, None, None],
        s_loc.shape[:-1] + (1,))
    p = jax.nn.softmax(jnp.concatenate([s_loc, s_ctx, sink_col], axis=-1), axis=-1).astype(v.dtype)
    n_loc = 3 * BLOCK
    n_ctx = kc.shape[1]
    o = (jnp.einsum('bnkgqj,bnjkd->bnqkgd', p[..., :n_loc], vb)
         + jnp.einsum('bnkgqj,bjkd->bnqkgd', p[..., n_loc:n_loc + n_ctx], vc))
    return o.reshape(bsz, seq, ATTN_WIDTH)


def context_attention(qc, kc, vc, sink):
    bsz, n_ctx = qc.shape[:2]
    g = ATTN_HEADS // ATTN_KV_HEADS
    qg = qc.reshape(bsz, n_ctx, ATTN_KV_HEADS, g, HEAD_DIM)
    s = jnp.einsum('bqkgd,bjkd->bkgqj', qg, kc).astype(jnp.float32) * (HEAD_DIM ** -0.5)
    sink_col = jnp.broadcast_to(
        sink.astype(jnp.float32).reshape(ATTN_KV_HEADS, g)[None, :, :, None, None], s.shape[:-1] + (1,))
    p = jax.nn.softmax(jnp.concatenate([s, sink_col], axis=-1), axis=-1)[..., :n_ctx].astype(vc.dtype)
    o = jnp.einsum('bkgqj,bjkd->bqkgd', p, vc)
    return o.reshape(bsz, n_ctx, ATTN_WIDTH)


def attn_conv_mixer(h, hc, w_in, w_out, sink, conv_w, cos, sin, need_ctx):
    cuts = [ATTN_WIDTH, ATTN_WIDTH + KV_WIDTH, ATTN_WIDTH + 2 * KV_WIDTH,
            ATTN_WIDTH + 2 * KV_WIDTH + SCONV_WIDTH, ATTN_WIDTH + 2 * KV_WIDTH + 2 * SCONV_WIDTH]

    def project(t):
        bsz, n = t.shape[:2]
        q, k, v, bg, cg, u = jnp.split(t @ w_in, cuts, axis=-1)
        return (q.reshape(bsz, n, ATTN_HEADS, HEAD_DIM), k.reshape(bsz, n, ATTN_KV_HEADS, HEAD_DIM),
                v.reshape(bsz, n, ATTN_KV_HEADS, HEAD_DIM), bg, cg, u)

    q, k, v, bg, cg, u = project(h)
    qc, kc, vc, bgc, cgc, uc = project(hc)
    q = apply_axial_rope(q, cos, sin)
    k = apply_axial_rope(k, cos, sin)
    attn = windowed_attention(q, k, v, kc, vc, sink)
    conv = bg * depthwise_conv(cg * u, conv_w, SCONV_LEFT)
    out = jnp.concatenate([attn, conv], axis=-1) @ w_out
    if not need_ctx:
        return out, None
    attn_c = context_attention(qc, kc, vc, sink)
    conv_c = bgc * depthwise_conv(cgc * uc, conv_w, SCONV_LEFT)
    out_c = jnp.concatenate([attn_c, conv_c], axis=-1) @ w_out
    return out, out_c


def linear_scan(a, b, h0, reverse):
    if reverse:
        a, b = jnp.flip(a, axis=1), jnp.flip(b, axis=1)
    b = b.at[:, 0].add(a[:, 0] * h0)

    def combine(lhs, rhs):
        a1, b1 = lhs
        a2, b2 = rhs
        return a1 * a2, a2 * b1 + b2

    _, h = lax.associative_scan(combine, (a, b), axis=1)
    if reverse:
        h = jnp.flip(h, axis=1)
    return h


def rglru_direction(u, h0, w_a, b_a, w_x, b_x, lam, reverse):
    bsz, t = u.shape[:2]
    ub = u.reshape(bsz, t, RNN_HEADS, RNN_BLOCK)
    r = jax.nn.sigmoid((jnp.einsum('bthi,hij->bthj', ub, w_a).reshape(bsz, t, D_RNN) + b_a).astype(jnp.float32))
    i = jax.nn.sigmoid((jnp.einsum('bthi,hij->bthj', ub, w_x).reshape(bsz, t, D_RNN) + b_x).astype(jnp.float32))
    log_a = -RG_C * r * jax.nn.softplus(-lam.astype(jnp.float32))
    a = jnp.exp(log_a)
    b = jnp.sqrt(-jnp.expm1(2 * log_a)) * i * u.astype(jnp.float32)
    return linear_scan(a, b, h0, reverse)


def rglru_mixer(h, hc, w_in, conv_w, conv_b, w_a, b_a, w_x, b_x, lam, w_out, need_ctx):
    gate, u = jnp.split(h @ w_in, 2, axis=-1)
    gate_c, u_c = jnp.split(hc @ w_in, 2, axis=-1)
    u = depthwise_conv(u, conv_w, RG_CONV_LEFT) + conv_b
    u_c = depthwise_conv(u_c, conv_w, RG_CONV_LEFT) + conv_b
    h_zero = jnp.zeros((hc.shape[0], D_RNN), jnp.float32)
    y = 0.0
    y_c = 0.0
    for d, reverse in enumerate((False, True)):
        hs_c = rglru_direction(u_c, h_zero, w_a[d], b_a[d], w_x[d], b_x[d], lam[d], reverse)
        h_last = hs_c[:, 0] if reverse else hs_c[:, -1]
        y = y + rglru_direction(u, h_last, w_a[d], b_a[d], w_x[d], b_x[d], lam[d], reverse)
        y_c = y_c + hs_c
    out = (y.astype(h.dtype) * jax.nn.gelu(gate)) @ w_out
    if not need_ctx:
        return out, None
    out_c = (y_c.astype(hc.dtype) * jax.nn.gelu(gate_c)) @ w_out
    return out, out_c


def setup_inputs(seed: int = 0) -> dict:
    key = jax.random.key(seed)
    ks = iter(jax.random.split(key, 32))

    def nrm(shape, scale):
        return jax.random.normal(next(ks), shape, jnp.float32) * scale

    d = D_MODEL
    u = jax.random.uniform(next(ks), (N_ODD, 2, D_RNN), jnp.float32, 0.9, 0.999)
    s = u ** (1.0 / RG_C)
    return {
        "x": nrm((BATCH, SEQ, d), 1.0),
        "c": nrm((BATCH, d), 1.0),
        "ctx": nrm((BATCH, CTX_LEN, d), 1.0),
        "c_ctx": nrm((d,), 1.0),
        "w_mod": nrm((DEPTH, d, N_MOD * d), 0.5 * d ** -0.5),
        "b_mod": nrm((DEPTH, N_MOD * d), 0.02),
        "norm_ffn1": 1.0 + nrm((DEPTH, d), 0.1),
        "norm_mix": 1.0 + nrm((DEPTH, d), 0.1),
        "norm_ffn2": 1.0 + nrm((DEPTH, d), 0.1),
        "ffn1_w_in": nrm((DEPTH, d, 2 * D_FF), d ** -0.5),
        "ffn1_w_out": nrm((DEPTH, D_FF, d), D_FF ** -0.5),
        "ffn2_w_in": nrm((DEPTH, d, 2 * D_FF), d ** -0.5),
        "ffn2_w_out": nrm((DEPTH, D_FF, d), D_FF ** -0.5),
        "ab_w_in": nrm((N_EVEN, d, AB_IN_WIDTH), d ** -0.5),
        "ab_w_out": nrm((N_EVEN, AB_MIX_WIDTH, d), AB_MIX_WIDTH ** -0.5),
        "ab_sink": nrm((N_EVEN, ATTN_HEADS), 0.5),
        "ab_conv_w": nrm((N_EVEN, SCONV_K, SCONV_WIDTH), SCONV_K ** -0.5),
        "rg_w_in": nrm((N_ODD, d, 2 * D_RNN), d ** -0.5),
        "rg_conv_w": nrm((N_ODD, RG_CONV_K, D_RNN), RG_CONV_K ** -0.5),
        "rg_conv_b": nrm((N_ODD, D_RNN), 0.02),
        "rg_w_a": nrm((N_ODD, 2, RNN_HEADS, RNN_BLOCK, RNN_BLOCK), RNN_BLOCK ** -0.5),
        "rg_b_a": nrm((N_ODD, 2, D_RNN), 0.02),
        "rg_w_x": nrm((N_ODD, 2, RNN_HEADS, RNN_BLOCK, RNN_BLOCK), RNN_BLOCK ** -0.5),
        "rg_b_x": nrm((N_ODD, 2, D_RNN), 0.02),
        "rg_lambda": jnp.log(s) - jnp.log1p(-s),
        "rg_w_out": nrm((N_ODD, D_RNN, d), D_RNN ** -0.5),
        "final_norm": 1.0 + nrm((d,), 0.1),
    }


def reference(x, c, ctx, c_ctx, w_mod, b_mod, norm_ffn1, norm_mix, norm_ffn2,
              ffn1_w_in, ffn1_w_out, ffn2_w_in, ffn2_w_out,
              ab_w_in, ab_w_out, ab_sink, ab_conv_w,
              rg_w_in, rg_conv_w, rg_conv_b, rg_w_a, rg_b_a, rg_w_x, rg_b_x, rg_lambda, rg_w_out,
              final_norm):
    seq = x.shape[1]
    cos, sin = axial_rope_tables(seq, x.dtype)
    xc = ctx
    for l in range(DEPTH):
        need_ctx = l < DEPTH - 1
        m = jnp.split((jax.nn.silu(c) @ w_mod[l] + b_mod[l])[:, None, :], N_MOD, axis=-1)
        mc = jnp.split((jax.nn.silu(c_ctx) @ w_mod[l] + b_mod[l])[None, None, :], N_MOD, axis=-1)
        x = x + 0.5 * m[2] * swiglu(modulate(x, norm_ffn1[l], m[0], m[1]), ffn1_w_in[l], ffn1_w_out[l])
        xc = xc + 0.5 * mc[2] * swiglu(modulate(xc, norm_ffn1[l], mc[0], mc[1]), ffn1_w_in[l], ffn1_w_out[l])
        h = modulate(x, norm_mix[l], m[3], m[4])
        hc = modulate(xc, norm_mix[l], mc[3], mc[4])
        j = l // 2
        if l % 2 == 0:
            o, oc = attn_conv_mixer(h, hc, ab_w_in[j], ab_w_out[j], ab_sink[j], ab_conv_w[j], cos, sin, need_ctx)
        else:
            o, oc = rglru_mixer(h, hc, rg_w_in[j], rg_conv_w[j], rg_conv_b[j], rg_w_a[j], rg_b_a[j],
                                rg_w_x[j], rg_b_x[j], rg_lambda[j], rg_w_out[j], need_ctx)
        x = x + m[5] * o
        x = x + 0.5 * m[8] * swiglu(modulate(x, norm_ffn2[l], m[6], m[7]), ffn2_w_in[l], ffn2_w_out[l])
        if need_ctx:
            xc = xc + mc[5] * oc
            xc = xc + 0.5 * mc[8] * swiglu(modulate(xc, norm_ffn2[l], mc[6], mc[7]), ffn2_w_in[l], ffn2_w_out[l])
    return rms_norm(x, final_norm)
```

```python
import numpy as np
from contextlib import ExitStack
import concourse.bass as bass
import concourse.mybir as mybir
from concourse.bass_utils import run_bass_kernel_spmd

F32 = mybir.dt.float32
BF16 = mybir.dt.bfloat16
AF = mybir.ActivationFunctionType
ALU = mybir.AluOpType
AX = mybir.AxisListType

D = 2048
KC = 16
DFF = 5632
FC = 44
NOWN = 2048
HALO = 128
NCTX = 256
NU = 2560
U_OWN = 384
EPS = 1e-6
NCORES = 8

ENGS = ["pe", "act", "dve", "pool", "sp"]
DMA_POOL = {"sp": 16, "act": 4, "pool": 12}
SAME_ENGINE_SYNC = True
SEM_MAX = 4000


class Res:
    __slots__ = ("w", "r")

    def __init__(self):
        self.w = None
        self.r = {}


class Prog:
    def __init__(self, nc, stack, n_phase_sems):
        self.nc = nc
        self.ops = {e: [] for e in ENGS}
        self.dsem = {q: [stack.enter_context(nc.semaphore(f"d_{q}{i}")) for i in range(k)]
                     for q, k in DMA_POOL.items()}
        self.dcnt = {q: 0 for q in DMA_POOL}
        self.free = [stack.enter_context(nc.semaphore(f"e{i}")) for i in range(n_phase_sems)]
        self.esem = {e: self.free.pop() for e in ENGS}
        self.cnt = {e: 0 for e in ENGS}
        self.last = {e: None for e in ENGS}
        self.ccsem = stack.enter_context(nc.semaphore("ccsem"))
        self.cccnt = 0

    def _deps(self, R, W):
        deps = []
        for r in R:
            if r.w is not None:
                deps.append(r.w)
        for w in W:
            if w.w is not None:
                deps.append(w.w)
            deps.extend(w.r.values())
        return deps

    def _commit(self, tok, R, W):
        for r in R:
            r.r[id(tok[0])] = tok
        for w in W:
            w.w = tok
            w.r = {}

    def op(self, eng, fn, R=(), W=()):
        deps = self._deps(R, W)
        if self.cnt[eng] >= SEM_MAX:
            self.esem[eng] = self.free.pop()
            self.cnt[eng] = 0
        self.cnt[eng] += 1
        tok = (self.esem[eng], self.cnt[eng], eng)
        self.last[eng] = tok
        self._commit(tok, R, W)
        self.ops[eng].append((fn, deps, tok[0], 1))
        return tok

    def dma(self, q, fn, R=(), W=()):
        deps = self._deps(R, W)
        j = self.dcnt[q]
        self.dcnt[q] += 1
        k = len(self.dsem[q])
        sem = self.dsem[q][j % k]
        val = 16 * (j // k + 1)
        if val > 16:
            deps.append((sem, val - 16, "dma"))
        tok = (sem, val, "dma")
        self._commit(tok, R, W)
        self.ops[q].append((fn, deps, sem, 16))
        return tok

    def cc(self, fn, R=(), W=()):
        deps = self._deps(R, W)
        self.cccnt += 1
        tok = (self.ccsem, self.cccnt, "cc")
        self._commit(tok, R, W)
        self.ops["pool"].append((fn, deps, self.ccsem, 1))
        return tok

    def all_tokens(self):
        toks = []
        for e in ENGS:
            if self.last[e] is not None:
                toks.append(self.last[e])
        if self.cccnt > 0:
            toks.append((self.ccsem, self.cccnt, "cc"))
        for q in DMA_POOL:
            k = len(self.dsem[q])
            for i in range(min(k, self.dcnt[q])):
                n_uses = (self.dcnt[q] - 1 - i) // k + 1
                toks.append((self.dsem[q][i], 16 * n_uses, "dma"))
        return toks

    def barrier(self):
        toks = self.all_tokens()
        for e in ENGS:
            self.ops[e].append((None, [(s, v, "x") for (s, v, _) in toks], None, 0))

    def final_wait(self):
        toks = self.all_tokens()
        self.ops["sp"].append((None, [(s, v, "x") for (s, v, _) in toks], None, 0))

    def emit(self):
        nc = self.nc
        with nc.Block() as block:
            def run(e):
                def body(engobj):
                    waited = {}
                    for fn, deps, sem, inc in self.ops[e]:
                        need = {}
                        for (s, v, de) in deps:
                            if de == e and (e == "pe" or not SAME_ENGINE_SYNC):
                                continue
                            key = id(s)
                            if waited.get(key, 0) >= v:
                                continue
                            if key not in need or need[key][1] < v:
                                need[key] = (s, v)
                        for key, (s, v) in need.items():
                            engobj.wait_ge(s, v)
                            waited[key] = v
                        if fn is not None:
                            fn(engobj).then_inc(sem, inc)
                return body
            block.tensor(run("pe"))
            block.scalar(run("act"))
            block.vector(run("dve"))
            block.gpsimd(run("pool"))
            block.sync(run("sp"))


class Arena:
    def __init__(self, nc, stack, nbytes):
        self.t32 = stack.enter_context(nc.sbuf_tensor("arena", [128, nbytes // 4], F32))
        self.t16 = self.t32.bitcast(BF16)
        self.n = nbytes
        self.off = 0

    def alloc(self, shape, dt):
        n = int(np.prod(shape))
        sz = 4 if dt == F32 else 2
        self.off = (self.off + 63) // 64 * 64
        o = self.off
        self.off += n * sz
        assert self.off <= self.n, f"arena overflow {self.off} > {self.n}"
        ap = (self.t32[:, o // 4:o // 4 + n] if dt == F32 else self.t16[:, o // 2:o // 2 + n])
        if len(shape) == 2:
            ap = ap.rearrange("p (a b) -> p a b", a=shape[0])
        elif len(shape) == 3:
            ap = ap.rearrange("p (a b c) -> p a b c", a=shape[0], b=shape[1])
        return ap

    def mark(self):
        return self.off

    def reset(self, m):
        self.off = m


class Ring:
    def __init__(self, bufs):
        self.bufs = [(b, Res()) for b in bufs]
        self.i = 0

    def next(self):
        b = self.bufs[self.i % len(self.bufs)]
        self.i += 1
        return b


def split_even(n, maxlen):
    k = -(-n // maxlen)
    base = n // k
    rem = n - base * k
    out = []
    o = 0
    for i in range(k):
        ln = base + (1 if i < rem else 0)
        out.append((o, ln))
        o += ln
    return out


def make_tiles(ranges, tmax):
    total = sum(b - a for a, b in ranges)
    ntile = -(-total // tmax)
    tl = -(-total // ntile)
    tl = (tl + 1) // 2 * 2
    tiles = []
    cur = []
    curlen = 0
    for a, b in ranges:
        pos = a
        while pos < b:
            lim = b if pos >= NCTX else min(b, NCTX)
            take = min(lim - pos, tl - curlen)
            cur.append((pos, take, curlen, 1 if pos < NCTX else 0))
            curlen += take
            pos += take
            if curlen == tl:
                tiles.append(cur)
                cur = []
                curlen = 0
    if cur:
        tiles.append(cur)
    return tiles


class K:
    def __init__(self, nc, stack, n_phase_sems=64):
        self.nc = nc
        self.st = stack
        self.p = Prog(nc, stack, n_phase_sems)
        self.arena = Arena(nc, stack, 175 * 1024)
        self.banks = [stack.enter_context(nc.psum_tensor(f"bank{i}", [128, 512], F32)) for i in range(8)]
        self.dram = {}
        self.bank_res = [Res() for _ in range(8)]

    def ext_in(self, name, shape, dt=F32):
        t = self.nc.dram_tensor(name, list(shape), dt, kind="ExternalInput").ap()
        self.dram[name] = t
        return t

    def ext_out(self, name, shape, dt=F32):
        t = self.nc.dram_tensor(name, list(shape), dt, kind="ExternalOutput").ap()
        self.dram[name] = t
        return t

    def scratch(self, name, shape, dt=F32):
        t = self.nc.dram_tensor(name, list(shape), dt, kind="Internal").ap()
        self.dram[name] = t
        return t

    def bank_ring(self, idx):
        r = Ring([])
        r.bufs = [(self.banks[i], self.bank_res[i]) for i in idx]
        return r


def load_consts(k, names_shapes):
    out = {}
    for name, shape in names_shapes:
        src = k.dram[name]
        t = k.arena.alloc(shape, F32)
        r = Res()
        pat = {1: None, 2: None, 3: None}
        k.p.dma("sp", lambda e, t=t, src=src: e.dma_start(out=t, in_=src), W=[r])
        out[name] = (t, r)
    return out


def make_basic_consts(k):
    a = k.arena
    p = k.p
    ident = a.alloc([128], F32)
    ones32 = a.alloc([128], F32)
    ones16 = a.alloc([128], BF16)
    r = Res()
    p.op("pool", lambda e: e.memset(ident, 0.0), W=[r])
    p.op("pool", lambda e: e.affine_select(out=ident, in_=ident, compare_op=ALU.not_equal, fill=1.0,
                                           base=0, pattern=[[-1, 128]], channel_multiplier=1), R=[r], W=[r])
    p.op("pool", lambda e: e.memset(ones32, 1.0), W=[r])
    p.op("pool", lambda e: e.memset(ones16, 1.0), W=[r])
    k.ident, k.ones32, k.ones16, k.cres = ident, ones32, ones16, r


def phase_transpose_in(k, xin, XU, barrier=True):
    p, a = k.p, k.arena
    m = a.mark()
    xt = Ring([a.alloc([D], F32) for _ in range(2)])
    xo = Ring([a.alloc([KC, 128], F32) for _ in range(2)])
    bk = k.bank_ring(range(8))
    XUv = XU.rearrange("c p t -> p c t")
    for blk in range(NU // 128):
        t, tr = xt.next()
        p.dma("sp", lambda e, t=t, blk=blk: e.dma_start(out=t, in_=xin[blk * 128:(blk + 1) * 128, :]), W=[tr])
        o, orr = xo.next()
        for g in range(4):
            b, br = bk.next()
            for q in range(4):
                c = 4 * g + q
                p.op("pe", lambda e, b=b, t=t, c=c, q=q: e.transpose(b[:, q * 128:(q + 1) * 128], t[:, c * 128:(c + 1) * 128], k.ident),
                     R=[tr, k.cres], W=[br])
            eng = "act" if g % 2 == 0 else "dve"
            if eng == "act":
                p.op("act", lambda e, o=o, b=b, g=g: e.activation(out=o[:, 4 * g:4 * g + 4, :].rearrange("p a b -> p (a b)"), in_=b[:, :], func=AF.Copy),
                     R=[br], W=[orr])
            else:
                p.op("dve", lambda e, o=o, b=b, g=g: e.tensor_copy(out=o[:, 4 * g:4 * g + 4, :].rearrange("p a b -> p (a b)"), in_=b[:, :]),
                     R=[br], W=[orr])
        p.dma("sp", lambda e, o=o, blk=blk: e.dma_start(out=XUv[:, :, blk * 128:(blk + 1) * 128], in_=o), R=[orr])
    if barrier:
        p.barrier()
        a.reset(m)


def phase_mod(k, w_mod, cvT, bmodT, modt, layers, njg=36):
    p, a = k.p, k.arena
    m = a.mark()
    sc = a.alloc([KC, 2], F32)
    scr = Res()
    p.op("act", lambda e: e.activation(out=sc, in_=cvT[0], func=AF.Silu), R=[cvT[1]], W=[scr])
    wr = Ring([a.alloc([KC, 512], F32) for _ in range(3)])
    bk = k.bank_ring([0, 1])
    mr = k.modt_res
    n = 0
    for l in layers:
        for jg in range(njg):
            w, wres = wr.next()
            q_ = "sp" if n % 2 == 0 else "act"
            n += 1
            p.dma(q_, lambda e, w=w, l=l, jg=jg: e.dma_start(out=w, in_=w_mod[l][:, jg * 512:(jg + 1) * 512].rearrange("(k p) n -> p k n", p=128)), W=[wres])
            b, br = bk.next()
            for q in range(4):
                for kc in range(KC):
                    p.op("pe", lambda e, b=b, w=w, q=q, kc=kc: e.matmul(b[:, 2 * q:2 * q + 2], lhsT=w[:, kc, q * 128:(q + 1) * 128], rhs=sc[:, kc, :],
                                                                         start=(kc == 0), stop=(kc == KC - 1)),
                         R=[wres, scr], W=[br])
            for q in range(4):
                j = jg * 4 + q
                p.op("dve", lambda e, b=b, q=q, j=j, l=l: e.tensor_scalar(out=modt[:, l, j, :], in0=b[:, 2 * q:2 * q + 2], scalar1=bmodT[0][:, l, j:j + 1], scalar2=None, op0=ALU.add),
                     R=[br, bmodT[1]], W=[mr])
    p.barrier()
    a.reset(m)


def derive_site(k, modt, normT, l, site, half):
    p, a = k.p, k.arena
    gs = a.alloc([KC, 2], F32)
    gt = a.alloc([KC, 2], F32)
    r = Res()
    j0 = 3 * site * KC
    sh = modt[:, l, j0:j0 + KC, :]
    p.op("dve", lambda e: e.tensor_scalar(out=gs, in0=modt[:, l, j0 + KC:j0 + 2 * KC, :], scalar1=1.0, scalar2=None, op0=ALU.add),
         R=[k.modt_res], W=[r])
    for rr in range(2):
        p.op("dve", lambda e, rr=rr: e.tensor_tensor(out=gs[:, :, rr], in0=gs[:, :, rr], in1=normT[0][:, l, site, :], op=ALU.mult),
             R=[r, normT[1]], W=[r])
    p.op("dve", lambda e: e.tensor_scalar(out=gt, in0=modt[:, l, j0 + 2 * KC:j0 + 3 * KC, :], scalar1=(0.5 if half else 1.0), scalar2=None, op0=ALU.mult),
         R=[k.modt_res], W=[r])
    return dict(gs=gs, sh=sh, gt=gt, res=r)


def emit_modulate(k, XU, tile, T, xs, xs_res, h, h_res, site, tmp_ring, rstd, rstd_res, stat_banks):
    p = k.p
    XUv = XU.rearrange("c p t -> p c t")
    for (u0, ln, off, r) in tile:
        p.dma("sp", lambda e, u0=u0, ln=ln, off=off: e.dma_start(out=xs[:, :, off:off + ln], in_=XUv[:, :, u0:u0 + ln]), W=[xs_res])
    subs = split_even(T, 512)
    banks = [stat_banks.next() for _ in subs]
    for kc in range(KC):
        sq, sqr = tmp_ring.next()
        p.op("act", lambda e, sq=sq, kc=kc: e.activation(out=sq[:, 0:T], in_=xs[:, kc, :], func=AF.Square), R=[xs_res], W=[sqr])
        for (n0, nl), (b, br) in zip(subs, banks):
            p.op("pe", lambda e, b=b, sq=sq, n0=n0, nl=nl, kc=kc: e.matmul(b[:, 0:nl], lhsT=k.ones32, rhs=sq[:, n0:n0 + nl], start=(kc == 0), stop=(kc == KC - 1)),
                 R=[sqr, k.cres], W=[br])
    for (n0, nl), (b, br) in zip(subs, banks):
        p.op("dve", lambda e, b=b, n0=n0, nl=nl: e.tensor_scalar(out=rstd[:, n0:n0 + nl], in0=b[:, 0:nl], scalar1=1.0 / D, scalar2=EPS, op0=ALU.mult, op1=ALU.add),
             R=[br], W=[rstd_res])
    p.op("act", lambda e: e.activation(out=rstd[:, 0:T], in_=rstd[:, 0:T], func=AF.Sqrt), R=[rstd_res], W=[rstd_res])
    p.op("dve", lambda e: e.reciprocal(out=rstd[:, 0:T], in_=rstd[:, 0:T]), R=[rstd_res], W=[rstd_res])
    i = 0
    for (u0, ln, off, r) in tile:
        for kc in range(KC):
            t, tr = tmp_ring.next()
            p.op("dve", lambda e, t=t, kc=kc, off=off, ln=ln, r=r: e.scalar_tensor_tensor(
                out=t[:, 0:ln], in0=xs[:, kc, off:off + ln], scalar=site["gs"][:, kc, r:r + 1], op0=ALU.mult,
                in1=rstd[:, off:off + ln], op1=ALU.mult), R=[xs_res, rstd_res, site["res"]], W=[tr])
            if True:
                p.op("act", lambda e, t=t, kc=kc, off=off, ln=ln, r=r: e.activation(
                    out=h[:, kc, off:off + ln], in_=t[:, 0:ln], func=AF.Identity, bias=site["sh"][:, kc, r:r + 1], scale=1.0),
                    R=[tr, k.modt_res], W=[h_res])
            else:
                p.op("pool", lambda e, t=t, kc=kc, off=off, ln=ln, r=r: e.tensor_scalar(
                    out=h[:, kc, off:off + ln], in0=t[:, 0:ln], scalar1=site["sh"][:, kc, r:r + 1], scalar2=None, op0=ALU.add),
                    R=[tr, k.modt_res], W=[h_res])
            i += 1


def emit_residual(k, XU, tile, m, sub_banks, subs, site, xm_ring, xo_ring, xres=None):
    p = k.p
    xm, xmr = xm_ring.next()
    xo, xor_ = xo_ring.next()
    for (u0, ln, off, r) in tile:
        p.dma("sp", lambda e, u0=u0, ln=ln, off=off, xm=xm: e.dma_start(out=xm[:, off:off + ln], in_=XU[m, :, u0:u0 + ln]), R=([xres] if xres is not None else []), W=[xmr])
    for (n0, nl), (b, br) in zip(subs, sub_banks):
        for (u0, ln, off, r) in tile:
            lo, hi = max(n0, off), min(n0 + nl, off + ln)
            if lo >= hi:
                continue
            p.op("dve", lambda e, b=b, lo=lo, hi=hi, n0=n0, r=r, xm=xm, xo=xo: e.scalar_tensor_tensor(
                out=xo[:, lo:hi], in0=b[:, lo - n0:hi - n0], scalar=site["gt"][:, m, r:r + 1], op0=ALU.mult,
                in1=xm[:, lo:hi], op1=ALU.add), R=[br, xmr, site["res"]], W=[xor_])
    for (u0, ln, off, r) in tile:
        p.dma("sp", lambda e, u0=u0, ln=ln, off=off, xo=xo: e.dma_start(out=XU[m, :, u0:u0 + ln], in_=xo[:, off:off + ln]), R=[xor_], W=([xres] if xres is not None else []))


TMAX = 640
TMAX_FFN = 1152
TMAX_BIG = 1280


class ModStream:
    def __init__(self, k, XU, tile, site, xc_ring, tmp_ring, rstd, rstd_res, stat_banks, sq_ring=None):
        self.sq_ring = sq_ring
        self.k, self.XU, self.tile, self.site = k, XU, tile, site
        self.T = sum(s[1] for s in tile)
        self.xc_ring, self.tmp_ring = xc_ring, tmp_ring
        self.rstd, self.rstd_res = rstd, rstd_res
        self.subs = split_even(self.T, 512)
        self.banks = stat_banks[:len(self.subs)]
        self.sq = {}

    def _load(self, kc):
        p = self.k.p
        xc, xcr = self.xc_ring.next()
        for (u0, ln, off, r) in self.tile:
            p.dma("sp", lambda e, xc=xc, u0=u0, ln=ln, off=off, kc=kc: e.dma_start(out=xc[:, off:off + ln], in_=self.XU[kc, :, u0:u0 + ln]), W=[xcr])
        return xc, xcr

    def load_sq(self, kc):
        p, T = self.k.p, self.T
        xc, xcr = self._load(kc)
        sq, sqr = (self.sq_ring or self.tmp_ring).next()
        p.op("act", lambda e, sq=sq, xc=xc, T=T: e.activation(out=sq[:, 0:T], in_=xc[:, 0:T], func=AF.Square), R=[xcr], W=[sqr])
        self.sq[kc] = (sq, sqr)

    def mm(self, kc):
        p, k = self.k.p, self.k
        sq, sqr = self.sq.pop(kc)
        for (n0, nl), (b, br) in zip(self.subs, self.banks):
            p.op("pe", lambda e, b=b, sq=sq, n0=n0, nl=nl, kc=kc: e.matmul(b[:, 0:nl], lhsT=(k.ones16 if self.sq_ring is not None else k.ones32), rhs=sq[:, n0:n0 + nl], start=(kc == 0), stop=(kc == KC - 1)),
                 R=[sqr, k.cres], W=[br])

    def finish(self, h, h_res):
        self.finish_rstd()
        for kc in range(KC):
            self.h_chunk(kc, h, h_res)

    def finish_rstd(self):
        p, k, T, rstd, rstd_res, site = self.k.p, self.k, self.T, self.rstd, self.rstd_res, self.site
        for (n0, nl), (b, br) in zip(self.subs, self.banks):
            p.op("dve", lambda e, b=b, n0=n0, nl=nl: e.tensor_scalar(out=rstd[:, n0:n0 + nl], in0=b[:, 0:nl], scalar1=1.0 / D, scalar2=EPS, op0=ALU.mult, op1=ALU.add),
                 R=[br], W=[rstd_res])
        p.op("act", lambda e: e.activation(out=rstd[:, 0:T], in_=rstd[:, 0:T], func=AF.Sqrt), R=[rstd_res], W=[rstd_res])
        p.op("dve", lambda e: e.reciprocal(out=rstd[:, 0:T], in_=rstd[:, 0:T]), R=[rstd_res], W=[rstd_res])

    def h_chunk(self, kc, h, h_res):
        p, k, T, rstd, rstd_res, site = self.k.p, self.k, self.T, self.rstd, self.rstd_res, self.site
        i = 0
        if True:
            xc, xcr = self._load(kc)
            for (u0, ln, off, r) in self.tile:
                t, tr = self.tmp_ring.next()
                p.op("dve", lambda e, t=t, xc=xc, kc=kc, off=off, ln=ln, r=r: e.scalar_tensor_tensor(
                    out=t[:, 0:ln], in0=xc[:, off:off + ln], scalar=site["gs"][:, kc, r:r + 1], op0=ALU.mult,
                    in1=rstd[:, off:off + ln], op1=ALU.mult), R=[xcr, rstd_res, site["res"]], W=[tr])
                if True:
                    p.op("act", lambda e, t=t, kc=kc, off=off, ln=ln, r=r: e.activation(
                        out=h[:, kc, off:off + ln], in_=t[:, 0:ln], func=AF.Identity, bias=site["sh"][:, kc, r:r + 1], scale=1.0),
                        R=[tr, k.modt_res], W=[h_res])
                else:
                    p.op("pool", lambda e, t=t, kc=kc, off=off, ln=ln, r=r: e.tensor_scalar(
                        out=h[:, kc, off:off + ln], in0=t[:, 0:ln], scalar1=site["sh"][:, kc, r:r + 1], scalar2=None, op0=ALU.add),
                        R=[tr, k.modt_res], W=[h_res])
                i += 1


def phase_ffn(k, XU, ranges, w_in, w_out, site):
    p, a = k.p, k.arena
    mk = a.mark()
    NH = 2
    FH = FC // NH
    tiles = make_tiles(ranges, TMAX_FFN)
    TM = max(sum(s[1] for s in t) for t in tiles)
    act = a.alloc([FH, TM], BF16)
    big_res = Res()
    h = a.alloc([KC, TM], BF16)
    h_res = Res()
    rstds = [(a.alloc([TM], F32), Res()), (a.alloc([TM], F32), Res())]
    xc_ring = Ring([a.alloc([TM], F32) for _ in range(2)])
    tmp_ring = Ring([a.alloc([TM], F32) for _ in range(2)])
    sq_ring = Ring([a.alloc([TM], BF16) for _ in range(2)])
    wi_ring = Ring([a.alloc([KC, 256], BF16) for _ in range(2)])
    wo_ring = Ring([a.alloc([FH, 128], BF16) for _ in range(3)])
    sg_ring = Ring([a.alloc([512], F32) for _ in range(2)])
    xo_ring = Ring([a.alloc([TM], F32) for _ in range(2)])
    bk8 = k.bank_ring(range(8))
    bk5 = k.bank_ring(range(5))
    stat_banks = [(k.banks[i], k.bank_res[i]) for i in (5, 6, 7)]
    w_in_v = w_in.rearrange("(k p) n -> p k n", p=128)
    w_out_v = w_out.rearrange("(j p) n -> p j n", p=128)

    def new_ms(ti):
        rs, rr = rstds[ti % 2]
        return ModStream(k, XU, tiles[ti], site, xc_ring, tmp_ring, rs, rr, stat_banks, sq_ring=sq_ring)

    mss = {0: new_ms(0)}
    for kc in range(KC):
        mss[0].load_sq(kc)
        mss[0].mm(kc)
    mss[0].finish_rstd()
    for kc in range(KC):
        mss[0].h_chunk(kc, h, h_res)
    for ti, tile in enumerate(tiles):
        T = sum(s[1] for s in tile)
        subs = split_even(T, 512)
        xu_res = [Res() for _ in range(KC)]
        for hf in range(NH):
            for jl in range(FH):
                j = hf * FH + jl
                wi, wir = wi_ring.next()
                p.dma("pool", lambda e, wi=wi, j=j: e.dma_start(out=wi[:, :, 0:128], in_=w_in_v[:, :, j * 128:(j + 1) * 128]), W=[wir])
                p.dma("pool", lambda e, wi=wi, j=j: e.dma_start(out=wi[:, :, 128:256], in_=w_in_v[:, :, DFF + j * 128:DFF + (j + 1) * 128]), W=[wir])
                for (n0, nl) in subs:
                    g_, gr = bk8.next()
                    u_, ur = bk8.next()
                    for kc in range(KC):
                        p.op("pe", lambda e, g_=g_, wi=wi, kc=kc, n0=n0, nl=nl: e.matmul(g_[:, 0:nl], lhsT=wi[:, kc, 0:128], rhs=h[:, kc, n0:n0 + nl], start=(kc == 0), stop=(kc == KC - 1)),
                             R=[wir, h_res], W=[gr])
                        p.op("pe", lambda e, u_=u_, wi=wi, kc=kc, n0=n0, nl=nl: e.matmul(u_[:, 0:nl], lhsT=wi[:, kc, 128:256], rhs=h[:, kc, n0:n0 + nl], start=(kc == 0), stop=(kc == KC - 1)),
                             R=[wir, h_res], W=[ur])
                    sg, sgr = sg_ring.next()
                    p.op("act", lambda e, sg=sg, g_=g_, nl=nl: e.activation(out=sg[:, 0:nl], in_=g_[:, 0:nl], func=AF.Silu), R=[gr], W=[sgr])
                    p.op("dve", lambda e, sg=sg, u_=u_, n0=n0, nl=nl, jl=jl: e.tensor_tensor(out=act[:, jl, n0:n0 + nl], in0=u_[:, 0:nl], in1=sg[:, 0:nl], op=ALU.mult),
                         R=[ur, sgr, gr], W=[big_res])
            last = (hf == NH - 1)
            nms = None
            if hf == 0 and ti + 1 < len(tiles):
                nms = mss[ti + 1] = new_ms(ti + 1)
            hms = mss.get(ti + 1) if last else None
            for m in range(KC):
                if hms is not None:
                    hms.h_chunk(m, h, h_res)
                if nms is not None:
                    nms.load_sq(m)
                wo, wor = wo_ring.next()
                p.dma("pool", lambda e, wo=wo, m=m, hf=hf: e.dma_start(out=wo, in_=w_out_v[:, hf * FH:(hf + 1) * FH, m * 128:(m + 1) * 128]), W=[wor])
                ob = [bk5.next() for _ in subs]
                for jl in range(FH):
                    for (n0, nl), (b, br) in zip(subs, ob):
                        p.op("pe", lambda e, b=b, wo=wo, jl=jl, n0=n0, nl=nl: e.matmul(b[:, 0:nl], lhsT=wo[:, jl, :], rhs=act[:, jl, n0:n0 + nl], start=(jl == 0), stop=(jl == FH - 1)),
                             R=[wor, big_res], W=[br])
                if nms is not None and m > 0:
                    nms.mm(m - 1)
                emit_residual(k, XU, tile, m, ob, subs, site, xc_ring, xo_ring, xres=xu_res[m])
            if nms is not None:
                nms.mm(KC - 1)
                nms.finish_rstd()
    p.barrier()
    a.reset(mk)


def fm(v):
    v = np.asarray(v, np.float32)
    lead = v.shape[:-1]
    n = v.shape[-1] // 128
    t = v.reshape(lead + (n, 128))
    t = np.moveaxis(t, -1, 0)
    return np.ascontiguousarray(t)


def core_xin(x, ctx, core):
    b, c = core // 4, core % 4
    lo = c * NOWN - HALO
    hi = (c + 1) * NOWN + HALO
    seq = x.shape[1]
    buf = np.zeros((NU, D), np.float32)
    buf[0:NCTX] = ctx[b]
    a0, a1 = max(lo, 0), min(hi, seq)
    buf[NCTX + (a0 - lo):NCTX + (a1 - lo)] = x[b, a0:a1]
    return buf


def rope_tables(core):
    c = core % 4
    u = np.arange(NU)
    t = c * NOWN - HALO + (u - NCTX)
    t = np.clip(t, 0, 4 * NOWN - 1)
    row = (t // 64).astype(np.float32)
    col = (t % 64).astype(np.float32)
    inv_freq = (np.float32(10000.0) ** (-np.arange(32, dtype=np.float32) / np.float32(32))).astype(np.float32)
    pidx = np.arange(128)
    axis = pidx // 64
    half = (pidx % 64) // 32
    fr = pidx % 32
    pos = np.where(axis[:, None] == 0, row[None, :], col[None, :]).astype(np.float32)
    ang = (pos * inv_freq[fr][:, None]).astype(np.float32)
    C = np.cos(ang).astype(np.float32)
    S = np.sin(ang).astype(np.float32)
    S = np.where(half[:, None] == 0, -S, S).astype(np.float32)
    C[:, :NCTX] = 1.0
    S[:, :NCTX] = 0.0
    return np.ascontiguousarray(C), np.ascontiguousarray(S)


def build_A(upto="all", dbg=False, fused=False):
    order = ["ffn1", "abin", "abmix", "about", "l0", "l1ffn1", "all"]
    lvl = order.index(upto)
    nc = bass.Bass("TRN2", target_bir_lowering=False)
    with ExitStack() as st:
        k = K(nc, st)
        p, a = k.p, k.arena
        xin = k.ext_in("xin", [NU, D])
        k.ext_in("cvT", [128, KC, 2])
        k.ext_in("bmodT", [128, 2, 36 if fused else 144])
        k.ext_in("normT", [128, 2, 3, KC])
        layers = [0, 1] if lvl >= 5 else [0]
        w_mod = {l: k.ext_in(f"w_mod{l}", [D, 9 * D // 4 if fused else 9 * D]) for l in layers}
        f1i = {l: k.ext_in(f"ffn1_w_in{l}", [D, 2 * DFF]) for l in layers}
        f1o = {l: k.ext_in(f"ffn1_w_out{l}", [DFF, D]) for l in layers}
        XU = k.ext_out("XU", [KC, 128, NU]) if dbg else k.scratch("XU", [KC, 128, NU])
        modt_out = None if fused else k.ext_out("MODT", [128, 2, 144, 2])
        names = [("cvT", [KC, 2]), ("bmodT", [2, 36 if fused else 144]), ("normT", [2, 3, KC])]
        if lvl >= 1:
            ab_w_in = k.ext_in("ab_w_in", [D, 4608])
            k.ext_in("ropeC", [128, NU])
            k.ext_in("ropeS", [128, NU])
            QU = k.scratch("QU", [8, 128, NU], BF16)
            KU = k.scratch("KU", [2, 128, NU], BF16)
            VU = k.scratch("VU", [NU, 256], BF16)
            ZU = k.scratch("ZU", [8, 128, NU])
            BGU = k.scratch("BGU", [8, 128, NU])
        if lvl >= 2:
            k.ext_in("masks", [128, 2, 128])
            k.ext_in("flags", [128, 2])
            k.ext_in("sinkT", [128, 8])
            k.ext_in("convT", [128, 3, 8])
            MIXU = k.ext_out("MIXU", [KC, 128, NU], BF16) if dbg else k.scratch("MIXU", [KC, 128, NU], BF16)
        if lvl >= 3:
            ab_w_out = k.ext_in("ab_w_out", [D, D])
        if lvl >= 4:
            f2i0 = k.ext_in("ffn2_w_in0", [D, 2 * DFF])
            f2o0 = k.ext_in("ffn2_w_out0", [DFF, D])
        if lvl >= 6 and not fused:
            rg_w_in = k.ext_in("rg_w_in", [D, 2 * D])
            XOWN = k.ext_out("XOWN", [KC, 128, NOWN])
            GOWN = k.ext_out("GOWN", [KC, 128, NOWN], BF16)
            UOWN = k.ext_out("UOWN", [KC, 128, NOWN])
            UCTX = k.ext_out("UCTX", [KC, 128, NCTX])
        if fused:
            rg_w_in = k.ext_in("rg_w_in", [D, 2 * D])
            GOWN = k.scratch("GOWN", [KC, 128, NOWN], BF16)
            UOWN = k.scratch("UOWN", [KC, 128, NOWN])
            UCTX = k.scratch("UCTX", [KC, 128, NCTX])
            MIX1 = k.scratch("MIX1", [KC, 128, NOWN], BF16)
            PUB1 = k.scratch("PUB1", [128 * 3, KC])
            GAT1 = k.scratch("GAT1", [4 * 128 * 3, KC])
            PUB2 = k.scratch("PUB2", [D, 4])
            GAT2 = k.scratch("GAT2", [4 * D, 4])
            k.ext_in("rgconvT", [128, KC, 5])
            k.ext_in("rgbaT", [128, 2, KC])
            k.ext_in("rgbxT", [128, 2, KC])
            k.ext_in("rglamT", [128, 2, KC])
            k.ext_in("xfl", [128, 4, 4])
            rg_wa = k.ext_in("rg_w_a", [2, KC, 128, 128])
            rg_wx = k.ext_in("rg_w_x", [2, KC, 128, 128])
            rg_w_out = k.ext_in("rg_w_out", [D, D])
            f2i1 = k.ext_in("ffn2_w_in1", [D, 2 * DFF])
            f2o1 = k.ext_in("ffn2_w_out1", [DFF, D])
            gfull = k.ext_in("gfull", [128, D])
            out = k.ext_out("out", [NOWN, D])
            names += [("xfl", [4, 4])]
        make_basic_consts(k)
        cs = load_consts(k, names)
        modt = a.alloc([2, 144, 2], F32)
        k.modt_res = Res()
        m0 = a.mark()
        phase_transpose_in(k, xin, XU, barrier=not fused)
        if not fused:
            phase_mod(k, w_mod, cs["cvT"], cs["bmodT"], modt, layers)
            p.dma("sp", lambda e: e.dma_start(out=modt_out, in_=modt), R=[k.modt_res])
        else:
            PUBM = k.scratch("PUBM", [128, 144])
            GATM = k.scratch("GATM", [512, 144])
            modq = a.alloc([2, 36, 2], F32)
            phase_mod(k, w_mod, cs["cvT"], cs["bmodT"], modq, layers, njg=9)
            p.dma("sp", lambda e: e.dma_start(out=PUBM.rearrange("p (l j r) -> p l j r", l=2, r=2), in_=modq), R=[k.modt_res])
            p.barrier()
            emit_allgather(k, PUBM, GATM)
            p.barrier()
            for rk in range(4):
                for l in range(2):
                    p.dma("sp", lambda e, rk=rk, l=l: e.dma_start(out=modt[:, l, rk * 36:(rk + 1) * 36, :],
                                                                   in_=GATM[rk * 128:(rk + 1) * 128, l * 72:(l + 1) * 72].rearrange("p (j r) -> p j r", r=2)), W=[k.modt_res])
            p.barrier()
            a.reset(m0)
        sites = {(l, s): derive_site(k, modt, cs["normT"], l, s, half=(s != 1)) for l in layers for s in range(3)}
        phase_ffn(k, XU, [(0, NU)], f1i[0], f1o[0], sites[(0, 0)])
        if lvl >= 1:
            phase_ab_in(k, XU, ab_w_in, sites[(0, 1)], QU, KU, VU, ZU, BGU)
        if lvl >= 2:
            phase_ab_mix(k, QU, KU, VU, ZU, BGU, MIXU)
        own_ctx = [(0, NCTX), (U_OWN, U_OWN + NOWN)]
        if lvl >= 3:
            phase_outproj(k, XU, lambda c, u0, ln: MIXU[c, :, u0:u0 + ln], own_ctx, ab_w_out, sites[(0, 1)])
        if lvl >= 4:
            phase_ffn(k, XU, own_ctx, f2i0, f2o0, sites[(0, 2)])
        if lvl >= 5:
            phase_ffn(k, XU, own_ctx, f1i[1], f1o[1], sites[(1, 0)])
        if fused:
            edges = a.alloc([3, KC], F32)
            carF = a.alloc([4, KC, 2], F32)
            carB = a.alloc([4, KC, 2], F32)
            edges_res, car_res = Res(), Res()
            ctxfin = a.alloc([KC, 2], F32)
            ctxfin_res = Res()
            ABS = k.scratch("ABS", [4, KC, 128, NOWN])
        if lvl >= 6:
            def g_of(c, u0, ln):
                return None if u0 < NCTX else GOWN[c, :, u0 - U_OWN:u0 - U_OWN + ln]

            def u_of(c, u0, ln):
                return UCTX[c, :, u0:u0 + ln] if u0 < NCTX else UOWN[c, :, u0 - U_OWN:u0 - U_OWN + ln]
            phase_rg_in(k, XU, rg_w_in, sites[(1, 1)], g_of, u_of)
            if not fused:
                for c in range(KC):
                    p.dma("sp", lambda e, c=c: e.dma_start(out=XOWN[c], in_=XU[c, :, U_OWN:U_OWN + NOWN]))
        if fused:
            phase_exchange_edges(k, UOWN, PUB1, GAT1, cs["xfl"], edges, edges_res)
            phase_rglru_carry(k, UOWN, UCTX, rg_wa, rg_wx, PUB2.rearrange("(c p) e -> p c e", p=128), (edges, edges_res), ABS, (ctxfin, ctxfin_res))
            phase_exchange_carries(k, PUB2, GAT2, cs["xfl"], carF, carB, car_res)
            phase_rglru_apply(k, ABS, (ctxfin, ctxfin_res), ((carF, car_res), (carB, car_res)), GOWN, MIX1)
            own = [(U_OWN, U_OWN + NOWN)]
            phase_outproj(k, XU, lambda c, u0, ln: MIX1[c, :, u0 - U_OWN:u0 - U_OWN + ln], own, rg_w_out, sites[(1, 1)])
            phase_ffn(k, XU, own, f2i1, f2o1, sites[(1, 2)])
            phase_final(k, XU, out, gfull)
        p.final_wait()
        p.emit()
    return nc


def core_inputs_A(inp, core, upto="all"):
    order = ["ffn1", "abin", "abmix", "about", "l0", "l1ffn1", "all"]
    lvl = order.index(upto)
    b, c = core // 4, core % 4
    cv = np.stack([inp["c"][b], inp["c_ctx"]], axis=-1)
    m = {"xin": core_xin(inp["x"], inp["ctx"], core),
         "cvT": np.ascontiguousarray(cv.reshape(KC, 128, 2).transpose(1, 0, 2)),
         "bmodT": fm(inp["b_mod"]),
         "normT": fm(np.stack([inp["norm_ffn1"], inp["norm_mix"], inp["norm_ffn2"]], axis=1))}
    for l in ([0, 1] if lvl >= 5 else [0]):
        m[f"w_mod{l}"] = inp["w_mod"][l]
        m[f"ffn1_w_in{l}"] = inp["ffn1_w_in"][l]
        m[f"ffn1_w_out{l}"] = inp["ffn1_w_out"][l]
    if lvl >= 1:
        m["ab_w_in"] = inp["ab_w_in"][0]
        m["ropeC"], m["ropeS"] = rope_tables(core)
    if lvl >= 2:
        j = np.arange(128)[:, None]
        i = np.arange(128)[None, :]
        m["masks"] = np.ascontiguousarray(np.stack([(j >= i), (j <= i)], axis=1).astype(np.float32))
        m["flags"] = np.ascontiguousarray(np.broadcast_to(np.array([c > 0, c < 3], np.float32)[None, :], (128, 2)))
        m["sinkT"] = np.ascontiguousarray(np.broadcast_to(inp["ab_sink"][0][None, :], (128, 8)).astype(np.float32))
        m["convT"] = np.ascontiguousarray(inp["ab_conv_w"][0].reshape(3, 8, 128).transpose(2, 0, 1))
    if lvl >= 3:
        m["ab_w_out"] = inp["ab_w_out"][0]
    if lvl >= 4:
        m["ffn2_w_in0"] = inp["ffn2_w_in"][0]
        m["ffn2_w_out0"] = inp["ffn2_w_out"][0]
    if lvl >= 6:
        m["rg_w_in"] = inp["rg_w_in"][0]
    return m


def phase_ab_in(k, XU, w_in, site, QU, KU, VU, ZU, BGU):
    p, a = k.p, k.arena
    mk = a.mark()
    tiles = make_tiles([(0, NU)], TMAX_BIG)
    TM = max(sum(s[1] for s in t) for t in tiles)
    h = a.alloc([KC, TM], BF16)
    h_res = Res()
    rstd = a.alloc([TM], F32)
    rstd_res = Res()
    xc_ring = Ring([a.alloc([TM], F32) for _ in range(3)])
    tmp_ring = Ring([a.alloc([TM], F32) for _ in range(3)])
    w_ring = Ring([a.alloc([KC, 128], BF16) for _ in range(4)])
    wsw_ring = Ring([a.alloc([KC, 128], BF16) for _ in range(2)])
    wv = a.alloc([KC, 256], BF16)
    wv_res = Res()
    st16 = Ring([a.alloc([TM], BF16) for _ in range(3)])
    st32 = Ring([a.alloc([TM], F32) for _ in range(3)])
    vst = Ring([a.alloc([256], BF16) for _ in range(2)])
    bk = k.bank_ring(range(8))
    stat_banks = [(k.banks[i], k.bank_res[i]) for i in (5, 6, 7)]
    cs = load_consts(k, [("ropeC", [NU]), ("ropeS", [NU])])
    COS, SIN = cs["ropeC"], cs["ropeS"]
    w_v = w_in.rearrange("(k p) n -> p k n", p=128)
    p.dma("pool", lambda e: e.dma_start(out=wv, in_=w_v[:, :, 1280:1536]), W=[wv_res])
    for tile in tiles:
        T = sum(s[1] for s in tile)
        tu0 = tile[0][0]
        assert T % 128 == 0
        subs = split_even(T, 512)
        ms = ModStream(k, XU, tile, site, xc_ring, tmp_ring, rstd, rstd_res, stat_banks)
        for kc in range(KC):
            ms.load_sq(kc)
            ms.mm(kc)
        ms.finish(h, h_res)
        for j in range(10):
            col0 = j * 128 if j < 8 else 1024 + (j - 8) * 128
            w, wr = w_ring.next()
            p.dma("pool", lambda e, w=w, col0=col0: e.dma_start(out=w, in_=w_v[:, :, col0:col0 + 128]), W=[wr])
            ws, wsr = wsw_ring.next()
            for (d0, s0) in ((0, 32), (32, 0), (64, 96), (96, 64)):
                p.op("pool", lambda e, ws=ws, w=w, d0=d0, s0=s0: e.tensor_copy(out=ws[:, :, d0:d0 + 32], in_=w[:, :, s0:s0 + 32]), R=[wr], W=[wsr])
            st, str_ = st16.next()
            for (n0, nl) in subs:
                ba, bar = bk.next()
                bb, bbr = bk.next()
                for kc in range(KC):
                    p.op("pe", lambda e, ba=ba, w=w, kc=kc, n0=n0, nl=nl: e.matmul(ba[:, 0:nl], lhsT=w[:, kc, :], rhs=h[:, kc, n0:n0 + nl], start=(kc == 0), stop=(kc == KC - 1)),
                         R=[wr, h_res], W=[bar])
                for kc in range(KC):
                    p.op("pe", lambda e, bb=bb, ws=ws, kc=kc, n0=n0, nl=nl: e.matmul(bb[:, 0:nl], lhsT=ws[:, kc, :], rhs=h[:, kc, n0:n0 + nl], start=(kc == 0), stop=(kc == KC - 1)),
                         R=[wsr, h_res], W=[bbr])
                t1, t1r = tmp_ring.next()
                t2, t2r = tmp_ring.next()
                p.op("dve", lambda e, tu0=tu0, t1=t1, ba=ba, n0=n0, nl=nl: e.tensor_tensor(out=t1[:, 0:nl], in0=ba[:, 0:nl], in1=COS[0][:, tu0 + n0:tu0 + n0 + nl], op=ALU.mult),
                     R=[bar, COS[1]], W=[t1r])
                p.op("dve", lambda e, tu0=tu0, t2=t2, bb=bb, n0=n0, nl=nl: e.tensor_tensor(out=t2[:, 0:nl], in0=bb[:, 0:nl], in1=SIN[0][:, tu0 + n0:tu0 + n0 + nl], op=ALU.mult),
                     R=[bbr, SIN[1]], W=[t2r])
                p.op("pool", lambda e, st=st, t1=t1, t2=t2, n0=n0, nl=nl: e.tensor_tensor(out=st[:, n0:n0 + nl], in0=t1[:, 0:nl], in1=t2[:, 0:nl], op=ALU.add),
                     R=[t1r, t2r], W=[str_])
            dst = QU[j] if j < 8 else KU[j - 8]
            p.dma("sp", lambda e, tu0=tu0, st=st, dst=dst, T=T: e.dma_start(out=dst[:, tu0:tu0 + T], in_=st[:, 0:T]), R=[str_])
        for blk in range(T // 128):
            b, br = bk.next()
            for kc in range(KC):
                p.op("pe", lambda e, b=b, kc=kc, blk=blk: e.matmul(b[:, 0:256], lhsT=h[:, kc, blk * 128:(blk + 1) * 128], rhs=wv[:, kc, :], start=(kc == 0), stop=(kc == KC - 1)),
                     R=[wv_res, h_res], W=[br])
            vs, vsr = vst.next()
            p.op("act", lambda e, vs=vs, b=b: e.activation(out=vs, in_=b[:, 0:256], func=AF.Copy), R=[br], W=[vsr])
            p.dma("sp", lambda e, tu0=tu0, vs=vs, blk=blk: e.dma_start(out=VU[tu0 + blk * 128:tu0 + (blk + 1) * 128, :], in_=vs), R=[vsr])
        for cc in range(8):
            ws3 = []
            for base in (1536, 2560, 3584):
                w, wr = w_ring.next()
                p.dma("pool", lambda e, w=w, c0=base + cc * 128: e.dma_start(out=w, in_=w_v[:, :, c0:c0 + 128]), W=[wr])
                ws3.append((w, wr))
            zs, zsr = st32.next()
            bs, bsr = st32.next()
            for (n0, nl) in subs:
                bks = [bk.next() for _ in range(3)]
                for (w, wr), (b, br) in zip(ws3, bks):
                    for kc in range(KC):
                        p.op("pe", lambda e, b=b, w=w, kc=kc, n0=n0, nl=nl: e.matmul(b[:, 0:nl], lhsT=w[:, kc, :], rhs=h[:, kc, n0:n0 + nl], start=(kc == 0), stop=(kc == KC - 1)),
                             R=[wr, h_res], W=[br])
                (bgb, bgr), (cgb, cgr), (ub, ur) = bks
                t1, t1r = tmp_ring.next()
                p.op("act", lambda e, t1=t1, ub=ub, nl=nl: e.activation(out=t1[:, 0:nl], in_=ub[:, 0:nl], func=AF.Copy), R=[ur], W=[t1r])
                p.op("dve", lambda e, zs=zs, cgb=cgb, t1=t1, n0=n0, nl=nl: e.tensor_tensor(out=zs[:, n0:n0 + nl], in0=cgb[:, 0:nl], in1=t1[:, 0:nl], op=ALU.mult),
                     R=[cgr, t1r], W=[zsr])
                p.op("act", lambda e, bs=bs, bgb=bgb, n0=n0, nl=nl: e.activation(out=bs[:, n0:n0 + nl], in_=bgb[:, 0:nl], func=AF.Copy), R=[bgr], W=[bsr])
            p.dma("sp", lambda e, tu0=tu0, zs=zs, cc=cc, T=T: e.dma_start(out=ZU[cc, :, tu0:tu0 + T], in_=zs[:, 0:T]), R=[zsr])
            p.dma("sp", lambda e, tu0=tu0, bs=bs, cc=cc, T=T: e.dma_start(out=BGU[cc, :, tu0:tu0 + T], in_=bs[:, 0:T]), R=[bsr])
    p.barrier()
    a.reset(mk)


def phase_ab_mix(k, QU, KU, VU, ZU, BGU, MIXU):
    p, a = k.p, k.arena
    mk = a.mark()
    cs = load_consts(k, [("masks", [2, 128]), ("flags", [2]), ("sinkT", [8]), ("convT", [3, 8])])
    masks, flags, sinkT, convT = cs["masks"], cs["flags"], cs["sinkT"], cs["convT"]
    scale = 128.0 ** -0.5
    mres = Res()
    mP = a.alloc([4, 128], BF16)
    mN = a.alloc([4, 128], BF16)
    mP0 = a.alloc([4, 128], BF16)
    mNL = a.alloc([4, 128], BF16)
    for hh in range(4):
        p.op("dve", lambda e, hh=hh: e.tensor_copy(out=mP[:, hh, :], in_=masks[0][:, 0, :]), R=[masks[1]], W=[mres])
        p.op("dve", lambda e, hh=hh: e.tensor_copy(out=mN[:, hh, :], in_=masks[0][:, 1, :]), R=[masks[1]], W=[mres])
    p.op("dve", lambda e: e.tensor_scalar(out=mP0, in0=mP, scalar1=flags[0][:, 0:1], scalar2=None, op0=ALU.mult), R=[mres, flags[1]], W=[mres])
    p.op("dve", lambda e: e.tensor_scalar(out=mNL, in0=mN, scalar1=flags[0][:, 1:2], scalar2=None, op0=ALU.mult), R=[mres, flags[1]], W=[mres])
    esink = a.alloc([8], F32)
    p.op("act", lambda e: e.activation(out=esink, in_=sinkT[0], func=AF.Exp), R=[sinkT[1]], W=[mres])
    gsets = [(a.alloc([NU], BF16), a.alloc([NU // 128, 128], BF16), a.alloc([4, NU], BF16), Res()) for _ in range(2)]
    E_ring = Ring([a.alloc([512], BF16) for _ in range(6)])
    rd_ring = Ring([a.alloc([512], F32) for _ in range(2)])
    os_ring = Ring([a.alloc([4, 128], BF16) for _ in range(2)])
    sc_banks = k.bank_ring([0, 1, 2, 3])
    acc_banks = k.bank_ring([4, 5, 6, 7])
    flat = lambda t: t.rearrange("p a b -> p (a b)")
    for g in range(2):
        kU, vU, qU, gres = gsets[g]
        p.dma("sp", lambda e, g=g, kU=kU: e.dma_start(out=kU, in_=KU[g]), W=[gres])
        p.dma("sp", lambda e, g=g, vU=vU: e.dma_start(out=vU, in_=VU[:, g * 128:(g + 1) * 128].rearrange("(b p) d -> p b d", p=128)), W=[gres])
        p.dma("sp", lambda e, g=g, qU=qU: e.dma_start(out=qU, in_=QU[g * 4:(g + 1) * 4].rearrange("h p t -> p h t")), W=[gres])
    for g in range(2):
        kU, vU, qU, gres = gsets[g]
        qblocks = [(0, [(0, None), (1, None)]), (1, [(0, None), (1, None)])]
        for i in range(16):
            qblocks.append((i + 3, [(i + 2, mP0 if i == 0 else mP), (i + 3, None), (i + 4, mNL if i == 15 else mN), (0, None), (1, None)]))
        for ub, keys in qblocks:
            den, denr = acc_banks.next()
            ob, obr = acc_banks.next()
            nk = len(keys)
            for ki, (kb, mask) in enumerate(keys):
                sb, sbr = sc_banks.next()
                p.op("pe", lambda e, sb=sb, kb=kb, ub=ub, kU=kU, qU=qU: e.matmul(sb[:, 0:512], lhsT=kU[:, kb * 128:(kb + 1) * 128], rhs=qU[:, :, ub * 128:(ub + 1) * 128], start=True, stop=True),
                     R=[gres], W=[sbr])
                E, Er = E_ring.next()
                p.op("act", lambda e, E=E, sb=sb: e.activation(out=E, in_=sb[:, 0:512], func=AF.Exp, scale=scale), R=[sbr], W=[Er])
                if mask is not None:
                    p.op("dve", lambda e, E=E, mask=mask: e.tensor_tensor(out=E, in0=E, in1=flat(mask), op=ALU.mult), R=[Er, mres], W=[Er])
                p.op("pe", lambda e, den=den, E=E, ki=ki, nk=nk: e.matmul(den[:, 0:512], lhsT=k.ones16, rhs=E, start=(ki == 0), stop=(ki == nk - 1)),
                     R=[Er, k.cres], W=[denr])
                p.op("pe", lambda e, ob=ob, E=E, kb=kb, ki=ki, nk=nk, vU=vU: e.matmul(ob[:, 0:512], lhsT=vU[:, kb, :], rhs=E, start=(ki == 0), stop=(ki == nk - 1)),
                     R=[Er, gres], W=[obr])
            rd, rdr = rd_ring.next()
            for hh in range(4):
                p.op("dve", lambda e, rd=rd, den=den, hh=hh, g=g: e.tensor_scalar(out=rd[:, hh * 128:(hh + 1) * 128], in0=den[:, hh * 128:(hh + 1) * 128],
                                                                                   scalar1=esink[:, g * 4 + hh:g * 4 + hh + 1], scalar2=None, op0=ALU.add),
                     R=[denr, mres], W=[rdr])
            p.op("dve", lambda e, rd=rd: e.reciprocal(out=rd, in_=rd), R=[rdr], W=[rdr])
            os_, osr = os_ring.next()
            p.op("dve", lambda e, os_=os_, ob=ob, rd=rd: e.tensor_tensor(out=flat(os_), in0=ob[:, 0:512], in1=rd, op=ALU.mult), R=[obr, rdr], W=[osr])
            p.dma("sp", lambda e, os_=os_, g=g, ub=ub: e.dma_start(out=MIXU[g * 4:(g + 1) * 4, :, ub * 128:(ub + 1) * 128].rearrange("h p t -> p h t"), in_=os_), R=[osr])
    zb_ring = Ring([a.alloc([NU + 2], F32) for _ in range(2)])
    bg_ring = Ring([a.alloc([NU], F32) for _ in range(2)])
    acc = a.alloc([NOWN], F32)
    accr = Res()
    zc = a.alloc([NCTX + 2], F32)
    accc = a.alloc([NCTX], F32)
    cst_ring = Ring([a.alloc([NU], BF16) for _ in range(2)])
    p.op("pool", lambda e: e.memset(zc, 0.0), W=[accr])
    for cc in range(8):
        zb, zbr = zb_ring.next()
        bg, bgr = bg_ring.next()
        p.dma("sp", lambda e, zb=zb, cc=cc: e.dma_start(out=zb[:, 1:NU + 1], in_=ZU[cc]), W=[zbr])
        p.dma("sp", lambda e, bg=bg, cc=cc: e.dma_start(out=bg, in_=BGU[cc]), W=[bgr])
        cst, cstr = cst_ring.next()
        w = lambda kk, cc=cc: convT[0][:, kk, cc:cc + 1]
        p.op("pool", lambda e, zb=zb: e.tensor_copy(out=zc[:, 1:NCTX + 1], in_=zb[:, 1:NCTX + 1]), R=[zbr], W=[accr])
        p.op("dve", lambda e, w=w: e.tensor_scalar(out=accc, in0=zc[:, 0:NCTX], scalar1=w(0), scalar2=None, op0=ALU.mult), R=[accr, convT[1]], W=[accr])
        for kk in (1, 2):
            p.op("dve", lambda e, w=w, kk=kk: e.scalar_tensor_tensor(out=accc, in0=zc[:, kk:kk + NCTX], scalar=w(kk), op0=ALU.mult, in1=accc, op1=ALU.add), R=[accr, convT[1]], W=[accr])
        p.op("dve", lambda e, cst=cst, bg=bg: e.tensor_tensor(out=cst[:, 0:NCTX], in0=accc, in1=bg[:, 0:NCTX], op=ALU.mult), R=[accr, bgr], W=[cstr])
        p.op("dve", lambda e, zb=zb: e.tensor_scalar(out=zb[:, U_OWN:U_OWN + 1], in0=zb[:, U_OWN:U_OWN + 1], scalar1=flags[0][:, 0:1], scalar2=None, op0=ALU.mult), R=[zbr, flags[1]], W=[zbr])
        p.op("dve", lambda e, zb=zb: e.tensor_scalar(out=zb[:, U_OWN + NOWN + 1:U_OWN + NOWN + 2], in0=zb[:, U_OWN + NOWN + 1:U_OWN + NOWN + 2], scalar1=flags[0][:, 1:2], scalar2=None, op0=ALU.mult), R=[zbr, flags[1]], W=[zbr])
        p.op("dve", lambda e, zb=zb, w=w: e.tensor_scalar(out=acc, in0=zb[:, U_OWN:U_OWN + NOWN], scalar1=w(0), scalar2=None, op0=ALU.mult), R=[zbr, convT[1]], W=[accr])
        for kk in (1, 2):
            p.op("dve", lambda e, zb=zb, w=w, kk=kk: e.scalar_tensor_tensor(out=acc, in0=zb[:, U_OWN + kk:U_OWN + kk + NOWN], scalar=w(kk), op0=ALU.mult, in1=acc, op1=ALU.add), R=[zbr, accr, convT[1]], W=[accr])
        p.op("dve", lambda e, cst=cst, bg=bg: e.tensor_tensor(out=cst[:, U_OWN:U_OWN + NOWN], in0=acc, in1=bg[:, U_OWN:U_OWN + NOWN], op=ALU.mult), R=[accr, bgr], W=[cstr])
        p.dma("sp", lambda e, cst=cst, cc=cc: e.dma_start(out=MIXU[8 + cc, :, 0:NCTX], in_=cst[:, 0:NCTX]), R=[cstr])
        p.dma("sp", lambda e, cst=cst, cc=cc: e.dma_start(out=MIXU[8 + cc, :, U_OWN:U_OWN + NOWN], in_=cst[:, U_OWN:U_OWN + NOWN]), R=[cstr])
    p.barrier()
    a.reset(mk)


def phase_outproj(k, XU, mix_of, ranges, w_out, site):
    p, a = k.p, k.arena
    mk = a.mark()
    tiles = make_tiles(ranges, TMAX_BIG)
    TM = max(sum(s[1] for s in t) for t in tiles)
    mix = a.alloc([KC, TM], BF16)
    mix_res = Res()
    w_ring = Ring([a.alloc([KC, 128], BF16) for _ in range(3)])
    xm_ring = Ring([a.alloc([TM], F32) for _ in range(2)])
    xo_ring = Ring([a.alloc([TM], F32) for _ in range(2)])
    bk = k.bank_ring(range(8))
    w_v = w_out.rearrange("(k p) n -> p k n", p=128)
    for tile in tiles:
        T = sum(s[1] for s in tile)
        subs = split_even(T, 512)
        for (u0, ln, off, r) in tile:
            for c in range(KC):
                p.dma("sp", lambda e, c=c, u0=u0, ln=ln, off=off: e.dma_start(out=mix[:, c, off:off + ln], in_=mix_of(c, u0, ln)), W=[mix_res])
        for m in range(KC):
            w, wr = w_ring.next()
            p.dma("pool", lambda e, w=w, m=m: e.dma_start(out=w, in_=w_v[:, :, m * 128:(m + 1) * 128]), W=[wr])
            ob = [bk.next() for _ in subs]
            for kc in range(KC):
                for (n0, nl), (b, br) in zip(subs, ob):
                    p.op("pe", lambda e, b=b, w=w, kc=kc, n0=n0, nl=nl: e.matmul(b[:, 0:nl], lhsT=w[:, kc, :], rhs=mix[:, kc, n0:n0 + nl], start=(kc == 0), stop=(kc == KC - 1)),
                         R=[wr, mix_res], W=[br])
            emit_residual(k, XU, tile, m, ob, subs, site, xm_ring, xo_ring)
    p.barrier()
    a.reset(mk)


def phase_rg_in(k, XU, w_in, site, g_of, u_of):
    p, a = k.p, k.arena
    mk = a.mark()
    tiles = make_tiles([(0, NCTX), (U_OWN, U_OWN + NOWN)], TMAX_BIG)
    TM = max(sum(s[1] for s in t) for t in tiles)
    h = a.alloc([KC, TM], BF16)
    h_res = Res()
    rstd = a.alloc([TM], F32)
    rstd_res = Res()
    xc_ring = Ring([a.alloc([TM], F32) for _ in range(3)])
    tmp_ring = Ring([a.alloc([TM], F32) for _ in range(3)])
    w_ring = Ring([a.alloc([KC, 128], BF16) for _ in range(4)])
    st16 = Ring([a.alloc([TM], BF16) for _ in range(2)])
    st32 = Ring([a.alloc([TM], F32) for _ in range(2)])
    bk = k.bank_ring(range(8))
    stat_banks = [(k.banks[i], k.bank_res[i]) for i in (5, 6, 7)]
    w_v = w_in.rearrange("(k p) n -> p k n", p=128)
    for tile in tiles:
        T = sum(s[1] for s in tile)
        subs = split_even(T, 512)
        ms = ModStream(k, XU, tile, site, xc_ring, tmp_ring, rstd, rstd_res, stat_banks)
        for kc in range(KC):
            ms.load_sq(kc)
            ms.mm(kc)
        ms.finish(h, h_res)
        for ch in range(32):
            w, wr = w_ring.next()
            p.dma("pool", lambda e, w=w, ch=ch: e.dma_start(out=w, in_=w_v[:, :, ch * 128:(ch + 1) * 128]), W=[wr])
            is_gate = ch < 16
            st, sr = (st16 if is_gate else st32).next()
            for (n0, nl) in subs:
                b, br = bk.next()
                for kc in range(KC):
                    p.op("pe", lambda e, b=b, w=w, kc=kc, n0=n0, nl=nl: e.matmul(b[:, 0:nl], lhsT=w[:, kc, :], rhs=h[:, kc, n0:n0 + nl], start=(kc == 0), stop=(kc == KC - 1)),
                         R=[wr, h_res], W=[br])
                if is_gate:
                    p.op("act", lambda e, st=st, b=b, n0=n0, nl=nl: e.activation(out=st[:, n0:n0 + nl], in_=b[:, 0:nl], func=AF.Gelu), R=[br], W=[sr])
                else:
                    p.op("dve", lambda e, st=st, b=b, n0=n0, nl=nl: e.tensor_copy(out=st[:, n0:n0 + nl], in_=b[:, 0:nl]), R=[br], W=[sr])
            for (u0, ln, off, r) in tile:
                dst = g_of(ch, u0, ln) if is_gate else u_of(ch - 16, u0, ln)
                if dst is None:
                    continue
                p.dma("sp", lambda e, st=st, dst=dst, off=off, ln=ln: e.dma_start(out=dst, in_=st[:, off:off + ln]), R=[sr])
    p.barrier()
    a.reset(mk)


NE = NOWN + 3
NS = NOWN + NCTX


def phase_rglru(k, mode, UE, UCTX, rg_w_a, rg_w_x, CAR=None, GOWN=None, MIX1=None, edges=None, car_tiles=None, UOWN=None, ABS=None, ctxfin=None):
    p, a = k.p, k.arena
    mk = a.mark()
    names = [("rgconvT", [KC, 5]), ("rgbaT", [2, KC]), ("rgbxT", [2, KC]), ("rglamT", [2, KC])]
    if mode == "full" and car_tiles is None:
        names += [("carF", [3, KC, 2]), ("carB", [3, KC, 2])]
    cs = load_consts(k, names)
    convT, baT, bxT, lamT = cs["rgconvT"], cs["rgbaT"], cs["rgbxT"], cs["rglamT"]
    cst = a.alloc([2, KC], F32)
    cres = Res()
    p.op("act", lambda e: e.activation(out=cst, in_=lamT[0], func=AF.Exp, scale=-1.0), R=[lamT[1]], W=[cres])
    p.op("act", lambda e: e.activation(out=cst, in_=cst, func=AF.Ln, bias=1.0, scale=1.0), R=[cres], W=[cres])
    p.op("dve", lambda e: e.tensor_scalar(out=cst, in0=cst, scalar1=-8.0, scalar2=None, op0=ALU.mult), R=[cres], W=[cres])
    wa = a.alloc([2, KC, 128], BF16)
    wx = a.alloc([2, KC, 128], BF16)
    wres = Res()
    p.dma("pool", lambda e: e.dma_start(out=wa, in_=rg_w_a.rearrange("d h i j -> i d h j")), W=[wres])
    p.dma("pool", lambda e: e.dma_start(out=wx, in_=rg_w_x.rearrange("d h i j -> i d h j")), W=[wres])
    ub_ring = Ring([a.alloc([NE], F32) for _ in range(2)])
    uc_ring = Ring([a.alloc([NCTX + 3], F32) for _ in range(2)])
    for (b_, r_) in uc_ring.bufs:
        p.op("pool", lambda e, b_=b_: e.memset(b_, 0.0), W=[r_])
    ucv = a.alloc([NS], F32)
    ucv_res = Res()
    u16 = a.alloc([NS], BF16)
    u16_res = Res()
    rbs = [a.alloc([NS], F32) for _ in range(2)]
    ibs = [a.alloc([NS], F32) for _ in range(2)]
    tbs = [a.alloc([NS], F32) for _ in range(2)]
    rres = [Res(), Res()]
    ires = [Res(), Res()]
    tress = [Res(), Res()]
    ab = [a.alloc([NS], F32) for _ in range(2)]
    bb = [a.alloc([NS], F32) for _ in range(2)]
    gres = [Res(), Res()]
    hb = [a.alloc([NS], F32) for _ in range(2)]
    hres = [Res(), Res()]
    sm = a.alloc([16], F32)
    smres = Res()
    if mode == "carry":
        car = a.alloc([KC, 4], F32)
        car_res = Res()
    else:
        g_ring = Ring([a.alloc([NOWN], BF16) for _ in range(2)])
        mo_ring = Ring([a.alloc([NOWN], BF16) for _ in range(2)])
        carF, carB = car_tiles if car_tiles is not None else (cs["carF"], cs["carB"])
        nstep = 4 if car_tiles is not None else 3
    bk = k.bank_ring(range(8))
    blocks = split_even(NS, 512)
    for c in range(KC):
        ub, ubr = ub_ring.next()
        ucx, ucxr = uc_ring.next()
        if edges is None:
            p.dma("sp", lambda e, ub=ub, c=c: e.dma_start(out=ub, in_=UE[c]), W=[ubr])
        else:
            p.dma("sp", lambda e, ub=ub, c=c: e.dma_start(out=ub[:, 2:2 + NOWN], in_=UOWN[c]), W=[ubr])
            for (dst_, src_) in ((0, 0), (1, 1), (2 + NOWN, 2)):
                p.op("pool", lambda e, ub=ub, c=c, dst_=dst_, src_=src_: e.tensor_copy(out=ub[:, dst_:dst_ + 1], in_=edges[0][:, src_, c:c + 1]), R=[edges[1]], W=[ubr])
        p.dma("sp", lambda e, ucx=ucx, c=c: e.dma_start(out=ucx[:, 2:2 + NCTX], in_=UCTX[c]), W=[ucxr])
        w = lambda kk, c=c: convT[0][:, c, kk:kk + 1]
        for (src, srcr, o0, n) in ((ub, ubr, 0, NOWN), (ucx, ucxr, NOWN, NCTX)):
            p.op("dve", lambda e, src=src, o0=o0, n=n, w=w: e.tensor_scalar(out=ucv[:, o0:o0 + n], in0=src[:, 0:n], scalar1=w(0), scalar2=w(4), op0=ALU.mult, op1=ALU.add),
                 R=[srcr, convT[1]], W=[ucv_res])
            for kk in (1, 2, 3):
                p.op("dve", lambda e, src=src, o0=o0, n=n, w=w, kk=kk: e.scalar_tensor_tensor(out=ucv[:, o0:o0 + n], in0=src[:, kk:kk + n], scalar=w(kk), op0=ALU.mult, in1=ucv[:, o0:o0 + n], op1=ALU.add),
                     R=[srcr, convT[1], ucv_res], W=[ucv_res])
        p.op("dve", lambda e: e.tensor_copy(out=u16, in_=ucv), R=[ucv_res], W=[u16_res])
        for d in range(2):
            rb, ib, tb = rbs[d], ibs[d], tbs[d]
            rr_, ir_, tres = rres[d], ires[d], tress[d]
            for (n0, nl) in blocks:
                rp, rpr = bk.next()
                ip, ipr = bk.next()
                p.op("pe", lambda e, rp=rp, d=d, c=c, n0=n0, nl=nl: e.matmul(rp[:, 0:nl], lhsT=wa[:, d, c, :], rhs=u16[:, n0:n0 + nl], start=True, stop=True), R=[wres, u16_res], W=[rpr])
                p.op("pe", lambda e, ip=ip, d=d, c=c, n0=n0, nl=nl: e.matmul(ip[:, 0:nl], lhsT=wx[:, d, c, :], rhs=u16[:, n0:n0 + nl], start=True, stop=True), R=[wres, u16_res], W=[ipr])
                p.op("act", lambda e, rp=rp, d=d, c=c, n0=n0, nl=nl, rb=rb: e.activation(out=rb[:, n0:n0 + nl], in_=rp[:, 0:nl], func=AF.Sigmoid, bias=baT[0][:, d, c:c + 1], scale=1.0), R=[rpr, baT[1]], W=[rr_])
                p.op("act", lambda e, ip=ip, d=d, c=c, n0=n0, nl=nl, ib=ib: e.activation(out=ib[:, n0:n0 + nl], in_=ip[:, 0:nl], func=AF.Sigmoid, bias=bxT[0][:, d, c:c + 1], scale=1.0), R=[ipr, bxT[1]], W=[ir_])
            A_, B_ = ab[d], bb[d]
            p.op("act", lambda e, A_=A_, d=d, c=c, rb=rb: e.activation(out=A_, in_=rb, func=AF.Exp, scale=cst[:, d, c:c + 1]), R=[rr_, cres], W=[gres[d]])
            p.op("dve", lambda e, A_=A_, tb=tb: e.tensor_tensor(out=tb, in0=A_, in1=A_, op=ALU.mult), R=[gres[d]], W=[tres])
            p.op("act", lambda e, tb=tb: e.activation(out=tb, in_=tb, func=AF.Sqrt, bias=1.0, scale=-1.0), R=[tres], W=[tres])
            p.op("dve", lambda e, tb=tb, ib=ib: e.tensor_tensor(out=tb, in0=tb, in1=ib, op=ALU.mult), R=[tres, ir_], W=[tres])
            p.op("dve", lambda e, B_=B_, tb=tb: e.tensor_tensor(out=B_, in0=tb, in1=ucv, op=ALU.mult), R=[tres, ucv_res], W=[gres[d]])
        H = hb
        p.op("dve", lambda e: e.tensor_tensor_scan(out=H[0][:, NOWN:NS], data0=ab[0][:, NOWN:NS], data1=bb[0][:, NOWN:NS], initial=0.0, op0=ALU.mult, op1=ALU.add),
             R=[gres[0]], W=[hres[0]])
        p.op("dve", lambda e: e.tensor_tensor_scan(out=H[1][:, NOWN:NS][:, ::-1], data0=ab[1][:, NOWN:NS][:, ::-1], data1=bb[1][:, NOWN:NS][:, ::-1], initial=0.0, op0=ALU.mult, op1=ALU.add),
             R=[gres[1]], W=[hres[1]])
        if mode == "carry":
            p.op("dve", lambda e: e.tensor_tensor_scan(out=H[0][:, 0:NOWN], data0=ab[0][:, 0:NOWN], data1=bb[0][:, 0:NOWN], initial=0.0, op0=ALU.mult, op1=ALU.add),
                 R=[gres[0]], W=[hres[0]])
            p.op("dve", lambda e: e.tensor_tensor_scan(out=H[1][:, 0:NOWN][:, ::-1], data0=ab[1][:, 0:NOWN][:, ::-1], data1=bb[1][:, 0:NOWN][:, ::-1], initial=0.0, op0=ALU.mult, op1=ALU.add),
                 R=[gres[1]], W=[hres[1]])
            p.op("dve", lambda e, c=c: e.tensor_reduce(out=car[:, c, 0:1], in_=ab[0][:, 0:NOWN], axis=AX.X, op=ALU.mult), R=[gres[0]], W=[car_res])
            p.op("dve", lambda e, c=c: e.tensor_copy(out=car[:, c, 1:2], in_=H[0][:, NOWN - 1:NOWN]), R=[hres[0]], W=[car_res])
            p.op("dve", lambda e, c=c: e.tensor_reduce(out=car[:, c, 2:3], in_=ab[1][:, 0:NOWN], axis=AX.X, op=ALU.mult), R=[gres[1]], W=[car_res])
            p.op("dve", lambda e, c=c: e.tensor_copy(out=car[:, c, 3:4], in_=H[1][:, 0:1]), R=[hres[1]], W=[car_res])
            if ABS is not None:
                for d in range(2):
                    p.dma("sp", lambda e, d=d, c=c: e.dma_start(out=ABS[2 * d, c], in_=ab[d][:, 0:NOWN]), R=[gres[d]])
                    p.dma("sp", lambda e, d=d, c=c: e.dma_start(out=ABS[2 * d + 1, c], in_=bb[d][:, 0:NOWN]), R=[gres[d]])
                p.op("dve", lambda e, c=c: e.tensor_copy(out=ctxfin[0][:, c, 0:1], in_=H[0][:, NS - 1:NS]), R=[hres[0]], W=[ctxfin[1]])
                p.op("dve", lambda e, c=c: e.tensor_copy(out=ctxfin[0][:, c, 1:2], in_=H[1][:, NOWN:NOWN + 1]), R=[hres[1]], W=[ctxfin[1]])
        else:
            p.op("dve", lambda e: e.tensor_copy(out=sm[:, 0:1], in_=H[0][:, NS - 1:NS]), R=[hres[0]], W=[smres])
            p.op("dve", lambda e: e.tensor_copy(out=sm[:, 1:2], in_=H[1][:, NOWN:NOWN + 1]), R=[hres[1]], W=[smres])
            for s_ in range(nstep):
                p.op("dve", lambda e, s_=s_, c=c: e.scalar_tensor_tensor(out=sm[:, 0:1], in0=sm[:, 0:1], scalar=carF[0][:, s_, c, 0:1], op0=ALU.mult, in1=carF[0][:, s_, c, 1:2], op1=ALU.add),
                     R=[smres, carF[1]], W=[smres])
                p.op("dve", lambda e, s_=s_, c=c: e.scalar_tensor_tensor(out=sm[:, 1:2], in0=sm[:, 1:2], scalar=carB[0][:, s_, c, 0:1], op0=ALU.mult, in1=carB[0][:, s_, c, 1:2], op1=ALU.add),
                     R=[smres, carB[1]], W=[smres])
            p.op("dve", lambda e: e.tensor_tensor_scan(out=H[0][:, 0:NOWN], data0=ab[0][:, 0:NOWN], data1=bb[0][:, 0:NOWN], initial=sm[:, 0:1], op0=ALU.mult, op1=ALU.add),
                 R=[gres[0], smres], W=[hres[0]])
            p.op("dve", lambda e: e.tensor_tensor_scan(out=H[1][:, 0:NOWN][:, ::-1], data0=ab[1][:, 0:NOWN][:, ::-1], data1=bb[1][:, 0:NOWN][:, ::-1], initial=sm[:, 1:2], op0=ALU.mult, op1=ALU.add),
                 R=[gres[1], smres], W=[hres[1]])
            g_, gr_ = g_ring.next()
            p.dma("sp", lambda e, g_=g_, c=c: e.dma_start(out=g_, in_=GOWN[c]), W=[gr_])
            p.op("pool", lambda e: e.tensor_tensor(out=H[0][:, 0:NOWN], in0=H[0][:, 0:NOWN], in1=H[1][:, 0:NOWN], op=ALU.add), R=[hres[0], hres[1]], W=[hres[0]])
            mo, mor = mo_ring.next()
            p.op("dve", lambda e, mo=mo, g_=g_: e.tensor_tensor(out=mo, in0=H[0][:, 0:NOWN], in1=g_, op=ALU.mult), R=[hres[0], gr_], W=[mor])
            p.dma("sp", lambda e, mo=mo, c=c: e.dma_start(out=MIX1[c], in_=mo), R=[mor])
    if mode == "carry":
        p.dma("sp", lambda e: e.dma_start(out=CAR, in_=car), R=[car_res])
    p.barrier()
    a.reset(mk)


def phase_rglru_apply(k, ABS, ctxfin, car_tiles, GOWN, MIX1):
    p, a = k.p, k.arena
    mk = a.mark()
    carF, carB = car_tiles
    sets = Ring([[a.alloc([NOWN], F32) for _ in range(4)] for _ in range(2)])
    h_ring = Ring([[a.alloc([NOWN], F32) for _ in range(2)] for _ in range(2)])
    g_ring = Ring([a.alloc([NOWN], BF16) for _ in range(2)])
    mo_ring = Ring([a.alloc([NOWN], BF16) for _ in range(2)])
    sm_ring = Ring([a.alloc([2], F32) for _ in range(2)])
    for c in range(KC):
        (af, bf, ab_, bb_), sr = sets.next()
        for i_, t_ in enumerate((af, bf, ab_, bb_)):
            p.dma("sp", lambda e, i_=i_, t_=t_, c=c: e.dma_start(out=t_, in_=ABS[i_, c]), W=[sr])
        g_, gr_ = g_ring.next()
        p.dma("sp", lambda e, g_=g_, c=c: e.dma_start(out=g_, in_=GOWN[c]), W=[gr_])
        sm, smr = sm_ring.next()
        p.op("dve", lambda e, sm=sm, c=c: e.tensor_copy(out=sm, in_=ctxfin[0][:, c, :]), R=[ctxfin[1]], W=[smr])
        for s_ in range(4):
            p.op("dve", lambda e, sm=sm, s_=s_, c=c: e.scalar_tensor_tensor(out=sm[:, 0:1], in0=sm[:, 0:1], scalar=carF[0][:, s_, c, 0:1], op0=ALU.mult, in1=carF[0][:, s_, c, 1:2], op1=ALU.add),
                 R=[smr, carF[1]], W=[smr])
            p.op("dve", lambda e, sm=sm, s_=s_, c=c: e.scalar_tensor_tensor(out=sm[:, 1:2], in0=sm[:, 1:2], scalar=carB[0][:, s_, c, 0:1], op0=ALU.mult, in1=carB[0][:, s_, c, 1:2], op1=ALU.add),
                 R=[smr, carB[1]], W=[smr])
        (hf, hb_), hr = h_ring.next()
        p.op("dve", lambda e, hf=hf, af=af, bf=bf, sm=sm: e.tensor_tensor_scan(out=hf, data0=af, data1=bf, initial=sm[:, 0:1], op0=ALU.mult, op1=ALU.add), R=[sr, smr], W=[hr])
        p.op("dve", lambda e, hb_=hb_, ab_=ab_, bb_=bb_, sm=sm: e.tensor_tensor_scan(out=hb_[:, ::-1], data0=ab_[:, ::-1], data1=bb_[:, ::-1], initial=sm[:, 1:2], op0=ALU.mult, op1=ALU.add), R=[sr, smr], W=[hr])
        p.op("dve", lambda e, hf=hf, hb_=hb_: e.tensor_tensor(out=hf, in0=hf, in1=hb_, op=ALU.add), R=[hr], W=[hr])
        mo, mor = mo_ring.next()
        p.op("pool", lambda e, mo=mo, hf=hf, g_=g_: e.tensor_tensor(out=mo, in0=hf, in1=g_, op=ALU.mult), R=[hr, gr_], W=[mor])
        p.dma("sp", lambda e, mo=mo, c=c: e.dma_start(out=MIX1[c], in_=mo), R=[mor])
    p.barrier()
    a.reset(mk)


RG4 = [[0, 1, 2, 3], [4, 5, 6, 7]]


def emit_allgather(k, src, dst):
    k.p.cc(lambda e: e.collective_compute("AllGather", ALU.bypass, replica_groups=RG4, ins=[src.opt()], outs=[dst.opt()]))


def phase_exchange_edges(k, UOWN, PUB1, GAT1, xfl, edges, edges_res):
    p, a = k.p, k.arena
    mk = a.mark()
    pub = a.alloc([3, KC], F32)
    r = Res()
    for (j, t) in ((0, 0), (1, NOWN - 2), (2, NOWN - 1)):
        p.dma("sp", lambda e, j=j, t=t: e.dma_start(out=pub[:, j, :], in_=UOWN[:, :, t:t + 1].rearrange("c p e -> p (c e)"), allow_slow_non_contiguous=True), W=[r])
    p.dma("sp", lambda e: e.dma_start(out=PUB1.rearrange("(p j) c -> p j c", j=3), in_=pub), R=[r])
    p.barrier()
    emit_allgather(k, PUB1, GAT1)
    p.barrier()
    g = a.alloc([4, 3, KC], F32)
    gr = Res()
    p.dma("sp", lambda e: e.dma_start(out=g, in_=GAT1.rearrange("(r p j) c -> p r j c", r=4, j=3)), W=[gr])
    for (dst_, col, kind) in ((0, 1, 0), (1, 2, 0), (2, 0, 1)):
        for rk in range(4):
            if rk == 0:
                p.op("dve", lambda e, dst_=dst_, col=col, kind=kind, rk=rk: e.tensor_scalar(out=edges[:, dst_, :], in0=g[:, rk, col, :], scalar1=xfl[0][:, kind, rk:rk + 1], scalar2=None, op0=ALU.mult),
                     R=[gr, xfl[1]], W=[edges_res])
            else:
                p.op("dve", lambda e, dst_=dst_, col=col, kind=kind, rk=rk: e.scalar_tensor_tensor(out=edges[:, dst_, :], in0=g[:, rk, col, :], scalar=xfl[0][:, kind, rk:rk + 1], op0=ALU.mult, in1=edges[:, dst_, :], op1=ALU.add),
                     R=[gr, xfl[1], edges_res], W=[edges_res])
    p.barrier()
    a.reset(mk)


def phase_exchange_carries(k, PUB2, GAT2, xfl, carF, carB, car_res):
    p, a = k.p, k.arena
    mk = a.mark()
    emit_allgather(k, PUB2, GAT2)
    p.barrier()
    g = a.alloc([4, KC, 4], F32)
    gr = Res()
    p.dma("sp", lambda e: e.dma_start(out=g, in_=GAT2.rearrange("(r c p) e -> p r c e", r=4, p=128)), W=[gr])
    for rk in range(4):
        for (dst, step, kind, ca, cb) in ((carF, rk, 2, 0, 1), (carB, 3 - rk, 3, 2, 3)):
            fl = xfl[0][:, kind, rk:rk + 1]
            p.op("dve", lambda e, dst=dst, step=step, fl=fl, ca=ca, rk=rk: e.tensor_scalar(out=dst[:, step, :, 0], in0=g[:, rk, :, ca], scalar1=-1.0, scalar2=fl, op0=ALU.add, op1=ALU.mult),
                 R=[gr, xfl[1]], W=[car_res])
            p.op("dve", lambda e, dst=dst, step=step: e.tensor_scalar(out=dst[:, step, :, 0], in0=dst[:, step, :, 0], scalar1=1.0, scalar2=None, op0=ALU.add),
                 R=[car_res], W=[car_res])
            p.op("dve", lambda e, dst=dst, step=step, fl=fl, cb=cb, rk=rk: e.tensor_scalar(out=dst[:, step, :, 1], in0=g[:, rk, :, cb], scalar1=fl, scalar2=None, op0=ALU.mult),
                 R=[gr, xfl[1]], W=[car_res])
    p.barrier()
    a.reset(mk)


def phase_final(k, XU, out, gfull_dram):
    p, a = k.p, k.arena
    mk = a.mark()
    gf = a.alloc([D], F32)
    gfr = Res()
    p.dma("sp", lambda e: e.dma_start(out=gf, in_=gfull_dram), W=[gfr])
    xb_ring = Ring([a.alloc([KC, 128], F32) for _ in range(2)])
    ob_ring = Ring([a.alloc([D], F32) for _ in range(2)])
    junk = a.alloc([512], F32)
    jres = Res()
    ss_ring = Ring([a.alloc([8], F32) for _ in range(2)])
    bk = k.bank_ring(range(8))
    XUv = XU.rearrange("c p t -> p c t")
    for blk in range(NOWN // 128):
        u0 = U_OWN + blk * 128
        xb, xbr = xb_ring.next()
        p.dma("sp", lambda e, xb=xb, u0=u0: e.dma_start(out=xb, in_=XUv[:, :, u0:u0 + 128]), W=[xbr])
        ss, ssr = ss_ring.next()
        banks = [bk.next() for _ in range(4)]
        for g, (b, br) in enumerate(banks):
            for q in range(4):
                c = 4 * g + q
                p.op("pe", lambda e, b=b, xb=xb, c=c, q=q: e.transpose(b[:, q * 128:(q + 1) * 128], xb[:, c, :], k.ident), R=[xbr, k.cres], W=[br])
            p.op("act", lambda e, b=b, ss=ss, g=g: e.activation(out=junk, in_=b[:, :], func=AF.Square, accum_out=ss[:, g:g + 1]), R=[br], W=[ssr, jres])
        p.op("dve", lambda e, ss=ss: e.tensor_reduce(out=ss[:, 4:5], in_=ss[:, 0:4], axis=AX.X, op=ALU.add), R=[ssr], W=[ssr])
        p.op("dve", lambda e, ss=ss: e.tensor_scalar(out=ss[:, 5:6], in0=ss[:, 4:5], scalar1=1.0 / D, scalar2=EPS, op0=ALU.mult, op1=ALU.add), R=[ssr], W=[ssr])
        p.op("act", lambda e, ss=ss: e.activation(out=ss[:, 6:7], in_=ss[:, 5:6], func=AF.Sqrt), R=[ssr], W=[ssr])
        p.op("dve", lambda e, ss=ss: e.reciprocal(out=ss[:, 7:8], in_=ss[:, 6:7]), R=[ssr], W=[ssr])
        ob, obr = ob_ring.next()
        for g, (b, br) in enumerate(banks):
            p.op("dve", lambda e, b=b, ob=ob, ss=ss, g=g: e.scalar_tensor_tensor(out=ob[:, g * 512:(g + 1) * 512], in0=b[:, :], scalar=ss[:, 7:8], op0=ALU.mult,
                                                                                 in1=gf[:, g * 512:(g + 1) * 512], op1=ALU.mult), R=[br, ssr, gfr], W=[obr])
        p.dma("sp", lambda e, ob=ob, blk=blk: e.dma_start(out=out[blk * 128:(blk + 1) * 128, :], in_=ob), R=[obr])
    p.barrier()
    a.reset(mk)


def rg_small_inputs(inp):
    cw = np.concatenate([inp["rg_conv_w"][0], inp["rg_conv_b"][0][None, :]], axis=0)
    return {"rgconvT": np.ascontiguousarray(cw.reshape(5, KC, 128).transpose(2, 1, 0)),
            "rgbaT": fm(inp["rg_b_a"][0]), "rgbxT": fm(inp["rg_b_x"][0]), "rglamT": fm(inp["rg_lambda"][0]),
            "rg_w_a": inp["rg_w_a"][0], "rg_w_x": inp["rg_w_x"][0]}


def build_B():
    nc = bass.Bass("TRN2", target_bir_lowering=False)
    with ExitStack() as st:
        k = K(nc, st)
        UE = k.ext_in("UE", [KC, 128, NE])
        UCTX = k.ext_in("UCTX", [KC, 128, NCTX])
        k.ext_in("rgconvT", [128, KC, 5])
        k.ext_in("rgbaT", [128, 2, KC])
        k.ext_in("rgbxT", [128, 2, KC])
        k.ext_in("rglamT", [128, 2, KC])
        wa = k.ext_in("rg_w_a", [2, KC, 128, 128])
        wx = k.ext_in("rg_w_x", [2, KC, 128, 128])
        CAR = k.ext_out("CAR", [128, KC, 4])
        make_basic_consts(k)
        phase_rglru(k, "carry", UE, UCTX, wa, wx, CAR=CAR)
        k.p.final_wait()
        k.p.emit()
    return nc


def build_C():
    nc = bass.Bass("TRN2", target_bir_lowering=False)
    with ExitStack() as st:
        k = K(nc, st)
        p, a = k.p, k.arena
        UE = k.ext_in("UE", [KC, 128, NE])
        UCTX = k.ext_in("UCTX", [KC, 128, NCTX])
        GOWN = k.ext_in("GOWN", [KC, 128, NOWN], BF16)
        XOWN = k.ext_in("XOWN", [KC, 128, NOWN])
        k.ext_in("MODT", [128, 2, 144, 2])
        k.ext_in("normT", [128, 2, 3, KC])
        k.ext_in("rgconvT", [128, KC, 5])
        k.ext_in("rgbaT", [128, 2, KC])
        k.ext_in("rgbxT", [128, 2, KC])
        k.ext_in("rglamT", [128, 2, KC])
        k.ext_in("carF", [128, 3, KC, 2])
        k.ext_in("carB", [128, 3, KC, 2])
        wa = k.ext_in("rg_w_a", [2, KC, 128, 128])
        wx = k.ext_in("rg_w_x", [2, KC, 128, 128])
        rg_w_out = k.ext_in("rg_w_out", [D, D])
        f2i = k.ext_in("ffn2_w_in1", [D, 2 * DFF])
        f2o = k.ext_in("ffn2_w_out1", [DFF, D])
        gfull = k.ext_in("gfull", [128, D])
        out = k.ext_out("out", [NOWN, D])
        X1 = k.scratch("X1", [KC, 128, NU])
        MIX1 = k.scratch("MIX1", [KC, 128, NOWN], BF16)
        make_basic_consts(k)
        cs = load_consts(k, [("MODT", [2, 144, 2]), ("normT", [2, 3, KC])])
        modt = cs["MODT"][0]
        k.modt_res = cs["MODT"][1]
        for c in range(KC):
            p.dma("sp", lambda e, c=c: e.dma_start(out=X1[c, :, U_OWN:U_OWN + NOWN], in_=XOWN[c]))
        sites = {(1, s): derive_site(k, modt, cs["normT"], 1, s, half=(s != 1)) for s in (1, 2)}
        phase_rglru(k, "full", UE, UCTX, wa, wx, GOWN=GOWN, MIX1=MIX1)
        own = [(U_OWN, U_OWN + NOWN)]
        phase_outproj(k, X1, lambda c, u0, ln: MIX1[c, :, u0 - U_OWN:u0 - U_OWN + ln], own, rg_w_out, sites[(1, 1)])
        phase_ffn(k, X1, own, f2i, f2o, sites[(1, 2)])
        phase_final(k, X1, out, gfull)
        p.final_wait()
        p.emit()
    return nc


_PROGS = {}


def _prog(name, fn):
    if name not in _PROGS:
        _PROGS[name] = fn()
    return _PROGS[name]


def core_inputs_F(inp, core):
    m = core_inputs_A(inp, core, "all")
    m.update(rg_small_inputs(inp))
    rank = core % 4
    q = 9 * D // 4
    for l in range(2):
        m[f"w_mod{l}"] = np.ascontiguousarray(inp["w_mod"][l][:, rank * q:(rank + 1) * q])
    m["bmodT"] = np.ascontiguousarray(fm(inp["b_mod"])[:, :, rank * 36:(rank + 1) * 36])
    xfl = np.zeros((4, 4), np.float32)
    for r in range(4):
        xfl[0, r] = 1.0 if r == rank - 1 else 0.0
        xfl[1, r] = 1.0 if r == rank + 1 else 0.0
        xfl[2, r] = 1.0 if r < rank else 0.0
        xfl[3, r] = 1.0 if r > rank else 0.0
    m["xfl"] = np.ascontiguousarray(np.broadcast_to(xfl[None], (128, 4, 4)))
    m["rg_w_out"] = inp["rg_w_out"][0]
    m["ffn2_w_in1"] = inp["ffn2_w_in"][1]
    m["ffn2_w_out1"] = inp["ffn2_w_out"][1]
    m["gfull"] = np.ascontiguousarray(np.broadcast_to(inp["final_norm"][None, :], (128, D)).astype(np.float32))
    return m


def kernel(**inp):
    inp = {k_: np.asarray(v) for k_, v in inp.items()}
    cores = list(range(NCORES))
    nc = _prog("F", lambda: build_A("all", fused=True))
    res = run_bass_kernel_spmd(nc, [core_inputs_F(inp, c) for c in cores], core_ids=cores).results
    out = np.empty((2, 4 * NOWN, D), np.float32)
    for c in cores:
        out[c // 4, (c % 4) * NOWN:(c % 4 + 1) * NOWN] = res[c]["out"]
    return out


def kernel_unfused(**inp):
    inp = {k_: np.asarray(v) for k_, v in inp.items()}
    cores = list(range(NCORES))
    ncA = _prog("A", lambda: build_A("all"))
    resA = run_bass_kernel_spmd(ncA, [core_inputs_A(inp, c, "all") for c in cores], core_ids=cores).results
    small = rg_small_inputs(inp)
    UE = []
    for c in cores:
        ue = np.zeros((KC, 128, NE), np.float32)
        ue[:, :, 2:2 + NOWN] = resA[c]["UOWN"]
        if c % 4 > 0:
            ue[:, :, 0:2] = resA[c - 1]["UOWN"][:, :, NOWN - 2:NOWN]
        if c % 4 < 3:
            ue[:, :, 2 + NOWN] = resA[c + 1]["UOWN"][:, :, 0]
        UE.append(ue)
    ncB = _prog("B", build_B)
    mapsB = [dict(UE=UE[c], UCTX=resA[c]["UCTX"], **small) for c in cores]
    resB = run_bass_kernel_spmd(ncB, mapsB, core_ids=cores).results
    ident = np.zeros((128, KC, 2), np.float32)
    ident[:, :, 0] = 1.0
    mapsC = []
    normT = fm(np.stack([inp["norm_ffn1"], inp["norm_mix"], inp["norm_ffn2"]], axis=1))
    gfull = np.ascontiguousarray(np.broadcast_to(inp["final_norm"][None, :], (128, D)).astype(np.float32))
    for c in cores:
        b, ci = c // 4, c % 4
        prev = [resB[b * 4 + j]["CAR"][:, :, 0:2] for j in range(ci)]
        nxt = [resB[b * 4 + j]["CAR"][:, :, 2:4] for j in range(3, ci, -1)]
        carF = np.stack([ident] * (3 - len(prev)) + prev, axis=1)
        carB = np.stack([ident] * (3 - len(nxt)) + nxt, axis=1)
        m = dict(UE=UE[c], UCTX=resA[c]["UCTX"], GOWN=resA[c]["GOWN"], XOWN=resA[c]["XOWN"], MODT=resA[c]["MODT"],
                 normT=normT, carF=np.ascontiguousarray(carF), carB=np.ascontiguousarray(carB), rg_w_out=inp["rg_w_out"][0],
                 ffn2_w_in1=inp["ffn2_w_in"][1], ffn2_w_out1=inp["ffn2_w_out"][1], gfull=gfull, **small)
        mapsC.append(m)
    ncC = _prog("C", build_C)
    resC = run_bass_kernel_spmd(ncC, mapsC, core_ids=cores).results
    out = np.empty((2, 4 * NOWN, D), np.float32)
    for c in cores:
        out[c // 4, (c % 4) * NOWN:(c % 4 + 1) * NOWN] = resC[c]["out"]
    return out


def phase_rglru_carry(k, UOWN, UCTX, rg_w_a, rg_w_x, CAR, edges, ABS, ctxfin):
    p, a = k.p, k.arena
    mk = a.mark()
    cs = load_consts(k, [("rgconvT", [KC, 5]), ("rgbaT", [2, KC]), ("rgbxT", [2, KC]), ("rglamT", [2, KC])])
    convT, baT, bxT, lamT = cs["rgconvT"], cs["rgbaT"], cs["rgbxT"], cs["rglamT"]
    cst = a.alloc([2, KC], F32)
    cres = Res()
    p.op("act", lambda e: e.activation(out=cst, in_=lamT[0], func=AF.Exp, scale=-1.0), R=[lamT[1]], W=[cres])
    p.op("act", lambda e: e.activation(out=cst, in_=cst, func=AF.Ln, bias=1.0, scale=1.0), R=[cres], W=[cres])
    p.op("dve", lambda e: e.tensor_scalar(out=cst, in0=cst, scalar1=-8.0, scalar2=None, op0=ALU.mult), R=[cres], W=[cres])
    wa = a.alloc([2, KC, 128], BF16)
    wx = a.alloc([2, KC, 128], BF16)
    wres = Res()
    p.dma("pool", lambda e: e.dma_start(out=wa, in_=rg_w_a.rearrange("d h i j -> i d h j")), W=[wres])
    p.dma("pool", lambda e: e.dma_start(out=wx, in_=rg_w_x.rearrange("d h i j -> i d h j")), W=[wres])
    ub_ring = Ring([a.alloc([NE], F32) for _ in range(2)])
    uc_ring = Ring([a.alloc([NCTX + 3], F32) for _ in range(2)])
    for (b_, r_) in uc_ring.bufs:
        p.op("pool", lambda e, b_=b_: e.memset(b_, 0.0), W=[r_])
    ucv_ring = Ring([a.alloc([NS], F32) for _ in range(2)])
    u16_ring = Ring([a.alloc([NS], BF16) for _ in range(2)])
    rbs = [a.alloc([NS], F32) for _ in range(2)]
    ibs = [a.alloc([NS], F32) for _ in range(2)]
    tbs = [a.alloc([NS], F32) for _ in range(2)]
    rres, ires, tress = [Res(), Res()], [Res(), Res()], [Res(), Res()]
    ab = [a.alloc([NS], F32) for _ in range(2)]
    bb = [a.alloc([NS], F32) for _ in range(2)]
    gres = [Res(), Res()]
    hj = a.alloc([NS], F32)
    hjr = Res()
    car = a.alloc([KC, 4], F32)
    car_res = Res()
    bk = k.bank_ring(range(8))
    blocks = split_even(NS, 512)
    state = {}

    def stage1a(c):
        ub, ubr = ub_ring.next()
        ucx, ucxr = uc_ring.next()
        ucv, ucv_res = ucv_ring.next()
        u16, u16_res = u16_ring.next()
        state[c] = (ucv, ucv_res)
        p.dma("sp", lambda e: e.dma_start(out=ub[:, 2:2 + NOWN], in_=UOWN[c]), W=[ubr])
        for (dst_, src_) in ((0, 0), (1, 1), (2 + NOWN, 2)):
            p.op("pool", lambda e, dst_=dst_, src_=src_: e.tensor_copy(out=ub[:, dst_:dst_ + 1], in_=edges[0][:, src_, c:c + 1]), R=[edges[1]], W=[ubr])
        p.dma("sp", lambda e: e.dma_start(out=ucx[:, 2:2 + NCTX], in_=UCTX[c]), W=[ucxr])
        w = lambda kk: convT[0][:, c, kk:kk + 1]
        for (src, srcr, o0, n) in ((ub, ubr, 0, NOWN), (ucx, ucxr, NOWN, NCTX)):
            p.op("dve", lambda e, src=src, o0=o0, n=n: e.tensor_scalar(out=ucv[:, o0:o0 + n], in0=src[:, 0:n], scalar1=w(0), scalar2=w(4), op0=ALU.mult, op1=ALU.add),
                 R=[srcr, convT[1]], W=[ucv_res])
            for kk in (1, 2, 3):
                p.op("dve", lambda e, src=src, o0=o0, n=n, kk=kk: e.scalar_tensor_tensor(out=ucv[:, o0:o0 + n], in0=src[:, kk:kk + n], scalar=w(kk), op0=ALU.mult, in1=ucv[:, o0:o0 + n], op1=ALU.add),
                     R=[srcr, convT[1], ucv_res], W=[ucv_res])
        p.op("dve", lambda e: e.tensor_copy(out=u16, in_=ucv), R=[ucv_res], W=[u16_res])
        for d in range(2):
            for (n0, nl) in blocks:
                rp, rpr = bk.next()
                ip, ipr = bk.next()
                p.op("pe", lambda e, rp=rp, d=d, n0=n0, nl=nl: e.matmul(rp[:, 0:nl], lhsT=wa[:, d, c, :], rhs=u16[:, n0:n0 + nl], start=True, stop=True), R=[wres, u16_res], W=[rpr])
                p.op("pe", lambda e, ip=ip, d=d, n0=n0, nl=nl: e.matmul(ip[:, 0:nl], lhsT=wx[:, d, c, :], rhs=u16[:, n0:n0 + nl], start=True, stop=True), R=[wres, u16_res], W=[ipr])
                p.op("act", lambda e, rp=rp, d=d, n0=n0, nl=nl: e.activation(out=rbs[d][:, n0:n0 + nl], in_=rp[:, 0:nl], func=AF.Sigmoid, bias=baT[0][:, d, c:c + 1], scale=1.0), R=[rpr, baT[1]], W=[rres[d]])
                p.op("act", lambda e, ip=ip, d=d, n0=n0, nl=nl: e.activation(out=ibs[d][:, n0:n0 + nl], in_=ip[:, 0:nl], func=AF.Sigmoid, bias=bxT[0][:, d, c:c + 1], scale=1.0), R=[ipr, bxT[1]], W=[ires[d]])

    def stage1b(c):
        ucv, ucv_res = state.pop(c)
        for d in range(2):
            p.op("act", lambda e, d=d: e.activation(out=ab[d], in_=rbs[d], func=AF.Exp, scale=cst[:, d, c:c + 1]), R=[rres[d], cres], W=[gres[d]])
        for d in range(2):
            p.op("pool", lambda e, d=d: e.tensor_tensor(out=tbs[d], in0=ab[d], in1=ab[d], op=ALU.mult), R=[gres[d]], W=[tress[d]])
        for d in range(2):
            p.op("act", lambda e, d=d: e.activation(out=tbs[d], in_=tbs[d], func=AF.Sqrt, bias=1.0, scale=-1.0), R=[tress[d]], W=[tress[d]])
        for d in range(2):
            p.op("pool", lambda e, d=d: e.tensor_tensor(out=tbs[d], in0=tbs[d], in1=ibs[d], op=ALU.mult), R=[tress[d], ires[d]], W=[tress[d]])
        for d in range(2):
            p.op("dve", lambda e, d=d: e.tensor_tensor(out=bb[d], in0=tbs[d], in1=ucv, op=ALU.mult), R=[tress[d], ucv_res], W=[gres[d]])

    def stage2(c):
        p.op("dve", lambda e: e.tensor_tensor_scan(out=hj[:, NOWN:NS], data0=ab[0][:, NOWN:NS], data1=bb[0][:, NOWN:NS], initial=0.0, op0=ALU.mult, op1=ALU.add), R=[gres[0]], W=[hjr])
        p.op("dve", lambda e: e.tensor_copy(out=ctxfin[0][:, c, 0:1], in_=hj[:, NS - 1:NS]), R=[hjr], W=[ctxfin[1]])
        p.op("dve", lambda e: e.tensor_tensor_scan(out=hj[:, NOWN:NS][:, ::-1], data0=ab[1][:, NOWN:NS][:, ::-1], data1=bb[1][:, NOWN:NS][:, ::-1], initial=0.0, op0=ALU.mult, op1=ALU.add), R=[gres[1]], W=[hjr])
        p.op("dve", lambda e: e.tensor_copy(out=ctxfin[0][:, c, 1:2], in_=hj[:, NOWN:NOWN + 1]), R=[hjr], W=[ctxfin[1]])
        p.op("dve", lambda e: e.tensor_tensor_scan(out=hj[:, 0:NOWN], data0=ab[0][:, 0:NOWN], data1=bb[0][:, 0:NOWN], initial=0.0, op0=ALU.mult, op1=ALU.add), R=[gres[0]], W=[hjr])
        p.op("dve", lambda e: e.tensor_copy(out=car[:, c, 1:2], in_=hj[:, NOWN - 1:NOWN]), R=[hjr], W=[car_res])
        p.op("dve", lambda e: e.tensor_tensor_scan(out=hj[:, 0:NOWN][:, ::-1], data0=ab[1][:, 0:NOWN][:, ::-1], data1=bb[1][:, 0:NOWN][:, ::-1], initial=0.0, op0=ALU.mult, op1=ALU.add), R=[gres[1]], W=[hjr])
        p.op("dve", lambda e: e.tensor_copy(out=car[:, c, 3:4], in_=hj[:, 0:1]), R=[hjr], W=[car_res])
        p.op("dve", lambda e: e.tensor_reduce(out=car[:, c, 0:1], in_=ab[0][:, 0:NOWN], axis=AX.X, op=ALU.mult), R=[gres[0]], W=[car_res])
        p.op("dve", lambda e: e.tensor_reduce(out=car[:, c, 2:3], in_=ab[1][:, 0:NOWN], axis=AX.X, op=ALU.mult), R=[gres[1]], W=[car_res])
        for d in range(2):
            p.dma("sp", lambda e, d=d: e.dma_start(out=ABS[2 * d, c], in_=ab[d][:, 0:NOWN]), R=[gres[d]])
            p.dma("sp", lambda e, d=d: e.dma_start(out=ABS[2 * d + 1, c], in_=bb[d][:, 0:NOWN]), R=[gres[d]])

    stage1a(0)
    stage1b(0)
    for c in range(KC):
        if c + 1 < KC:
            stage1a(c + 1)
        stage2(c)
        if c + 1 < KC:
            stage1b(c + 1)
    p.dma("sp", lambda e: e.dma_start(out=CAR, in_=car), R=[car_res])
    p.barrier()
    a.reset(mk)
```

```python
import numpy as np
from contextlib import ExitStack
import concourse.bass as bass
import concourse.mybir as mybir
from concourse.bass_utils import run_bass_kernel_spmd

F32 = mybir.dt.float32
BF16 = mybir.dt.bfloat16
AF = mybir.ActivationFunctionType
ALU = mybir.AluOpType
AX = mybir.AxisListType

D = 2048
KC = 16
DFF = 5632
FC = 44
NOWN = 2048
HALO = 128
NCTX = 256
NU = 2560
U_OWN = 384
EPS = 1e-6
NCORES = 8

ENGS = ["pe", "act", "dve", "pool", "sp"]
DMA_POOL = {"sp": 8, "act": 4, "pool": 6}
SAME_ENGINE_SYNC = True
SEM_MAX = 4000


class Res:
    __slots__ = ("w", "r")

    def __init__(self):
        self.w = None
        self.r = {}


class Prog:
    def __init__(self, nc, stack, n_phase_sems):
        self.nc = nc
        self.ops = {e: [] for e in ENGS}
        self.dsem = {q: [stack.enter_context(nc.semaphore(f"d_{q}{i}")) for i in range(k)]
                     for q, k in DMA_POOL.items()}
        self.dcnt = {q: 0 for q in DMA_POOL}
        self.free = [stack.enter_context(nc.semaphore(f"e{i}")) for i in range(n_phase_sems)]
        self.esem = {e: self.free.pop() for e in ENGS}
        self.cnt = {e: 0 for e in ENGS}
        self.last = {e: None for e in ENGS}
        self.ccsem = stack.enter_context(nc.semaphore("ccsem"))
        self.cccnt = 0

    def _deps(self, R, W):
        deps = []
        for r in R:
            if r.w is not None:
                deps.append(r.w)
        for w in W:
            if w.w is not None:
                deps.append(w.w)
            deps.extend(w.r.values())
        return deps

    def _commit(self, tok, R, W):
        for r in R:
            r.r[id(tok[0])] = tok
        for w in W:
            w.w = tok
            w.r = {}

    def op(self, eng, fn, R=(), W=()):
        deps = self._deps(R, W)
        if self.cnt[eng] >= SEM_MAX:
            self.esem[eng] = self.free.pop()
            self.cnt[eng] = 0
        self.cnt[eng] += 1
        tok = (self.esem[eng], self.cnt[eng], eng)
        self.last[eng] = tok
        self._commit(tok, R, W)
        self.ops[eng].append((fn, deps, tok[0], 1))
        return tok

    def dma(self, q, fn, R=(), W=()):
        deps = self._deps(R, W)
        j = self.dcnt[q]
        self.dcnt[q] += 1
        k = len(self.dsem[q])
        sem = self.dsem[q][j % k]
        val = 16 * (j // k + 1)
        if val > 16:
            deps.append((sem, val - 16, "dma"))
        tok = (sem, val, "dma")
        self._commit(tok, R, W)
        self.ops[q].append((fn, deps, sem, 16))
        return tok

    def cc(self, fn, R=(), W=()):
        deps = self._deps(R, W)
        self.cccnt += 1
        tok = (self.ccsem, self.cccnt, "cc")
        self._commit(tok, R, W)
        self.ops["pool"].append((fn, deps, self.ccsem, 1))
        return tok

    def all_tokens(self):
        toks = []
        for e in ENGS:
            if self.last[e] is not None:
                toks.append(self.last[e])
        if self.cccnt > 0:
            toks.append((self.ccsem, self.cccnt, "cc"))
        for q in DMA_POOL:
            k = len(self.dsem[q])
            for i in range(min(k, self.dcnt[q])):
                n_uses = (self.dcnt[q] - 1 - i) // k + 1
                toks.append((self.dsem[q][i], 16 * n_uses, "dma"))
        return toks

    def barrier(self):
        toks = self.all_tokens()
        for e in ENGS:
            self.ops[e].append((None, [(s, v, "x") for (s, v, _) in toks], None, 0))

    def final_wait(self):
        toks = self.all_tokens()
        self.ops["sp"].append((None, [(s, v, "x") for (s, v, _) in toks], None, 0))

    def emit(self):
        nc = self.nc
        with nc.Block() as block:
            def run(e):
                def body(engobj):
                    waited = {}
                    for fn, deps, sem, inc in self.ops[e]:
                        need = {}
                        for (s, v, de) in deps:
                            if de == e and (e == "pe" or not SAME_ENGINE_SYNC):
                                continue
                            key = id(s)
                            if waited.get(key, 0) >= v:
                                continue
                            if key not in need or need[key][1] < v:
                                need[key] = (s, v)
                        for key, (s, v) in need.items():
                            engobj.wait_ge(s, v)
                            waited[key] = v
                        if fn is not None:
                            fn(engobj).then_inc(sem, inc)
                return body
            block.tensor(run("pe"))
            block.scalar(run("act"))
            block.vector(run("dve"))
            block.gpsimd(run("pool"))
            block.sync(run("sp"))


class Arena:
    def __init__(self, nc, stack, nbytes):
        self.t32 = stack.enter_context(nc.sbuf_tensor("arena", [128, nbytes // 4], F32))
        self.t16 = self.t32.bitcast(BF16)
        self.n = nbytes
        self.off = 0

    def alloc(self, shape, dt):
        n = int(np.prod(shape))
        sz = 4 if dt == F32 else 2
        self.off = (self.off + 63) // 64 * 64
        o = self.off
        self.off += n * sz
        assert self.off <= self.n, f"arena overflow {self.off} > {self.n}"
        ap = (self.t32[:, o // 4:o // 4 + n] if dt == F32 else self.t16[:, o // 2:o // 2 + n])
        if len(shape) == 2:
            ap = ap.rearrange("p (a b) -> p a b", a=shape[0])
        elif len(shape) == 3:
            ap = ap.rearrange("p (a b c) -> p a b c", a=shape[0], b=shape[1])
        return ap

    def mark(self):
        return self.off

    def reset(self, m):
        self.off = m


class Ring:
    def __init__(self, bufs):
        self.bufs = [(b, Res()) for b in bufs]
        self.i = 0

    def next(self):
        b = self.bufs[self.i % len(self.bufs)]
        self.i += 1
        return b


def split_even(n, maxlen):
    k = -(-n // maxlen)
    base = n // k
    rem = n - base * k
    out = []
    o = 0
    for i in range(k):
        ln = base + (1 if i < rem else 0)
        out.append((o, ln))
        o += ln
    return out


def make_tiles(ranges, tmax):
    total = sum(b - a for a, b in ranges)
    ntile = -(-total // tmax)
    tl = -(-total // ntile)
    tl = (tl + 1) // 2 * 2
    tiles = []
    cur = []
    curlen = 0
    for a, b in ranges:
        pos = a
        while pos < b:
            lim = b if pos >= NCTX else min(b, NCTX)
            take = min(lim - pos, tl - curlen)
            cur.append((pos, take, curlen, 1 if pos < NCTX else 0))
            curlen += take
            pos += take
            if curlen == tl:
                tiles.append(cur)
                cur = []
                curlen = 0
    if cur:
        tiles.append(cur)
    return tiles


class K:
    def __init__(self, nc, stack, n_phase_sems=80):
        self.nc = nc
        self.st = stack
        self.p = Prog(nc, stack, n_phase_sems)
        self.arena = Arena(nc, stack, 175 * 1024)
        self.banks = [stack.enter_context(nc.psum_tensor(f"bank{i}", [128, 512], F32)) for i in range(8)]
        self.dram = {}
        self.bank_res = [Res() for _ in range(8)]

    def ext_in(self, name, shape, dt=F32):
        t = self.nc.dram_tensor(name, list(shape), dt, kind="ExternalInput").ap()
        self.dram[name] = t
        return t

    def ext_out(self, name, shape, dt=F32):
        t = self.nc.dram_tensor(name, list(shape), dt, kind="ExternalOutput").ap()
        self.dram[name] = t
        return t

    def scratch(self, name, shape, dt=F32):
        t = self.nc.dram_tensor(name, list(shape), dt, kind="Internal").ap()
        self.dram[name] = t
        return t

    def bank_ring(self, idx):
        r = Ring([])
        r.bufs = [(self.banks[i], self.bank_res[i]) for i in idx]
        return r


def load_consts(k, names_shapes):
    out = {}
    for name, shape in names_shapes:
        src = k.dram[name]
        t = k.arena.alloc(shape, F32)
        r = Res()
        pat = {1: None, 2: None, 3: None}
        k.p.dma("sp", lambda e, t=t, src=src: e.dma_start(out=t, in_=src), W=[r])
        out[name] = (t, r)
    return out


def make_basic_consts(k):
    a = k.arena
    p = k.p
    ident = a.alloc([128], F32)
    ones32 = a.alloc([128], F32)
    ones16 = a.alloc([128], BF16)
    r = Res()
    p.op("pool", lambda e: e.memset(ident, 0.0), W=[r])
    p.op("pool", lambda e: e.affine_select(out=ident, in_=ident, compare_op=ALU.not_equal, fill=1.0,
                                           base=0, pattern=[[-1, 128]], channel_multiplier=1), R=[r], W=[r])
    p.op("pool", lambda e: e.memset(ones32, 1.0), W=[r])
    p.op("pool", lambda e: e.memset(ones16, 1.0), W=[r])
    k.ident, k.ones32, k.ones16, k.cres = ident, ones32, ones16, r


def phase_transpose_in(k, xin, XU, barrier=True):
    p, a = k.p, k.arena
    m = a.mark()
    xt = Ring([a.alloc([D], F32) for _ in range(3)])
    xo = Ring([a.alloc([KC, 128], F32) for _ in range(2)])
    bk = k.bank_ring(range(8))
    XUv = XU.rearrange("c p t -> p c t")
    nblk = NU // 128
    loaded = {}

    def t0_load(blk):
        t, tr = xt.next()
        p.dma("sp", lambda e, t=t, blk=blk: e.dma_start(out=t, in_=xin[blk * 128:(blk + 1) * 128, :]), W=[tr])
        loaded[blk] = (t, tr)

    t0_load(0)
    for blk in range(nblk):
        if blk + 1 < nblk:
            t0_load(blk + 1)
        t, tr = loaded.pop(blk)
        o, orr = xo.next()
        for g in range(4):
            b, br = bk.next()
            for q in range(4):
                c = 4 * g + q
                p.op("pe", lambda e, b=b, t=t, c=c, q=q: e.transpose(b[:, q * 128:(q + 1) * 128], t[:, c * 128:(c + 1) * 128], k.ident),
                     R=[tr, k.cres], W=[br])
            eng = "act" if g % 2 == 0 else "dve"
            if eng == "act":
                p.op("act", lambda e, o=o, b=b, g=g: e.activation(out=o[:, 4 * g:4 * g + 4, :].rearrange("p a b -> p (a b)"), in_=b[:, :], func=AF.Copy),
                     R=[br], W=[orr])
            else:
                p.op("dve", lambda e, o=o, b=b, g=g: e.tensor_copy(out=o[:, 4 * g:4 * g + 4, :].rearrange("p a b -> p (a b)"), in_=b[:, :]),
                     R=[br], W=[orr])
        p.dma("sp", lambda e, o=o, blk=blk: e.dma_start(out=XUv[:, :, blk * 128:(blk + 1) * 128], in_=o), R=[orr])
    if barrier:
        p.barrier()
        a.reset(m)


def phase_mod(k, w_mod, cvT, bmodT, modt, layers, njg=36):
    p, a = k.p, k.arena
    m = a.mark()
    sc = a.alloc([KC, 2], F32)
    scr = Res()
    p.op("act", lambda e: e.activation(out=sc, in_=cvT[0], func=AF.Silu), R=[cvT[1]], W=[scr])
    wr = Ring([a.alloc([KC, 512], F32) for _ in range(3)])
    bk = k.bank_ring([0, 1])
    mr = k.modt_res
    n = 0
    for l in layers:
        for jg in range(njg):
            w, wres = wr.next()
            q_ = "sp" if n % 2 == 0 else "act"
            n += 1
            p.dma(q_, lambda e, w=w, l=l, jg=jg: e.dma_start(out=w, in_=w_mod[l][:, jg * 512:(jg + 1) * 512].rearrange("(k p) n -> p k n", p=128)), W=[wres])
            b, br = bk.next()
            for q in range(4):
                for kc in range(KC):
                    p.op("pe", lambda e, b=b, w=w, q=q, kc=kc: e.matmul(b[:, 2 * q:2 * q + 2], lhsT=w[:, kc, q * 128:(q + 1) * 128], rhs=sc[:, kc, :],
                                                                         start=(kc == 0), stop=(kc == KC - 1)),
                         R=[wres, scr], W=[br])
            for q in range(4):
                j = jg * 4 + q
                p.op("dve", lambda e, b=b, q=q, j=j, l=l: e.tensor_scalar(out=modt[:, l, j, :], in0=b[:, 2 * q:2 * q + 2], scalar1=bmodT[0][:, l, j:j + 1], scalar2=None, op0=ALU.add),
                     R=[br, bmodT[1]], W=[mr])
    p.barrier()
    a.reset(m)


def derive_site(k, modt, normT, l, site, half):
    p, a = k.p, k.arena
    gs = a.alloc([KC, 2], F32)
    gt = a.alloc([KC, 2], F32)
    r = Res()
    j0 = 3 * site * KC
    sh = modt[:, l, j0:j0 + KC, :]
    p.op("dve", lambda e: e.tensor_scalar(out=gs, in0=modt[:, l, j0 + KC:j0 + 2 * KC, :], scalar1=1.0, scalar2=None, op0=ALU.add),
         R=[k.modt_res], W=[r])
    for rr in range(2):
        p.op("dve", lambda e, rr=rr: e.tensor_tensor(out=gs[:, :, rr], in0=gs[:, :, rr], in1=normT[0][:, l, site, :], op=ALU.mult),
             R=[r, normT[1]], W=[r])
    p.op("dve", lambda e: e.tensor_scalar(out=gt, in0=modt[:, l, j0 + 2 * KC:j0 + 3 * KC, :], scalar1=(0.5 if half else 1.0), scalar2=None, op0=ALU.mult),
         R=[k.modt_res], W=[r])
    return dict(gs=gs, sh=sh, gt=gt, res=r)


def emit_modulate(k, XU, tile, T, xs, xs_res, h, h_res, site, tmp_ring, rstd, rstd_res, stat_banks):
    p = k.p
    XUv = XU.rearrange("c p t -> p c t")
    for (u0, ln, off, r) in tile:
        p.dma("sp", lambda e, u0=u0, ln=ln, off=off: e.dma_start(out=xs[:, :, off:off + ln], in_=XUv[:, :, u0:u0 + ln]), W=[xs_res])
    subs = split_even(T, 512)
    banks = [stat_banks.next() for _ in subs]
    for kc in range(KC):
        sq, sqr = tmp_ring.next()
        p.op("act", lambda e, sq=sq, kc=kc: e.activation(out=sq[:, 0:T], in_=xs[:, kc, :], func=AF.Square), R=[xs_res], W=[sqr])
        for (n0, nl), (b, br) in zip(subs, banks):
            p.op("pe", lambda e, b=b, sq=sq, n0=n0, nl=nl, kc=kc: e.matmul(b[:, 0:nl], lhsT=k.ones32, rhs=sq[:, n0:n0 + nl], start=(kc == 0), stop=(kc == KC - 1)),
                 R=[sqr, k.cres], W=[br])
    for (n0, nl), (b, br) in zip(subs, banks):
        p.op("dve", lambda e, b=b, n0=n0, nl=nl: e.tensor_scalar(out=rstd[:, n0:n0 + nl], in0=b[:, 0:nl], scalar1=1.0 / D, scalar2=EPS, op0=ALU.mult, op1=ALU.add),
             R=[br], W=[rstd_res])
    p.op("act", lambda e: e.activation(out=rstd[:, 0:T], in_=rstd[:, 0:T], func=AF.Sqrt), R=[rstd_res], W=[rstd_res])
    p.op("dve", lambda e: e.reciprocal(out=rstd[:, 0:T], in_=rstd[:, 0:T]), R=[rstd_res], W=[rstd_res])
    i = 0
    for (u0, ln, off, r) in tile:
        for kc in range(KC):
            t, tr = tmp_ring.next()
            p.op("dve", lambda e, t=t, kc=kc, off=off, ln=ln, r=r: e.scalar_tensor_tensor(
                out=t[:, 0:ln], in0=xs[:, kc, off:off + ln], scalar=site["gs"][:, kc, r:r + 1], op0=ALU.mult,
                in1=rstd[:, off:off + ln], op1=ALU.mult), R=[xs_res, rstd_res, site["res"]], W=[tr])
            if True:
                p.op("act", lambda e, t=t, kc=kc, off=off, ln=ln, r=r: e.activation(
                    out=h[:, kc, off:off + ln], in_=t[:, 0:ln], func=AF.Identity, bias=site["sh"][:, kc, r:r + 1], scale=1.0),
                    R=[tr, k.modt_res], W=[h_res])
            else:
                p.op("pool", lambda e, t=t, kc=kc, off=off, ln=ln, r=r: e.tensor_scalar(
                    out=h[:, kc, off:off + ln], in0=t[:, 0:ln], scalar1=site["sh"][:, kc, r:r + 1], scalar2=None, op0=ALU.add),
                    R=[tr, k.modt_res], W=[h_res])
            i += 1


def emit_residual(k, XU, tile, m, sub_banks, subs, site, xm_ring, xo_ring, xres=None):
    p = k.p
    xm, xmr = xm_ring.next()
    xo, xor_ = xo_ring.next()
    for (u0, ln, off, r) in tile:
        p.dma("sp", lambda e, u0=u0, ln=ln, off=off, xm=xm: e.dma_start(out=xm[:, off:off + ln], in_=XU[m, :, u0:u0 + ln]), R=([xres] if xres is not None else []), W=[xmr])
    for (n0, nl), (b, br) in zip(subs, sub_banks):
        for (u0, ln, off, r) in tile:
            lo, hi = max(n0, off), min(n0 + nl, off + ln)
            if lo >= hi:
                continue
            p.op("dve", lambda e, b=b, lo=lo, hi=hi, n0=n0, r=r, xm=xm, xo=xo: e.scalar_tensor_tensor(
                out=xo[:, lo:hi], in0=b[:, lo - n0:hi - n0], scalar=site["gt"][:, m, r:r + 1], op0=ALU.mult,
                in1=xm[:, lo:hi], op1=ALU.add), R=[br, xmr, site["res"]], W=[xor_])
    for (u0, ln, off, r) in tile:
        p.dma("sp", lambda e, u0=u0, ln=ln, off=off, xo=xo: e.dma_start(out=XU[m, :, u0:u0 + ln], in_=xo[:, off:off + ln]), R=[xor_], W=([xres] if xres is not None else []))


TMAX = 640
TMAX_FFN = 1152
TMAX_BIG = 1280


class ModStream:
    def __init__(self, k, XU, tile, site, xc_ring, tmp_ring, rstd, rstd_res, stat_banks, sq_ring=None):
        self.sq_ring = sq_ring
        self.k, self.XU, self.tile, self.site = k, XU, tile, site
        self.T = sum(s[1] for s in tile)
        self.xc_ring, self.tmp_ring = xc_ring, tmp_ring
        self.rstd, self.rstd_res = rstd, rstd_res
        self.subs = split_even(self.T, 512)
        self.banks = stat_banks[:len(self.subs)]
        self.sq = {}

    def _load(self, kc):
        p = self.k.p
        xc, xcr = self.xc_ring.next()
        for (u0, ln, off, r) in self.tile:
            p.dma("sp", lambda e, xc=xc, u0=u0, ln=ln, off=off, kc=kc: e.dma_start(out=xc[:, off:off + ln], in_=self.XU[kc, :, u0:u0 + ln]), W=[xcr])
        return xc, xcr

    def load_sq(self, kc):
        p, T = self.k.p, self.T
        xc, xcr = self._load(kc)
        sq, sqr = (self.sq_ring or self.tmp_ring).next()
        p.op("act", lambda e, sq=sq, xc=xc, T=T: e.activation(out=sq[:, 0:T], in_=xc[:, 0:T], func=AF.Square), R=[xcr], W=[sqr])
        self.sq[kc] = (sq, sqr)

    def mm(self, kc):
        p, k = self.k.p, self.k
        sq, sqr = self.sq.pop(kc)
        for (n0, nl), (b, br) in zip(self.subs, self.banks):
            p.op("pe", lambda e, b=b, sq=sq, n0=n0, nl=nl, kc=kc: e.matmul(b[:, 0:nl], lhsT=(k.ones16 if self.sq_ring is not None else k.ones32), rhs=sq[:, n0:n0 + nl], start=(kc == 0), stop=(kc == KC - 1)),
                 R=[sqr, k.cres], W=[br])

    def finish(self, h, h_res):
        self.finish_rstd()
        for kc in range(KC):
            self.h_chunk(kc, h, h_res)

    def finish_rstd(self):
        p, k, T, rstd, rstd_res, site = self.k.p, self.k, self.T, self.rstd, self.rstd_res, self.site
        for (n0, nl), (b, br) in zip(self.subs, self.banks):
            p.op("dve", lambda e, b=b, n0=n0, nl=nl: e.tensor_scalar(out=rstd[:, n0:n0 + nl], in0=b[:, 0:nl], scalar1=1.0 / D, scalar2=EPS, op0=ALU.mult, op1=ALU.add),
                 R=[br], W=[rstd_res])
        p.op("act", lambda e: e.activation(out=rstd[:, 0:T], in_=rstd[:, 0:T], func=AF.Sqrt), R=[rstd_res], W=[rstd_res])
        p.op("dve", lambda e: e.reciprocal(out=rstd[:, 0:T], in_=rstd[:, 0:T]), R=[rstd_res], W=[rstd_res])

    def h_chunk(self, kc, h, h_res):
        p, k, T, rstd, rstd_res, site = self.k.p, self.k, self.T, self.rstd, self.rstd_res, self.site
        i = 0
        if True:
            xc, xcr = self._load(kc)
            for (u0, ln, off, r) in self.tile:
                t, tr = self.tmp_ring.next()
                p.op("dve", lambda e, t=t, xc=xc, kc=kc, off=off, ln=ln, r=r: e.scalar_tensor_tensor(
                    out=t[:, 0:ln], in0=xc[:, off:off + ln], scalar=site["gs"][:, kc, r:r + 1], op0=ALU.mult,
                    in1=rstd[:, off:off + ln], op1=ALU.mult), R=[xcr, rstd_res, site["res"]], W=[tr])
                if True:
                    p.op("act", lambda e, t=t, kc=kc, off=off, ln=ln, r=r: e.activation(
                        out=h[:, kc, off:off + ln], in_=t[:, 0:ln], func=AF.Identity, bias=site["sh"][:, kc, r:r + 1], scale=1.0),
                        R=[tr, k.modt_res], W=[h_res])
                else:
                    p.op("pool", lambda e, t=t, kc=kc, off=off, ln=ln, r=r: e.tensor_scalar(
                        out=h[:, kc, off:off + ln], in0=t[:, 0:ln], scalar1=site["sh"][:, kc, r:r + 1], scalar2=None, op0=ALU.add),
                        R=[tr, k.modt_res], W=[h_res])
                i += 1


def phase_ffn(k, XU, ranges, w_in, w_out, site):
    p, a = k.p, k.arena
    mk = a.mark()
    NH = 2
    FH = FC // NH
    tiles = make_tiles(ranges, TMAX_FFN)
    TM = max(sum(s[1] for s in t) for t in tiles)
    act = a.alloc([FH, TM], BF16)
    big_res = Res()
    h = a.alloc([KC, TM], BF16)
    h_res = Res()
    rstds = [(a.alloc([TM], F32), Res()), (a.alloc([TM], F32), Res())]
    xc_ring = Ring([a.alloc([TM], F32) for _ in range(2)])
    tmp_ring = Ring([a.alloc([TM], F32) for _ in range(2)])
    sq_ring = Ring([a.alloc([TM], BF16) for _ in range(2)])
    wi_ring = Ring([a.alloc([KC, 256], BF16) for _ in range(2)])
    wo_ring = Ring([a.alloc([FH, 128], BF16) for _ in range(3)])
    sg_ring = Ring([a.alloc([512], F32) for _ in range(2)])
    xo_ring = Ring([a.alloc([TM], F32) for _ in range(2)])
    bk8 = k.bank_ring(range(8))
    bk5 = k.bank_ring(range(5))
    stat_banks = [(k.banks[i], k.bank_res[i]) for i in (5, 6, 7)]
    w_in_v = w_in.rearrange("(k p) n -> p k n", p=128)
    w_out_v = w_out.rearrange("(j p) n -> p j n", p=128)

    def new_ms(ti):
        rs, rr = rstds[ti % 2]
        return ModStream(k, XU, tiles[ti], site, xc_ring, tmp_ring, rs, rr, stat_banks, sq_ring=sq_ring)

    mss = {0: new_ms(0)}
    for kc in range(KC):
        mss[0].load_sq(kc)
        mss[0].mm(kc)
    mss[0].finish_rstd()
    for kc in range(KC):
        mss[0].h_chunk(kc, h, h_res)
    for ti, tile in enumerate(tiles):
        T = sum(s[1] for s in tile)
        subs = split_even(T, 512)
        xu_res = [Res() for _ in range(KC)]
        for hf in range(NH):
            for jl in range(FH):
                j = hf * FH + jl
                wi, wir = wi_ring.next()
                p.dma("pool", lambda e, wi=wi, j=j: e.dma_start(out=wi[:, :, 0:128], in_=w_in_v[:, :, j * 128:(j + 1) * 128]), W=[wir])
                p.dma("pool", lambda e, wi=wi, j=j: e.dma_start(out=wi[:, :, 128:256], in_=w_in_v[:, :, DFF + j * 128:DFF + (j + 1) * 128]), W=[wir])
                for (n0, nl) in subs:
                    g_, gr = bk8.next()
                    u_, ur = bk8.next()
                    for kc in range(KC):
                        p.op("pe", lambda e, g_=g_, wi=wi, kc=kc, n0=n0, nl=nl: e.matmul(g_[:, 0:nl], lhsT=wi[:, kc, 0:128], rhs=h[:, kc, n0:n0 + nl], start=(kc == 0), stop=(kc == KC - 1)),
                             R=[wir, h_res], W=[gr])
                        p.op("pe", lambda e, u_=u_, wi=wi, kc=kc, n0=n0, nl=nl: e.matmul(u_[:, 0:nl], lhsT=wi[:, kc, 128:256], rhs=h[:, kc, n0:n0 + nl], start=(kc == 0), stop=(kc == KC - 1)),
                             R=[wir, h_res], W=[ur])
                    sg, sgr = sg_ring.next()
                    p.op("act", lambda e, sg=sg, g_=g_, nl=nl: e.activation(out=sg[:, 0:nl], in_=g_[:, 0:nl], func=AF.Silu), R=[gr], W=[sgr])
                    p.op("dve", lambda e, sg=sg, u_=u_, n0=n0, nl=nl, jl=jl: e.tensor_tensor(out=act[:, jl, n0:n0 + nl], in0=u_[:, 0:nl], in1=sg[:, 0:nl], op=ALU.mult),
                         R=[ur, sgr, gr], W=[big_res])
            last = (hf == NH - 1)
            nms = None
            if hf == 0 and ti + 1 < len(tiles):
                nms = mss[ti + 1] = new_ms(ti + 1)
            hms = mss.get(ti + 1) if last else None
            for m in range(KC):
                if hms is not None:
                    hms.h_chunk(m, h, h_res)
                if nms is not None:
                    nms.load_sq(m)
                wo, wor = wo_ring.next()
                p.dma("pool", lambda e, wo=wo, m=m, hf=hf: e.dma_start(out=wo, in_=w_out_v[:, hf * FH:(hf + 1) * FH, m * 128:(m + 1) * 128]), W=[wor])
                ob = [bk5.next() for _ in subs]
                for jl in range(FH):
                    for (n0, nl), (b, br) in zip(subs, ob):
                        p.op("pe", lambda e, b=b, wo=wo, jl=jl, n0=n0, nl=nl: e.matmul(b[:, 0:nl], lhsT=wo[:, jl, :], rhs=act[:, jl, n0:n0 + nl], start=(jl == 0), stop=(jl == FH - 1)),
                             R=[wor, big_res], W=[br])
                if nms is not None and m > 0:
                    nms.mm(m - 1)
                emit_residual(k, XU, tile, m, ob, subs, site, xc_ring, xo_ring, xres=xu_res[m])
            if nms is not None:
                nms.mm(KC - 1)
                nms.finish_rstd()
    p.barrier()
    a.reset(mk)


def fm(v):
    v = np.asarray(v, np.float32)
    lead = v.shape[:-1]
    n = v.shape[-1] // 128
    t = v.reshape(lead + (n, 128))
    t = np.moveaxis(t, -1, 0)
    return np.ascontiguousarray(t)


def core_xin(x, ctx, core):
    b, c = core // 4, core % 4
    lo = c * NOWN - HALO
    hi = (c + 1) * NOWN + HALO
    seq = x.shape[1]
    buf = np.zeros((NU, D), np.float32)
    buf[0:NCTX] = ctx[b]
    a0, a1 = max(lo, 0), min(hi, seq)
    buf[NCTX + (a0 - lo):NCTX + (a1 - lo)] = x[b, a0:a1]
    return buf


def rope_tables(core):
    c = core % 4
    u = np.arange(NU)
    t = c * NOWN - HALO + (u - NCTX)
    t = np.clip(t, 0, 4 * NOWN - 1)
    row = (t // 64).astype(np.float32)
    col = (t % 64).astype(np.float32)
    inv_freq = (np.float32(10000.0) ** (-np.arange(32, dtype=np.float32) / np.float32(32))).astype(np.float32)
    pidx = np.arange(128)
    axis = pidx // 64
    half = (pidx % 64) // 32
    fr = pidx % 32
    pos = np.where(axis[:, None] == 0, row[None, :], col[None, :]).astype(np.float32)
    ang = (pos * inv_freq[fr][:, None]).astype(np.float32)
    C = np.cos(ang).astype(np.float32)
    S = np.sin(ang).astype(np.float32)
    S = np.where(half[:, None] == 0, -S, S).astype(np.float32)
    C[:, :NCTX] = 1.0
    S[:, :NCTX] = 0.0
    return np.ascontiguousarray(C), np.ascontiguousarray(S)


def build_A(upto="all", dbg=False, fused=False):
    order = ["ffn1", "abin", "abmix", "about", "l0", "l1ffn1", "all"]
    lvl = order.index(upto)
    nc = bass.Bass("TRN2", target_bir_lowering=False)
    with ExitStack() as st:
        k = K(nc, st)
        p, a = k.p, k.arena
        xin = k.ext_in("xin", [NU, D])
        k.ext_in("cvT", [128, KC, 2])
        k.ext_in("bmodT", [128, 2, 36 if fused else 144])
        k.ext_in("normT", [128, 2, 3, KC])
        layers = [0, 1] if lvl >= 5 else [0]
        w_mod = {l: k.ext_in(f"w_mod{l}", [D, 9 * D // 4 if fused else 9 * D]) for l in layers}
        f1i = {l: k.ext_in(f"ffn1_w_in{l}", [D, 2 * DFF]) for l in layers}
        f1o = {l: k.ext_in(f"ffn1_w_out{l}", [DFF, D]) for l in layers}
        XU = k.ext_out("XU", [KC, 128, NU]) if dbg else k.scratch("XU", [KC, 128, NU])
        modt_out = None if fused else k.ext_out("MODT", [128, 2, 144, 2])
        names = [("cvT", [KC, 2]), ("bmodT", [2, 36 if fused else 144]), ("normT", [2, 3, KC])]
        if lvl >= 1:
            ab_w_in = k.ext_in("ab_w_in", [D, 4608])
            k.ext_in("ropeC", [128, NU])
            k.ext_in("ropeS", [128, NU])
            QU = k.scratch("QU", [8, 128, NU], BF16)
            KU = k.scratch("KU", [2, 128, NU], BF16)
            VU = k.scratch("VU", [NU, 256], BF16)
            ZU = k.scratch("ZU", [8, 128, NU])
            BGU = k.scratch("BGU", [8, 128, NU])
        if lvl >= 2:
            k.ext_in("masks", [128, 2, 128])
            k.ext_in("flags", [128, 2])
            k.ext_in("sinkT", [128, 8])
            k.ext_in("convT", [128, 3, 8])
            MIXU = k.ext_out("MIXU", [KC, 128, NU], BF16) if dbg else k.scratch("MIXU", [KC, 128, NU], BF16)
        if lvl >= 3:
            ab_w_out = k.ext_in("ab_w_out", [D, D])
        if lvl >= 4:
            f2i0 = k.ext_in("ffn2_w_in0", [D, 2 * DFF])
            f2o0 = k.ext_in("ffn2_w_out0", [DFF, D])
        if lvl >= 6 and not fused:
            rg_w_in = k.ext_in("rg_w_in", [D, 2 * D])
            XOWN = k.ext_out("XOWN", [KC, 128, NOWN])
            GOWN = k.ext_out("GOWN", [KC, 128, NOWN], BF16)
            UOWN = k.ext_out("UOWN", [KC, 128, NOWN])
            UCTX = k.ext_out("UCTX", [KC, 128, NCTX])
        if fused:
            rg_w_in = k.ext_in("rg_w_in", [D, 2 * D])
            GOWN = k.scratch("GOWN", [KC, 128, NOWN], BF16)
            UOWN = k.scratch("UOWN", [KC, 128, NOWN])
            UCTX = k.scratch("UCTX", [KC, 128, NCTX])
            MIX1 = k.scratch("MIX1", [KC, 128, NOWN], BF16)
            PUB1 = k.scratch("PUB1", [128 * 3, KC])
            GAT1 = k.scratch("GAT1", [4 * 128 * 3, KC])
            PUB2 = k.scratch("PUB2", [D, 4])
            GAT2 = k.scratch("GAT2", [4 * D, 4])
            k.ext_in("rgconvT", [128, KC, 5])
            k.ext_in("rgbaT", [128, 2, KC])
            k.ext_in("rgbxT", [128, 2, KC])
            k.ext_in("rglamT", [128, 2, KC])
            k.ext_in("xfl", [128, 4, 4])
            rg_wa = k.ext_in("rg_w_a", [2, KC, 128, 128])
            rg_wx = k.ext_in("rg_w_x", [2, KC, 128, 128])
            rg_w_out = k.ext_in("rg_w_out", [D, D])
            f2i1 = k.ext_in("ffn2_w_in1", [D, 2 * DFF])
            f2o1 = k.ext_in("ffn2_w_out1", [DFF, D])
            gfull = k.ext_in("gfull", [128, D])
            out = k.ext_out("out", [NOWN, D])
            names += [("xfl", [4, 4])]
        make_basic_consts(k)
        cs = load_consts(k, names)
        modt = a.alloc([2, 144, 2], F32)
        k.modt_res = Res()
        m0 = a.mark()
        phase_transpose_in(k, xin, XU, barrier=not fused)
        if not fused:
            phase_mod(k, w_mod, cs["cvT"], cs["bmodT"], modt, layers)
            p.dma("sp", lambda e: e.dma_start(out=modt_out, in_=modt), R=[k.modt_res])
        else:
            PUBM = k.scratch("PUBM", [128, 144])
            GATM = k.scratch("GATM", [512, 144])
            modq = a.alloc([2, 36, 2], F32)
            phase_mod(k, w_mod, cs["cvT"], cs["bmodT"], modq, layers, njg=9)
            p.dma("sp", lambda e: e.dma_start(out=PUBM.rearrange("p (l j r) -> p l j r", l=2, r=2), in_=modq), R=[k.modt_res])
            p.barrier()
            emit_allgather(k, PUBM, GATM)
            p.barrier()
            for rk in range(4):
                for l in range(2):
                    p.dma("sp", lambda e, rk=rk, l=l: e.dma_start(out=modt[:, l, rk * 36:(rk + 1) * 36, :],
                                                                   in_=GATM[rk * 128:(rk + 1) * 128, l * 72:(l + 1) * 72].rearrange("p (j r) -> p j r", r=2)), W=[k.modt_res])
            p.barrier()
            a.reset(m0)
        sites = {(l, s): derive_site(k, modt, cs["normT"], l, s, half=(s != 1)) for l in layers for s in range(3)}
        phase_ffn(k, XU, [(0, NU)], f1i[0], f1o[0], sites[(0, 0)])
        if lvl >= 1:
            phase_ab_in(k, XU, ab_w_in, sites[(0, 1)], QU, KU, VU, ZU, BGU)
        if lvl >= 2:
            phase_ab_mix(k, QU, KU, VU, ZU, BGU, MIXU)
        own_ctx = [(0, NCTX), (U_OWN, U_OWN + NOWN)]
        if lvl >= 3:
            phase_outproj(k, XU, lambda c, u0, ln: MIXU[c, :, u0:u0 + ln], own_ctx, ab_w_out, sites[(0, 1)])
        if lvl >= 4:
            phase_ffn(k, XU, own_ctx, f2i0, f2o0, sites[(0, 2)])
        if lvl >= 5:
            phase_ffn(k, XU, own_ctx, f1i[1], f1o[1], sites[(1, 0)])
        if fused:
            edges = a.alloc([3, KC], F32)
            carF = a.alloc([4, KC, 2], F32)
            carB = a.alloc([4, KC, 2], F32)
            edges_res, car_res = Res(), Res()
            ctxfin = a.alloc([KC, 2], F32)
            ctxfin_res = Res()
            ABS = k.scratch("ABS", [4, KC, 128, NOWN])
        if lvl >= 6:
            def g_of(c, u0, ln):
                return None if u0 < NCTX else GOWN[c, :, u0 - U_OWN:u0 - U_OWN + ln]

            def u_of(c, u0, ln):
                return UCTX[c, :, u0:u0 + ln] if u0 < NCTX else UOWN[c, :, u0 - U_OWN:u0 - U_OWN + ln]
            phase_rg_in(k, XU, rg_w_in, sites[(1, 1)], g_of, u_of)
            if not fused:
                for c in range(KC):
                    p.dma("sp", lambda e, c=c: e.dma_start(out=XOWN[c], in_=XU[c, :, U_OWN:U_OWN + NOWN]))
        if fused:
            phase_exchange_edges(k, UOWN, PUB1, GAT1, cs["xfl"], edges, edges_res)
            phase_rglru_carry(k, UOWN, UCTX, rg_wa, rg_wx, PUB2.rearrange("(c p) e -> p c e", p=128), (edges, edges_res), ABS, (ctxfin, ctxfin_res))
            phase_exchange_carries(k, PUB2, GAT2, cs["xfl"], carF, carB, car_res)
            phase_rglru_apply(k, ABS, (ctxfin, ctxfin_res), ((carF, car_res), (carB, car_res)), GOWN, MIX1)
            own = [(U_OWN, U_OWN + NOWN)]
            phase_outproj(k, XU, lambda c, u0, ln: MIX1[c, :, u0 - U_OWN:u0 - U_OWN + ln], own, rg_w_out, sites[(1, 1)])
            phase_ffn(k, XU, own, f2i1, f2o1, sites[(1, 2)])
            phase_final(k, XU, out, gfull)
        p.final_wait()
        p.emit()
    return nc


def core_inputs_A(inp, core, upto="all"):
    order = ["ffn1", "abin", "abmix", "about", "l0", "l1ffn1", "all"]
    lvl = order.index(upto)
    b, c = core // 4, core % 4
    cv = np.stack([inp["c"][b], inp["c_ctx"]], axis=-1)
    m = {"xin": core_xin(inp["x"], inp["ctx"], core),
         "cvT": np.ascontiguousarray(cv.reshape(KC, 128, 2).transpose(1, 0, 2)),
         "bmodT": fm(inp["b_mod"]),
         "normT": fm(np.stack([inp["norm_ffn1"], inp["norm_mix"], inp["norm_ffn2"]], axis=1))}
    for l in ([0, 1] if lvl >= 5 else [0]):
        m[f"w_mod{l}"] = inp["w_mod"][l]
        m[f"ffn1_w_in{l}"] = inp["ffn1_w_in"][l]
        m[f"ffn1_w_out{l}"] = inp["ffn1_w_out"][l]
    if lvl >= 1:
        m["ab_w_in"] = inp["ab_w_in"][0]
        m["ropeC"], m["ropeS"] = rope_tables(core)
    if lvl >= 2:
        j = np.arange(128)[:, None]
        i = np.arange(128)[None, :]
        m["masks"] = np.ascontiguousarray(np.stack([(j >= i), (j <= i)], axis=1).astype(np.float32))
        m["flags"] = np.ascontiguousarray(np.broadcast_to(np.array([c > 0, c < 3], np.float32)[None, :], (128, 2)))
        m["sinkT"] = np.ascontiguousarray(np.broadcast_to(inp["ab_sink"][0][None, :], (128, 8)).astype(np.float32))
        m["convT"] = np.ascontiguousarray(inp["ab_conv_w"][0].reshape(3, 8, 128).transpose(2, 0, 1))
    if lvl >= 3:
        m["ab_w_out"] = inp["ab_w_out"][0]
    if lvl >= 4:
        m["ffn2_w_in0"] = inp["ffn2_w_in"][0]
        m["ffn2_w_out0"] = inp["ffn2_w_out"][0]
    if lvl >= 6:
        m["rg_w_in"] = inp["rg_w_in"][0]
    return m


def phase_ab_in(k, XU, w_in, site, QU, KU, VU, ZU, BGU):
    p, a = k.p, k.arena
    mk = a.mark()
    tiles = make_tiles([(0, NU)], TMAX_BIG)
    TM = max(sum(s[1] for s in t) for t in tiles)
    h = a.alloc([KC, TM], BF16)
    h_res = Res()
    rstd = a.alloc([TM], F32)
    rstd_res = Res()
    xc_ring = Ring([a.alloc([TM], F32) for _ in range(3)])
    tmp_ring = Ring([a.alloc([TM], F32) for _ in range(3)])
    w_ring = Ring([a.alloc([KC, 128], BF16) for _ in range(4)])
    wsw_ring = Ring([a.alloc([KC, 128], BF16) for _ in range(2)])
    wv = a.alloc([KC, 256], BF16)
    wv_res = Res()
    st16 = Ring([a.alloc([TM], BF16) for _ in range(3)])
    st32 = Ring([a.alloc([TM], F32) for _ in range(3)])
    vst = Ring([a.alloc([256], BF16) for _ in range(2)])
    bk = k.bank_ring(range(8))
    stat_banks = [(k.banks[i], k.bank_res[i]) for i in (5, 6, 7)]
    cs = load_consts(k, [("ropeC", [NU]), ("ropeS", [NU])])
    COS, SIN = cs["ropeC"], cs["ropeS"]
    w_v = w_in.rearrange("(k p) n -> p k n", p=128)
    p.dma("pool", lambda e: e.dma_start(out=wv, in_=w_v[:, :, 1280:1536]), W=[wv_res])
    for tile in tiles:
        T = sum(s[1] for s in tile)
        tu0 = tile[0][0]
        assert T % 128 == 0
        subs = split_even(T, 512)
        ms = ModStream(k, XU, tile, site, xc_ring, tmp_ring, rstd, rstd_res, stat_banks)
        for kc in range(KC):
            ms.load_sq(kc)
            ms.mm(kc)
        ms.finish(h, h_res)
        for j in range(10):
            col0 = j * 128 if j < 8 else 1024 + (j - 8) * 128
            w, wr = w_ring.next()
            p.dma("pool", lambda e, w=w, col0=col0: e.dma_start(out=w, in_=w_v[:, :, col0:col0 + 128]), W=[wr])
            ws, wsr = wsw_ring.next()
            for (d0, s0) in ((0, 32), (32, 0), (64, 96), (96, 64)):
                p.op("pool", lambda e, ws=ws, w=w, d0=d0, s0=s0: e.tensor_copy(out=ws[:, :, d0:d0 + 32], in_=w[:, :, s0:s0 + 32]), R=[wr], W=[wsr])
            st, str_ = st16.next()
            for (n0, nl) in subs:
                ba, bar = bk.next()
                bb, bbr = bk.next()
                for kc in range(KC):
                    p.op("pe", lambda e, ba=ba, w=w, kc=kc, n0=n0, nl=nl: e.matmul(ba[:, 0:nl], lhsT=w[:, kc, :], rhs=h[:, kc, n0:n0 + nl], start=(kc == 0), stop=(kc == KC - 1)),
                         R=[wr, h_res], W=[bar])
                for kc in range(KC):
                    p.op("pe", lambda e, bb=bb, ws=ws, kc=kc, n0=n0, nl=nl: e.matmul(bb[:, 0:nl], lhsT=ws[:, kc, :], rhs=h[:, kc, n0:n0 + nl], start=(kc == 0), stop=(kc == KC - 1)),
                         R=[wsr, h_res], W=[bbr])
                t1, t1r = tmp_ring.next()
                t2, t2r = tmp_ring.next()
                p.op("dve", lambda e, tu0=tu0, t1=t1, ba=ba, n0=n0, nl=nl: e.tensor_tensor(out=t1[:, 0:nl], in0=ba[:, 0:nl], in1=COS[0][:, tu0 + n0:tu0 + n0 + nl], op=ALU.mult),
                     R=[bar, COS[1]], W=[t1r])
                p.op("dve", lambda e, tu0=tu0, t2=t2, bb=bb, n0=n0, nl=nl: e.tensor_tensor(out=t2[:, 0:nl], in0=bb[:, 0:nl], in1=SIN[0][:, tu0 + n0:tu0 + n0 + nl], op=ALU.mult),
                     R=[bbr, SIN[1]], W=[t2r])
                p.op("pool", lambda e, st=st, t1=t1, t2=t2, n0=n0, nl=nl: e.tensor_tensor(out=st[:, n0:n0 + nl], in0=t1[:, 0:nl], in1=t2[:, 0:nl], op=ALU.add),
                     R=[t1r, t2r], W=[str_])
            dst = QU[j] if j < 8 else KU[j - 8]
            p.dma("sp", lambda e, tu0=tu0, st=st, dst=dst, T=T: e.dma_start(out=dst[:, tu0:tu0 + T], in_=st[:, 0:T]), R=[str_])
        for blk in range(T // 128):
            b, br = bk.next()
            for kc in range(KC):
                p.op("pe", lambda e, b=b, kc=kc, blk=blk: e.matmul(b[:, 0:256], lhsT=h[:, kc, blk * 128:(blk + 1) * 128], rhs=wv[:, kc, :], start=(kc == 0), stop=(kc == KC - 1)),
                     R=[wv_res, h_res], W=[br])
            vs, vsr = vst.next()
            p.op("act", lambda e, vs=vs, b=b: e.activation(out=vs, in_=b[:, 0:256], func=AF.Copy), R=[br], W=[vsr])
            p.dma("sp", lambda e, tu0=tu0, vs=vs, blk=blk: e.dma_start(out=VU[tu0 + blk * 128:tu0 + (blk + 1) * 128, :], in_=vs), R=[vsr])
        for cc in range(8):
            ws3 = []
            for base in (1536, 2560, 3584):
                w, wr = w_ring.next()
                p.dma("pool", lambda e, w=w, c0=base + cc * 128: e.dma_start(out=w, in_=w_v[:, :, c0:c0 + 128]), W=[wr])
                ws3.append((w, wr))
            zs, zsr = st32.next()
            bs, bsr = st32.next()
            for (n0, nl) in subs:
                bks = [bk.next() for _ in range(3)]
                for (w, wr), (b, br) in zip(ws3, bks):
                    for kc in range(KC):
                        p.op("pe", lambda e, b=b, w=w, kc=kc, n0=n0, nl=nl: e.matmul(b[:, 0:nl], lhsT=w[:, kc, :], rhs=h[:, kc, n0:n0 + nl], start=(kc == 0), stop=(kc == KC - 1)),
                             R=[wr, h_res], W=[br])
                (bgb, bgr), (cgb, cgr), (ub, ur) = bks
                t1, t1r = tmp_ring.next()
                p.op("act", lambda e, t1=t1, ub=ub, nl=nl: e.activation(out=t1[:, 0:nl], in_=ub[:, 0:nl], func=AF.Copy), R=[ur], W=[t1r])
                p.op("dve", lambda e, zs=zs, cgb=cgb, t1=t1, n0=n0, nl=nl: e.tensor_tensor(out=zs[:, n0:n0 + nl], in0=cgb[:, 0:nl], in1=t1[:, 0:nl], op=ALU.mult),
                     R=[cgr, t1r], W=[zsr])
                p.op("act", lambda e, bs=bs, bgb=bgb, n0=n0, nl=nl: e.activation(out=bs[:, n0:n0 + nl], in_=bgb[:, 0:nl], func=AF.Copy), R=[bgr], W=[bsr])
            p.dma("sp", lambda e, tu0=tu0, zs=zs, cc=cc, T=T: e.dma_start(out=ZU[cc, :, tu0:tu0 + T], in_=zs[:, 0:T]), R=[zsr])
            p.dma("sp", lambda e, tu0=tu0, bs=bs, cc=cc, T=T: e.dma_start(out=BGU[cc, :, tu0:tu0 + T], in_=bs[:, 0:T]), R=[bsr])
    p.barrier()
    a.reset(mk)


def phase_ab_mix(k, QU, KU, VU, ZU, BGU, MIXU):
    p, a = k.p, k.arena
    mk = a.mark()
    cs = load_consts(k, [("masks", [2, 128]), ("flags", [2]), ("sinkT", [8]), ("convT", [3, 8])])
    masks, flags, sinkT, convT = cs["masks"], cs["flags"], cs["sinkT"], cs["convT"]
    scale = 128.0 ** -0.5
    mres = Res()
    mP = a.alloc([4, 128], BF16)
    mN = a.alloc([4, 128], BF16)
    mP0 = a.alloc([4, 128], BF16)
    mNL = a.alloc([4, 128], BF16)
    for hh in range(4):
        p.op("dve", lambda e, hh=hh: e.tensor_copy(out=mP[:, hh, :], in_=masks[0][:, 0, :]), R=[masks[1]], W=[mres])
        p.op("dve", lambda e, hh=hh: e.tensor_copy(out=mN[:, hh, :], in_=masks[0][:, 1, :]), R=[masks[1]], W=[mres])
    p.op("dve", lambda e: e.tensor_scalar(out=mP0, in0=mP, scalar1=flags[0][:, 0:1], scalar2=None, op0=ALU.mult), R=[mres, flags[1]], W=[mres])
    p.op("dve", lambda e: e.tensor_scalar(out=mNL, in0=mN, scalar1=flags[0][:, 1:2], scalar2=None, op0=ALU.mult), R=[mres, flags[1]], W=[mres])
    esink = a.alloc([8], F32)
    p.op("act", lambda e: e.activation(out=esink, in_=sinkT[0], func=AF.Exp), R=[sinkT[1]], W=[mres])
    gsets = [(a.alloc([NU], BF16), a.alloc([NU // 128, 128], BF16), a.alloc([4, NU], BF16), Res()) for _ in range(2)]
    E_ring = Ring([a.alloc([512], BF16) for _ in range(6)])
    rd_ring = Ring([a.alloc([512], F32) for _ in range(2)])
    os_ring = Ring([a.alloc([4, 128], BF16) for _ in range(2)])
    sc_banks = k.bank_ring([0, 1, 2, 3])
    acc_banks = k.bank_ring([4, 5, 6, 7])
    flat = lambda t: t.rearrange("p a b -> p (a b)")
    for g in range(2):
        kU, vU, qU, gres = gsets[g]
        p.dma("sp", lambda e, g=g, kU=kU: e.dma_start(out=kU, in_=KU[g]), W=[gres])
        p.dma("sp", lambda e, g=g, vU=vU: e.dma_start(out=vU, in_=VU[:, g * 128:(g + 1) * 128].rearrange("(b p) d -> p b d", p=128)), W=[gres])
        p.dma("sp", lambda e, g=g, qU=qU: e.dma_start(out=qU, in_=QU[g * 4:(g + 1) * 4].rearrange("h p t -> p h t")), W=[gres])
    for g in range(2):
        kU, vU, qU, gres = gsets[g]
        qblocks = [(0, [(0, None), (1, None)]), (1, [(0, None), (1, None)])]
        for i in range(16):
            qblocks.append((i + 3, [(i + 2, mP0 if i == 0 else mP), (i + 3, None), (i + 4, mNL if i == 15 else mN), (0, None), (1, None)]))
        for ub, keys in qblocks:
            den, denr = acc_banks.next()
            ob, obr = acc_banks.next()
            nk = len(keys)
            for ki, (kb, mask) in enumerate(keys):
                sb, sbr = sc_banks.next()
                p.op("pe", lambda e, sb=sb, kb=kb, ub=ub, kU=kU, qU=qU: e.matmul(sb[:, 0:512], lhsT=kU[:, kb * 128:(kb + 1) * 128], rhs=qU[:, :, ub * 128:(ub + 1) * 128], start=True, stop=True),
                     R=[gres], W=[sbr])
                E, Er = E_ring.next()
                p.op("act", lambda e, E=E, sb=sb: e.activation(out=E, in_=sb[:, 0:512], func=AF.Exp, scale=scale), R=[sbr], W=[Er])
                if mask is not None:
                    p.op("dve", lambda e, E=E, mask=mask: e.tensor_tensor(out=E, in0=E, in1=flat(mask), op=ALU.mult), R=[Er, mres], W=[Er])
                p.op("pe", lambda e, den=den, E=E, ki=ki, nk=nk: e.matmul(den[:, 0:512], lhsT=k.ones16, rhs=E, start=(ki == 0), stop=(ki == nk - 1)),
                     R=[Er, k.cres], W=[denr])
                p.op("pe", lambda e, ob=ob, E=E, kb=kb, ki=ki, nk=nk, vU=vU: e.matmul(ob[:, 0:512], lhsT=vU[:, kb, :], rhs=E, start=(ki == 0), stop=(ki == nk - 1)),
                     R=[Er, gres], W=[obr])
            rd, rdr = rd_ring.next()
            for hh in range(4):
                p.op("dve", lambda e, rd=rd, den=den, hh=hh, g=g: e.tensor_scalar(out=rd[:, hh * 128:(hh + 1) * 128], in0=den[:, hh * 128:(hh + 1) * 128],
                                                                                   scalar1=esink[:, g * 4 + hh:g * 4 + hh + 1], scalar2=None, op0=ALU.add),
                     R=[denr, mres], W=[rdr])
            p.op("dve", lambda e, rd=rd: e.reciprocal(out=rd, in_=rd), R=[rdr], W=[rdr])
            os_, osr = os_ring.next()
            p.op("dve", lambda e, os_=os_, ob=ob, rd=rd: e.tensor_tensor(out=flat(os_), in0=ob[:, 0:512], in1=rd, op=ALU.mult), R=[obr, rdr], W=[osr])
            p.dma("sp", lambda e, os_=os_, g=g, ub=ub: e.dma_start(out=MIXU[g * 4:(g + 1) * 4, :, ub * 128:(ub + 1) * 128].rearrange("h p t -> p h t"), in_=os_), R=[osr])
    zb_ring = Ring([a.alloc([NU + 2], F32) for _ in range(2)])
    bg_ring = Ring([a.alloc([NU], F32) for _ in range(2)])
    acc = a.alloc([NOWN], F32)
    accr = Res()
    zc = a.alloc([NCTX + 2], F32)
    accc = a.alloc([NCTX], F32)
    cst_ring = Ring([a.alloc([NU], BF16) for _ in range(2)])
    p.op("pool", lambda e: e.memset(zc, 0.0), W=[accr])
    for cc in range(8):
        zb, zbr = zb_ring.next()
        bg, bgr = bg_ring.next()
        p.dma("sp", lambda e, zb=zb, cc=cc: e.dma_start(out=zb[:, 1:NU + 1], in_=ZU[cc]), W=[zbr])
        p.dma("sp", lambda e, bg=bg, cc=cc: e.dma_start(out=bg, in_=BGU[cc]), W=[bgr])
        cst, cstr = cst_ring.next()
        w = lambda kk, cc=cc: convT[0][:, kk, cc:cc + 1]
        p.op("pool", lambda e, zb=zb: e.tensor_copy(out=zc[:, 1:NCTX + 1], in_=zb[:, 1:NCTX + 1]), R=[zbr], W=[accr])
        p.op("dve", lambda e, w=w: e.tensor_scalar(out=accc, in0=zc[:, 0:NCTX], scalar1=w(0), scalar2=None, op0=ALU.mult), R=[accr, convT[1]], W=[accr])
        for kk in (1, 2):
            p.op("dve", lambda e, w=w, kk=kk: e.scalar_tensor_tensor(out=accc, in0=zc[:, kk:kk + NCTX], scalar=w(kk), op0=ALU.mult, in1=accc, op1=ALU.add), R=[accr, convT[1]], W=[accr])
        p.op("dve", lambda e, cst=cst, bg=bg: e.tensor_tensor(out=cst[:, 0:NCTX], in0=accc, in1=bg[:, 0:NCTX], op=ALU.mult), R=[accr, bgr], W=[cstr])
        p.op("dve", lambda e, zb=zb: e.tensor_scalar(out=zb[:, U_OWN:U_OWN + 1], in0=zb[:, U_OWN:U_OWN + 1], scalar1=flags[0][:, 0:1], scalar2=None, op0=ALU.mult), R=[zbr, flags[1]], W=[zbr])
        p.op("dve", lambda e, zb=zb: e.tensor_scalar(out=zb[:, U_OWN + NOWN + 1:U_OWN + NOWN + 2], in0=zb[:, U_OWN + NOWN + 1:U_OWN + NOWN + 2], scalar1=flags[0][:, 1:2], scalar2=None, op0=ALU.mult), R=[zbr, flags[1]], W=[zbr])
        p.op("dve", lambda e, zb=zb, w=w: e.tensor_scalar(out=acc, in0=zb[:, U_OWN:U_OWN + NOWN], scalar1=w(0), scalar2=None, op0=ALU.mult), R=[zbr, convT[1]], W=[accr])
        for kk in (1, 2):
            p.op("dve", lambda e, zb=zb, w=w, kk=kk: e.scalar_tensor_tensor(out=acc, in0=zb[:, U_OWN + kk:U_OWN + kk + NOWN], scalar=w(kk), op0=ALU.mult, in1=acc, op1=ALU.add), R=[zbr, accr, convT[1]], W=[accr])
        p.op("dve", lambda e, cst=cst, bg=bg: e.tensor_tensor(out=cst[:, U_OWN:U_OWN + NOWN], in0=acc, in1=bg[:, U_OWN:U_OWN + NOWN], op=ALU.mult), R=[accr, bgr], W=[cstr])
        p.dma("sp", lambda e, cst=cst, cc=cc: e.dma_start(out=MIXU[8 + cc, :, 0:NCTX], in_=cst[:, 0:NCTX]), R=[cstr])
        p.dma("sp", lambda e, cst=cst, cc=cc: e.dma_start(out=MIXU[8 + cc, :, U_OWN:U_OWN + NOWN], in_=cst[:, U_OWN:U_OWN + NOWN]), R=[cstr])
    p.barrier()
    a.reset(mk)


def phase_outproj(k, XU, mix_of, ranges, w_out, site):
    p, a = k.p, k.arena
    mk = a.mark()
    tiles = make_tiles(ranges, TMAX_BIG)
    TM = max(sum(s[1] for s in t) for t in tiles)
    mix = a.alloc([KC, TM], BF16)
    mix_res = Res()
    w_ring = Ring([a.alloc([KC, 128], BF16) for _ in range(3)])
    xm_ring = Ring([a.alloc([TM], F32) for _ in range(2)])
    xo_ring = Ring([a.alloc([TM], F32) for _ in range(2)])
    bk = k.bank_ring(range(8))
    w_v = w_out.rearrange("(k p) n -> p k n", p=128)
    for tile in tiles:
        T = sum(s[1] for s in tile)
        subs = split_even(T, 512)
        for (u0, ln, off, r) in tile:
            for c in range(KC):
                p.dma("sp", lambda e, c=c, u0=u0, ln=ln, off=off: e.dma_start(out=mix[:, c, off:off + ln], in_=mix_of(c, u0, ln)), W=[mix_res])
        for m in range(KC):
            w, wr = w_ring.next()
            p.dma("pool", lambda e, w=w, m=m: e.dma_start(out=w, in_=w_v[:, :, m * 128:(m + 1) * 128]), W=[wr])
            ob = [bk.next() for _ in subs]
            for kc in range(KC):
                for (n0, nl), (b, br) in zip(subs, ob):
                    p.op("pe", lambda e, b=b, w=w, kc=kc, n0=n0, nl=nl: e.matmul(b[:, 0:nl], lhsT=w[:, kc, :], rhs=mix[:, kc, n0:n0 + nl], start=(kc == 0), stop=(kc == KC - 1)),
                         R=[wr, mix_res], W=[br])
            emit_residual(k, XU, tile, m, ob, subs, site, xm_ring, xo_ring)
    p.barrier()
    a.reset(mk)


def phase_rg_in(k, XU, w_in, site, g_of, u_of):
    p, a = k.p, k.arena
    mk = a.mark()
    tiles = make_tiles([(0, NCTX), (U_OWN, U_OWN + NOWN)], TMAX_BIG)
    TM = max(sum(s[1] for s in t) for t in tiles)
    h = a.alloc([KC, TM], BF16)
    h_res = Res()
    rstd = a.alloc([TM], F32)
    rstd_res = Res()
    xc_ring = Ring([a.alloc([TM], F32) for _ in range(3)])
    tmp_ring = Ring([a.alloc([TM], F32) for _ in range(3)])
    w_ring = Ring([a.alloc([KC, 128], BF16) for _ in range(4)])
    st16 = Ring([a.alloc([TM], BF16) for _ in range(2)])
    st32 = Ring([a.alloc([TM], F32) for _ in range(2)])
    bk = k.bank_ring(range(8))
    stat_banks = [(k.banks[i], k.bank_res[i]) for i in (5, 6, 7)]
    w_v = w_in.rearrange("(k p) n -> p k n", p=128)
    for tile in tiles:
        T = sum(s[1] for s in tile)
        subs = split_even(T, 512)
        ms = ModStream(k, XU, tile, site, xc_ring, tmp_ring, rstd, rstd_res, stat_banks)
        for kc in range(KC):
            ms.load_sq(kc)
            ms.mm(kc)
        ms.finish(h, h_res)
        for ch in range(32):
            w, wr = w_ring.next()
            p.dma("pool", lambda e, w=w, ch=ch: e.dma_start(out=w, in_=w_v[:, :, ch * 128:(ch + 1) * 128]), W=[wr])
            is_gate = ch < 16
            st, sr = (st16 if is_gate else st32).next()
            for (n0, nl) in subs:
                b, br = bk.next()
                for kc in range(KC):
                    p.op("pe", lambda e, b=b, w=w, kc=kc, n0=n0, nl=nl: e.matmul(b[:, 0:nl], lhsT=w[:, kc, :], rhs=h[:, kc, n0:n0 + nl], start=(kc == 0), stop=(kc == KC - 1)),
                         R=[wr, h_res], W=[br])
                if is_gate:
                    p.op("act", lambda e, st=st, b=b, n0=n0, nl=nl: e.activation(out=st[:, n0:n0 + nl], in_=b[:, 0:nl], func=AF.Gelu), R=[br], W=[sr])
                else:
                    p.op("dve", lambda e, st=st, b=b, n0=n0, nl=nl: e.tensor_copy(out=st[:, n0:n0 + nl], in_=b[:, 0:nl]), R=[br], W=[sr])
            for (u0, ln, off, r) in tile:
                dst = g_of(ch, u0, ln) if is_gate else u_of(ch - 16, u0, ln)
                if dst is None:
                    continue
                p.dma("sp", lambda e, st=st, dst=dst, off=off, ln=ln: e.dma_start(out=dst, in_=st[:, off:off + ln]), R=[sr])
    p.barrier()
    a.reset(mk)


NE = NOWN + 3
NS = NOWN + NCTX


def phase_rglru(k, mode, UE, UCTX, rg_w_a, rg_w_x, CAR=None, GOWN=None, MIX1=None, edges=None, car_tiles=None, UOWN=None, ABS=None, ctxfin=None):
    p, a = k.p, k.arena
    mk = a.mark()
    names = [("rgconvT", [KC, 5]), ("rgbaT", [2, KC]), ("rgbxT", [2, KC]), ("rglamT", [2, KC])]
    if mode == "full" and car_tiles is None:
        names += [("carF", [3, KC, 2]), ("carB", [3, KC, 2])]
    cs = load_consts(k, names)
    convT, baT, bxT, lamT = cs["rgconvT"], cs["rgbaT"], cs["rgbxT"], cs["rglamT"]
    cst = a.alloc([2, KC], F32)
    cres = Res()
    p.op("act", lambda e: e.activation(out=cst, in_=lamT[0], func=AF.Exp, scale=-1.0), R=[lamT[1]], W=[cres])
    p.op("act", lambda e: e.activation(out=cst, in_=cst, func=AF.Ln, bias=1.0, scale=1.0), R=[cres], W=[cres])
    p.op("dve", lambda e: e.tensor_scalar(out=cst, in0=cst, scalar1=-8.0, scalar2=None, op0=ALU.mult), R=[cres], W=[cres])
    wa = a.alloc([2, KC, 128], BF16)
    wx = a.alloc([2, KC, 128], BF16)
    wres = Res()
    p.dma("pool", lambda e: e.dma_start(out=wa, in_=rg_w_a.rearrange("d h i j -> i d h j")), W=[wres])
    p.dma("pool", lambda e: e.dma_start(out=wx, in_=rg_w_x.rearrange("d h i j -> i d h j")), W=[wres])
    ub_ring = Ring([a.alloc([NE], F32) for _ in range(2)])
    uc_ring = Ring([a.alloc([NCTX + 3], F32) for _ in range(2)])
    for (b_, r_) in uc_ring.bufs:
        p.op("pool", lambda e, b_=b_: e.memset(b_, 0.0), W=[r_])
    ucv = a.alloc([NS], F32)
    ucv_res = Res()
    u16 = a.alloc([NS], BF16)
    u16_res = Res()
    rbs = [a.alloc([NS], F32) for _ in range(2)]
    ibs = [a.alloc([NS], F32) for _ in range(2)]
    tbs = [a.alloc([NS], F32) for _ in range(2)]
    rres = [Res(), Res()]
    ires = [Res(), Res()]
    tress = [Res(), Res()]
    ab = [a.alloc([NS], F32) for _ in range(2)]
    bb = [a.alloc([NS], F32) for _ in range(2)]
    gres = [Res(), Res()]
    hb = [a.alloc([NS], F32) for _ in range(2)]
    hres = [Res(), Res()]
    sm = a.alloc([16], F32)
    smres = Res()
    if mode == "carry":
        car = a.alloc([KC, 4], F32)
        car_res = Res()
    else:
        g_ring = Ring([a.alloc([NOWN], BF16) for _ in range(2)])
        mo_ring = Ring([a.alloc([NOWN], BF16) for _ in range(2)])
        carF, carB = car_tiles if car_tiles is not None else (cs["carF"], cs["carB"])
        nstep = 4 if car_tiles is not None else 3
    bk = k.bank_ring(range(8))
    blocks = split_even(NS, 512)
    for c in range(KC):
        ub, ubr = ub_ring.next()
        ucx, ucxr = uc_ring.next()
        if edges is None:
            p.dma("sp", lambda e, ub=ub, c=c: e.dma_start(out=ub, in_=UE[c]), W=[ubr])
        else:
            p.dma("sp", lambda e, ub=ub, c=c: e.dma_start(out=ub[:, 2:2 + NOWN], in_=UOWN[c]), W=[ubr])
            for (dst_, src_) in ((0, 0), (1, 1), (2 + NOWN, 2)):
                p.op("pool", lambda e, ub=ub, c=c, dst_=dst_, src_=src_: e.tensor_copy(out=ub[:, dst_:dst_ + 1], in_=edges[0][:, src_, c:c + 1]), R=[edges[1]], W=[ubr])
        p.dma("sp", lambda e, ucx=ucx, c=c: e.dma_start(out=ucx[:, 2:2 + NCTX], in_=UCTX[c]), W=[ucxr])
        w = lambda kk, c=c: convT[0][:, c, kk:kk + 1]
        for (src, srcr, o0, n) in ((ub, ubr, 0, NOWN), (ucx, ucxr, NOWN, NCTX)):
            p.op("dve", lambda e, src=src, o0=o0, n=n, w=w: e.tensor_scalar(out=ucv[:, o0:o0 + n], in0=src[:, 0:n], scalar1=w(0), scalar2=w(4), op0=ALU.mult, op1=ALU.add),
                 R=[srcr, convT[1]], W=[ucv_res])
            for kk in (1, 2, 3):
                p.op("dve", lambda e, src=src, o0=o0, n=n, w=w, kk=kk: e.scalar_tensor_tensor(out=ucv[:, o0:o0 + n], in0=src[:, kk:kk + n], scalar=w(kk), op0=ALU.mult, in1=ucv[:, o0:o0 + n], op1=ALU.add),
                     R=[srcr, convT[1], ucv_res], W=[ucv_res])
        p.op("dve", lambda e: e.tensor_copy(out=u16, in_=ucv), R=[ucv_res], W=[u16_res])
        for d in range(2):
            rb, ib, tb = rbs[d], ibs[d], tbs[d]
            rr_, ir_, tres = rres[d], ires[d], tress[d]
            for (n0, nl) in blocks:
                rp, rpr = bk.next()
                ip, ipr = bk.next()
                p.op("pe", lambda e, rp=rp, d=d, c=c, n0=n0, nl=nl: e.matmul(rp[:, 0:nl], lhsT=wa[:, d, c, :], rhs=u16[:, n0:n0 + nl], start=True, stop=True), R=[wres, u16_res], W=[rpr])
                p.op("pe", lambda e, ip=ip, d=d, c=c, n0=n0, nl=nl: e.matmul(ip[:, 0:nl], lhsT=wx[:, d, c, :], rhs=u16[:, n0:n0 + nl], start=True, stop=True), R=[wres, u16_res], W=[ipr])
                p.op("act", lambda e, rp=rp, d=d, c=c, n0=n0, nl=nl, rb=rb: e.activation(out=rb[:, n0:n0 + nl], in_=rp[:, 0:nl], func=AF.Sigmoid, bias=baT[0][:, d, c:c + 1], scale=1.0), R=[rpr, baT[1]], W=[rr_])
                p.op("act", lambda e, ip=ip, d=d, c=c, n0=n0, nl=nl, ib=ib: e.activation(out=ib[:, n0:n0 + nl], in_=ip[:, 0:nl], func=AF.Sigmoid, bias=bxT[0][:, d, c:c + 1], scale=1.0), R=[ipr, bxT[1]], W=[ir_])
            A_, B_ = ab[d], bb[d]
            p.op("act", lambda e, A_=A_, d=d, c=c, rb=rb: e.activation(out=A_, in_=rb, func=AF.Exp, scale=cst[:, d, c:c + 1]), R=[rr_, cres], W=[gres[d]])
            p.op("dve", lambda e, A_=A_, tb=tb: e.tensor_tensor(out=tb, in0=A_, in1=A_, op=ALU.mult), R=[gres[d]], W=[tres])
            p.op("act", lambda e, tb=tb: e.activation(out=tb, in_=tb, func=AF.Sqrt, bias=1.0, scale=-1.0), R=[tres], W=[tres])
            p.op("dve", lambda e, tb=tb, ib=ib: e.tensor_tensor(out=tb, in0=tb, in1=ib, op=ALU.mult), R=[tres, ir_], W=[tres])
            p.op("dve", lambda e, B_=B_, tb=tb: e.tensor_tensor(out=B_, in0=tb, in1=ucv, op=ALU.mult), R=[tres, ucv_res], W=[gres[d]])
        H = hb
        p.op("dve", lambda e: e.tensor_tensor_scan(out=H[0][:, NOWN:NS], data0=ab[0][:, NOWN:NS], data1=bb[0][:, NOWN:NS], initial=0.0, op0=ALU.mult, op1=ALU.add),
             R=[gres[0]], W=[hres[0]])
        p.op("dve", lambda e: e.tensor_tensor_scan(out=H[1][:, NOWN:NS][:, ::-1], data0=ab[1][:, NOWN:NS][:, ::-1], data1=bb[1][:, NOWN:NS][:, ::-1], initial=0.0, op0=ALU.mult, op1=ALU.add),
             R=[gres[1]], W=[hres[1]])
        if mode == "carry":
            p.op("dve", lambda e: e.tensor_tensor_scan(out=H[0][:, 0:NOWN], data0=ab[0][:, 0:NOWN], data1=bb[0][:, 0:NOWN], initial=0.0, op0=ALU.mult, op1=ALU.add),
                 R=[gres[0]], W=[hres[0]])
            p.op("dve", lambda e: e.tensor_tensor_scan(out=H[1][:, 0:NOWN][:, ::-1], data0=ab[1][:, 0:NOWN][:, ::-1], data1=bb[1][:, 0:NOWN][:, ::-1], initial=0.0, op0=ALU.mult, op1=ALU.add),
                 R=[gres[1]], W=[hres[1]])
            p.op("dve", lambda e, c=c: e.tensor_reduce(out=car[:, c, 0:1], in_=ab[0][:, 0:NOWN], axis=AX.X, op=ALU.mult), R=[gres[0]], W=[car_res])
            p.op("dve", lambda e, c=c: e.tensor_copy(out=car[:, c, 1:2], in_=H[0][:, NOWN - 1:NOWN]), R=[hres[0]], W=[car_res])
            p.op("dve", lambda e, c=c: e.tensor_reduce(out=car[:, c, 2:3], in_=ab[1][:, 0:NOWN], axis=AX.X, op=ALU.mult), R=[gres[1]], W=[car_res])
            p.op("dve", lambda e, c=c: e.tensor_copy(out=car[:, c, 3:4], in_=H[1][:, 0:1]), R=[hres[1]], W=[car_res])
            if ABS is not None:
                for d in range(2):
                    p.dma("sp", lambda e, d=d, c=c: e.dma_start(out=ABS[2 * d, c], in_=ab[d][:, 0:NOWN]), R=[gres[d]])
                    p.dma("sp", lambda e, d=d, c=c: e.dma_start(out=ABS[2 * d + 1, c], in_=bb[d][:, 0:NOWN]), R=[gres[d]])
                p.op("dve", lambda e, c=c: e.tensor_copy(out=ctxfin[0][:, c, 0:1], in_=H[0][:, NS - 1:NS]), R=[hres[0]], W=[ctxfin[1]])
                p.op("dve", lambda e, c=c: e.tensor_copy(out=ctxfin[0][:, c, 1:2], in_=H[1][:, NOWN:NOWN + 1]), R=[hres[1]], W=[ctxfin[1]])
        else:
            p.op("dve", lambda e: e.tensor_copy(out=sm[:, 0:1], in_=H[0][:, NS - 1:NS]), R=[hres[0]], W=[smres])
            p.op("dve", lambda e: e.tensor_copy(out=sm[:, 1:2], in_=H[1][:, NOWN:NOWN + 1]), R=[hres[1]], W=[smres])
            for s_ in range(nstep):
                p.op("dve", lambda e, s_=s_, c=c: e.scalar_tensor_tensor(out=sm[:, 0:1], in0=sm[:, 0:1], scalar=carF[0][:, s_, c, 0:1], op0=ALU.mult, in1=carF[0][:, s_, c, 1:2], op1=ALU.add),
                     R=[smres, carF[1]], W=[smres])
                p.op("dve", lambda e, s_=s_, c=c: e.scalar_tensor_tensor(out=sm[:, 1:2], in0=sm[:, 1:2], scalar=carB[0][:, s_, c, 0:1], op0=ALU.mult, in1=carB[0][:, s_, c, 1:2], op1=ALU.add),
                     R=[smres, carB[1]], W=[smres])
            p.op("dve", lambda e: e.tensor_tensor_scan(out=H[0][:, 0:NOWN], data0=ab[0][:, 0:NOWN], data1=bb[0][:, 0:NOWN], initial=sm[:, 0:1], op0=ALU.mult, op1=ALU.add),
                 R=[gres[0], smres], W=[hres[0]])
            p.op("dve", lambda e: e.tensor_tensor_scan(out=H[1][:, 0:NOWN][:, ::-1], data0=ab[1][:, 0:NOWN][:, ::-1], data1=bb[1][:, 0:NOWN][:, ::-1], initial=sm[:, 1:2], op0=ALU.mult, op1=ALU.add),
                 R=[gres[1], smres], W=[hres[1]])
            g_, gr_ = g_ring.next()
            p.dma("sp", lambda e, g_=g_, c=c: e.dma_start(out=g_, in_=GOWN[c]), W=[gr_])
            p.op("pool", lambda e: e.tensor_tensor(out=H[0][:, 0:NOWN], in0=H[0][:, 0:NOWN], in1=H[1][:, 0:NOWN], op=ALU.add), R=[hres[0], hres[1]], W=[hres[0]])
            mo, mor = mo_ring.next()
            p.op("dve", lambda e, mo=mo, g_=g_: e.tensor_tensor(out=mo, in0=H[0][:, 0:NOWN], in1=g_, op=ALU.mult), R=[hres[0], gr_], W=[mor])
            p.dma("sp", lambda e, mo=mo, c=c: e.dma_start(out=MIX1[c], in_=mo), R=[mor])
    if mode == "carry":
        p.dma("sp", lambda e: e.dma_start(out=CAR, in_=car), R=[car_res])
    p.barrier()
    a.reset(mk)


def phase_rglru_apply(k, ABS, ctxfin, car_tiles, GOWN, MIX1):
    p, a = k.p, k.arena
    mk = a.mark()
    carF, carB = car_tiles
    sets = Ring([[a.alloc([NOWN], F32) for _ in range(4)] for _ in range(2)])
    h_ring = Ring([[a.alloc([NOWN], F32) for _ in range(2)] for _ in range(2)])
    g_ring = Ring([a.alloc([NOWN], BF16) for _ in range(2)])
    mo_ring = Ring([a.alloc([NOWN], BF16) for _ in range(2)])
    sm_ring = Ring([a.alloc([2], F32) for _ in range(2)])
    for c in range(KC):
        (af, bf, ab_, bb_), sr = sets.next()
        for i_, t_ in enumerate((af, bf, ab_, bb_)):
            p.dma("sp", lambda e, i_=i_, t_=t_, c=c: e.dma_start(out=t_, in_=ABS[i_, c]), W=[sr])
        g_, gr_ = g_ring.next()
        p.dma("sp", lambda e, g_=g_, c=c: e.dma_start(out=g_, in_=GOWN[c]), W=[gr_])
        sm, smr = sm_ring.next()
        p.op("dve", lambda e, sm=sm, c=c: e.tensor_copy(out=sm, in_=ctxfin[0][:, c, :]), R=[ctxfin[1]], W=[smr])
        for s_ in range(4):
            p.op("dve", lambda e, sm=sm, s_=s_, c=c: e.scalar_tensor_tensor(out=sm[:, 0:1], in0=sm[:, 0:1], scalar=carF[0][:, s_, c, 0:1], op0=ALU.mult, in1=carF[0][:, s_, c, 1:2], op1=ALU.add),
                 R=[smr, carF[1]], W=[smr])
            p.op("dve", lambda e, sm=sm, s_=s_, c=c: e.scalar_tensor_tensor(out=sm[:, 1:2], in0=sm[:, 1:2], scalar=carB[0][:, s_, c, 0:1], op0=ALU.mult, in1=carB[0][:, s_, c, 1:2], op1=ALU.add),
                 R=[smr, carB[1]], W=[smr])
        (hf, hb_), hr = h_ring.next()
        p.op("dve", lambda e, hf=hf, af=af, bf=bf, sm=sm: e.tensor_tensor_scan(out=hf, data0=af, data1=bf, initial=sm[:, 0:1], op0=ALU.mult, op1=ALU.add), R=[sr, smr], W=[hr])
        p.op("dve", lambda e, hb_=hb_, ab_=ab_, bb_=bb_, sm=sm: e.tensor_tensor_scan(out=hb_[:, ::-1], data0=ab_[:, ::-1], data1=bb_[:, ::-1], initial=sm[:, 1:2], op0=ALU.mult, op1=ALU.add), R=[sr, smr], W=[hr])
        p.op("dve", lambda e, hf=hf, hb_=hb_: e.tensor_tensor(out=hf, in0=hf, in1=hb_, op=ALU.add), R=[hr], W=[hr])
        mo, mor = mo_ring.next()
        p.op("pool", lambda e, mo=mo, hf=hf, g_=g_: e.tensor_tensor(out=mo, in0=hf, in1=g_, op=ALU.mult), R=[hr, gr_], W=[mor])
        p.dma("sp", lambda e, mo=mo, c=c: e.dma_start(out=MIX1[c], in_=mo), R=[mor])
    p.barrier()
    a.reset(mk)


RG4 = [[0, 1, 2, 3], [4, 5, 6, 7]]


def emit_allgather(k, src, dst):
    k.p.cc(lambda e: e.collective_compute("AllGather", ALU.bypass, replica_groups=RG4, ins=[src.opt()], outs=[dst.opt()]))


def phase_exchange_edges(k, UOWN, PUB1, GAT1, xfl, edges, edges_res):
    p, a = k.p, k.arena
    mk = a.mark()
    pub = a.alloc([3, KC], F32)
    r = Res()
    for (j, t) in ((0, 0), (1, NOWN - 2), (2, NOWN - 1)):
        p.dma("sp", lambda e, j=j, t=t: e.dma_start(out=pub[:, j, :], in_=UOWN[:, :, t:t + 1].rearrange("c p e -> p (c e)"), allow_slow_non_contiguous=True), W=[r])
    p.dma("sp", lambda e: e.dma_start(out=PUB1.rearrange("(p j) c -> p j c", j=3), in_=pub), R=[r])
    p.barrier()
    emit_allgather(k, PUB1, GAT1)
    p.barrier()
    g = a.alloc([4, 3, KC], F32)
    gr = Res()
    p.dma("sp", lambda e: e.dma_start(out=g, in_=GAT1.rearrange("(r p j) c -> p r j c", r=4, j=3)), W=[gr])
    for (dst_, col, kind) in ((0, 1, 0), (1, 2, 0), (2, 0, 1)):
        for rk in range(4):
            if rk == 0:
                p.op("dve", lambda e, dst_=dst_, col=col, kind=kind, rk=rk: e.tensor_scalar(out=edges[:, dst_, :], in0=g[:, rk, col, :], scalar1=xfl[0][:, kind, rk:rk + 1], scalar2=None, op0=ALU.mult),
                     R=[gr, xfl[1]], W=[edges_res])
            else:
                p.op("dve", lambda e, dst_=dst_, col=col, kind=kind, rk=rk: e.scalar_tensor_tensor(out=edges[:, dst_, :], in0=g[:, rk, col, :], scalar=xfl[0][:, kind, rk:rk + 1], op0=ALU.mult, in1=edges[:, dst_, :], op1=ALU.add),
                     R=[gr, xfl[1], edges_res], W=[edges_res])
    p.barrier()
    a.reset(mk)


def phase_exchange_carries(k, PUB2, GAT2, xfl, carF, carB, car_res):
    p, a = k.p, k.arena
    mk = a.mark()
    emit_allgather(k, PUB2, GAT2)
    p.barrier()
    g = a.alloc([4, KC, 4], F32)
    gr = Res()
    p.dma("sp", lambda e: e.dma_start(out=g, in_=GAT2.rearrange("(r c p) e -> p r c e", r=4, p=128)), W=[gr])
    for rk in range(4):
        for (dst, step, kind, ca, cb) in ((carF, rk, 2, 0, 1), (carB, 3 - rk, 3, 2, 3)):
            fl = xfl[0][:, kind, rk:rk + 1]
            p.op("dve", lambda e, dst=dst, step=step, fl=fl, ca=ca, rk=rk: e.tensor_scalar(out=dst[:, step, :, 0], in0=g[:, rk, :, ca], scalar1=-1.0, scalar2=fl, op0=ALU.add, op1=ALU.mult),
                 R=[gr, xfl[1]], W=[car_res])
            p.op("dve", lambda e, dst=dst, step=step: e.tensor_scalar(out=dst[:, step, :, 0], in0=dst[:, step, :, 0], scalar1=1.0, scalar2=None, op0=ALU.add),
                 R=[car_res], W=[car_res])
            p.op("dve", lambda e, dst=dst, step=step, fl=fl, cb=cb, rk=rk: e.tensor_scalar(out=dst[:, step, :, 1], in0=g[:, rk, :, cb], scalar1=fl, scalar2=None, op0=ALU.mult),
                 R=[gr, xfl[1]], W=[car_res])
    p.barrier()
    a.reset(mk)


def phase_final(k, XU, out, gfull_dram):
    p, a = k.p, k.arena
    mk = a.mark()
    gf = a.alloc([D], F32)
    gfr = Res()
    p.dma("sp", lambda e: e.dma_start(out=gf, in_=gfull_dram), W=[gfr])
    xb_ring = Ring([a.alloc([KC, 128], F32) for _ in range(3)])
    ob_ring = Ring([a.alloc([D], F32) for _ in range(2)])
    junk = a.alloc([512], F32)
    jres = Res()
    ss_ring = Ring([a.alloc([8], F32) for _ in range(2)])
    bk = k.bank_ring(range(8))
    XUv = XU.rearrange("c p t -> p c t")
    fl = {}

    def fin_load(blk):
        u0 = U_OWN + blk * 128
        xb, xbr = xb_ring.next()
        p.dma("sp", lambda e, xb=xb, u0=u0: e.dma_start(out=xb, in_=XUv[:, :, u0:u0 + 128]), W=[xbr])
        fl[blk] = (xb, xbr)

    fin_load(0)
    for blk in range(NOWN // 128):
        if blk + 1 < NOWN // 128:
            fin_load(blk + 1)
        xb, xbr = fl.pop(blk)
        ss, ssr = ss_ring.next()
        banks = [bk.next() for _ in range(4)]
        for g, (b, br) in enumerate(banks):
            for q in range(4):
                c = 4 * g + q
                p.op("pe", lambda e, b=b, xb=xb, c=c, q=q: e.transpose(b[:, q * 128:(q + 1) * 128], xb[:, c, :], k.ident), R=[xbr, k.cres], W=[br])
            p.op("act", lambda e, b=b, ss=ss, g=g: e.activation(out=junk, in_=b[:, :], func=AF.Square, accum_out=ss[:, g:g + 1]), R=[br], W=[ssr, jres])
        p.op("dve", lambda e, ss=ss: e.tensor_reduce(out=ss[:, 4:5], in_=ss[:, 0:4], axis=AX.X, op=ALU.add), R=[ssr], W=[ssr])
        p.op("dve", lambda e, ss=ss: e.tensor_scalar(out=ss[:, 5:6], in0=ss[:, 4:5], scalar1=1.0 / D, scalar2=EPS, op0=ALU.mult, op1=ALU.add), R=[ssr], W=[ssr])
        p.op("act", lambda e, ss=ss: e.activation(out=ss[:, 6:7], in_=ss[:, 5:6], func=AF.Sqrt), R=[ssr], W=[ssr])
        p.op("dve", lambda e, ss=ss: e.reciprocal(out=ss[:, 7:8], in_=ss[:, 6:7]), R=[ssr], W=[ssr])
        ob, obr = ob_ring.next()
        for g, (b, br) in enumerate(banks):
            p.op("dve", lambda e, b=b, ob=ob, ss=ss, g=g: e.scalar_tensor_tensor(out=ob[:, g * 512:(g + 1) * 512], in0=b[:, :], scalar=ss[:, 7:8], op0=ALU.mult,
                                                                                 in1=gf[:, g * 512:(g + 1) * 512], op1=ALU.mult), R=[br, ssr, gfr], W=[obr])
        p.dma("sp", lambda e, ob=ob, blk=blk: e.dma_start(out=out[blk * 128:(blk + 1) * 128, :], in_=ob), R=[obr])
    p.barrier()
    a.reset(mk)


def rg_small_inputs(inp):
    cw = np.concatenate([inp["rg_conv_w"][0], inp["rg_conv_b"][0][None, :]], axis=0)
    return {"rgconvT": np.ascontiguousarray(cw.reshape(5, KC, 128).transpose(2, 1, 0)),
            "rgbaT": fm(inp["rg_b_a"][0]), "rgbxT": fm(inp["rg_b_x"][0]), "rglamT": fm(inp["rg_lambda"][0]),
            "rg_w_a": inp["rg_w_a"][0], "rg_w_x": inp["rg_w_x"][0]}


def build_B():
    nc = bass.Bass("TRN2", target_bir_lowering=False)
    with ExitStack() as st:
        k = K(nc, st)
        UE = k.ext_in("UE", [KC, 128, NE])
        UCTX = k.ext_in("UCTX", [KC, 128, NCTX])
        k.ext_in("rgconvT", [128, KC, 5])
        k.ext_in("rgbaT", [128, 2, KC])
        k.ext_in("rgbxT", [128, 2, KC])
        k.ext_in("rglamT", [128, 2, KC])
        wa = k.ext_in("rg_w_a", [2, KC, 128, 128])
        wx = k.ext_in("rg_w_x", [2, KC, 128, 128])
        CAR = k.ext_out("CAR", [128, KC, 4])
        make_basic_consts(k)
        phase_rglru(k, "carry", UE, UCTX, wa, wx, CAR=CAR)
        k.p.final_wait()
        k.p.emit()
    return nc


def build_C():
    nc = bass.Bass("TRN2", target_bir_lowering=False)
    with ExitStack() as st:
        k = K(nc, st)
        p, a = k.p, k.arena
        UE = k.ext_in("UE", [KC, 128, NE])
        UCTX = k.ext_in("UCTX", [KC, 128, NCTX])
        GOWN = k.ext_in("GOWN", [KC, 128, NOWN], BF16)
        XOWN = k.ext_in("XOWN", [KC, 128, NOWN])
        k.ext_in("MODT", [128, 2, 144, 2])
        k.ext_in("normT", [128, 2, 3, KC])
        k.ext_in("rgconvT", [128, KC, 5])
        k.ext_in("rgbaT", [128, 2, KC])
        k.ext_in("rgbxT", [128, 2, KC])
        k.ext_in("rglamT", [128, 2, KC])
        k.ext_in("carF", [128, 3, KC, 2])
        k.ext_in("carB", [128, 3, KC, 2])
        wa = k.ext_in("rg_w_a", [2, KC, 128, 128])
        wx = k.ext_in("rg_w_x", [2, KC, 128, 128])
        rg_w_out = k.ext_in("rg_w_out", [D, D])
        f2i = k.ext_in("ffn2_w_in1", [D, 2 * DFF])
        f2o = k.ext_in("ffn2_w_out1", [DFF, D])
        gfull = k.ext_in("gfull", [128, D])
        out = k.ext_out("out", [NOWN, D])
        X1 = k.scratch("X1", [KC, 128, NU])
        MIX1 = k.scratch("MIX1", [KC, 128, NOWN], BF16)
        make_basic_consts(k)
        cs = load_consts(k, [("MODT", [2, 144, 2]), ("normT", [2, 3, KC])])
        modt = cs["MODT"][0]
        k.modt_res = cs["MODT"][1]
        for c in range(KC):
            p.dma("sp", lambda e, c=c: e.dma_start(out=X1[c, :, U_OWN:U_OWN + NOWN], in_=XOWN[c]))
        sites = {(1, s): derive_site(k, modt, cs["normT"], 1, s, half=(s != 1)) for s in (1, 2)}
        phase_rglru(k, "full", UE, UCTX, wa, wx, GOWN=GOWN, MIX1=MIX1)
        own = [(U_OWN, U_OWN + NOWN)]
        phase_outproj(k, X1, lambda c, u0, ln: MIX1[c, :, u0 - U_OWN:u0 - U_OWN + ln], own, rg_w_out, sites[(1, 1)])
        phase_ffn(k, X1, own, f2i, f2o, sites[(1, 2)])
        phase_final(k, X1, out, gfull)
        p.final_wait()
        p.emit()
    return nc


_PROGS = {}


def _prog(name, fn):
    if name not in _PROGS:
        _PROGS[name] = fn()
    return _PROGS[name]


def core_inputs_F(inp, core):
    m = core_inputs_A(inp, core, "all")
    m.update(rg_small_inputs(inp))
    rank = core % 4
    q = 9 * D // 4
    for l in range(2):
        m[f"w_mod{l}"] = np.ascontiguousarray(inp["w_mod"][l][:, rank * q:(rank + 1) * q])
    m["bmodT"] = np.ascontiguousarray(fm(inp["b_mod"])[:, :, rank * 36:(rank + 1) * 36])
    xfl = np.zeros((4, 4), np.float32)
    for r in range(4):
        xfl[0, r] = 1.0 if r == rank - 1 else 0.0
        xfl[1, r] = 1.0 if r == rank + 1 else 0.0
        xfl[2, r] = 1.0 if r < rank else 0.0
        xfl[3, r] = 1.0 if r > rank else 0.0
    m["xfl"] = np.ascontiguousarray(np.broadcast_to(xfl[None], (128, 4, 4)))
    m["rg_w_out"] = inp["rg_w_out"][0]
    m["ffn2_w_in1"] = inp["ffn2_w_in"][1]
    m["ffn2_w_out1"] = inp["ffn2_w_out"][1]
    m["gfull"] = np.ascontiguousarray(np.broadcast_to(inp["final_norm"][None, :], (128, D)).astype(np.float32))
    return m


def kernel(**inp):
    inp = {k_: np.asarray(v) for k_, v in inp.items()}
    cores = list(range(NCORES))
    nc = _prog("F", lambda: build_A("all", fused=True))
    res = run_bass_kernel_spmd(nc, [core_inputs_F(inp, c) for c in cores], core_ids=cores).results
    out = np.empty((2, 4 * NOWN, D), np.float32)
    for c in cores:
        out[c // 4, (c % 4) * NOWN:(c % 4 + 1) * NOWN] = res[c]["out"]
    return out


def kernel_unfused(**inp):
    inp = {k_: np.asarray(v) for k_, v in inp.items()}
    cores = list(range(NCORES))
    ncA = _prog("A", lambda: build_A("all"))
    resA = run_bass_kernel_spmd(ncA, [core_inputs_A(inp, c, "all") for c in cores], core_ids=cores).results
    small = rg_small_inputs(inp)
    UE = []
    for c in cores:
        ue = np.zeros((KC, 128, NE), np.float32)
        ue[:, :, 2:2 + NOWN] = resA[c]["UOWN"]
        if c % 4 > 0:
            ue[:, :, 0:2] = resA[c - 1]["UOWN"][:, :, NOWN - 2:NOWN]
        if c % 4 < 3:
            ue[:, :, 2 + NOWN] = resA[c + 1]["UOWN"][:, :, 0]
        UE.append(ue)
    ncB = _prog("B", build_B)
    mapsB = [dict(UE=UE[c], UCTX=resA[c]["UCTX"], **small) for c in cores]
    resB = run_bass_kernel_spmd(ncB, mapsB, core_ids=cores).results
    ident = np.zeros((128, KC, 2), np.float32)
    ident[:, :, 0] = 1.0
    mapsC = []
    normT = fm(np.stack([inp["norm_ffn1"], inp["norm_mix"], inp["norm_ffn2"]], axis=1))
    gfull = np.ascontiguousarray(np.broadcast_to(inp["final_norm"][None, :], (128, D)).astype(np.float32))
    for c in cores:
        b, ci = c // 4, c % 4
        prev = [resB[b * 4 + j]["CAR"][:, :, 0:2] for j in range(ci)]
        nxt = [resB[b * 4 + j]["CAR"][:, :, 2:4] for j in range(3, ci, -1)]
        carF = np.stack([ident] * (3 - len(prev)) + prev, axis=1)
        carB = np.stack([ident] * (3 - len(nxt)) + nxt, axis=1)
        m = dict(UE=UE[c], UCTX=resA[c]["UCTX"], GOWN=resA[c]["GOWN"], XOWN=resA[c]["XOWN"], MODT=resA[c]["MODT"],
                 normT=normT, carF=np.ascontiguousarray(carF), carB=np.ascontiguousarray(carB), rg_w_out=inp["rg_w_out"][0],
                 ffn2_w_in1=inp["ffn2_w_in"][1], ffn2_w_out1=inp["ffn2_w_out"][1], gfull=gfull, **small)
        mapsC.append(m)
    ncC = _prog("C", build_C)
    resC = run_bass_kernel_spmd(ncC, mapsC, core_ids=cores).results
    out = np.empty((2, 4 * NOWN, D), np.float32)
    for c in cores:
        out[c // 4, (c % 4) * NOWN:(c % 4 + 1) * NOWN] = resC[c]["out"]
    return out


def phase_rglru_carry(k, UOWN, UCTX, rg_w_a, rg_w_x, CAR, edges, ABS, ctxfin):
    p, a = k.p, k.arena
    mk = a.mark()
    cs = load_consts(k, [("rgconvT", [KC, 5]), ("rgbaT", [2, KC]), ("rgbxT", [2, KC]), ("rglamT", [2, KC])])
    convT, baT, bxT, lamT = cs["rgconvT"], cs["rgbaT"], cs["rgbxT"], cs["rglamT"]
    cst = a.alloc([2, KC], F32)
    cres = Res()
    p.op("act", lambda e: e.activation(out=cst, in_=lamT[0], func=AF.Exp, scale=-1.0), R=[lamT[1]], W=[cres])
    p.op("act", lambda e: e.activation(out=cst, in_=cst, func=AF.Ln, bias=1.0, scale=1.0), R=[cres], W=[cres])
    p.op("dve", lambda e: e.tensor_scalar(out=cst, in0=cst, scalar1=-8.0, scalar2=None, op0=ALU.mult), R=[cres], W=[cres])
    wa = a.alloc([2, KC, 128], BF16)
    wx = a.alloc([2, KC, 128], BF16)
    wres = Res()
    p.dma("pool", lambda e: e.dma_start(out=wa, in_=rg_w_a.rearrange("d h i j -> i d h j")), W=[wres])
    p.dma("pool", lambda e: e.dma_start(out=wx, in_=rg_w_x.rearrange("d h i j -> i d h j")), W=[wres])
    ub_ring = Ring([a.alloc([NE], F32) for _ in range(2)])
    uc_ring = Ring([a.alloc([NCTX + 3], F32) for _ in range(2)])
    for (b_, r_) in uc_ring.bufs:
        p.op("pool", lambda e, b_=b_: e.memset(b_, 0.0), W=[r_])
    ucv_ring = Ring([a.alloc([NS], F32) for _ in range(2)])
    u16_ring = Ring([a.alloc([NS], BF16) for _ in range(2)])
    rbs = [a.alloc([NS], F32) for _ in range(2)]
    ibs = [a.alloc([NS], F32) for _ in range(2)]
    tbs = [a.alloc([NS], F32) for _ in range(2)]
    rres, ires, tress = [Res(), Res()], [Res(), Res()], [Res(), Res()]
    ab = [a.alloc([NS], F32) for _ in range(2)]
    bb = [a.alloc([NS], F32) for _ in range(2)]
    gres = [Res(), Res()]
    hj = a.alloc([NS], F32)
    hjr = Res()
    car = a.alloc([KC, 4], F32)
    car_res = Res()
    bk = k.bank_ring(range(8))
    blocks = split_even(NS, 512)
    state = {}

    def stage1a(c):
        ub, ubr = ub_ring.next()
        ucx, ucxr = uc_ring.next()
        ucv, ucv_res = ucv_ring.next()
        u16, u16_res = u16_ring.next()
        state[c] = (ucv, ucv_res)
        p.dma("sp", lambda e: e.dma_start(out=ub[:, 2:2 + NOWN], in_=UOWN[c]), W=[ubr])
        for (dst_, src_) in ((0, 0), (1, 1), (2 + NOWN, 2)):
            p.op("pool", lambda e, dst_=dst_, src_=src_: e.tensor_copy(out=ub[:, dst_:dst_ + 1], in_=edges[0][:, src_, c:c + 1]), R=[edges[1]], W=[ubr])
        p.dma("sp", lambda e: e.dma_start(out=ucx[:, 2:2 + NCTX], in_=UCTX[c]), W=[ucxr])
        w = lambda kk: convT[0][:, c, kk:kk + 1]
        for (src, srcr, o0, n) in ((ub, ubr, 0, NOWN), (ucx, ucxr, NOWN, NCTX)):
            p.op("dve", lambda e, src=src, o0=o0, n=n: e.tensor_scalar(out=ucv[:, o0:o0 + n], in0=src[:, 0:n], scalar1=w(0), scalar2=w(4), op0=ALU.mult, op1=ALU.add),
                 R=[srcr, convT[1]], W=[ucv_res])
            for kk in (1, 2, 3):
                p.op("dve", lambda e, src=src, o0=o0, n=n, kk=kk: e.scalar_tensor_tensor(out=ucv[:, o0:o0 + n], in0=src[:, kk:kk + n], scalar=w(kk), op0=ALU.mult, in1=ucv[:, o0:o0 + n], op1=ALU.add),
                     R=[srcr, convT[1], ucv_res], W=[ucv_res])
        p.op("dve", lambda e: e.tensor_copy(out=u16, in_=ucv), R=[ucv_res], W=[u16_res])
        for d in range(2):
            for (n0, nl) in blocks:
                rp, rpr = bk.next()
                ip, ipr = bk.next()
                p.op("pe", lambda e, rp=rp, d=d, n0=n0, nl=nl: e.matmul(rp[:, 0:nl], lhsT=wa[:, d, c, :], rhs=u16[:, n0:n0 + nl], start=True, stop=True), R=[wres, u16_res], W=[rpr])
                p.op("pe", lambda e, ip=ip, d=d, n0=n0, nl=nl: e.matmul(ip[:, 0:nl], lhsT=wx[:, d, c, :], rhs=u16[:, n0:n0 + nl], start=True, stop=True), R=[wres, u16_res], W=[ipr])
                p.op("act", lambda e, rp=rp, d=d, n0=n0, nl=nl: e.activation(out=rbs[d][:, n0:n0 + nl], in_=rp[:, 0:nl], func=AF.Sigmoid, bias=baT[0][:, d, c:c + 1], scale=1.0), R=[rpr, baT[1]], W=[rres[d]])
                p.op("act", lambda e, ip=ip, d=d, n0=n0, nl=nl: e.activation(out=ibs[d][:, n0:n0 + nl], in_=ip[:, 0:nl], func=AF.Sigmoid, bias=bxT[0][:, d, c:c + 1], scale=1.0), R=[ipr, bxT[1]], W=[ires[d]])

    def stage1b(c):
        ucv, ucv_res = state.pop(c)
        for d in range(2):
            p.op("act", lambda e, d=d: e.activation(out=ab[d], in_=rbs[d], func=AF.Exp, scale=cst[:, d, c:c + 1]), R=[rres[d], cres], W=[gres[d]])
        for d in range(2):
            p.op("pool", lambda e, d=d: e.tensor_tensor(out=tbs[d], in0=ab[d], in1=ab[d], op=ALU.mult), R=[gres[d]], W=[tress[d]])
        for d in range(2):
            p.op("act", lambda e, d=d: e.activation(out=tbs[d], in_=tbs[d], func=AF.Sqrt, bias=1.0, scale=-1.0), R=[tress[d]], W=[tress[d]])
        for d in range(2):
            p.op("pool", lambda e, d=d: e.tensor_tensor(out=tbs[d], in0=tbs[d], in1=ibs[d], op=ALU.mult), R=[tress[d], ires[d]], W=[tress[d]])
        for d in range(2):
            p.op("dve", lambda e, d=d: e.tensor_tensor(out=bb[d], in0=tbs[d], in1=ucv, op=ALU.mult), R=[tress[d], ucv_res], W=[gres[d]])

    def stage2(c):
        p.op("dve", lambda e: e.tensor_tensor_scan(out=hj[:, NOWN:NS], data0=ab[0][:, NOWN:NS], data1=bb[0][:, NOWN:NS], initial=0.0, op0=ALU.mult, op1=ALU.add), R=[gres[0]], W=[hjr])
        p.op("dve", lambda e: e.tensor_copy(out=ctxfin[0][:, c, 0:1], in_=hj[:, NS - 1:NS]), R=[hjr], W=[ctxfin[1]])
        p.op("dve", lambda e: e.tensor_tensor_scan(out=hj[:, NOWN:NS][:, ::-1], data0=ab[1][:, NOWN:NS][:, ::-1], data1=bb[1][:, NOWN:NS][:, ::-1], initial=0.0, op0=ALU.mult, op1=ALU.add), R=[gres[1]], W=[hjr])
        p.op("dve", lambda e: e.tensor_copy(out=ctxfin[0][:, c, 1:2], in_=hj[:, NOWN:NOWN + 1]), R=[hjr], W=[ctxfin[1]])
        p.op("dve", lambda e: e.tensor_tensor_scan(out=hj[:, 0:NOWN], data0=ab[0][:, 0:NOWN], data1=bb[0][:, 0:NOWN], initial=0.0, op0=ALU.mult, op1=ALU.add), R=[gres[0]], W=[hjr])
        p.op("dve", lambda e: e.tensor_copy(out=car[:, c, 1:2], in_=hj[:, NOWN - 1:NOWN]), R=[hjr], W=[car_res])
        p.op("dve", lambda e: e.tensor_tensor_scan(out=hj[:, 0:NOWN][:, ::-1], data0=ab[1][:, 0:NOWN][:, ::-1], data1=bb[1][:, 0:NOWN][:, ::-1], initial=0.0, op0=ALU.mult, op1=ALU.add), R=[gres[1]], W=[hjr])
        p.op("dve", lambda e: e.tensor_copy(out=car[:, c, 3:4], in_=hj[:, 0:1]), R=[hjr], W=[car_res])
        p.op("dve", lambda e: e.tensor_reduce(out=car[:, c, 0:1], in_=ab[0][:, 0:NOWN], axis=AX.X, op=ALU.mult), R=[gres[0]], W=[car_res])
        p.op("dve", lambda e: e.tensor_reduce(out=car[:, c, 2:3], in_=ab[1][:, 0:NOWN], axis=AX.X, op=ALU.mult), R=[gres[1]], W=[car_res])
        for d in range(2):
            p.dma("sp", lambda e, d=d: e.dma_start(out=ABS[2 * d, c], in_=ab[d][:, 0:NOWN]), R=[gres[d]])
            p.dma("sp", lambda e, d=d: e.dma_start(out=ABS[2 * d + 1, c], in_=bb[d][:, 0:NOWN]), R=[gres[d]])

    stage1a(0)
    stage1b(0)
    for c in range(KC):
        if c + 1 < KC:
            stage1a(c + 1)
        stage2(c)
        if c + 1 < KC:
            stage1b(c + 1)
    p.dma("sp", lambda e: e.dma_start(out=CAR, in_=car), R=[car_res])
    p.barrier()
    a.reset(mk)
```

```python
import numpy as np
from contextlib import ExitStack
import concourse.bass as bass
import concourse.mybir as mybir
from concourse.bass_utils import run_bass_kernel_spmd

F32 = mybir.dt.float32
BF16 = mybir.dt.bfloat16
AF = mybir.ActivationFunctionType
ALU = mybir.AluOpType
AX = mybir.AxisListType

D = 2048
KC = 16
DFF = 5632
FC = 44
NOWN = 2048
HALO = 128
NCTX = 256
NU = 2560
U_OWN = 384
EPS = 1e-6
NCORES = 8

ENGS = ["pe", "act", "dve", "pool", "sp"]
DMA_POOL = {"sp": 8, "act": 4, "pool": 6}
SAME_ENGINE_SYNC = True
SEM_MAX = 4000


class Res:
    __slots__ = ("w", "r")

    def __init__(self):
        self.w = None
        self.r = {}


class Prog:
    def __init__(self, nc, stack, n_phase_sems):
        self.nc = nc
        self.ops = {e: [] for e in ENGS}
        self.dsem = {q: [stack.enter_context(nc.semaphore(f"d_{q}{i}")) for i in range(k)]
                     for q, k in DMA_POOL.items()}
        self.dcnt = {q: 0 for q in DMA_POOL}
        self.free = [stack.enter_context(nc.semaphore(f"e{i}")) for i in range(n_phase_sems)]
        self.esem = {e: self.free.pop() for e in ENGS}
        self.cnt = {e: 0 for e in ENGS}
        self.last = {e: None for e in ENGS}
        self.ccsem = stack.enter_context(nc.semaphore("ccsem"))
        self.cccnt = 0

    def _deps(self, R, W):
        deps = []
        for r in R:
            if r.w is not None:
                deps.append(r.w)
        for w in W:
            if w.w is not None:
                deps.append(w.w)
            deps.extend(w.r.values())
        return deps

    def _commit(self, tok, R, W):
        for r in R:
            r.r[id(tok[0])] = tok
        for w in W:
            w.w = tok
            w.r = {}

    def op(self, eng, fn, R=(), W=()):
        deps = self._deps(R, W)
        if self.cnt[eng] >= SEM_MAX:
            self.esem[eng] = self.free.pop()
            self.cnt[eng] = 0
        self.cnt[eng] += 1
        tok = (self.esem[eng], self.cnt[eng], eng)
        self.last[eng] = tok
        self._commit(tok, R, W)
        self.ops[eng].append((fn, deps, tok[0], 1))
        return tok

    def dma(self, q, fn, R=(), W=()):
        deps = self._deps(R, W)
        j = self.dcnt[q]
        self.dcnt[q] += 1
        k = len(self.dsem[q])
        sem = self.dsem[q][j % k]
        val = 16 * (j // k + 1)
        if val > 16:
            deps.append((sem, val - 16, "dma"))
        tok = (sem, val, "dma")
        self._commit(tok, R, W)
        self.ops[q].append((fn, deps, sem, 16))
        return tok

    def cc(self, fn, R=(), W=()):
        deps = self._deps(R, W)
        self.cccnt += 1
        tok = (self.ccsem, self.cccnt, "cc")
        self._commit(tok, R, W)
        self.ops["pool"].append((fn, deps, self.ccsem, 1))
        return tok

    def all_tokens(self):
        toks = []
        for e in ENGS:
            if self.last[e] is not None:
                toks.append(self.last[e])
        if self.cccnt > 0:
            toks.append((self.ccsem, self.cccnt, "cc"))
        for q in DMA_POOL:
            k = len(self.dsem[q])
            for i in range(min(k, self.dcnt[q])):
                n_uses = (self.dcnt[q] - 1 - i) // k + 1
                toks.append((self.dsem[q][i], 16 * n_uses, "dma"))
        return toks

    def barrier(self):
        toks = self.all_tokens()
        for e in ENGS:
            self.ops[e].append((None, [(s, v, "x") for (s, v, _) in toks], None, 0))

    def final_wait(self):
        toks = self.all_tokens()
        self.ops["sp"].append((None, [(s, v, "x") for (s, v, _) in toks], None, 0))

    def emit(self):
        nc = self.nc
        with nc.Block() as block:
            def run(e):
                def body(engobj):
                    waited = {}
                    for fn, deps, sem, inc in self.ops[e]:
                        need = {}
                        for (s, v, de) in deps:
                            if de == e and (e == "pe" or not SAME_ENGINE_SYNC):
                                continue
                            key = id(s)
                            if waited.get(key, 0) >= v:
                                continue
                            if key not in need or need[key][1] < v:
                                need[key] = (s, v)
                        for key, (s, v) in need.items():
                            engobj.wait_ge(s, v)
                            waited[key] = v
                        if fn is not None:
                            fn(engobj).then_inc(sem, inc)
                return body
            block.tensor(run("pe"))
            block.scalar(run("act"))
            block.vector(run("dve"))
            block.gpsimd(run("pool"))
            block.sync(run("sp"))


class Arena:
    def __init__(self, nc, stack, nbytes):
        self.t32 = stack.enter_context(nc.sbuf_tensor("arena", [128, nbytes // 4], F32))
        self.t16 = self.t32.bitcast(BF16)
        self.n = nbytes
        self.off = 0

    def alloc(self, shape, dt):
        n = int(np.prod(shape))
        sz = 4 if dt == F32 else 2
        self.off = (self.off + 63) // 64 * 64
        o = self.off
        self.off += n * sz
        assert self.off <= self.n, f"arena overflow {self.off} > {self.n}"
        ap = (self.t32[:, o // 4:o // 4 + n] if dt == F32 else self.t16[:, o // 2:o // 2 + n])
        if len(shape) == 2:
            ap = ap.rearrange("p (a b) -> p a b", a=shape[0])
        elif len(shape) == 3:
            ap = ap.rearrange("p (a b c) -> p a b c", a=shape[0], b=shape[1])
        return ap

    def mark(self):
        return self.off

    def reset(self, m):
        self.off = m


class Ring:
    def __init__(self, bufs):
        self.bufs = [(b, Res()) for b in bufs]
        self.i = 0

    def next(self):
        b = self.bufs[self.i % len(self.bufs)]
        self.i += 1
        return b


def split_even(n, maxlen):
    k = -(-n // maxlen)
    base = n // k
    rem = n - base * k
    out = []
    o = 0
    for i in range(k):
        ln = base + (1 if i < rem else 0)
        out.append((o, ln))
        o += ln
    return out


def make_tiles(ranges, tmax):
    total = sum(b - a for a, b in ranges)
    ntile = -(-total // tmax)
    tl = -(-total // ntile)
    tl = (tl + 1) // 2 * 2
    tiles = []
    cur = []
    curlen = 0
    for a, b in ranges:
        pos = a
        while pos < b:
            lim = b if pos >= NCTX else min(b, NCTX)
            take = min(lim - pos, tl - curlen)
            cur.append((pos, take, curlen, 1 if pos < NCTX else 0))
            curlen += take
            pos += take
            if curlen == tl:
                tiles.append(cur)
                cur = []
                curlen = 0
    if cur:
        tiles.append(cur)
    return tiles


class K:
    def __init__(self, nc, stack, n_phase_sems=80):
        self.nc = nc
        self.st = stack
        self.p = Prog(nc, stack, n_phase_sems)
        self.arena = Arena(nc, stack, 175 * 1024)
        self.banks = [stack.enter_context(nc.psum_tensor(f"bank{i}", [128, 512], F32)) for i in range(8)]
        self.dram = {}
        self.bank_res = [Res() for _ in range(8)]

    def ext_in(self, name, shape, dt=F32):
        t = self.nc.dram_tensor(name, list(shape), dt, kind="ExternalInput").ap()
        self.dram[name] = t
        return t

    def ext_out(self, name, shape, dt=F32):
        t = self.nc.dram_tensor(name, list(shape), dt, kind="ExternalOutput").ap()
        self.dram[name] = t
        return t

    def scratch(self, name, shape, dt=F32):
        t = self.nc.dram_tensor(name, list(shape), dt, kind="Internal").ap()
        self.dram[name] = t
        return t

    def bank_ring(self, idx):
        r = Ring([])
        r.bufs = [(self.banks[i], self.bank_res[i]) for i in idx]
        return r


def load_consts(k, names_shapes):
    out = {}
    for name, shape in names_shapes:
        src = k.dram[name]
        t = k.arena.alloc(shape, F32)
        r = Res()
        pat = {1: None, 2: None, 3: None}
        k.p.dma("sp", lambda e, t=t, src=src: e.dma_start(out=t, in_=src), W=[r])
        out[name] = (t, r)
    return out


def make_basic_consts(k):
    a = k.arena
    p = k.p
    ident = a.alloc([128], F32)
    ones32 = a.alloc([128], F32)
    ones16 = a.alloc([128], BF16)
    r = Res()
    p.op("pool", lambda e: e.memset(ident, 0.0), W=[r])
    p.op("pool", lambda e: e.affine_select(out=ident, in_=ident, compare_op=ALU.not_equal, fill=1.0,
                                           base=0, pattern=[[-1, 128]], channel_multiplier=1), R=[r], W=[r])
    p.op("pool", lambda e: e.memset(ones32, 1.0), W=[r])
    p.op("pool", lambda e: e.memset(ones16, 1.0), W=[r])
    k.ident, k.ones32, k.ones16, k.cres = ident, ones32, ones16, r


def phase_transpose_in(k, xin, XU, barrier=True):
    p, a = k.p, k.arena
    m = a.mark()
    xt = Ring([a.alloc([D], F32) for _ in range(3)])
    xo = Ring([a.alloc([KC, 128], F32) for _ in range(2)])
    bk = k.bank_ring(range(8))
    XUv = XU.rearrange("c p t -> p c t")
    nblk = NU // 128
    loaded = {}

    def t0_load(blk):
        t, tr = xt.next()
        p.dma("sp", lambda e, t=t, blk=blk: e.dma_start(out=t, in_=xin[blk * 128:(blk + 1) * 128, :]), W=[tr])
        loaded[blk] = (t, tr)

    t0_load(0)
    for blk in range(nblk):
        if blk + 1 < nblk:
            t0_load(blk + 1)
        t, tr = loaded.pop(blk)
        o, orr = xo.next()
        for g in range(4):
            b, br = bk.next()
            for q in range(4):
                c = 4 * g + q
                p.op("pe", lambda e, b=b, t=t, c=c, q=q: e.transpose(b[:, q * 128:(q + 1) * 128], t[:, c * 128:(c + 1) * 128], k.ident),
                     R=[tr, k.cres], W=[br])
            eng = "act" if g % 2 == 0 else "dve"
            if eng == "act":
                p.op("act", lambda e, o=o, b=b, g=g: e.activation(out=o[:, 4 * g:4 * g + 4, :].rearrange("p a b -> p (a b)"), in_=b[:, :], func=AF.Copy),
                     R=[br], W=[orr])
            else:
                p.op("dve", lambda e, o=o, b=b, g=g: e.tensor_copy(out=o[:, 4 * g:4 * g + 4, :].rearrange("p a b -> p (a b)"), in_=b[:, :]),
                     R=[br], W=[orr])
        p.dma("sp", lambda e, o=o, blk=blk: e.dma_start(out=XUv[:, :, blk * 128:(blk + 1) * 128], in_=o), R=[orr])
    if barrier:
        p.barrier()
        a.reset(m)


def phase_mod(k, w_mod, cvT, bmodT, modt, layers, njg=36):
    p, a = k.p, k.arena
    m = a.mark()
    sc = a.alloc([KC, 2], F32)
    scr = Res()
    p.op("act", lambda e: e.activation(out=sc, in_=cvT[0], func=AF.Silu), R=[cvT[1]], W=[scr])
    wr = Ring([a.alloc([KC, 512], F32) for _ in range(3)])
    bk = k.bank_ring([0, 1])
    mr = k.modt_res
    n = 0
    for l in layers:
        for jg in range(njg):
            w, wres = wr.next()
            q_ = "sp" if n % 2 == 0 else "act"
            n += 1
            p.dma(q_, lambda e, w=w, l=l, jg=jg: e.dma_start(out=w, in_=w_mod[l][:, jg * 512:(jg + 1) * 512].rearrange("(k p) n -> p k n", p=128)), W=[wres])
            b, br = bk.next()
            for q in range(4):
                for kc in range(KC):
                    p.op("pe", lambda e, b=b, w=w, q=q, kc=kc: e.matmul(b[:, 2 * q:2 * q + 2], lhsT=w[:, kc, q * 128:(q + 1) * 128], rhs=sc[:, kc, :],
                                                                         start=(kc == 0), stop=(kc == KC - 1)),
                         R=[wres, scr], W=[br])
            for q in range(4):
                j = jg * 4 + q
                p.op("dve", lambda e, b=b, q=q, j=j, l=l: e.tensor_scalar(out=modt[:, l, j, :], in0=b[:, 2 * q:2 * q + 2], scalar1=bmodT[0][:, l, j:j + 1], scalar2=None, op0=ALU.add),
                     R=[br, bmodT[1]], W=[mr])
    p.barrier()
    a.reset(m)


def derive_site(k, modt, normT, l, site, half):
    p, a = k.p, k.arena
    gs = a.alloc([KC, 2], F32)
    gt = a.alloc([KC, 2], F32)
    r = Res()
    j0 = 3 * site * KC
    sh = modt[:, l, j0:j0 + KC, :]
    p.op("dve", lambda e: e.tensor_scalar(out=gs, in0=modt[:, l, j0 + KC:j0 + 2 * KC, :], scalar1=1.0, scalar2=None, op0=ALU.add),
         R=[k.modt_res], W=[r])
    for rr in range(2):
        p.op("dve", lambda e, rr=rr: e.tensor_tensor(out=gs[:, :, rr], in0=gs[:, :, rr], in1=normT[0][:, l, site, :], op=ALU.mult),
             R=[r, normT[1]], W=[r])
    p.op("dve", lambda e: e.tensor_scalar(out=gt, in0=modt[:, l, j0 + 2 * KC:j0 + 3 * KC, :], scalar1=(0.5 if half else 1.0), scalar2=None, op0=ALU.mult),
         R=[k.modt_res], W=[r])
    return dict(gs=gs, sh=sh, gt=gt, res=r)


def emit_modulate(k, XU, tile, T, xs, xs_res, h, h_res, site, tmp_ring, rstd, rstd_res, stat_banks):
    p = k.p
    XUv = XU.rearrange("c p t -> p c t")
    for (u0, ln, off, r) in tile:
        p.dma("sp", lambda e, u0=u0, ln=ln, off=off: e.dma_start(out=xs[:, :, off:off + ln], in_=XUv[:, :, u0:u0 + ln]), W=[xs_res])
    subs = split_even(T, 512)
    banks = [stat_banks.next() for _ in subs]
    for kc in range(KC):
        sq, sqr = tmp_ring.next()
        p.op("act", lambda e, sq=sq, kc=kc: e.activation(out=sq[:, 0:T], in_=xs[:, kc, :], func=AF.Square), R=[xs_res], W=[sqr])
        for (n0, nl), (b, br) in zip(subs, banks):
            p.op("pe", lambda e, b=b, sq=sq, n0=n0, nl=nl, kc=kc: e.matmul(b[:, 0:nl], lhsT=k.ones32, rhs=sq[:, n0:n0 + nl], start=(kc == 0), stop=(kc == KC - 1)),
                 R=[sqr, k.cres], W=[br])
    for (n0, nl), (b, br) in zip(subs, banks):
        p.op("dve", lambda e, b=b, n0=n0, nl=nl: e.tensor_scalar(out=rstd[:, n0:n0 + nl], in0=b[:, 0:nl], scalar1=1.0 / D, scalar2=EPS, op0=ALU.mult, op1=ALU.add),
             R=[br], W=[rstd_res])
    p.op("act", lambda e: e.activation(out=rstd[:, 0:T], in_=rstd[:, 0:T], func=AF.Sqrt), R=[rstd_res], W=[rstd_res])
    p.op("dve", lambda e: e.reciprocal(out=rstd[:, 0:T], in_=rstd[:, 0:T]), R=[rstd_res], W=[rstd_res])
    i = 0
    for (u0, ln, off, r) in tile:
        for kc in range(KC):
            t, tr = tmp_ring.next()
            p.op("dve", lambda e, t=t, kc=kc, off=off, ln=ln, r=r: e.scalar_tensor_tensor(
                out=t[:, 0:ln], in0=xs[:, kc, off:off + ln], scalar=site["gs"][:, kc, r:r + 1], op0=ALU.mult,
                in1=rstd[:, off:off + ln], op1=ALU.mult), R=[xs_res, rstd_res, site["res"]], W=[tr])
            if True:
                p.op("act", lambda e, t=t, kc=kc, off=off, ln=ln, r=r: e.activation(
                    out=h[:, kc, off:off + ln], in_=t[:, 0:ln], func=AF.Identity, bias=site["sh"][:, kc, r:r + 1], scale=1.0),
                    R=[tr, k.modt_res], W=[h_res])
            else:
                p.op("pool", lambda e, t=t, kc=kc, off=off, ln=ln, r=r: e.tensor_scalar(
                    out=h[:, kc, off:off + ln], in0=t[:, 0:ln], scalar1=site["sh"][:, kc, r:r + 1], scalar2=None, op0=ALU.add),
                    R=[tr, k.modt_res], W=[h_res])
            i += 1


def emit_residual(k, XU, tile, m, sub_banks, subs, site, xm_ring, xo_ring, xres=None):
    p = k.p
    xm, xmr = xm_ring.next()
    xo, xor_ = xo_ring.next()
    for (u0, ln, off, r) in tile:
        p.dma("sp", lambda e, u0=u0, ln=ln, off=off, xm=xm: e.dma_start(out=xm[:, off:off + ln], in_=XU[m, :, u0:u0 + ln]), R=([xres] if xres is not None else []), W=[xmr])
    for (n0, nl), (b, br) in zip(subs, sub_banks):
        for (u0, ln, off, r) in tile:
            lo, hi = max(n0, off), min(n0 + nl, off + ln)
            if lo >= hi:
                continue
            p.op("dve", lambda e, b=b, lo=lo, hi=hi, n0=n0, r=r, xm=xm, xo=xo: e.scalar_tensor_tensor(
                out=xo[:, lo:hi], in0=b[:, lo - n0:hi - n0], scalar=site["gt"][:, m, r:r + 1], op0=ALU.mult,
                in1=xm[:, lo:hi], op1=ALU.add), R=[br, xmr, site["res"]], W=[xor_])
    for (u0, ln, off, r) in tile:
        p.dma("sp", lambda e, u0=u0, ln=ln, off=off, xo=xo: e.dma_start(out=XU[m, :, u0:u0 + ln], in_=xo[:, off:off + ln]), R=[xor_], W=([xres] if xres is not None else []))


TMAX = 640
TMAX_FFN = 1152
TMAX_BIG = 1280


class ModStream:
    def __init__(self, k, XU, tile, site, xc_ring, tmp_ring, rstd, rstd_res, stat_banks, sq_ring=None):
        self.sq_ring = sq_ring
        self.k, self.XU, self.tile, self.site = k, XU, tile, site
        self.T = sum(s[1] for s in tile)
        self.xc_ring, self.tmp_ring = xc_ring, tmp_ring
        self.rstd, self.rstd_res = rstd, rstd_res
        self.subs = split_even(self.T, 512)
        self.banks = stat_banks[:len(self.subs)]
        self.sq = {}

    def _load(self, kc):
        p = self.k.p
        xc, xcr = self.xc_ring.next()
        for (u0, ln, off, r) in self.tile:
            p.dma("sp", lambda e, xc=xc, u0=u0, ln=ln, off=off, kc=kc: e.dma_start(out=xc[:, off:off + ln], in_=self.XU[kc, :, u0:u0 + ln]), W=[xcr])
        return xc, xcr

    def load_sq(self, kc):
        p, T = self.k.p, self.T
        xc, xcr = self._load(kc)
        sq, sqr = (self.sq_ring or self.tmp_ring).next()
        p.op("act", lambda e, sq=sq, xc=xc, T=T: e.activation(out=sq[:, 0:T], in_=xc[:, 0:T], func=AF.Square), R=[xcr], W=[sqr])
        self.sq[kc] = (sq, sqr)

    def mm(self, kc):
        p, k = self.k.p, self.k
        sq, sqr = self.sq.pop(kc)
        for (n0, nl), (b, br) in zip(self.subs, self.banks):
            p.op("pe", lambda e, b=b, sq=sq, n0=n0, nl=nl, kc=kc: e.matmul(b[:, 0:nl], lhsT=(k.ones16 if self.sq_ring is not None else k.ones32), rhs=sq[:, n0:n0 + nl], start=(kc == 0), stop=(kc == KC - 1)),
                 R=[sqr, k.cres], W=[br])

    def finish(self, h, h_res):
        self.finish_rstd()
        for kc in range(KC):
            self.h_chunk(kc, h, h_res)

    def finish_rstd(self):
        p, k, T, rstd, rstd_res, site = self.k.p, self.k, self.T, self.rstd, self.rstd_res, self.site
        for (n0, nl), (b, br) in zip(self.subs, self.banks):
            p.op("dve", lambda e, b=b, n0=n0, nl=nl: e.tensor_scalar(out=rstd[:, n0:n0 + nl], in0=b[:, 0:nl], scalar1=1.0 / D, scalar2=EPS, op0=ALU.mult, op1=ALU.add),
                 R=[br], W=[rstd_res])
        p.op("act", lambda e: e.activation(out=rstd[:, 0:T], in_=rstd[:, 0:T], func=AF.Sqrt), R=[rstd_res], W=[rstd_res])
        p.op("dve", lambda e: e.reciprocal(out=rstd[:, 0:T], in_=rstd[:, 0:T]), R=[rstd_res], W=[rstd_res])

    def h_chunk(self, kc, h, h_res):
        p, k, T, rstd, rstd_res, site = self.k.p, self.k, self.T, self.rstd, self.rstd_res, self.site
        i = 0
        if True:
            xc, xcr = self._load(kc)
            for (u0, ln, off, r) in self.tile:
                t, tr = self.tmp_ring.next()
                p.op("dve", lambda e, t=t, xc=xc, kc=kc, off=off, ln=ln, r=r: e.scalar_tensor_tensor(
                    out=t[:, 0:ln], in0=xc[:, off:off + ln], scalar=site["gs"][:, kc, r:r + 1], op0=ALU.mult,
                    in1=rstd[:, off:off + ln], op1=ALU.mult), R=[xcr, rstd_res, site["res"]], W=[tr])
                if True:
                    p.op("act", lambda e, t=t, kc=kc, off=off, ln=ln, r=r: e.activation(
                        out=h[:, kc, off:off + ln], in_=t[:, 0:ln], func=AF.Identity, bias=site["sh"][:, kc, r:r + 1], scale=1.0),
                        R=[tr, k.modt_res], W=[h_res])
                else:
                    p.op("pool", lambda e, t=t, kc=kc, off=off, ln=ln, r=r: e.tensor_scalar(
                        out=h[:, kc, off:off + ln], in0=t[:, 0:ln], scalar1=site["sh"][:, kc, r:r + 1], scalar2=None, op0=ALU.add),
                        R=[tr, k.modt_res], W=[h_res])
                i += 1


def phase_ffn(k, XU, ranges, w_in, w_out, site):
    p, a = k.p, k.arena
    mk = a.mark()
    NH = 2
    FH = FC // NH
    tiles = make_tiles(ranges, TMAX_FFN)
    TM = max(sum(s[1] for s in t) for t in tiles)
    act = a.alloc([FH, TM], BF16)
    big_res = Res()
    h = a.alloc([KC, TM], BF16)
    h_res = Res()
    rstds = [(a.alloc([TM], F32), Res()), (a.alloc([TM], F32), Res())]
    xc_ring = Ring([a.alloc([TM], F32) for _ in range(2)])
    tmp_ring = Ring([a.alloc([TM], F32) for _ in range(2)])
    sq_ring = Ring([a.alloc([TM], BF16) for _ in range(2)])
    wi_ring = Ring([a.alloc([KC, 256], BF16) for _ in range(2)])
    wo_ring = Ring([a.alloc([FH, 128], BF16) for _ in range(3)])
    sg_ring = Ring([a.alloc([512], F32) for _ in range(2)])
    xo_ring = Ring([a.alloc([TM], F32) for _ in range(2)])
    bk8 = k.bank_ring(range(8))
    bk5 = k.bank_ring(range(5))
    stat_banks = [(k.banks[i], k.bank_res[i]) for i in (5, 6, 7)]
    w_in_v = w_in.rearrange("(k p) n -> p k n", p=128)
    w_out_v = w_out.rearrange("(j p) n -> p j n", p=128)

    def new_ms(ti):
        rs, rr = rstds[ti % 2]
        return ModStream(k, XU, tiles[ti], site, xc_ring, tmp_ring, rs, rr, stat_banks, sq_ring=sq_ring)

    mss = {0: new_ms(0)}
    for kc in range(KC):
        mss[0].load_sq(kc)
        mss[0].mm(kc)
    mss[0].finish_rstd()
    for kc in range(KC):
        mss[0].h_chunk(kc, h, h_res)
    for ti, tile in enumerate(tiles):
        T = sum(s[1] for s in tile)
        subs = split_even(T, 512)
        xu_res = [Res() for _ in range(KC)]
        for hf in range(NH):
            for jl in range(FH):
                j = hf * FH + jl
                wi, wir = wi_ring.next()
                p.dma("pool", lambda e, wi=wi, j=j: e.dma_start(out=wi[:, :, 0:128], in_=w_in_v[:, :, j * 128:(j + 1) * 128]), W=[wir])
                p.dma("pool", lambda e, wi=wi, j=j: e.dma_start(out=wi[:, :, 128:256], in_=w_in_v[:, :, DFF + j * 128:DFF + (j + 1) * 128]), W=[wir])
                for (n0, nl) in subs:
                    g_, gr = bk8.next()
                    u_, ur = bk8.next()
                    for kc in range(KC):
                        p.op("pe", lambda e, g_=g_, wi=wi, kc=kc, n0=n0, nl=nl: e.matmul(g_[:, 0:nl], lhsT=wi[:, kc, 0:128], rhs=h[:, kc, n0:n0 + nl], start=(kc == 0), stop=(kc == KC - 1)),
                             R=[wir, h_res], W=[gr])
                        p.op("pe", lambda e, u_=u_, wi=wi, kc=kc, n0=n0, nl=nl: e.matmul(u_[:, 0:nl], lhsT=wi[:, kc, 128:256], rhs=h[:, kc, n0:n0 + nl], start=(kc == 0), stop=(kc == KC - 1)),
                             R=[wir, h_res], W=[ur])
                    sg, sgr = sg_ring.next()
                    p.op("act", lambda e, sg=sg, g_=g_, nl=nl: e.activation(out=sg[:, 0:nl], in_=g_[:, 0:nl], func=AF.Silu), R=[gr], W=[sgr])
                    p.op("dve", lambda e, sg=sg, u_=u_, n0=n0, nl=nl, jl=jl: e.tensor_tensor(out=act[:, jl, n0:n0 + nl], in0=u_[:, 0:nl], in1=sg[:, 0:nl], op=ALU.mult),
                         R=[ur, sgr, gr], W=[big_res])
            last = (hf == NH - 1)
            nms = None
            if hf == 0 and ti + 1 < len(tiles):
                nms = mss[ti + 1] = new_ms(ti + 1)
            hms = mss.get(ti + 1) if last else None
            for m in range(KC):
                if hms is not None:
                    hms.h_chunk(m, h, h_res)
                if nms is not None:
                    nms.load_sq(m)
                wo, wor = wo_ring.next()
                p.dma("pool", lambda e, wo=wo, m=m, hf=hf: e.dma_start(out=wo, in_=w_out_v[:, hf * FH:(hf + 1) * FH, m * 128:(m + 1) * 128]), W=[wor])
                ob = [bk5.next() for _ in subs]
                for jl in range(FH):
                    for (n0, nl), (b, br) in zip(subs, ob):
                        p.op("pe", lambda e, b=b, wo=wo, jl=jl, n0=n0, nl=nl: e.matmul(b[:, 0:nl], lhsT=wo[:, jl, :], rhs=act[:, jl, n0:n0 + nl], start=(jl == 0), stop=(jl == FH - 1)),
                             R=[wor, big_res], W=[br])
                if nms is not None and m > 0:
                    nms.mm(m - 1)
                emit_residual(k, XU, tile, m, ob, subs, site, xc_ring, xo_ring, xres=xu_res[m])
            if nms is not None:
                nms.mm(KC - 1)
                nms.finish_rstd()
    p.barrier()
    a.reset(mk)


def fm(v):
    v = np.asarray(v, np.float32)
    lead = v.shape[:-1]
    n = v.shape[-1] // 128
    t = v.reshape(lead + (n, 128))
    t = np.moveaxis(t, -1, 0)
    return np.ascontiguousarray(t)


def core_xin(x, ctx, core):
    b, c = core // 4, core % 4
    lo = c * NOWN - HALO
    hi = (c + 1) * NOWN + HALO
    seq = x.shape[1]
    buf = np.zeros((NU, D), np.float32)
    buf[0:NCTX] = ctx[b]
    a0, a1 = max(lo, 0), min(hi, seq)
    buf[NCTX + (a0 - lo):NCTX + (a1 - lo)] = x[b, a0:a1]
    return buf


def rope_tables(core):
    c = core % 4
    u = np.arange(NU)
    t = c * NOWN - HALO + (u - NCTX)
    t = np.clip(t, 0, 4 * NOWN - 1)
    row = (t // 64).astype(np.float32)
    col = (t % 64).astype(np.float32)
    inv_freq = (np.float32(10000.0) ** (-np.arange(32, dtype=np.float32) / np.float32(32))).astype(np.float32)
    pidx = np.arange(128)
    axis = pidx // 64
    half = (pidx % 64) // 32
    fr = pidx % 32
    pos = np.where(axis[:, None] == 0, row[None, :], col[None, :]).astype(np.float32)
    ang = (pos * inv_freq[fr][:, None]).astype(np.float32)
    C = np.cos(ang).astype(np.float32)
    S = np.sin(ang).astype(np.float32)
    S = np.where(half[:, None] == 0, -S, S).astype(np.float32)
    C[:, :NCTX] = 1.0
    S[:, :NCTX] = 0.0
    return np.ascontiguousarray(C), np.ascontiguousarray(S)


def build_A(upto="all", dbg=False, fused=False):
    order = ["ffn1", "abin", "abmix", "about", "l0", "l1ffn1", "all"]
    lvl = order.index(upto)
    nc = bass.Bass("TRN2", target_bir_lowering=False)
    with ExitStack() as st:
        k = K(nc, st)
        p, a = k.p, k.arena
        xin = k.ext_in("xin", [NU, D])
        k.ext_in("cvT", [128, KC, 2])
        k.ext_in("bmodT", [128, 2, 36 if fused else 144])
        k.ext_in("normT", [128, 2, 3, KC])
        layers = [0, 1] if lvl >= 5 else [0]
        w_mod = {l: k.ext_in(f"w_mod{l}", [D, 9 * D // 4 if fused else 9 * D]) for l in layers}
        f1i = {l: k.ext_in(f"ffn1_w_in{l}", [D, 2 * DFF]) for l in layers}
        f1o = {l: k.ext_in(f"ffn1_w_out{l}", [DFF, D]) for l in layers}
        XU = k.ext_out("XU", [KC, 128, NU]) if dbg else k.scratch("XU", [KC, 128, NU])
        modt_out = None if fused else k.ext_out("MODT", [128, 2, 144, 2])
        names = [("cvT", [KC, 2]), ("bmodT", [2, 36 if fused else 144]), ("normT", [2, 3, KC])]
        if lvl >= 1:
            ab_w_in = k.ext_in("ab_w_in", [D, 4608])
            k.ext_in("ropeC", [128, NU])
            k.ext_in("ropeS", [128, NU])
            QU = k.scratch("QU", [8, 128, NU], BF16)
            KU = k.scratch("KU", [2, 128, NU], BF16)
            VU = k.scratch("VU", [NU, 256], BF16)
            ZU = k.scratch("ZU", [8, 128, NU])
            BGU = k.scratch("BGU", [8, 128, NU])
        if lvl >= 2:
            k.ext_in("masks", [128, 2, 128])
            k.ext_in("flags", [128, 2])
            k.ext_in("sinkT", [128, 8])
            k.ext_in("convT", [128, 3, 8])
            MIXU = k.ext_out("MIXU", [KC, 128, NU], BF16) if dbg else k.scratch("MIXU", [KC, 128, NU], BF16)
        if lvl >= 3:
            ab_w_out = k.ext_in("ab_w_out", [D, D])
        if lvl >= 4:
            f2i0 = k.ext_in("ffn2_w_in0", [D, 2 * DFF])
            f2o0 = k.ext_in("ffn2_w_out0", [DFF, D])
        if lvl >= 6 and not fused:
            rg_w_in = k.ext_in("rg_w_in", [D, 2 * D])
            XOWN = k.ext_out("XOWN", [KC, 128, NOWN])
            GOWN = k.ext_out("GOWN", [KC, 128, NOWN], BF16)
            UOWN = k.ext_out("UOWN", [KC, 128, NOWN])
            UCTX = k.ext_out("UCTX", [KC, 128, NCTX])
        if fused:
            rg_w_in = k.ext_in("rg_w_in", [D, 2 * D])
            GOWN = k.scratch("GOWN", [KC, 128, NOWN], BF16)
            UOWN = k.scratch("UOWN", [KC, 128, NOWN])
            UCTX = k.scratch("UCTX", [KC, 128, NCTX])
            MIX1 = k.scratch("MIX1", [KC, 128, NOWN], BF16)
            PUB1 = k.scratch("PUB1", [128 * 3, KC])
            GAT1 = k.scratch("GAT1", [4 * 128 * 3, KC])
            PUB2 = k.scratch("PUB2", [D, 4])
            GAT2 = k.scratch("GAT2", [4 * D, 4])
            k.ext_in("rgconvT", [128, KC, 5])
            k.ext_in("rgbaT", [128, 2, KC])
            k.ext_in("rgbxT", [128, 2, KC])
            k.ext_in("rglamT", [128, 2, KC])
            k.ext_in("xfl", [128, 4, 4])
            rg_wa = k.ext_in("rg_w_a", [2, KC, 128, 128])
            rg_wx = k.ext_in("rg_w_x", [2, KC, 128, 128])
            rg_w_out = k.ext_in("rg_w_out", [D, D])
            f2i1 = k.ext_in("ffn2_w_in1", [D, 2 * DFF])
            f2o1 = k.ext_in("ffn2_w_out1", [DFF, D])
            gfull = k.ext_in("gfull", [128, D])
            out = k.ext_out("out", [NOWN, D])
            names += [("xfl", [4, 4])]
        make_basic_consts(k)
        cs = load_consts(k, names)
        modt = a.alloc([2, 144, 2], F32)
        k.modt_res = Res()
        m0 = a.mark()
        phase_transpose_in(k, xin, XU, barrier=not fused)
        if not fused:
            phase_mod(k, w_mod, cs["cvT"], cs["bmodT"], modt, layers)
            p.dma("sp", lambda e: e.dma_start(out=modt_out, in_=modt), R=[k.modt_res])
        else:
            PUBM = k.scratch("PUBM", [128, 144])
            GATM = k.scratch("GATM", [512, 144])
            modq = a.alloc([2, 36, 2], F32)
            phase_mod(k, w_mod, cs["cvT"], cs["bmodT"], modq, layers, njg=9)
            p.dma("sp", lambda e: e.dma_start(out=PUBM.rearrange("p (l j r) -> p l j r", l=2, r=2), in_=modq), R=[k.modt_res])
            p.barrier()
            emit_allgather(k, PUBM, GATM)
            p.barrier()
            for rk in range(4):
                for l in range(2):
                    p.dma("sp", lambda e, rk=rk, l=l: e.dma_start(out=modt[:, l, rk * 36:(rk + 1) * 36, :],
                                                                   in_=GATM[rk * 128:(rk + 1) * 128, l * 72:(l + 1) * 72].rearrange("p (j r) -> p j r", r=2)), W=[k.modt_res])
            p.barrier()
            a.reset(m0)
        sites = {(l, s): derive_site(k, modt, cs["normT"], l, s, half=(s != 1)) for l in layers for s in range(3)}
        phase_ffn(k, XU, [(0, NU)], f1i[0], f1o[0], sites[(0, 0)])
        if lvl >= 1:
            phase_ab_in(k, XU, ab_w_in, sites[(0, 1)], QU, KU, VU, ZU, BGU)
        if lvl >= 2:
            phase_ab_mix(k, QU, KU, VU, ZU, BGU, MIXU)
        own_ctx = [(0, NCTX), (U_OWN, U_OWN + NOWN)]
        if lvl >= 3:
            phase_outproj(k, XU, lambda c, u0, ln: MIXU[c, :, u0:u0 + ln], own_ctx, ab_w_out, sites[(0, 1)])
        if lvl >= 4:
            phase_ffn(k, XU, own_ctx, f2i0, f2o0, sites[(0, 2)])
        if lvl >= 5:
            phase_ffn(k, XU, own_ctx, f1i[1], f1o[1], sites[(1, 0)])
        if fused:
            edges = a.alloc([3, KC], F32)
            carF = a.alloc([4, KC, 2], F32)
            carB = a.alloc([4, KC, 2], F32)
            edges_res, car_res = Res(), Res()
            ctxfin = a.alloc([KC, 2], F32)
            ctxfin_res = Res()
            ABS = k.scratch("ABS", [4, KC, 128, NOWN])
        if lvl >= 6:
            def g_of(c, u0, ln):
                return None if u0 < NCTX else GOWN[c, :, u0 - U_OWN:u0 - U_OWN + ln]

            def u_of(c, u0, ln):
                return UCTX[c, :, u0:u0 + ln] if u0 < NCTX else UOWN[c, :, u0 - U_OWN:u0 - U_OWN + ln]
            phase_rg_in(k, XU, rg_w_in, sites[(1, 1)], g_of, u_of)
            if not fused:
                for c in range(KC):
                    p.dma("sp", lambda e, c=c: e.dma_start(out=XOWN[c], in_=XU[c, :, U_OWN:U_OWN + NOWN]))
        if fused:
            phase_exchange_edges(k, UOWN, PUB1, GAT1, cs["xfl"], edges, edges_res)
            phase_rglru_carry(k, UOWN, UCTX, rg_wa, rg_wx, PUB2.rearrange("(c p) e -> p c e", p=128), (edges, edges_res), ABS, (ctxfin, ctxfin_res))
            phase_exchange_carries(k, PUB2, GAT2, cs["xfl"], carF, carB, car_res)
            phase_rglru_apply(k, ABS, (ctxfin, ctxfin_res), ((carF, car_res), (carB, car_res)), GOWN, MIX1)
            own = [(U_OWN, U_OWN + NOWN)]
            phase_outproj(k, XU, lambda c, u0, ln: MIX1[c, :, u0 - U_OWN:u0 - U_OWN + ln], own, rg_w_out, sites[(1, 1)])
            phase_ffn(k, XU, own, f2i1, f2o1, sites[(1, 2)])
            phase_final(k, XU, out, gfull)
        p.final_wait()
        p.emit()
    return nc


def core_inputs_A(inp, core, upto="all"):
    order = ["ffn1", "abin", "abmix", "about", "l0", "l1ffn1", "all"]
    lvl = order.index(upto)
    b, c = core // 4, core % 4
    cv = np.stack([inp["c"][b], inp["c_ctx"]], axis=-1)
    m = {"xin": core_xin(inp["x"], inp["ctx"], core),
         "cvT": np.ascontiguousarray(cv.reshape(KC, 128, 2).transpose(1, 0, 2)),
         "bmodT": fm(inp["b_mod"]),
         "normT": fm(np.stack([inp["norm_ffn1"], inp["norm_mix"], inp["norm_ffn2"]], axis=1))}
    for l in ([0, 1] if lvl >= 5 else [0]):
        m[f"w_mod{l}"] = inp["w_mod"][l]
        m[f"ffn1_w_in{l}"] = inp["ffn1_w_in"][l]
        m[f"ffn1_w_out{l}"] = inp["ffn1_w_out"][l]
    if lvl >= 1:
        m["ab_w_in"] = inp["ab_w_in"][0]
        m["ropeC"], m["ropeS"] = rope_tables(core)
    if lvl >= 2:
        j = np.arange(128)[:, None]
        i = np.arange(128)[None, :]
        m["masks"] = np.ascontiguousarray(np.stack([(j >= i), (j <= i)], axis=1).astype(np.float32))
        m["flags"] = np.ascontiguousarray(np.broadcast_to(np.array([c > 0, c < 3], np.float32)[None, :], (128, 2)))
        m["sinkT"] = np.ascontiguousarray(np.broadcast_to(inp["ab_sink"][0][None, :], (128, 8)).astype(np.float32))
        m["convT"] = np.ascontiguousarray(inp["ab_conv_w"][0].reshape(3, 8, 128).transpose(2, 0, 1))
    if lvl >= 3:
        m["ab_w_out"] = inp["ab_w_out"][0]
    if lvl >= 4:
        m["ffn2_w_in0"] = inp["ffn2_w_in"][0]
        m["ffn2_w_out0"] = inp["ffn2_w_out"][0]
    if lvl >= 6:
        m["rg_w_in"] = inp["rg_w_in"][0]
    return m


def phase_ab_in(k, XU, w_in, site, QU, KU, VU, ZU, BGU):
    p, a = k.p, k.arena
    mk = a.mark()
    tiles = make_tiles([(0, NU)], TMAX_BIG)
    TM = max(sum(s[1] for s in t) for t in tiles)
    h = a.alloc([KC, TM], BF16)
    h_res = Res()
    rstd = a.alloc([TM], F32)
    rstd_res = Res()
    xc_ring = Ring([a.alloc([TM], F32) for _ in range(3)])
    tmp_ring = Ring([a.alloc([TM], F32) for _ in range(3)])
    w_ring = Ring([a.alloc([KC, 128], BF16) for _ in range(4)])
    wsw_ring = Ring([a.alloc([KC, 128], BF16) for _ in range(2)])
    wv = a.alloc([KC, 256], BF16)
    wv_res = Res()
    st16 = Ring([a.alloc([TM], BF16) for _ in range(3)])
    st32 = Ring([a.alloc([TM], F32) for _ in range(3)])
    vst = Ring([a.alloc([256], BF16) for _ in range(2)])
    bk = k.bank_ring(range(8))
    stat_banks = [(k.banks[i], k.bank_res[i]) for i in (5, 6, 7)]
    cs = load_consts(k, [("ropeC", [NU]), ("ropeS", [NU])])
    COS, SIN = cs["ropeC"], cs["ropeS"]
    w_v = w_in.rearrange("(k p) n -> p k n", p=128)
    p.dma("pool", lambda e: e.dma_start(out=wv, in_=w_v[:, :, 1280:1536]), W=[wv_res])
    for tile in tiles:
        T = sum(s[1] for s in tile)
        tu0 = tile[0][0]
        assert T % 128 == 0
        subs = split_even(T, 512)
        ms = ModStream(k, XU, tile, site, xc_ring, tmp_ring, rstd, rstd_res, stat_banks)
        for kc in range(KC):
            ms.load_sq(kc)
            ms.mm(kc)
        ms.finish(h, h_res)
        for j in range(10):
            col0 = j * 128 if j < 8 else 1024 + (j - 8) * 128
            w, wr = w_ring.next()
            p.dma("pool", lambda e, w=w, col0=col0: e.dma_start(out=w, in_=w_v[:, :, col0:col0 + 128]), W=[wr])
            ws, wsr = wsw_ring.next()
            for (d0, s0) in ((0, 32), (32, 0), (64, 96), (96, 64)):
                p.op("pool", lambda e, ws=ws, w=w, d0=d0, s0=s0: e.tensor_copy(out=ws[:, :, d0:d0 + 32], in_=w[:, :, s0:s0 + 32]), R=[wr], W=[wsr])
            st, str_ = st16.next()
            for (n0, nl) in subs:
                ba, bar = bk.next()
                bb, bbr = bk.next()
                for kc in range(KC):
                    p.op("pe", lambda e, ba=ba, w=w, kc=kc, n0=n0, nl=nl: e.matmul(ba[:, 0:nl], lhsT=w[:, kc, :], rhs=h[:, kc, n0:n0 + nl], start=(kc == 0), stop=(kc == KC - 1)),
                         R=[wr, h_res], W=[bar])
                for kc in range(KC):
                    p.op("pe", lambda e, bb=bb, ws=ws, kc=kc, n0=n0, nl=nl: e.matmul(bb[:, 0:nl], lhsT=ws[:, kc, :], rhs=h[:, kc, n0:n0 + nl], start=(kc == 0), stop=(kc == KC - 1)),
                         R=[wsr, h_res], W=[bbr])
                t1, t1r = tmp_ring.next()
                t2, t2r = tmp_ring.next()
                p.op("dve", lambda e, tu0=tu0, t1=t1, ba=ba, n0=n0, nl=nl: e.tensor_tensor(out=t1[:, 0:nl], in0=ba[:, 0:nl], in1=COS[0][:, tu0 + n0:tu0 + n0 + nl], op=ALU.mult),
                     R=[bar, COS[1]], W=[t1r])
                p.op("dve", lambda e, tu0=tu0, t2=t2, bb=bb, n0=n0, nl=nl: e.tensor_tensor(out=t2[:, 0:nl], in0=bb[:, 0:nl], in1=SIN[0][:, tu0 + n0:tu0 + n0 + nl], op=ALU.mult),
                     R=[bbr, SIN[1]], W=[t2r])
                p.op("pool", lambda e, st=st, t1=t1, t2=t2, n0=n0, nl=nl: e.tensor_tensor(out=st[:, n0:n0 + nl], in0=t1[:, 0:nl], in1=t2[:, 0:nl], op=ALU.add),
                     R=[t1r, t2r], W=[str_])
            dst = QU[j] if j < 8 else KU[j - 8]
            p.dma("sp", lambda e, tu0=tu0, st=st, dst=dst, T=T: e.dma_start(out=dst[:, tu0:tu0 + T], in_=st[:, 0:T]), R=[str_])
        for blk in range(T // 128):
            b, br = bk.next()
            for kc in range(KC):
                p.op("pe", lambda e, b=b, kc=kc, blk=blk: e.matmul(b[:, 0:256], lhsT=h[:, kc, blk * 128:(blk + 1) * 128], rhs=wv[:, kc, :], start=(kc == 0), stop=(kc == KC - 1)),
                     R=[wv_res, h_res], W=[br])
            vs, vsr = vst.next()
            p.op("act", lambda e, vs=vs, b=b: e.activation(out=vs, in_=b[:, 0:256], func=AF.Copy), R=[br], W=[vsr])
            p.dma("sp", lambda e, tu0=tu0, vs=vs, blk=blk: e.dma_start(out=VU[tu0 + blk * 128:tu0 + (blk + 1) * 128, :], in_=vs), R=[vsr])
        for cc in range(8):
            ws3 = []
            for base in (1536, 2560, 3584):
                w, wr = w_ring.next()
                p.dma("pool", lambda e, w=w, c0=base + cc * 128: e.dma_start(out=w, in_=w_v[:, :, c0:c0 + 128]), W=[wr])
                ws3.append((w, wr))
            zs, zsr = st32.next()
            bs, bsr = st32.next()
            for (n0, nl) in subs:
                bks = [bk.next() for _ in range(3)]
                for (w, wr), (b, br) in zip(ws3, bks):
                    for kc in range(KC):
                        p.op("pe", lambda e, b=b, w=w, kc=kc, n0=n0, nl=nl: e.matmul(b[:, 0:nl], lhsT=w[:, kc, :], rhs=h[:, kc, n0:n0 + nl], start=(kc == 0), stop=(kc == KC - 1)),
                             R=[wr, h_res], W=[br])
                (bgb, bgr), (cgb, cgr), (ub, ur) = bks
                t1, t1r = tmp_ring.next()
                p.op("act", lambda e, t1=t1, ub=ub, nl=nl: e.activation(out=t1[:, 0:nl], in_=ub[:, 0:nl], func=AF.Copy), R=[ur], W=[t1r])
                p.op("dve", lambda e, zs=zs, cgb=cgb, t1=t1, n0=n0, nl=nl: e.tensor_tensor(out=zs[:, n0:n0 + nl], in0=cgb[:, 0:nl], in1=t1[:, 0:nl], op=ALU.mult),
                     R=[cgr, t1r], W=[zsr])
                p.op("act", lambda e, bs=bs, bgb=bgb, n0=n0, nl=nl: e.activation(out=bs[:, n0:n0 + nl], in_=bgb[:, 0:nl], func=AF.Copy), R=[bgr], W=[bsr])
            p.dma("sp", lambda e, tu0=tu0, zs=zs, cc=cc, T=T: e.dma_start(out=ZU[cc, :, tu0:tu0 + T], in_=zs[:, 0:T]), R=[zsr])
            p.dma("sp", lambda e, tu0=tu0, bs=bs, cc=cc, T=T: e.dma_start(out=BGU[cc, :, tu0:tu0 + T], in_=bs[:, 0:T]), R=[bsr])
    p.barrier()
    a.reset(mk)


def phase_ab_mix(k, QU, KU, VU, ZU, BGU, MIXU):
    p, a = k.p, k.arena
    mk = a.mark()
    cs = load_consts(k, [("masks", [2, 128]), ("flags", [2]), ("sinkT", [8]), ("convT", [3, 8])])
    masks, flags, sinkT, convT = cs["masks"], cs["flags"], cs["sinkT"], cs["convT"]
    scale = 128.0 ** -0.5
    mres = Res()
    mP = a.alloc([4, 128], BF16)
    mN = a.alloc([4, 128], BF16)
    mP0 = a.alloc([4, 128], BF16)
    mNL = a.alloc([4, 128], BF16)
    for hh in range(4):
        p.op("dve", lambda e, hh=hh: e.tensor_copy(out=mP[:, hh, :], in_=masks[0][:, 0, :]), R=[masks[1]], W=[mres])
        p.op("dve", lambda e, hh=hh: e.tensor_copy(out=mN[:, hh, :], in_=masks[0][:, 1, :]), R=[masks[1]], W=[mres])
    p.op("dve", lambda e: e.tensor_scalar(out=mP0, in0=mP, scalar1=flags[0][:, 0:1], scalar2=None, op0=ALU.mult), R=[mres, flags[1]], W=[mres])
    p.op("dve", lambda e: e.tensor_scalar(out=mNL, in0=mN, scalar1=flags[0][:, 1:2], scalar2=None, op0=ALU.mult), R=[mres, flags[1]], W=[mres])
    esink = a.alloc([8], F32)
    p.op("act", lambda e: e.activation(out=esink, in_=sinkT[0], func=AF.Exp), R=[sinkT[1]], W=[mres])
    gsets = [(a.alloc([NU], BF16), a.alloc([NU // 128, 128], BF16), a.alloc([4, NU], BF16), Res()) for _ in range(2)]
    E_ring = Ring([a.alloc([512], BF16) for _ in range(6)])
    rd_ring = Ring([a.alloc([512], F32) for _ in range(2)])
    os_ring = Ring([a.alloc([4, 128], BF16) for _ in range(2)])
    sc_banks = k.bank_ring([0, 1, 2, 3])
    acc_banks = k.bank_ring([4, 5, 6, 7])
    flat = lambda t: t.rearrange("p a b -> p (a b)")
    for g in range(2):
        kU, vU, qU, gres = gsets[g]
        p.dma("sp", lambda e, g=g, kU=kU: e.dma_start(out=kU, in_=KU[g]), W=[gres])
        p.dma("sp", lambda e, g=g, vU=vU: e.dma_start(out=vU, in_=VU[:, g * 128:(g + 1) * 128].rearrange("(b p) d -> p b d", p=128)), W=[gres])
        p.dma("sp", lambda e, g=g, qU=qU: e.dma_start(out=qU, in_=QU[g * 4:(g + 1) * 4].rearrange("h p t -> p h t")), W=[gres])
    for g in range(2):
        kU, vU, qU, gres = gsets[g]
        qblocks = [(0, [(0, None), (1, None)]), (1, [(0, None), (1, None)])]
        for i in range(16):
            qblocks.append((i + 3, [(i + 2, mP0 if i == 0 else mP), (i + 3, None), (i + 4, mNL if i == 15 else mN), (0, None), (1, None)]))
        for ub, keys in qblocks:
            den, denr = acc_banks.next()
            ob, obr = acc_banks.next()
            nk = len(keys)
            for ki, (kb, mask) in enumerate(keys):
                sb, sbr = sc_banks.next()
                p.op("pe", lambda e, sb=sb, kb=kb, ub=ub, kU=kU, qU=qU: e.matmul(sb[:, 0:512], lhsT=kU[:, kb * 128:(kb + 1) * 128], rhs=qU[:, :, ub * 128:(ub + 1) * 128], start=True, stop=True),
                     R=[gres], W=[sbr])
                E, Er = E_ring.next()
                p.op("act", lambda e, E=E, sb=sb: e.activation(out=E, in_=sb[:, 0:512], func=AF.Exp, scale=scale), R=[sbr], W=[Er])
                if mask is not None:
                    p.op("dve", lambda e, E=E, mask=mask: e.tensor_tensor(out=E, in0=E, in1=flat(mask), op=ALU.mult), R=[Er, mres], W=[Er])
                p.op("pe", lambda e, den=den, E=E, ki=ki, nk=nk: e.matmul(den[:, 0:512], lhsT=k.ones16, rhs=E, start=(ki == 0), stop=(ki == nk - 1)),
                     R=[Er, k.cres], W=[denr])
                p.op("pe", lambda e, ob=ob, E=E, kb=kb, ki=ki, nk=nk, vU=vU: e.matmul(ob[:, 0:512], lhsT=vU[:, kb, :], rhs=E, start=(ki == 0), stop=(ki == nk - 1)),
                     R=[Er, gres], W=[obr])
            rd, rdr = rd_ring.next()
            for hh in range(4):
                p.op("dve", lambda e, rd=rd, den=den, hh=hh, g=g: e.tensor_scalar(out=rd[:, hh * 128:(hh + 1) * 128], in0=den[:, hh * 128:(hh + 1) * 128],
                                                                                   scalar1=esink[:, g * 4 + hh:g * 4 + hh + 1], scalar2=None, op0=ALU.add),
                     R=[denr, mres], W=[rdr])
            p.op("dve", lambda e, rd=rd: e.reciprocal(out=rd, in_=rd), R=[rdr], W=[rdr])
            os_, osr = os_ring.next()
            p.op("dve", lambda e, os_=os_, ob=ob, rd=rd: e.tensor_tensor(out=flat(os_), in0=ob[:, 0:512], in1=rd, op=ALU.mult), R=[obr, rdr], W=[osr])
            p.dma("sp", lambda e, os_=os_, g=g, ub=ub: e.dma_start(out=MIXU[g * 4:(g + 1) * 4, :, ub * 128:(ub + 1) * 128].rearrange("h p t -> p h t"), in_=os_), R=[osr])
    zb_ring = Ring([a.alloc([NU + 2], F32) for _ in range(2)])
    bg_ring = Ring([a.alloc([NU], F32) for _ in range(2)])
    acc = a.alloc([NOWN], F32)
    accr = Res()
    zc = a.alloc([NCTX + 2], F32)
    accc = a.alloc([NCTX], F32)
    cst_ring = Ring([a.alloc([NU], BF16) for _ in range(2)])
    p.op("pool", lambda e: e.memset(zc, 0.0), W=[accr])
    for cc in range(8):
        zb, zbr = zb_ring.next()
        bg, bgr = bg_ring.next()
        p.dma("sp", lambda e, zb=zb, cc=cc: e.dma_start(out=zb[:, 1:NU + 1], in_=ZU[cc]), W=[zbr])
        p.dma("sp", lambda e, bg=bg, cc=cc: e.dma_start(out=bg, in_=BGU[cc]), W=[bgr])
        cst, cstr = cst_ring.next()
        w = lambda kk, cc=cc: convT[0][:, kk, cc:cc + 1]
        p.op("pool", lambda e, zb=zb: e.tensor_copy(out=zc[:, 1:NCTX + 1], in_=zb[:, 1:NCTX + 1]), R=[zbr], W=[accr])
        p.op("dve", lambda e, w=w: e.tensor_scalar(out=accc, in0=zc[:, 0:NCTX], scalar1=w(0), scalar2=None, op0=ALU.mult), R=[accr, convT[1]], W=[accr])
        for kk in (1, 2):
            p.op("dve", lambda e, w=w, kk=kk: e.scalar_tensor_tensor(out=accc, in0=zc[:, kk:kk + NCTX], scalar=w(kk), op0=ALU.mult, in1=accc, op1=ALU.add), R=[accr, convT[1]], W=[accr])
        p.op("dve", lambda e, cst=cst, bg=bg: e.tensor_tensor(out=cst[:, 0:NCTX], in0=accc, in1=bg[:, 0:NCTX], op=ALU.mult), R=[accr, bgr], W=[cstr])
        p.op("dve", lambda e, zb=zb: e.tensor_scalar(out=zb[:, U_OWN:U_OWN + 1], in0=zb[:, U_OWN:U_OWN + 1], scalar1=flags[0][:, 0:1], scalar2=None, op0=ALU.mult), R=[zbr, flags[1]], W=[zbr])
        p.op("dve", lambda e, zb=zb: e.tensor_scalar(out=zb[:, U_OWN + NOWN + 1:U_OWN + NOWN + 2], in0=zb[:, U_OWN + NOWN + 1:U_OWN + NOWN + 2], scalar1=flags[0][:, 1:2], scalar2=None, op0=ALU.mult), R=[zbr, flags[1]], W=[zbr])
        p.op("dve", lambda e, zb=zb, w=w: e.tensor_scalar(out=acc, in0=zb[:, U_OWN:U_OWN + NOWN], scalar1=w(0), scalar2=None, op0=ALU.mult), R=[zbr, convT[1]], W=[accr])
        for kk in (1, 2):
            p.op("dve", lambda e, zb=zb, w=w, kk=kk: e.scalar_tensor_tensor(out=acc, in0=zb[:, U_OWN + kk:U_OWN + kk + NOWN], scalar=w(kk), op0=ALU.mult, in1=acc, op1=ALU.add), R=[zbr, accr, convT[1]], W=[accr])
        p.op("dve", lambda e, cst=cst, bg=bg: e.tensor_tensor(out=cst[:, U_OWN:U_OWN + NOWN], in0=acc, in1=bg[:, U_OWN:U_OWN + NOWN], op=ALU.mult), R=[accr, bgr], W=[cstr])
        p.dma("sp", lambda e, cst=cst, cc=cc: e.dma_start(out=MIXU[8 + cc, :, 0:NCTX], in_=cst[:, 0:NCTX]), R=[cstr])
        p.dma("sp", lambda e, cst=cst, cc=cc: e.dma_start(out=MIXU[8 + cc, :, U_OWN:U_OWN + NOWN], in_=cst[:, U_OWN:U_OWN + NOWN]), R=[cstr])
    p.barrier()
    a.reset(mk)


def phase_outproj(k, XU, mix_of, ranges, w_out, site):
    p, a = k.p, k.arena
    mk = a.mark()
    tiles = make_tiles(ranges, TMAX_BIG)
    TM = max(sum(s[1] for s in t) for t in tiles)
    mix = a.alloc([KC, TM], BF16)
    mix_res = Res()
    w_ring = Ring([a.alloc([KC, 128], BF16) for _ in range(3)])
    xm_ring = Ring([a.alloc([TM], F32) for _ in range(2)])
    xo_ring = Ring([a.alloc([TM], F32) for _ in range(2)])
    bk = k.bank_ring(range(8))
    w_v = w_out.rearrange("(k p) n -> p k n", p=128)
    for tile in tiles:
        T = sum(s[1] for s in tile)
        subs = split_even(T, 512)
        for (u0, ln, off, r) in tile:
            for c in range(KC):
                p.dma("sp", lambda e, c=c, u0=u0, ln=ln, off=off: e.dma_start(out=mix[:, c, off:off + ln], in_=mix_of(c, u0, ln)), W=[mix_res])
        for m in range(KC):
            w, wr = w_ring.next()
            p.dma("pool", lambda e, w=w, m=m: e.dma_start(out=w, in_=w_v[:, :, m * 128:(m + 1) * 128]), W=[wr])
            ob = [bk.next() for _ in subs]
            for kc in range(KC):
                for (n0, nl), (b, br) in zip(subs, ob):
                    p.op("pe", lambda e, b=b, w=w, kc=kc, n0=n0, nl=nl: e.matmul(b[:, 0:nl], lhsT=w[:, kc, :], rhs=mix[:, kc, n0:n0 + nl], start=(kc == 0), stop=(kc == KC - 1)),
                         R=[wr, mix_res], W=[br])
            emit_residual(k, XU, tile, m, ob, subs, site, xm_ring, xo_ring)
    p.barrier()
    a.reset(mk)


def phase_rg_in(k, XU, w_in, site, g_of, u_of):
    p, a = k.p, k.arena
    mk = a.mark()
    tiles = make_tiles([(0, NCTX), (U_OWN, U_OWN + NOWN)], TMAX_BIG)
    TM = max(sum(s[1] for s in t) for t in tiles)
    h = a.alloc([KC, TM], BF16)
    h_res = Res()
    rstd = a.alloc([TM], F32)
    rstd_res = Res()
    xc_ring = Ring([a.alloc([TM], F32) for _ in range(3)])
    tmp_ring = Ring([a.alloc([TM], F32) for _ in range(3)])
    w_ring = Ring([a.alloc([KC, 128], BF16) for _ in range(4)])
    st16 = Ring([a.alloc([TM], BF16) for _ in range(2)])
    st32 = Ring([a.alloc([TM], F32) for _ in range(2)])
    bk = k.bank_ring(range(8))
    stat_banks = [(k.banks[i], k.bank_res[i]) for i in (5, 6, 7)]
    w_v = w_in.rearrange("(k p) n -> p k n", p=128)
    for tile in tiles:
        T = sum(s[1] for s in tile)
        subs = split_even(T, 512)
        ms = ModStream(k, XU, tile, site, xc_ring, tmp_ring, rstd, rstd_res, stat_banks)
        for kc in range(KC):
            ms.load_sq(kc)
            ms.mm(kc)
        ms.finish(h, h_res)
        for ch in range(32):
            w, wr = w_ring.next()
            p.dma("pool", lambda e, w=w, ch=ch: e.dma_start(out=w, in_=w_v[:, :, ch * 128:(ch + 1) * 128]), W=[wr])
            is_gate = ch < 16
            st, sr = (st16 if is_gate else st32).next()
            for (n0, nl) in subs:
                b, br = bk.next()
                for kc in range(KC):
                    p.op("pe", lambda e, b=b, w=w, kc=kc, n0=n0, nl=nl: e.matmul(b[:, 0:nl], lhsT=w[:, kc, :], rhs=h[:, kc, n0:n0 + nl], start=(kc == 0), stop=(kc == KC - 1)),
                         R=[wr, h_res], W=[br])
                if is_gate:
                    p.op("act", lambda e, st=st, b=b, n0=n0, nl=nl: e.activation(out=st[:, n0:n0 + nl], in_=b[:, 0:nl], func=AF.Gelu), R=[br], W=[sr])
                else:
                    p.op("dve", lambda e, st=st, b=b, n0=n0, nl=nl: e.tensor_copy(out=st[:, n0:n0 + nl], in_=b[:, 0:nl]), R=[br], W=[sr])
            for (u0, ln, off, r) in tile:
                dst = g_of(ch, u0, ln) if is_gate else u_of(ch - 16, u0, ln)
                if dst is None:
                    continue
                p.dma("sp", lambda e, st=st, dst=dst, off=off, ln=ln: e.dma_start(out=dst, in_=st[:, off:off + ln]), R=[sr])
    p.barrier()
    a.reset(mk)


NE = NOWN + 3
NS = NOWN + NCTX


def phase_rglru(k, mode, UE, UCTX, rg_w_a, rg_w_x, CAR=None, GOWN=None, MIX1=None, edges=None, car_tiles=None, UOWN=None, ABS=None, ctxfin=None):
    p, a = k.p, k.arena
    mk = a.mark()
    names = [("rgconvT", [KC, 5]), ("rgbaT", [2, KC]), ("rgbxT", [2, KC]), ("rglamT", [2, KC])]
    if mode == "full" and car_tiles is None:
        names += [("carF", [3, KC, 2]), ("carB", [3, KC, 2])]
    cs = load_consts(k, names)
    convT, baT, bxT, lamT = cs["rgconvT"], cs["rgbaT"], cs["rgbxT"], cs["rglamT"]
    cst = a.alloc([2, KC], F32)
    cres = Res()
    p.op("act", lambda e: e.activation(out=cst, in_=lamT[0], func=AF.Exp, scale=-1.0), R=[lamT[1]], W=[cres])
    p.op("act", lambda e: e.activation(out=cst, in_=cst, func=AF.Ln, bias=1.0, scale=1.0), R=[cres], W=[cres])
    p.op("dve", lambda e: e.tensor_scalar(out=cst, in0=cst, scalar1=-8.0, scalar2=None, op0=ALU.mult), R=[cres], W=[cres])
    wa = a.alloc([2, KC, 128], BF16)
    wx = a.alloc([2, KC, 128], BF16)
    wres = Res()
    p.dma("pool", lambda e: e.dma_start(out=wa, in_=rg_w_a.rearrange("d h i j -> i d h j")), W=[wres])
    p.dma("pool", lambda e: e.dma_start(out=wx, in_=rg_w_x.rearrange("d h i j -> i d h j")), W=[wres])
    ub_ring = Ring([a.alloc([NE], F32) for _ in range(2)])
    uc_ring = Ring([a.alloc([NCTX + 3], F32) for _ in range(2)])
    for (b_, r_) in uc_ring.bufs:
        p.op("pool", lambda e, b_=b_: e.memset(b_, 0.0), W=[r_])
    ucv = a.alloc([NS], F32)
    ucv_res = Res()
    u16 = a.alloc([NS], BF16)
    u16_res = Res()
    rbs = [a.alloc([NS], F32) for _ in range(2)]
    ibs = [a.alloc([NS], F32) for _ in range(2)]
    tbs = [a.alloc([NS], F32) for _ in range(2)]
    rres = [Res(), Res()]
    ires = [Res(), Res()]
    tress = [Res(), Res()]
    ab = [a.alloc([NS], F32) for _ in range(2)]
    bb = [a.alloc([NS], F32) for _ in range(2)]
    gres = [Res(), Res()]
    hb = [a.alloc([NS], F32) for _ in range(2)]
    hres = [Res(), Res()]
    sm = a.alloc([16], F32)
    smres = Res()
    if mode == "carry":
        car = a.alloc([KC, 4], F32)
        car_res = Res()
    else:
        g_ring = Ring([a.alloc([NOWN], BF16) for _ in range(2)])
        mo_ring = Ring([a.alloc([NOWN], BF16) for _ in range(2)])
        carF, carB = car_tiles if car_tiles is not None else (cs["carF"], cs["carB"])
        nstep = 4 if car_tiles is not None else 3
    bk = k.bank_ring(range(8))
    blocks = split_even(NS, 512)
    for c in range(KC):
        ub, ubr = ub_ring.next()
        ucx, ucxr = uc_ring.next()
        if edges is None:
            p.dma("sp", lambda e, ub=ub, c=c: e.dma_start(out=ub, in_=UE[c]), W=[ubr])
        else:
            p.dma("sp", lambda e, ub=ub, c=c: e.dma_start(out=ub[:, 2:2 + NOWN], in_=UOWN[c]), W=[ubr])
            for (dst_, src_) in ((0, 0), (1, 1), (2 + NOWN, 2)):
                p.op("pool", lambda e, ub=ub, c=c, dst_=dst_, src_=src_: e.tensor_copy(out=ub[:, dst_:dst_ + 1], in_=edges[0][:, src_, c:c + 1]), R=[edges[1]], W=[ubr])
        p.dma("sp", lambda e, ucx=ucx, c=c: e.dma_start(out=ucx[:, 2:2 + NCTX], in_=UCTX[c]), W=[ucxr])
        w = lambda kk, c=c: convT[0][:, c, kk:kk + 1]
        for (src, srcr, o0, n) in ((ub, ubr, 0, NOWN), (ucx, ucxr, NOWN, NCTX)):
            p.op("dve", lambda e, src=src, o0=o0, n=n, w=w: e.tensor_scalar(out=ucv[:, o0:o0 + n], in0=src[:, 0:n], scalar1=w(0), scalar2=w(4), op0=ALU.mult, op1=ALU.add),
                 R=[srcr, convT[1]], W=[ucv_res])
            for kk in (1, 2, 3):
                p.op("dve", lambda e, src=src, o0=o0, n=n, w=w, kk=kk: e.scalar_tensor_tensor(out=ucv[:, o0:o0 + n], in0=src[:, kk:kk + n], scalar=w(kk), op0=ALU.mult, in1=ucv[:, o0:o0 + n], op1=ALU.add),
                     R=[srcr, convT[1], ucv_res], W=[ucv_res])
        p.op("dve", lambda e: e.tensor_copy(out=u16, in_=ucv), R=[ucv_res], W=[u16_res])
        for d in range(2):
            rb, ib, tb = rbs[d], ibs[d], tbs[d]
            rr_, ir_, tres = rres[d], ires[d], tress[d]
            for (n0, nl) in blocks:
                rp, rpr = bk.next()
                ip, ipr = bk.next()
                p.op("pe", lambda e, rp=rp, d=d, c=c, n0=n0, nl=nl: e.matmul(rp[:, 0:nl], lhsT=wa[:, d, c, :], rhs=u16[:, n0:n0 + nl], start=True, stop=True), R=[wres, u16_res], W=[rpr])
                p.op("pe", lambda e, ip=ip, d=d, c=c, n0=n0, nl=nl: e.matmul(ip[:, 0:nl], lhsT=wx[:, d, c, :], rhs=u16[:, n0:n0 + nl], start=True, stop=True), R=[wres, u16_res], W=[ipr])
                p.op("act", lambda e, rp=rp, d=d, c=c, n0=n0, nl=nl, rb=rb: e.activation(out=rb[:, n0:n0 + nl], in_=rp[:, 0:nl], func=AF.Sigmoid, bias=baT[0][:, d, c:c + 1], scale=1.0), R=[rpr, baT[1]], W=[rr_])
                p.op("act", lambda e, ip=ip, d=d, c=c, n0=n0, nl=nl, ib=ib: e.activation(out=ib[:, n0:n0 + nl], in_=ip[:, 0:nl], func=AF.Sigmoid, bias=bxT[0][:, d, c:c + 1], scale=1.0), R=[ipr, bxT[1]], W=[ir_])
            A_, B_ = ab[d], bb[d]
            p.op("act", lambda e, A_=A_, d=d, c=c, rb=rb: e.activation(out=A_, in_=rb, func=AF.Exp, scale=cst[:, d, c:c + 1]), R=[rr_, cres], W=[gres[d]])
            p.op("dve", lambda e, A_=A_, tb=tb: e.tensor_tensor(out=tb, in0=A_, in1=A_, op=ALU.mult), R=[gres[d]], W=[tres])
            p.op("act", lambda e, tb=tb: e.activation(out=tb, in_=tb, func=AF.Sqrt, bias=1.0, scale=-1.0), R=[tres], W=[tres])
            p.op("dve", lambda e, tb=tb, ib=ib: e.tensor_tensor(out=tb, in0=tb, in1=ib, op=ALU.mult), R=[tres, ir_], W=[tres])
            p.op("dve", lambda e, B_=B_, tb=tb: e.tensor_tensor(out=B_, in0=tb, in1=ucv, op=ALU.mult), R=[tres, ucv_res], W=[gres[d]])
        H = hb
        p.op("dve", lambda e: e.tensor_tensor_scan(out=H[0][:, NOWN:NS], data0=ab[0][:, NOWN:NS], data1=bb[0][:, NOWN:NS], initial=0.0, op0=ALU.mult, op1=ALU.add),
             R=[gres[0]], W=[hres[0]])
        p.op("dve", lambda e: e.tensor_tensor_scan(out=H[1][:, NOWN:NS][:, ::-1], data0=ab[1][:, NOWN:NS][:, ::-1], data1=bb[1][:, NOWN:NS][:, ::-1], initial=0.0, op0=ALU.mult, op1=ALU.add),
             R=[gres[1]], W=[hres[1]])
        if mode == "carry":
            p.op("dve", lambda e: e.tensor_tensor_scan(out=H[0][:, 0:NOWN], data0=ab[0][:, 0:NOWN], data1=bb[0][:, 0:NOWN], initial=0.0, op0=ALU.mult, op1=ALU.add),
                 R=[gres[0]], W=[hres[0]])
            p.op("dve", lambda e: e.tensor_tensor_scan(out=H[1][:, 0:NOWN][:, ::-1], data0=ab[1][:, 0:NOWN][:, ::-1], data1=bb[1][:, 0:NOWN][:, ::-1], initial=0.0, op0=ALU.mult, op1=ALU.add),
                 R=[gres[1]], W=[hres[1]])
            p.op("dve", lambda e, c=c: e.tensor_reduce(out=car[:, c, 0:1], in_=ab[0][:, 0:NOWN], axis=AX.X, op=ALU.mult), R=[gres[0]], W=[car_res])
            p.op("dve", lambda e, c=c: e.tensor_copy(out=car[:, c, 1:2], in_=H[0][:, NOWN - 1:NOWN]), R=[hres[0]], W=[car_res])
            p.op("dve", lambda e, c=c: e.tensor_reduce(out=car[:, c, 2:3], in_=ab[1][:, 0:NOWN], axis=AX.X, op=ALU.mult), R=[gres[1]], W=[car_res])
            p.op("dve", lambda e, c=c: e.tensor_copy(out=car[:, c, 3:4], in_=H[1][:, 0:1]), R=[hres[1]], W=[car_res])
            if ABS is not None:
                for d in range(2):
                    p.dma("sp", lambda e, d=d, c=c: e.dma_start(out=ABS[2 * d, c], in_=ab[d][:, 0:NOWN]), R=[gres[d]])
                    p.dma("sp", lambda e, d=d, c=c: e.dma_start(out=ABS[2 * d + 1, c], in_=bb[d][:, 0:NOWN]), R=[gres[d]])
                p.op("dve", lambda e, c=c: e.tensor_copy(out=ctxfin[0][:, c, 0:1], in_=H[0][:, NS - 1:NS]), R=[hres[0]], W=[ctxfin[1]])
                p.op("dve", lambda e, c=c: e.tensor_copy(out=ctxfin[0][:, c, 1:2], in_=H[1][:, NOWN:NOWN + 1]), R=[hres[1]], W=[ctxfin[1]])
        else:
            p.op("dve", lambda e: e.tensor_copy(out=sm[:, 0:1], in_=H[0][:, NS - 1:NS]), R=[hres[0]], W=[smres])
            p.op("dve", lambda e: e.tensor_copy(out=sm[:, 1:2], in_=H[1][:, NOWN:NOWN + 1]), R=[hres[1]], W=[smres])
            for s_ in range(nstep):
                p.op("dve", lambda e, s_=s_, c=c: e.scalar_tensor_tensor(out=sm[:, 0:1], in0=sm[:, 0:1], scalar=carF[0][:, s_, c, 0:1], op0=ALU.mult, in1=carF[0][:, s_, c, 1:2], op1=ALU.add),
                     R=[smres, carF[1]], W=[smres])
                p.op("dve", lambda e, s_=s_, c=c: e.scalar_tensor_tensor(out=sm[:, 1:2], in0=sm[:, 1:2], scalar=carB[0][:, s_, c, 0:1], op0=ALU.mult, in1=carB[0][:, s_, c, 1:2], op1=ALU.add),
                     R=[smres, carB[1]], W=[smres])
            p.op("dve", lambda e: e.tensor_tensor_scan(out=H[0][:, 0:NOWN], data0=ab[0][:, 0:NOWN], data1=bb[0][:, 0:NOWN], initial=sm[:, 0:1], op0=ALU.mult, op1=ALU.add),
                 R=[gres[0], smres], W=[hres[0]])
            p.op("dve", lambda e: e.tensor_tensor_scan(out=H[1][:, 0:NOWN][:, ::-1], data0=ab[1][:, 0:NOWN][:, ::-1], data1=bb[1][:, 0:NOWN][:, ::-1], initial=sm[:, 1:2], op0=ALU.mult, op1=ALU.add),
                 R=[gres[1], smres], W=[hres[1]])
            g_, gr_ = g_ring.next()
            p.dma("sp", lambda e, g_=g_, c=c: e.dma_start(out=g_, in_=GOWN[c]), W=[gr_])
            p.op("pool", lambda e: e.tensor_tensor(out=H[0][:, 0:NOWN], in0=H[0][:, 0:NOWN], in1=H[1][:, 0:NOWN], op=ALU.add), R=[hres[0], hres[1]], W=[hres[0]])
            mo, mor = mo_ring.next()
            p.op("dve", lambda e, mo=mo, g_=g_: e.tensor_tensor(out=mo, in0=H[0][:, 0:NOWN], in1=g_, op=ALU.mult), R=[hres[0], gr_], W=[mor])
            p.dma("sp", lambda e, mo=mo, c=c: e.dma_start(out=MIX1[c], in_=mo), R=[mor])
    if mode == "carry":
        p.dma("sp", lambda e: e.dma_start(out=CAR, in_=car), R=[car_res])
    p.barrier()
    a.reset(mk)


def phase_rglru_apply(k, ABS, ctxfin, car_tiles, GOWN, MIX1):
    p, a = k.p, k.arena
    mk = a.mark()
    carF, carB = car_tiles
    sets = Ring([[a.alloc([NOWN], F32) for _ in range(4)] for _ in range(2)])
    h_ring = Ring([[a.alloc([NOWN], F32) for _ in range(2)] for _ in range(2)])
    g_ring = Ring([a.alloc([NOWN], BF16) for _ in range(2)])
    mo_ring = Ring([a.alloc([NOWN], BF16) for _ in range(2)])
    sm_ring = Ring([a.alloc([2], F32) for _ in range(2)])
    pre = {}

    def ap_load(c):
        (af, bf, ab_, bb_), sr = sets.next()
        for i_, t_ in enumerate((af, bf, ab_, bb_)):
            p.dma("sp", lambda e, i_=i_, t_=t_, c=c: e.dma_start(out=t_, in_=ABS[i_, c]), W=[sr])
        g_, gr_ = g_ring.next()
        p.dma("sp", lambda e, g_=g_, c=c: e.dma_start(out=g_, in_=GOWN[c]), W=[gr_])
        pre[c] = (af, bf, ab_, bb_, sr, g_, gr_)

    ap_load(0)
    for c in range(KC):
        if c + 1 < KC:
            ap_load(c + 1)
        af, bf, ab_, bb_, sr, g_, gr_ = pre.pop(c)
        sm, smr = sm_ring.next()
        p.op("dve", lambda e, sm=sm, c=c: e.tensor_copy(out=sm, in_=ctxfin[0][:, c, :]), R=[ctxfin[1]], W=[smr])
        for s_ in range(4):
            p.op("dve", lambda e, sm=sm, s_=s_, c=c: e.scalar_tensor_tensor(out=sm[:, 0:1], in0=sm[:, 0:1], scalar=carF[0][:, s_, c, 0:1], op0=ALU.mult, in1=carF[0][:, s_, c, 1:2], op1=ALU.add),
                 R=[smr, carF[1]], W=[smr])
            p.op("dve", lambda e, sm=sm, s_=s_, c=c: e.scalar_tensor_tensor(out=sm[:, 1:2], in0=sm[:, 1:2], scalar=carB[0][:, s_, c, 0:1], op0=ALU.mult, in1=carB[0][:, s_, c, 1:2], op1=ALU.add),
                 R=[smr, carB[1]], W=[smr])
        (hf, hb_), hr = h_ring.next()
        p.op("dve", lambda e, hf=hf, af=af, bf=bf, sm=sm: e.tensor_tensor_scan(out=hf, data0=af, data1=bf, initial=sm[:, 0:1], op0=ALU.mult, op1=ALU.add), R=[sr, smr], W=[hr])
        p.op("dve", lambda e, hb_=hb_, ab_=ab_, bb_=bb_, sm=sm: e.tensor_tensor_scan(out=hb_[:, ::-1], data0=ab_[:, ::-1], data1=bb_[:, ::-1], initial=sm[:, 1:2], op0=ALU.mult, op1=ALU.add), R=[sr, smr], W=[hr])
        p.op("dve", lambda e, hf=hf, hb_=hb_: e.tensor_tensor(out=hf, in0=hf, in1=hb_, op=ALU.add), R=[hr], W=[hr])
        mo, mor = mo_ring.next()
        p.op("pool", lambda e, mo=mo, hf=hf, g_=g_: e.tensor_tensor(out=mo, in0=hf, in1=g_, op=ALU.mult), R=[hr, gr_], W=[mor])
        p.dma("sp", lambda e, mo=mo, c=c: e.dma_start(out=MIX1[c], in_=mo), R=[mor])
    p.barrier()
    a.reset(mk)


RG4 = [[0, 1, 2, 3], [4, 5, 6, 7]]


def emit_allgather(k, src, dst):
    k.p.cc(lambda e: e.collective_compute("AllGather", ALU.bypass, replica_groups=RG4, ins=[src.opt()], outs=[dst.opt()]))


def phase_exchange_edges(k, UOWN, PUB1, GAT1, xfl, edges, edges_res):
    p, a = k.p, k.arena
    mk = a.mark()
    pub = a.alloc([3, KC], F32)
    r = Res()
    for (j, t) in ((0, 0), (1, NOWN - 2), (2, NOWN - 1)):
        p.dma("sp", lambda e, j=j, t=t: e.dma_start(out=pub[:, j, :], in_=UOWN[:, :, t:t + 1].rearrange("c p e -> p (c e)"), allow_slow_non_contiguous=True), W=[r])
    p.dma("sp", lambda e: e.dma_start(out=PUB1.rearrange("(p j) c -> p j c", j=3), in_=pub), R=[r])
    p.barrier()
    emit_allgather(k, PUB1, GAT1)
    p.barrier()
    g = a.alloc([4, 3, KC], F32)
    gr = Res()
    p.dma("sp", lambda e: e.dma_start(out=g, in_=GAT1.rearrange("(r p j) c -> p r j c", r=4, j=3)), W=[gr])
    for (dst_, col, kind) in ((0, 1, 0), (1, 2, 0), (2, 0, 1)):
        for rk in range(4):
            if rk == 0:
                p.op("dve", lambda e, dst_=dst_, col=col, kind=kind, rk=rk: e.tensor_scalar(out=edges[:, dst_, :], in0=g[:, rk, col, :], scalar1=xfl[0][:, kind, rk:rk + 1], scalar2=None, op0=ALU.mult),
                     R=[gr, xfl[1]], W=[edges_res])
            else:
                p.op("dve", lambda e, dst_=dst_, col=col, kind=kind, rk=rk: e.scalar_tensor_tensor(out=edges[:, dst_, :], in0=g[:, rk, col, :], scalar=xfl[0][:, kind, rk:rk + 1], op0=ALU.mult, in1=edges[:, dst_, :], op1=ALU.add),
                     R=[gr, xfl[1], edges_res], W=[edges_res])
    p.barrier()
    a.reset(mk)


def phase_exchange_carries(k, PUB2, GAT2, xfl, carF, carB, car_res):
    p, a = k.p, k.arena
    mk = a.mark()
    emit_allgather(k, PUB2, GAT2)
    p.barrier()
    g = a.alloc([4, KC, 4], F32)
    gr = Res()
    p.dma("sp", lambda e: e.dma_start(out=g, in_=GAT2.rearrange("(r c p) e -> p r c e", r=4, p=128)), W=[gr])
    for rk in range(4):
        for (dst, step, kind, ca, cb) in ((carF, rk, 2, 0, 1), (carB, 3 - rk, 3, 2, 3)):
            fl = xfl[0][:, kind, rk:rk + 1]
            p.op("dve", lambda e, dst=dst, step=step, fl=fl, ca=ca, rk=rk: e.tensor_scalar(out=dst[:, step, :, 0], in0=g[:, rk, :, ca], scalar1=-1.0, scalar2=fl, op0=ALU.add, op1=ALU.mult),
                 R=[gr, xfl[1]], W=[car_res])
            p.op("dve", lambda e, dst=dst, step=step: e.tensor_scalar(out=dst[:, step, :, 0], in0=dst[:, step, :, 0], scalar1=1.0, scalar2=None, op0=ALU.add),
                 R=[car_res], W=[car_res])
            p.op("dve", lambda e, dst=dst, step=step, fl=fl, cb=cb, rk=rk: e.tensor_scalar(out=dst[:, step, :, 1], in0=g[:, rk, :, cb], scalar1=fl, scalar2=None, op0=ALU.mult),
                 R=[gr, xfl[1]], W=[car_res])
    p.barrier()
    a.reset(mk)


def phase_final(k, XU, out, gfull_dram):
    p, a = k.p, k.arena
    mk = a.mark()
    gf = a.alloc([D], F32)
    gfr = Res()
    p.dma("sp", lambda e: e.dma_start(out=gf, in_=gfull_dram), W=[gfr])
    xb_ring = Ring([a.alloc([KC, 128], F32) for _ in range(3)])
    ob_ring = Ring([a.alloc([D], F32) for _ in range(2)])
    junk = a.alloc([512], F32)
    jres = Res()
    ss_ring = Ring([a.alloc([8], F32) for _ in range(2)])
    bk = k.bank_ring(range(8))
    XUv = XU.rearrange("c p t -> p c t")
    fl = {}

    def fin_load(blk):
        u0 = U_OWN + blk * 128
        xb, xbr = xb_ring.next()
        p.dma("sp", lambda e, xb=xb, u0=u0: e.dma_start(out=xb, in_=XUv[:, :, u0:u0 + 128]), W=[xbr])
        fl[blk] = (xb, xbr)

    fin_load(0)
    for blk in range(NOWN // 128):
        if blk + 1 < NOWN // 128:
            fin_load(blk + 1)
        xb, xbr = fl.pop(blk)
        ss, ssr = ss_ring.next()
        banks = [bk.next() for _ in range(4)]
        for g, (b, br) in enumerate(banks):
            for q in range(4):
                c = 4 * g + q
                p.op("pe", lambda e, b=b, xb=xb, c=c, q=q: e.transpose(b[:, q * 128:(q + 1) * 128], xb[:, c, :], k.ident), R=[xbr, k.cres], W=[br])
            p.op("act", lambda e, b=b, ss=ss, g=g: e.activation(out=junk, in_=b[:, :], func=AF.Square, accum_out=ss[:, g:g + 1]), R=[br], W=[ssr, jres])
        p.op("dve", lambda e, ss=ss: e.tensor_reduce(out=ss[:, 4:5], in_=ss[:, 0:4], axis=AX.X, op=ALU.add), R=[ssr], W=[ssr])
        p.op("dve", lambda e, ss=ss: e.tensor_scalar(out=ss[:, 5:6], in0=ss[:, 4:5], scalar1=1.0 / D, scalar2=EPS, op0=ALU.mult, op1=ALU.add), R=[ssr], W=[ssr])
        p.op("act", lambda e, ss=ss: e.activation(out=ss[:, 6:7], in_=ss[:, 5:6], func=AF.Sqrt), R=[ssr], W=[ssr])
        p.op("dve", lambda e, ss=ss: e.reciprocal(out=ss[:, 7:8], in_=ss[:, 6:7]), R=[ssr], W=[ssr])
        ob, obr = ob_ring.next()
        for g, (b, br) in enumerate(banks):
            p.op("dve", lambda e, b=b, ob=ob, ss=ss, g=g: e.scalar_tensor_tensor(out=ob[:, g * 512:(g + 1) * 512], in0=b[:, :], scalar=ss[:, 7:8], op0=ALU.mult,
                                                                                 in1=gf[:, g * 512:(g + 1) * 512], op1=ALU.mult), R=[br, ssr, gfr], W=[obr])
        p.dma("sp", lambda e, ob=ob, blk=blk: e.dma_start(out=out[blk * 128:(blk + 1) * 128, :], in_=ob), R=[obr])
    p.barrier()
    a.reset(mk)


def rg_small_inputs(inp):
    cw = np.concatenate([inp["rg_conv_w"][0], inp["rg_conv_b"][0][None, :]], axis=0)
    return {"rgconvT": np.ascontiguousarray(cw.reshape(5, KC, 128).transpose(2, 1, 0)),
            "rgbaT": fm(inp["rg_b_a"][0]), "rgbxT": fm(inp["rg_b_x"][0]), "rglamT": fm(inp["rg_lambda"][0]),
            "rg_w_a": inp["rg_w_a"][0], "rg_w_x": inp["rg_w_x"][0]}


def build_B():
    nc = bass.Bass("TRN2", target_bir_lowering=False)
    with ExitStack() as st:
        k = K(nc, st)
        UE = k.ext_in("UE", [KC, 128, NE])
        UCTX = k.ext_in("UCTX", [KC, 128, NCTX])
        k.ext_in("rgconvT", [128, KC, 5])
        k.ext_in("rgbaT", [128, 2, KC])
        k.ext_in("rgbxT", [128, 2, KC])
        k.ext_in("rglamT", [128, 2, KC])
        wa = k.ext_in("rg_w_a", [2, KC, 128, 128])
        wx = k.ext_in("rg_w_x", [2, KC, 128, 128])
        CAR = k.ext_out("CAR", [128, KC, 4])
        make_basic_consts(k)
        phase_rglru(k, "carry", UE, UCTX, wa, wx, CAR=CAR)
        k.p.final_wait()
        k.p.emit()
    return nc


def build_C():
    nc = bass.Bass("TRN2", target_bir_lowering=False)
    with ExitStack() as st:
        k = K(nc, st)
        p, a = k.p, k.arena
        UE = k.ext_in("UE", [KC, 128, NE])
        UCTX = k.ext_in("UCTX", [KC, 128, NCTX])
        GOWN = k.ext_in("GOWN", [KC, 128, NOWN], BF16)
        XOWN = k.ext_in("XOWN", [KC, 128, NOWN])
        k.ext_in("MODT", [128, 2, 144, 2])
        k.ext_in("normT", [128, 2, 3, KC])
        k.ext_in("rgconvT", [128, KC, 5])
        k.ext_in("rgbaT", [128, 2, KC])
        k.ext_in("rgbxT", [128, 2, KC])
        k.ext_in("rglamT", [128, 2, KC])
        k.ext_in("carF", [128, 3, KC, 2])
        k.ext_in("carB", [128, 3, KC, 2])
        wa = k.ext_in("rg_w_a", [2, KC, 128, 128])
        wx = k.ext_in("rg_w_x", [2, KC, 128, 128])
        rg_w_out = k.ext_in("rg_w_out", [D, D])
        f2i = k.ext_in("ffn2_w_in1", [D, 2 * DFF])
        f2o = k.ext_in("ffn2_w_out1", [DFF, D])
        gfull = k.ext_in("gfull", [128, D])
        out = k.ext_out("out", [NOWN, D])
        X1 = k.scratch("X1", [KC, 128, NU])
        MIX1 = k.scratch("MIX1", [KC, 128, NOWN], BF16)
        make_basic_consts(k)
        cs = load_consts(k, [("MODT", [2, 144, 2]), ("normT", [2, 3, KC])])
        modt = cs["MODT"][0]
        k.modt_res = cs["MODT"][1]
        for c in range(KC):
            p.dma("sp", lambda e, c=c: e.dma_start(out=X1[c, :, U_OWN:U_OWN + NOWN], in_=XOWN[c]))
        sites = {(1, s): derive_site(k, modt, cs["normT"], 1, s, half=(s != 1)) for s in (1, 2)}
        phase_rglru(k, "full", UE, UCTX, wa, wx, GOWN=GOWN, MIX1=MIX1)
        own = [(U_OWN, U_OWN + NOWN)]
        phase_outproj(k, X1, lambda c, u0, ln: MIX1[c, :, u0 - U_OWN:u0 - U_OWN + ln], own, rg_w_out, sites[(1, 1)])
        phase_ffn(k, X1, own, f2i, f2o, sites[(1, 2)])
        phase_final(k, X1, out, gfull)
        p.final_wait()
        p.emit()
    return nc


_PROGS = {}


def _prog(name, fn):
    if name not in _PROGS:
        _PROGS[name] = fn()
    return _PROGS[name]


def core_inputs_F(inp, core):
    m = core_inputs_A(inp, core, "all")
    m.update(rg_small_inputs(inp))
    rank = core % 4
    q = 9 * D // 4
    for l in range(2):
        m[f"w_mod{l}"] = np.ascontiguousarray(inp["w_mod"][l][:, rank * q:(rank + 1) * q])
    m["bmodT"] = np.ascontiguousarray(fm(inp["b_mod"])[:, :, rank * 36:(rank + 1) * 36])
    xfl = np.zeros((4, 4), np.float32)
    for r in range(4):
        xfl[0, r] = 1.0 if r == rank - 1 else 0.0
        xfl[1, r] = 1.0 if r == rank + 1 else 0.0
        xfl[2, r] = 1.0 if r < rank else 0.0
        xfl[3, r] = 1.0 if r > rank else 0.0
    m["xfl"] = np.ascontiguousarray(np.broadcast_to(xfl[None], (128, 4, 4)))
    m["rg_w_out"] = inp["rg_w_out"][0]
    m["ffn2_w_in1"] = inp["ffn2_w_in"][1]
    m["ffn2_w_out1"] = inp["ffn2_w_out"][1]
    m["gfull"] = np.ascontiguousarray(np.broadcast_to(inp["final_norm"][None, :], (128, D)).astype(np.float32))
    return m


def kernel(**inp):
    inp = {k_: np.asarray(v) for k_, v in inp.items()}
    cores = list(range(NCORES))
    nc = _prog("F", lambda: build_A("all", fused=True))
    res = run_bass_kernel_spmd(nc, [core_inputs_F(inp, c) for c in cores], core_ids=cores).results
    out = np.empty((2, 4 * NOWN, D), np.float32)
    for c in cores:
        out[c // 4, (c % 4) * NOWN:(c % 4 + 1) * NOWN] = res[c]["out"]
    return out


def kernel_unfused(**inp):
    inp = {k_: np.asarray(v) for k_, v in inp.items()}
    cores = list(range(NCORES))
    ncA = _prog("A", lambda: build_A("all"))
    resA = run_bass_kernel_spmd(ncA, [core_inputs_A(inp, c, "all") for c in cores], core_ids=cores).results
    small = rg_small_inputs(inp)
    UE = []
    for c in cores:
        ue = np.zeros((KC, 128, NE), np.float32)
        ue[:, :, 2:2 + NOWN] = resA[c]["UOWN"]
        if c % 4 > 0:
            ue[:, :, 0:2] = resA[c - 1]["UOWN"][:, :, NOWN - 2:NOWN]
        if c % 4 < 3:
            ue[:, :, 2 + NOWN] = resA[c + 1]["UOWN"][:, :, 0]
        UE.append(ue)
    ncB = _prog("B", build_B)
    mapsB = [dict(UE=UE[c], UCTX=resA[c]["UCTX"], **small) for c in cores]
    resB = run_bass_kernel_spmd(ncB, mapsB, core_ids=cores).results
    ident = np.zeros((128, KC, 2), np.float32)
    ident[:, :, 0] = 1.0
    mapsC = []
    normT = fm(np.stack([inp["norm_ffn1"], inp["norm_mix"], inp["norm_ffn2"]], axis=1))
    gfull = np.ascontiguousarray(np.broadcast_to(inp["final_norm"][None, :], (128, D)).astype(np.float32))
    for c in cores:
        b, ci = c // 4, c % 4
        prev = [resB[b * 4 + j]["CAR"][:, :, 0:2] for j in range(ci)]
        nxt = [resB[b * 4 + j]["CAR"][:, :, 2:4] for j in range(3, ci, -1)]
        carF = np.stack([ident] * (3 - len(prev)) + prev, axis=1)
        carB = np.stack([ident] * (3 - len(nxt)) + nxt, axis=1)
        m = dict(UE=UE[c], UCTX=resA[c]["UCTX"], GOWN=resA[c]["GOWN"], XOWN=resA[c]["XOWN"], MODT=resA[c]["MODT"],
                 normT=normT, carF=np.ascontiguousarray(carF), carB=np.ascontiguousarray(carB), rg_w_out=inp["rg_w_out"][0],
                 ffn2_w_in1=inp["ffn2_w_in"][1], ffn2_w_out1=inp["ffn2_w_out"][1], gfull=gfull, **small)
        mapsC.append(m)
    ncC = _prog("C", build_C)
    resC = run_bass_kernel_spmd(ncC, mapsC, core_ids=cores).results
    out = np.empty((2, 4 * NOWN, D), np.float32)
    for c in cores:
        out[c // 4, (c % 4) * NOWN:(c % 4 + 1) * NOWN] = resC[c]["out"]
    return out


def phase_rglru_carry(k, UOWN, UCTX, rg_w_a, rg_w_x, CAR, edges, ABS, ctxfin):
    p, a = k.p, k.arena
    mk = a.mark()
    cs = load_consts(k, [("rgconvT", [KC, 5]), ("rgbaT", [2, KC]), ("rgbxT", [2, KC]), ("rglamT", [2, KC])])
    convT, baT, bxT, lamT = cs["rgconvT"], cs["rgbaT"], cs["rgbxT"], cs["rglamT"]
    cst = a.alloc([2, KC], F32)
    cres = Res()
    p.op("act", lambda e: e.activation(out=cst, in_=lamT[0], func=AF.Exp, scale=-1.0), R=[lamT[1]], W=[cres])
    p.op("act", lambda e: e.activation(out=cst, in_=cst, func=AF.Ln, bias=1.0, scale=1.0), R=[cres], W=[cres])
    p.op("dve", lambda e: e.tensor_scalar(out=cst, in0=cst, scalar1=-8.0, scalar2=None, op0=ALU.mult), R=[cres], W=[cres])
    wa = a.alloc([2, KC, 128], BF16)
    wx = a.alloc([2, KC, 128], BF16)
    wres = Res()
    p.dma("pool", lambda e: e.dma_start(out=wa, in_=rg_w_a.rearrange("d h i j -> i d h j")), W=[wres])
    p.dma("pool", lambda e: e.dma_start(out=wx, in_=rg_w_x.rearrange("d h i j -> i d h j")), W=[wres])
    ub_ring = Ring([a.alloc([NE], F32) for _ in range(2)])
    uc_ring = Ring([a.alloc([NCTX + 3], F32) for _ in range(2)])
    for (b_, r_) in uc_ring.bufs:
        p.op("pool", lambda e, b_=b_: e.memset(b_, 0.0), W=[r_])
    ucv_ring = Ring([a.alloc([NS], F32) for _ in range(2)])
    u16_ring = Ring([a.alloc([NS], BF16) for _ in range(2)])
    rbs = [a.alloc([NS], F32) for _ in range(2)]
    ibs = [a.alloc([NS], F32) for _ in range(2)]
    tbs = [a.alloc([NS], F32) for _ in range(2)]
    rres, ires, tress = [Res(), Res()], [Res(), Res()], [Res(), Res()]
    ab = [a.alloc([NS], F32) for _ in range(2)]
    bb = [a.alloc([NS], F32) for _ in range(2)]
    gres = [Res(), Res()]
    hj = a.alloc([NS], F32)
    hjr = Res()
    car = a.alloc([KC, 4], F32)
    car_res = Res()
    bk = k.bank_ring(range(8))
    blocks = split_even(NS, 512)
    state = {}

    def stage1a(c):
        ub, ubr = ub_ring.next()
        ucx, ucxr = uc_ring.next()
        ucv, ucv_res = ucv_ring.next()
        u16, u16_res = u16_ring.next()
        state[c] = (ucv, ucv_res)
        p.dma("sp", lambda e: e.dma_start(out=ub[:, 2:2 + NOWN], in_=UOWN[c]), W=[ubr])
        for (dst_, src_) in ((0, 0), (1, 1), (2 + NOWN, 2)):
            p.op("pool", lambda e, dst_=dst_, src_=src_: e.tensor_copy(out=ub[:, dst_:dst_ + 1], in_=edges[0][:, src_, c:c + 1]), R=[edges[1]], W=[ubr])
        p.dma("sp", lambda e: e.dma_start(out=ucx[:, 2:2 + NCTX], in_=UCTX[c]), W=[ucxr])
        w = lambda kk: convT[0][:, c, kk:kk + 1]
        for (src, srcr, o0, n) in ((ub, ubr, 0, NOWN), (ucx, ucxr, NOWN, NCTX)):
            p.op("dve", lambda e, src=src, o0=o0, n=n: e.tensor_scalar(out=ucv[:, o0:o0 + n], in0=src[:, 0:n], scalar1=w(0), scalar2=w(4), op0=ALU.mult, op1=ALU.add),
                 R=[srcr, convT[1]], W=[ucv_res])
            for kk in (1, 2, 3):
                p.op("dve", lambda e, src=src, o0=o0, n=n, kk=kk: e.scalar_tensor_tensor(out=ucv[:, o0:o0 + n], in0=src[:, kk:kk + n], scalar=w(kk), op0=ALU.mult, in1=ucv[:, o0:o0 + n], op1=ALU.add),
                     R=[srcr, convT[1], ucv_res], W=[ucv_res])
        p.op("dve", lambda e: e.tensor_copy(out=u16, in_=ucv), R=[ucv_res], W=[u16_res])
        for d in range(2):
            for (n0, nl) in blocks:
                rp, rpr = bk.next()
                ip, ipr = bk.next()
                p.op("pe", lambda e, rp=rp, d=d, n0=n0, nl=nl: e.matmul(rp[:, 0:nl], lhsT=wa[:, d, c, :], rhs=u16[:, n0:n0 + nl], start=True, stop=True), R=[wres, u16_res], W=[rpr])
                p.op("pe", lambda e, ip=ip, d=d, n0=n0, nl=nl: e.matmul(ip[:, 0:nl], lhsT=wx[:, d, c, :], rhs=u16[:, n0:n0 + nl], start=True, stop=True), R=[wres, u16_res], W=[ipr])
                p.op("act", lambda e, rp=rp, d=d, n0=n0, nl=nl: e.activation(out=rbs[d][:, n0:n0 + nl], in_=rp[:, 0:nl], func=AF.Sigmoid, bias=baT[0][:, d, c:c + 1], scale=1.0), R=[rpr, baT[1]], W=[rres[d]])
                p.op("act", lambda e, ip=ip, d=d, n0=n0, nl=nl: e.activation(out=ibs[d][:, n0:n0 + nl], in_=ip[:, 0:nl], func=AF.Sigmoid, bias=bxT[0][:, d, c:c + 1], scale=1.0), R=[ipr, bxT[1]], W=[ires[d]])

    def stage1b(c):
        ucv, ucv_res = state.pop(c)
        for d in range(2):
            p.op("act", lambda e, d=d: e.activation(out=ab[d], in_=rbs[d], func=AF.Exp, scale=cst[:, d, c:c + 1]), R=[rres[d], cres], W=[gres[d]])
        for d in range(2):
            p.op("pool", lambda e, d=d: e.tensor_tensor(out=tbs[d], in0=ab[d], in1=ab[d], op=ALU.mult), R=[gres[d]], W=[tress[d]])
        for d in range(2):
            p.op("act", lambda e, d=d: e.activation(out=tbs[d], in_=tbs[d], func=AF.Sqrt, bias=1.0, scale=-1.0), R=[tress[d]], W=[tress[d]])
        for d in range(2):
            p.op("pool", lambda e, d=d: e.tensor_tensor(out=tbs[d], in0=tbs[d], in1=ibs[d], op=ALU.mult), R=[tress[d], ires[d]], W=[tress[d]])
        for d in range(2):
            p.op("dve", lambda e, d=d: e.tensor_tensor(out=bb[d], in0=tbs[d], in1=ucv, op=ALU.mult), R=[tress[d], ucv_res], W=[gres[d]])

    def stage2(c):
        p.op("dve", lambda e: e.tensor_tensor_scan(out=hj[:, NOWN:NS], data0=ab[0][:, NOWN:NS], data1=bb[0][:, NOWN:NS], initial=0.0, op0=ALU.mult, op1=ALU.add), R=[gres[0]], W=[hjr])
        p.op("dve", lambda e: e.tensor_copy(out=ctxfin[0][:, c, 0:1], in_=hj[:, NS - 1:NS]), R=[hjr], W=[ctxfin[1]])
        p.op("dve", lambda e: e.tensor_tensor_scan(out=hj[:, NOWN:NS][:, ::-1], data0=ab[1][:, NOWN:NS][:, ::-1], data1=bb[1][:, NOWN:NS][:, ::-1], initial=0.0, op0=ALU.mult, op1=ALU.add), R=[gres[1]], W=[hjr])
        p.op("dve", lambda e: e.tensor_copy(out=ctxfin[0][:, c, 1:2], in_=hj[:, NOWN:NOWN + 1]), R=[hjr], W=[ctxfin[1]])
        p.op("dve", lambda e: e.tensor_tensor_scan(out=hj[:, 0:NOWN], data0=ab[0][:, 0:NOWN], data1=bb[0][:, 0:NOWN], initial=0.0, op0=ALU.mult, op1=ALU.add), R=[gres[0]], W=[hjr])
        p.op("dve", lambda e: e.tensor_copy(out=car[:, c, 1:2], in_=hj[:, NOWN - 1:NOWN]), R=[hjr], W=[car_res])
        p.op("dve", lambda e: e.tensor_tensor_scan(out=hj[:, 0:NOWN][:, ::-1], data0=ab[1][:, 0:NOWN][:, ::-1], data1=bb[1][:, 0:NOWN][:, ::-1], initial=0.0, op0=ALU.mult, op1=ALU.add), R=[gres[1]], W=[hjr])
        p.op("dve", lambda e: e.tensor_copy(out=car[:, c, 3:4], in_=hj[:, 0:1]), R=[hjr], W=[car_res])
        p.op("dve", lambda e: e.tensor_reduce(out=car[:, c, 0:1], in_=ab[0][:, 0:NOWN], axis=AX.X, op=ALU.mult), R=[gres[0]], W=[car_res])
        p.op("dve", lambda e: e.tensor_reduce(out=car[:, c, 2:3], in_=ab[1][:, 0:NOWN], axis=AX.X, op=ALU.mult), R=[gres[1]], W=[car_res])
        for d in range(2):
            p.dma("sp", lambda e, d=d: e.dma_start(out=ABS[2 * d, c], in_=ab[d][:, 0:NOWN]), R=[gres[d]])
            p.dma("sp", lambda e, d=d: e.dma_start(out=ABS[2 * d + 1, c], in_=bb[d][:, 0:NOWN]), R=[gres[d]])

    stage1a(0)
    stage1b(0)
    for c in range(KC):
        if c + 1 < KC:
            stage1a(c + 1)
        stage2(c)
        if c + 1 < KC:
            stage1b(c + 1)
    p.dma("sp", lambda e: e.dma_start(out=CAR, in_=car), R=[car_res])
    p.barrier()
    a.reset(mk)
```

```python
import numpy as np
from contextlib import ExitStack
import concourse.bass as bass
import concourse.mybir as mybir
from concourse.bass_utils import run_bass_kernel_spmd

F32 = mybir.dt.float32
BF16 = mybir.dt.bfloat16
AF = mybir.ActivationFunctionType
ALU = mybir.AluOpType
AX = mybir.AxisListType

D = 2048
KC = 16
DFF = 5632
FC = 44
NOWN = 2048
HALO = 128
NCTX = 256
NU = 2560
U_OWN = 384
EPS = 1e-6
NCORES = 8

ENGS = ["pe", "act", "dve", "pool", "sp"]
DMA_POOL = {"sp": 8, "act": 4, "pool": 6}
SAME_ENGINE_SYNC = True
SEM_MAX = 4000


class Res:
    __slots__ = ("w", "r")

    def __init__(self):
        self.w = None
        self.r = {}


class Prog:
    def __init__(self, nc, stack, n_phase_sems):
        self.nc = nc
        self.ops = {e: [] for e in ENGS}
        self.dsem = {q: [stack.enter_context(nc.semaphore(f"d_{q}{i}")) for i in range(k)]
                     for q, k in DMA_POOL.items()}
        self.dcnt = {q: 0 for q in DMA_POOL}
        self.free = [stack.enter_context(nc.semaphore(f"e{i}")) for i in range(n_phase_sems)]
        self.esem = {e: self.free.pop() for e in ENGS}
        self.cnt = {e: 0 for e in ENGS}
        self.last = {e: None for e in ENGS}
        self.ccsem = stack.enter_context(nc.semaphore("ccsem"))
        self.cccnt = 0

    def _deps(self, R, W):
        deps = []
        for r in R:
            if r.w is not None:
                deps.append(r.w)
        for w in W:
            if w.w is not None:
                deps.append(w.w)
            deps.extend(w.r.values())
        return deps

    def _commit(self, tok, R, W):
        for r in R:
            r.r[id(tok[0])] = tok
        for w in W:
            w.w = tok
            w.r = {}

    def op(self, eng, fn, R=(), W=()):
        deps = self._deps(R, W)
        if self.cnt[eng] >= SEM_MAX:
            self.esem[eng] = self.free.pop()
            self.cnt[eng] = 0
        self.cnt[eng] += 1
        tok = (self.esem[eng], self.cnt[eng], eng)
        self.last[eng] = tok
        self._commit(tok, R, W)
        self.ops[eng].append((fn, deps, tok[0], 1))
        return tok

    def dma(self, q, fn, R=(), W=()):
        deps = self._deps(R, W)
        j = self.dcnt[q]
        self.dcnt[q] += 1
        k = len(self.dsem[q])
        sem = self.dsem[q][j % k]
        val = 16 * (j // k + 1)
        if val > 16:
            deps.append((sem, val - 16, "dma"))
        tok = (sem, val, "dma")
        self._commit(tok, R, W)
        self.ops[q].append((fn, deps, sem, 16))
        return tok

    def cc(self, fn, R=(), W=()):
        deps = self._deps(R, W)
        self.cccnt += 1
        tok = (self.ccsem, self.cccnt, "cc")
        self._commit(tok, R, W)
        self.ops["pool"].append((fn, deps, self.ccsem, 1))
        return tok

    def all_tokens(self):
        toks = []
        for e in ENGS:
            if self.last[e] is not None:
                toks.append(self.last[e])
        if self.cccnt > 0:
            toks.append((self.ccsem, self.cccnt, "cc"))
        for q in DMA_POOL:
            k = len(self.dsem[q])
            for i in range(min(k, self.dcnt[q])):
                n_uses = (self.dcnt[q] - 1 - i) // k + 1
                toks.append((self.dsem[q][i], 16 * n_uses, "dma"))
        return toks

    def barrier(self):
        toks = self.all_tokens()
        for e in ENGS:
            self.ops[e].append((None, [(s, v, "x") for (s, v, _) in toks], None, 0))

    def final_wait(self):
        toks = self.all_tokens()
        self.ops["sp"].append((None, [(s, v, "x") for (s, v, _) in toks], None, 0))

    def emit(self):
        nc = self.nc
        with nc.Block() as block:
            def run(e):
                def body(engobj):
                    waited = {}
                    for fn, deps, sem, inc in self.ops[e]:
                        need = {}
                        for (s, v, de) in deps:
                            if de == e and (e == "pe" or not SAME_ENGINE_SYNC):
                                continue
                            key = id(s)
                            if waited.get(key, 0) >= v:
                                continue
                            if key not in need or need[key][1] < v:
                                need[key] = (s, v)
                        for key, (s, v) in need.items():
                            engobj.wait_ge(s, v)
                            waited[key] = v
                        if fn is not None:
                            fn(engobj).then_inc(sem, inc)
                return body
            block.tensor(run("pe"))
            block.scalar(run("act"))
            block.vector(run("dve"))
            block.gpsimd(run("pool"))
            block.sync(run("sp"))


class Arena:
    def __init__(self, nc, stack, nbytes):
        self.t32 = stack.enter_context(nc.sbuf_tensor("arena", [128, nbytes // 4], F32))
        self.t16 = self.t32.bitcast(BF16)
        self.n = nbytes
        self.off = 0

    def alloc(self, shape, dt):
        n = int(np.prod(shape))
        sz = 4 if dt == F32 else 2
        self.off = (self.off + 63) // 64 * 64
        o = self.off
        self.off += n * sz
        assert self.off <= self.n, f"arena overflow {self.off} > {self.n}"
        ap = (self.t32[:, o // 4:o // 4 + n] if dt == F32 else self.t16[:, o // 2:o // 2 + n])
        if len(shape) == 2:
            ap = ap.rearrange("p (a b) -> p a b", a=shape[0])
        elif len(shape) == 3:
            ap = ap.rearrange("p (a b c) -> p a b c", a=shape[0], b=shape[1])
        return ap

    def mark(self):
        return self.off

    def reset(self, m):
        self.off = m


class Ring:
    def __init__(self, bufs):
        self.bufs = [(b, Res()) for b in bufs]
        self.i = 0

    def next(self):
        b = self.bufs[self.i % len(self.bufs)]
        self.i += 1
        return b


def split_even(n, maxlen):
    k = -(-n // maxlen)
    base = n // k
    rem = n - base * k
    out = []
    o = 0
    for i in range(k):
        ln = base + (1 if i < rem else 0)
        out.append((o, ln))
        o += ln
    return out


def make_tiles(ranges, tmax):
    total = sum(b - a for a, b in ranges)
    ntile = -(-total // tmax)
    tl = -(-total // ntile)
    tl = (tl + 1) // 2 * 2
    tiles = []
    cur = []
    curlen = 0
    for a, b in ranges:
        pos = a
        while pos < b:
            lim = b if pos >= NCTX else min(b, NCTX)
            take = min(lim - pos, tl - curlen)
            cur.append((pos, take, curlen, 1 if pos < NCTX else 0))
            curlen += take
            pos += take
            if curlen == tl:
                tiles.append(cur)
                cur = []
                curlen = 0
    if cur:
        tiles.append(cur)
    return tiles


class K:
    def __init__(self, nc, stack, n_phase_sems=80):
        self.nc = nc
        self.st = stack
        self.p = Prog(nc, stack, n_phase_sems)
        self.arena = Arena(nc, stack, 175 * 1024)
        self.banks = [stack.enter_context(nc.psum_tensor(f"bank{i}", [128, 512], F32)) for i in range(8)]
        self.dram = {}
        self.bank_res = [Res() for _ in range(8)]

    def ext_in(self, name, shape, dt=F32):
        t = self.nc.dram_tensor(name, list(shape), dt, kind="ExternalInput").ap()
        self.dram[name] = t
        return t

    def ext_out(self, name, shape, dt=F32):
        t = self.nc.dram_tensor(name, list(shape), dt, kind="ExternalOutput").ap()
        self.dram[name] = t
        return t

    def scratch(self, name, shape, dt=F32):
        t = self.nc.dram_tensor(name, list(shape), dt, kind="Internal").ap()
        self.dram[name] = t
        return t

    def bank_ring(self, idx):
        r = Ring([])
        r.bufs = [(self.banks[i], self.bank_res[i]) for i in idx]
        return r


def load_consts(k, names_shapes):
    out = {}
    for name, shape in names_shapes:
        src = k.dram[name]
        t = k.arena.alloc(shape, F32)
        r = Res()
        pat = {1: None, 2: None, 3: None}
        k.p.dma("sp", lambda e, t=t, src=src: e.dma_start(out=t, in_=src), W=[r])
        out[name] = (t, r)
    return out


def make_basic_consts(k):
    a = k.arena
    p = k.p
    ident = a.alloc([128], F32)
    ones32 = a.alloc([128], F32)
    ones16 = a.alloc([128], BF16)
    r = Res()
    p.op("pool", lambda e: e.memset(ident, 0.0), W=[r])
    p.op("pool", lambda e: e.affine_select(out=ident, in_=ident, compare_op=ALU.not_equal, fill=1.0,
                                           base=0, pattern=[[-1, 128]], channel_multiplier=1), R=[r], W=[r])
    p.op("pool", lambda e: e.memset(ones32, 1.0), W=[r])
    p.op("pool", lambda e: e.memset(ones16, 1.0), W=[r])
    k.ident, k.ones32, k.ones16, k.cres = ident, ones32, ones16, r


def phase_transpose_in(k, xin, XU, barrier=True):
    p, a = k.p, k.arena
    m = a.mark()
    xt = Ring([a.alloc([D], F32) for _ in range(3)])
    xo = Ring([a.alloc([KC, 128], F32) for _ in range(2)])
    bk = k.bank_ring(range(8))
    XUv = XU.rearrange("c p t -> p c t")
    nblk = NU // 128
    loaded = {}

    def t0_load(blk):
        t, tr = xt.next()
        p.dma("sp", lambda e, t=t, blk=blk: e.dma_start(out=t, in_=xin[blk * 128:(blk + 1) * 128, :]), W=[tr])
        loaded[blk] = (t, tr)

    t0_load(0)
    for blk in range(nblk):
        if blk + 1 < nblk:
            t0_load(blk + 1)
        t, tr = loaded.pop(blk)
        o, orr = xo.next()
        for g in range(4):
            b, br = bk.next()
            for q in range(4):
                c = 4 * g + q
                p.op("pe", lambda e, b=b, t=t, c=c, q=q: e.transpose(b[:, q * 128:(q + 1) * 128], t[:, c * 128:(c + 1) * 128], k.ident),
                     R=[tr, k.cres], W=[br])
            eng = "act" if g % 2 == 0 else "dve"
            if eng == "act":
                p.op("act", lambda e, o=o, b=b, g=g: e.activation(out=o[:, 4 * g:4 * g + 4, :].rearrange("p a b -> p (a b)"), in_=b[:, :], func=AF.Copy),
                     R=[br], W=[orr])
            else:
                p.op("dve", lambda e, o=o, b=b, g=g: e.tensor_copy(out=o[:, 4 * g:4 * g + 4, :].rearrange("p a b -> p (a b)"), in_=b[:, :]),
                     R=[br], W=[orr])
        p.dma("sp", lambda e, o=o, blk=blk: e.dma_start(out=XUv[:, :, blk * 128:(blk + 1) * 128], in_=o), R=[orr])
    if barrier:
        p.barrier()
        a.reset(m)


def phase_mod(k, w_mod, cvT, bmodT, modt, layers, njg=36):
    p, a = k.p, k.arena
    m = a.mark()
    sc = a.alloc([KC, 2], F32)
    scr = Res()
    p.op("act", lambda e: e.activation(out=sc, in_=cvT[0], func=AF.Silu), R=[cvT[1]], W=[scr])
    wr = Ring([a.alloc([KC, 512], F32) for _ in range(3)])
    bk = k.bank_ring([0, 1])
    mr = k.modt_res
    n = 0
    for l in layers:
        for jg in range(njg):
            w, wres = wr.next()
            q_ = "sp" if n % 2 == 0 else "act"
            n += 1
            p.dma(q_, lambda e, w=w, l=l, jg=jg: e.dma_start(out=w, in_=w_mod[l][:, jg * 512:(jg + 1) * 512].rearrange("(k p) n -> p k n", p=128)), W=[wres])
            b, br = bk.next()
            for q in range(4):
                for kc in range(KC):
                    p.op("pe", lambda e, b=b, w=w, q=q, kc=kc: e.matmul(b[:, 2 * q:2 * q + 2], lhsT=w[:, kc, q * 128:(q + 1) * 128], rhs=sc[:, kc, :],
                                                                         start=(kc == 0), stop=(kc == KC - 1)),
                         R=[wres, scr], W=[br])
            for q in range(4):
                j = jg * 4 + q
                p.op("dve", lambda e, b=b, q=q, j=j, l=l: e.tensor_scalar(out=modt[:, l, j, :], in0=b[:, 2 * q:2 * q + 2], scalar1=bmodT[0][:, l, j:j + 1], scalar2=None, op0=ALU.add),
                     R=[br, bmodT[1]], W=[mr])
    p.barrier()
    a.reset(m)


def derive_site(k, modt, normT, l, site, half):
    p, a = k.p, k.arena
    gs = a.alloc([KC, 2], F32)
    gt = a.alloc([KC, 2], F32)
    r = Res()
    j0 = 3 * site * KC
    sh = modt[:, l, j0:j0 + KC, :]
    p.op("dve", lambda e: e.tensor_scalar(out=gs, in0=modt[:, l, j0 + KC:j0 + 2 * KC, :], scalar1=1.0, scalar2=None, op0=ALU.add),
         R=[k.modt_res], W=[r])
    for rr in range(2):
        p.op("dve", lambda e, rr=rr: e.tensor_tensor(out=gs[:, :, rr], in0=gs[:, :, rr], in1=normT[0][:, l, site, :], op=ALU.mult),
             R=[r, normT[1]], W=[r])
    p.op("dve", lambda e: e.tensor_scalar(out=gt, in0=modt[:, l, j0 + 2 * KC:j0 + 3 * KC, :], scalar1=(0.5 if half else 1.0), scalar2=None, op0=ALU.mult),
         R=[k.modt_res], W=[r])
    return dict(gs=gs, sh=sh, gt=gt, res=r)


def emit_modulate(k, XU, tile, T, xs, xs_res, h, h_res, site, tmp_ring, rstd, rstd_res, stat_banks):
    p = k.p
    XUv = XU.rearrange("c p t -> p c t")
    for (u0, ln, off, r) in tile:
        p.dma("sp", lambda e, u0=u0, ln=ln, off=off: e.dma_start(out=xs[:, :, off:off + ln], in_=XUv[:, :, u0:u0 + ln]), W=[xs_res])
    subs = split_even(T, 512)
    banks = [stat_banks.next() for _ in subs]
    for kc in range(KC):
        sq, sqr = tmp_ring.next()
        p.op("act", lambda e, sq=sq, kc=kc: e.activation(out=sq[:, 0:T], in_=xs[:, kc, :], func=AF.Square), R=[xs_res], W=[sqr])
        for (n0, nl), (b, br) in zip(subs, banks):
            p.op("pe", lambda e, b=b, sq=sq, n0=n0, nl=nl, kc=kc: e.matmul(b[:, 0:nl], lhsT=k.ones32, rhs=sq[:, n0:n0 + nl], start=(kc == 0), stop=(kc == KC - 1)),
                 R=[sqr, k.cres], W=[br])
    for (n0, nl), (b, br) in zip(subs, banks):
        p.op("dve", lambda e, b=b, n0=n0, nl=nl: e.tensor_scalar(out=rstd[:, n0:n0 + nl], in0=b[:, 0:nl], scalar1=1.0 / D, scalar2=EPS, op0=ALU.mult, op1=ALU.add),
             R=[br], W=[rstd_res])
    p.op("act", lambda e: e.activation(out=rstd[:, 0:T], in_=rstd[:, 0:T], func=AF.Sqrt), R=[rstd_res], W=[rstd_res])
    p.op("dve", lambda e: e.reciprocal(out=rstd[:, 0:T], in_=rstd[:, 0:T]), R=[rstd_res], W=[rstd_res])
    i = 0
    for (u0, ln, off, r) in tile:
        for kc in range(KC):
            t, tr = tmp_ring.next()
            p.op("dve", lambda e, t=t, kc=kc, off=off, ln=ln, r=r: e.scalar_tensor_tensor(
                out=t[:, 0:ln], in0=xs[:, kc, off:off + ln], scalar=site["gs"][:, kc, r:r + 1], op0=ALU.mult,
                in1=rstd[:, off:off + ln], op1=ALU.mult), R=[xs_res, rstd_res, site["res"]], W=[tr])
            if True:
                p.op("act", lambda e, t=t, kc=kc, off=off, ln=ln, r=r: e.activation(
                    out=h[:, kc, off:off + ln], in_=t[:, 0:ln], func=AF.Identity, bias=site["sh"][:, kc, r:r + 1], scale=1.0),
                    R=[tr, k.modt_res], W=[h_res])
            else:
                p.op("pool", lambda e, t=t, kc=kc, off=off, ln=ln, r=r: e.tensor_scalar(
                    out=h[:, kc, off:off + ln], in0=t[:, 0:ln], scalar1=site["sh"][:, kc, r:r + 1], scalar2=None, op0=ALU.add),
                    R=[tr, k.modt_res], W=[h_res])
            i += 1


def emit_residual(k, XU, tile, m, sub_banks, subs, site, xm_ring, xo_ring, xres=None):
    p = k.p
    xm, xmr = xm_ring.next()
    xo, xor_ = xo_ring.next()
    for (u0, ln, off, r) in tile:
        p.dma("sp", lambda e, u0=u0, ln=ln, off=off, xm=xm: e.dma_start(out=xm[:, off:off + ln], in_=XU[m, :, u0:u0 + ln]), R=([xres] if xres is not None else []), W=[xmr])
    for (n0, nl), (b, br) in zip(subs, sub_banks):
        for (u0, ln, off, r) in tile:
            lo, hi = max(n0, off), min(n0 + nl, off + ln)
            if lo >= hi:
                continue
            p.op("dve", lambda e, b=b, lo=lo, hi=hi, n0=n0, r=r, xm=xm, xo=xo: e.scalar_tensor_tensor(
                out=xo[:, lo:hi], in0=b[:, lo - n0:hi - n0], scalar=site["gt"][:, m, r:r + 1], op0=ALU.mult,
                in1=xm[:, lo:hi], op1=ALU.add), R=[br, xmr, site["res"]], W=[xor_])
    for (u0, ln, off, r) in tile:
        p.dma("sp", lambda e, u0=u0, ln=ln, off=off, xo=xo: e.dma_start(out=XU[m, :, u0:u0 + ln], in_=xo[:, off:off + ln]), R=[xor_], W=([xres] if xres is not None else []))


TMAX = 640
TMAX_FFN = 1152
TMAX_BIG = 1280


class ModStream:
    def __init__(self, k, XU, tile, site, xc_ring, tmp_ring, rstd, rstd_res, stat_banks, sq_ring=None):
        self.sq_ring = sq_ring
        self.k, self.XU, self.tile, self.site = k, XU, tile, site
        self.T = sum(s[1] for s in tile)
        self.xc_ring, self.tmp_ring = xc_ring, tmp_ring
        self.rstd, self.rstd_res = rstd, rstd_res
        self.subs = split_even(self.T, 512)
        self.banks = stat_banks[:len(self.subs)]
        self.sq = {}

    def _load(self, kc):
        p = self.k.p
        xc, xcr = self.xc_ring.next()
        for (u0, ln, off, r) in self.tile:
            p.dma("sp", lambda e, xc=xc, u0=u0, ln=ln, off=off, kc=kc: e.dma_start(out=xc[:, off:off + ln], in_=self.XU[kc, :, u0:u0 + ln]), W=[xcr])
        return xc, xcr

    def load_sq(self, kc):
        p, T = self.k.p, self.T
        xc, xcr = self._load(kc)
        sq, sqr = (self.sq_ring or self.tmp_ring).next()
        p.op("act", lambda e, sq=sq, xc=xc, T=T: e.activation(out=sq[:, 0:T], in_=xc[:, 0:T], func=AF.Square), R=[xcr], W=[sqr])
        self.sq[kc] = (sq, sqr)

    def mm(self, kc):
        p, k = self.k.p, self.k
        sq, sqr = self.sq.pop(kc)
        for (n0, nl), (b, br) in zip(self.subs, self.banks):
            p.op("pe", lambda e, b=b, sq=sq, n0=n0, nl=nl, kc=kc: e.matmul(b[:, 0:nl], lhsT=(k.ones16 if self.sq_ring is not None else k.ones32), rhs=sq[:, n0:n0 + nl], start=(kc == 0), stop=(kc == KC - 1)),
                 R=[sqr, k.cres], W=[br])

    def finish(self, h, h_res):
        self.finish_rstd()
        for kc in range(KC):
            self.h_chunk(kc, h, h_res)

    def finish_rstd(self):
        p, k, T, rstd, rstd_res, site = self.k.p, self.k, self.T, self.rstd, self.rstd_res, self.site
        for (n0, nl), (b, br) in zip(self.subs, self.banks):
            p.op("dve", lambda e, b=b, n0=n0, nl=nl: e.tensor_scalar(out=rstd[:, n0:n0 + nl], in0=b[:, 0:nl], scalar1=1.0 / D, scalar2=EPS, op0=ALU.mult, op1=ALU.add),
                 R=[br], W=[rstd_res])
        p.op("act", lambda e: e.activation(out=rstd[:, 0:T], in_=rstd[:, 0:T], func=AF.Sqrt), R=[rstd_res], W=[rstd_res])
        p.op("dve", lambda e: e.reciprocal(out=rstd[:, 0:T], in_=rstd[:, 0:T]), R=[rstd_res], W=[rstd_res])

    def h_chunk(self, kc, h, h_res):
        p, k, T, rstd, rstd_res, site = self.k.p, self.k, self.T, self.rstd, self.rstd_res, self.site
        i = 0
        if True:
            xc, xcr = self._load(kc)
            for (u0, ln, off, r) in self.tile:
                t, tr = self.tmp_ring.next()
                p.op("dve", lambda e, t=t, xc=xc, kc=kc, off=off, ln=ln, r=r: e.scalar_tensor_tensor(
                    out=t[:, 0:ln], in0=xc[:, off:off + ln], scalar=site["gs"][:, kc, r:r + 1], op0=ALU.mult,
                    in1=rstd[:, off:off + ln], op1=ALU.mult), R=[xcr, rstd_res, site["res"]], W=[tr])
                if True:
                    p.op("act", lambda e, t=t, kc=kc, off=off, ln=ln, r=r: e.activation(
                        out=h[:, kc, off:off + ln], in_=t[:, 0:ln], func=AF.Identity, bias=site["sh"][:, kc, r:r + 1], scale=1.0),
                        R=[tr, k.modt_res], W=[h_res])
                else:
                    p.op("pool", lambda e, t=t, kc=kc, off=off, ln=ln, r=r: e.tensor_scalar(
                        out=h[:, kc, off:off + ln], in0=t[:, 0:ln], scalar1=site["sh"][:, kc, r:r + 1], scalar2=None, op0=ALU.add),
                        R=[tr, k.modt_res], W=[h_res])
                i += 1


def phase_ffn(k, XU, ranges, w_in, w_out, site):
    p, a = k.p, k.arena
    mk = a.mark()
    NH = 2
    FH = FC // NH
    tiles = make_tiles(ranges, TMAX_FFN)
    TM = max(sum(s[1] for s in t) for t in tiles)
    act = a.alloc([FH, TM], BF16)
    big_res = Res()
    h = a.alloc([KC, TM], BF16)
    h_res = Res()
    rstds = [(a.alloc([TM], F32), Res()), (a.alloc([TM], F32), Res())]
    xc_ring = Ring([a.alloc([TM], F32) for _ in range(2)])
    tmp_ring = Ring([a.alloc([TM], F32) for _ in range(2)])
    sq_ring = Ring([a.alloc([TM], BF16) for _ in range(2)])
    wi_ring = Ring([a.alloc([KC, 256], BF16) for _ in range(2)])
    wo_ring = Ring([a.alloc([FH, 128], BF16) for _ in range(3)])
    sg_ring = Ring([a.alloc([512], F32) for _ in range(2)])
    xo_ring = Ring([a.alloc([TM], F32) for _ in range(2)])
    bk8 = k.bank_ring(range(8))
    bk5 = k.bank_ring(range(5))
    stat_banks = [(k.banks[i], k.bank_res[i]) for i in (5, 6, 7)]
    w_in_v = w_in.rearrange("(k p) n -> p k n", p=128)
    w_out_v = w_out.rearrange("(j p) n -> p j n", p=128)

    def new_ms(ti):
        rs, rr = rstds[ti % 2]
        return ModStream(k, XU, tiles[ti], site, xc_ring, tmp_ring, rs, rr, stat_banks, sq_ring=sq_ring)

    mss = {0: new_ms(0)}
    for kc in range(KC):
        mss[0].load_sq(kc)
        mss[0].mm(kc)
    mss[0].finish_rstd()
    for kc in range(KC):
        mss[0].h_chunk(kc, h, h_res)
    for ti, tile in enumerate(tiles):
        T = sum(s[1] for s in tile)
        subs = split_even(T, 512)
        xu_res = [Res() for _ in range(KC)]
        for hf in range(NH):
            for jl in range(FH):
                j = hf * FH + jl
                wi, wir = wi_ring.next()
                p.dma("pool", lambda e, wi=wi, j=j: e.dma_start(out=wi[:, :, 0:128], in_=w_in_v[:, :, j * 128:(j + 1) * 128]), W=[wir])
                p.dma("pool", lambda e, wi=wi, j=j: e.dma_start(out=wi[:, :, 128:256], in_=w_in_v[:, :, DFF + j * 128:DFF + (j + 1) * 128]), W=[wir])
                for (n0, nl) in subs:
                    g_, gr = bk8.next()
                    u_, ur = bk8.next()
                    for kc in range(KC):
                        p.op("pe", lambda e, g_=g_, wi=wi, kc=kc, n0=n0, nl=nl: e.matmul(g_[:, 0:nl], lhsT=wi[:, kc, 0:128], rhs=h[:, kc, n0:n0 + nl], start=(kc == 0), stop=(kc == KC - 1)),
                             R=[wir, h_res], W=[gr])
                        p.op("pe", lambda e, u_=u_, wi=wi, kc=kc, n0=n0, nl=nl: e.matmul(u_[:, 0:nl], lhsT=wi[:, kc, 128:256], rhs=h[:, kc, n0:n0 + nl], start=(kc == 0), stop=(kc == KC - 1)),
                             R=[wir, h_res], W=[ur])
                    sg, sgr = sg_ring.next()
                    p.op("act", lambda e, sg=sg, g_=g_, nl=nl: e.activation(out=sg[:, 0:nl], in_=g_[:, 0:nl], func=AF.Silu), R=[gr], W=[sgr])
                    p.op("dve", lambda e, sg=sg, u_=u_, n0=n0, nl=nl, jl=jl: e.tensor_tensor(out=act[:, jl, n0:n0 + nl], in0=u_[:, 0:nl], in1=sg[:, 0:nl], op=ALU.mult),
                         R=[ur, sgr, gr], W=[big_res])
            last = (hf == NH - 1)
            nms = None
            if hf == 0 and ti + 1 < len(tiles):
                nms = mss[ti + 1] = new_ms(ti + 1)
            hms = mss.get(ti + 1) if last else None
            for m in range(KC):
                if hms is not None:
                    hms.h_chunk(m, h, h_res)
                if nms is not None:
                    nms.load_sq(m)
                wo, wor = wo_ring.next()
                p.dma("pool", lambda e, wo=wo, m=m, hf=hf: e.dma_start(out=wo, in_=w_out_v[:, hf * FH:(hf + 1) * FH, m * 128:(m + 1) * 128]), W=[wor])
                ob = [bk5.next() for _ in subs]
                for jl in range(FH):
                    for (n0, nl), (b, br) in zip(subs, ob):
                        p.op("pe", lambda e, b=b, wo=wo, jl=jl, n0=n0, nl=nl: e.matmul(b[:, 0:nl], lhsT=wo[:, jl, :], rhs=act[:, jl, n0:n0 + nl], start=(jl == 0), stop=(jl == FH - 1)),
                             R=[wor, big_res], W=[br])
                if nms is not None and m > 0:
                    nms.mm(m - 1)
                emit_residual(k, XU, tile, m, ob, subs, site, xc_ring, xo_ring, xres=xu_res[m])
            if nms is not None:
                nms.mm(KC - 1)
                nms.finish_rstd()
    p.barrier()
    a.reset(mk)


def fm(v):
    v = np.asarray(v, np.float32)
    lead = v.shape[:-1]
    n = v.shape[-1] // 128
    t = v.reshape(lead + (n, 128))
    t = np.moveaxis(t, -1, 0)
    return np.ascontiguousarray(t)


def core_xin(x, ctx, core):
    b, c = core // 4, core % 4
    lo = c * NOWN - HALO
    hi = (c + 1) * NOWN + HALO
    seq = x.shape[1]
    buf = np.zeros((NU, D), np.float32)
    buf[0:NCTX] = ctx[b]
    a0, a1 = max(lo, 0), min(hi, seq)
    buf[NCTX + (a0 - lo):NCTX + (a1 - lo)] = x[b, a0:a1]
    return buf


def rope_tables(core):
    c = core % 4
    u = np.arange(NU)
    t = c * NOWN - HALO + (u - NCTX)
    t = np.clip(t, 0, 4 * NOWN - 1)
    row = (t // 64).astype(np.float32)
    col = (t % 64).astype(np.float32)
    inv_freq = (np.float32(10000.0) ** (-np.arange(32, dtype=np.float32) / np.float32(32))).astype(np.float32)
    pidx = np.arange(128)
    axis = pidx // 64
    half = (pidx % 64) // 32
    fr = pidx % 32
    pos = np.where(axis[:, None] == 0, row[None, :], col[None, :]).astype(np.float32)
    ang = (pos * inv_freq[fr][:, None]).astype(np.float32)
    C = np.cos(ang).astype(np.float32)
    S = np.sin(ang).astype(np.float32)
    S = np.where(half[:, None] == 0, -S, S).astype(np.float32)
    C[:, :NCTX] = 1.0
    S[:, :NCTX] = 0.0
    return np.ascontiguousarray(C), np.ascontiguousarray(S)


def build_A(upto="all", dbg=False, fused=False):
    order = ["ffn1", "abin", "abmix", "about", "l0", "l1ffn1", "all"]
    lvl = order.index(upto)
    nc = bass.Bass("TRN2", target_bir_lowering=False)
    with ExitStack() as st:
        k = K(nc, st)
        p, a = k.p, k.arena
        xin = k.ext_in("xin", [NU, D])
        k.ext_in("cvT", [128, KC, 2])
        k.ext_in("bmodT", [128, 2, 36 if fused else 144])
        k.ext_in("normT", [128, 2, 3, KC])
        layers = [0, 1] if lvl >= 5 else [0]
        w_mod = {l: k.ext_in(f"w_mod{l}", [D, 9 * D // 4 if fused else 9 * D]) for l in layers}
        f1i = {l: k.ext_in(f"ffn1_w_in{l}", [D, 2 * DFF]) for l in layers}
        f1o = {l: k.ext_in(f"ffn1_w_out{l}", [DFF, D]) for l in layers}
        XU = k.ext_out("XU", [KC, 128, NU]) if dbg else k.scratch("XU", [KC, 128, NU])
        modt_out = None if fused else k.ext_out("MODT", [128, 2, 144, 2])
        names = [("cvT", [KC, 2]), ("bmodT", [2, 36 if fused else 144]), ("normT", [2, 3, KC])]
        if lvl >= 1:
            ab_w_in = k.ext_in("ab_w_in", [D, 4608])
            k.ext_in("ropeC", [128, NU])
            k.ext_in("ropeS", [128, NU])
            QU = k.scratch("QU", [8, 128, NU], BF16)
            KU = k.scratch("KU", [2, 128, NU], BF16)
            VU = k.scratch("VU", [NU, 256], BF16)
            ZU = k.scratch("ZU", [8, 128, NU])
            BGU = k.scratch("BGU", [8, 128, NU])
        if lvl >= 2:
            k.ext_in("masks", [128, 2, 128])
            k.ext_in("flags", [128, 2])
            k.ext_in("sinkT", [128, 8])
            k.ext_in("convT", [128, 3, 8])
            MIXU = k.ext_out("MIXU", [KC, 128, NU], BF16) if dbg else k.scratch("MIXU", [KC, 128, NU], BF16)
        if lvl >= 3:
            ab_w_out = k.ext_in("ab_w_out", [D, D])
        if lvl >= 4:
            f2i0 = k.ext_in("ffn2_w_in0", [D, 2 * DFF])
            f2o0 = k.ext_in("ffn2_w_out0", [DFF, D])
        if lvl >= 6 and not fused:
            rg_w_in = k.ext_in("rg_w_in", [D, 2 * D])
            XOWN = k.ext_out("XOWN", [KC, 128, NOWN])
            GOWN = k.ext_out("GOWN", [KC, 128, NOWN], BF16)
            UOWN = k.ext_out("UOWN", [KC, 128, NOWN])
            UCTX = k.ext_out("UCTX", [KC, 128, NCTX])
        if fused:
            rg_w_in = k.ext_in("rg_w_in", [D, 2 * D])
            GOWN = k.scratch("GOWN", [KC, 128, NOWN], BF16)
            UOWN = k.scratch("UOWN", [KC, 128, NOWN])
            UCTX = k.scratch("UCTX", [KC, 128, NCTX])
            MIX1 = k.scratch("MIX1", [KC, 128, NOWN], BF16)
            PUB1 = k.scratch("PUB1", [128 * 3, KC])
            GAT1 = k.scratch("GAT1", [4 * 128 * 3, KC])
            PUB2 = k.scratch("PUB2", [D, 4])
            GAT2 = k.scratch("GAT2", [4 * D, 4])
            k.ext_in("rgconvT", [128, KC, 5])
            k.ext_in("rgbaT", [128, 2, KC])
            k.ext_in("rgbxT", [128, 2, KC])
            k.ext_in("rglamT", [128, 2, KC])
            k.ext_in("xfl", [128, 4, 4])
            rg_wa = k.ext_in("rg_w_a", [2, KC, 128, 128])
            rg_wx = k.ext_in("rg_w_x", [2, KC, 128, 128])
            rg_w_out = k.ext_in("rg_w_out", [D, D])
            f2i1 = k.ext_in("ffn2_w_in1", [D, 2 * DFF])
            f2o1 = k.ext_in("ffn2_w_out1", [DFF, D])
            gfull = k.ext_in("gfull", [128, D])
            out = k.ext_out("out", [NOWN, D])
            names += [("xfl", [4, 4])]
        make_basic_consts(k)
        cs = load_consts(k, names)
        modt = a.alloc([2, 144, 2], F32)
        k.modt_res = Res()
        m0 = a.mark()
        phase_transpose_in(k, xin, XU, barrier=not fused)
        if not fused:
            phase_mod(k, w_mod, cs["cvT"], cs["bmodT"], modt, layers)
            p.dma("sp", lambda e: e.dma_start(out=modt_out, in_=modt), R=[k.modt_res])
        else:
            PUBM = k.scratch("PUBM", [128, 144])
            GATM = k.scratch("GATM", [512, 144])
            modq = a.alloc([2, 36, 2], F32)
            phase_mod(k, w_mod, cs["cvT"], cs["bmodT"], modq, layers, njg=9)
            p.dma("sp", lambda e: e.dma_start(out=PUBM.rearrange("p (l j r) -> p l j r", l=2, r=2), in_=modq), R=[k.modt_res])
            p.barrier()
            emit_allgather(k, PUBM, GATM)
            p.barrier()
            for rk in range(4):
                for l in range(2):
                    p.dma("sp", lambda e, rk=rk, l=l: e.dma_start(out=modt[:, l, rk * 36:(rk + 1) * 36, :],
                                                                   in_=GATM[rk * 128:(rk + 1) * 128, l * 72:(l + 1) * 72].rearrange("p (j r) -> p j r", r=2)), W=[k.modt_res])
            p.barrier()
            a.reset(m0)
        sites = {(l, s): derive_site(k, modt, cs["normT"], l, s, half=(s != 1)) for l in layers for s in range(3)}
        phase_ffn(k, XU, [(0, NU)], f1i[0], f1o[0], sites[(0, 0)])
        if lvl >= 1:
            phase_ab_in(k, XU, ab_w_in, sites[(0, 1)], QU, KU, VU, ZU, BGU)
        if lvl >= 2:
            phase_ab_mix(k, QU, KU, VU, ZU, BGU, MIXU)
        own_ctx = [(0, NCTX), (U_OWN, U_OWN + NOWN)]
        if lvl >= 3:
            phase_outproj(k, XU, lambda c, u0, ln: MIXU[c, :, u0:u0 + ln], own_ctx, ab_w_out, sites[(0, 1)])
        if lvl >= 4:
            phase_ffn(k, XU, own_ctx, f2i0, f2o0, sites[(0, 2)])
        if lvl >= 5:
            phase_ffn(k, XU, own_ctx, f1i[1], f1o[1], sites[(1, 0)])
        if fused:
            edges = a.alloc([3, KC], F32)
            carF = a.alloc([4, KC, 2], F32)
            carB = a.alloc([4, KC, 2], F32)
            edges_res, car_res = Res(), Res()
            ctxfin = a.alloc([KC, 2], F32)
            ctxfin_res = Res()
            ABS = k.scratch("ABS", [4, KC, 128, NOWN])
        if lvl >= 6:
            def g_of(c, u0, ln):
                return None if u0 < NCTX else GOWN[c, :, u0 - U_OWN:u0 - U_OWN + ln]

            def u_of(c, u0, ln):
                return UCTX[c, :, u0:u0 + ln] if u0 < NCTX else UOWN[c, :, u0 - U_OWN:u0 - U_OWN + ln]
            phase_rg_in(k, XU, rg_w_in, sites[(1, 1)], g_of, u_of)
            if not fused:
                for c in range(KC):
                    p.dma("sp", lambda e, c=c: e.dma_start(out=XOWN[c], in_=XU[c, :, U_OWN:U_OWN + NOWN]))
        if fused:
            phase_exchange_edges(k, UOWN, PUB1, GAT1, cs["xfl"], edges, edges_res)
            phase_rglru_carry(k, UOWN, UCTX, rg_wa, rg_wx, PUB2.rearrange("(c p) e -> p c e", p=128), (edges, edges_res), ABS, (ctxfin, ctxfin_res))
            phase_exchange_carries(k, PUB2, GAT2, cs["xfl"], carF, carB, car_res)
            phase_rglru_apply(k, ABS, (ctxfin, ctxfin_res), ((carF, car_res), (carB, car_res)), GOWN, MIX1)
            own = [(U_OWN, U_OWN + NOWN)]
            phase_outproj(k, XU, lambda c, u0, ln: MIX1[c, :, u0 - U_OWN:u0 - U_OWN + ln], own, rg_w_out, sites[(1, 1)])
            phase_ffn(k, XU, own, f2i1, f2o1, sites[(1, 2)])
            phase_final(k, XU, out, gfull)
        p.final_wait()
        p.emit()
    return nc


def core_inputs_A(inp, core, upto="all"):
    order = ["ffn1", "abin", "abmix", "about", "l0", "l1ffn1", "all"]
    lvl = order.index(upto)
    b, c = core // 4, core % 4
    cv = np.stack([inp["c"][b], inp["c_ctx"]], axis=-1)
    m = {"xin": core_xin(inp["x"], inp["ctx"], core),
         "cvT": np.ascontiguousarray(cv.reshape(KC, 128, 2).transpose(1, 0, 2)),
         "bmodT": fm(inp["b_mod"]),
         "normT": fm(np.stack([inp["norm_ffn1"], inp["norm_mix"], inp["norm_ffn2"]], axis=1))}
    for l in ([0, 1] if lvl >= 5 else [0]):
        m[f"w_mod{l}"] = inp["w_mod"][l]
        m[f"ffn1_w_in{l}"] = inp["ffn1_w_in"][l]
        m[f"ffn1_w_out{l}"] = inp["ffn1_w_out"][l]
    if lvl >= 1:
        m["ab_w_in"] = inp["ab_w_in"][0]
        m["ropeC"], m["ropeS"] = rope_tables(core)
    if lvl >= 2:
        j = np.arange(128)[:, None]
        i = np.arange(128)[None, :]
        m["masks"] = np.ascontiguousarray(np.stack([(j >= i), (j <= i)], axis=1).astype(np.float32))
        m["flags"] = np.ascontiguousarray(np.broadcast_to(np.array([c > 0, c < 3], np.float32)[None, :], (128, 2)))
        m["sinkT"] = np.ascontiguousarray(np.broadcast_to(inp["ab_sink"][0][None, :], (128, 8)).astype(np.float32))
        m["convT"] = np.ascontiguousarray(inp["ab_conv_w"][0].reshape(3, 8, 128).transpose(2, 0, 1))
    if lvl >= 3:
        m["ab_w_out"] = inp["ab_w_out"][0]
    if lvl >= 4:
        m["ffn2_w_in0"] = inp["ffn2_w_in"][0]
        m["ffn2_w_out0"] = inp["ffn2_w_out"][0]
    if lvl >= 6:
        m["rg_w_in"] = inp["rg_w_in"][0]
    return m


def phase_ab_in(k, XU, w_in, site, QU, KU, VU, ZU, BGU):
    p, a = k.p, k.arena
    mk = a.mark()
    tiles = make_tiles([(0, NU)], TMAX_BIG)
    TM = max(sum(s[1] for s in t) for t in tiles)
    h = a.alloc([KC, TM], BF16)
    h_res = Res()
    rstd = a.alloc([TM], F32)
    rstd_res = Res()
    xc_ring = Ring([a.alloc([TM], F32) for _ in range(3)])
    tmp_ring = Ring([a.alloc([TM], F32) for _ in range(3)])
    w_ring = Ring([a.alloc([KC, 128], BF16) for _ in range(4)])
    wsw_ring = Ring([a.alloc([KC, 128], BF16) for _ in range(2)])
    wv = a.alloc([KC, 256], BF16)
    wv_res = Res()
    st16 = Ring([a.alloc([TM], BF16) for _ in range(3)])
    st32 = Ring([a.alloc([TM], F32) for _ in range(3)])
    vst = Ring([a.alloc([256], BF16) for _ in range(2)])
    bk = k.bank_ring(range(8))
    stat_banks = [(k.banks[i], k.bank_res[i]) for i in (5, 6, 7)]
    cs = load_consts(k, [("ropeC", [NU]), ("ropeS", [NU])])
    COS, SIN = cs["ropeC"], cs["ropeS"]
    w_v = w_in.rearrange("(k p) n -> p k n", p=128)
    p.dma("pool", lambda e: e.dma_start(out=wv, in_=w_v[:, :, 1280:1536]), W=[wv_res])
    for tile in tiles:
        T = sum(s[1] for s in tile)
        tu0 = tile[0][0]
        assert T % 128 == 0
        subs = split_even(T, 512)
        ms = ModStream(k, XU, tile, site, xc_ring, tmp_ring, rstd, rstd_res, stat_banks)
        for kc in range(KC):
            ms.load_sq(kc)
            ms.mm(kc)
        ms.finish(h, h_res)
        for j in range(10):
            col0 = j * 128 if j < 8 else 1024 + (j - 8) * 128
            w, wr = w_ring.next()
            p.dma("pool", lambda e, w=w, col0=col0: e.dma_start(out=w, in_=w_v[:, :, col0:col0 + 128]), W=[wr])
            ws, wsr = wsw_ring.next()
            for (d0, s0) in ((0, 32), (32, 0), (64, 96), (96, 64)):
                p.op("pool", lambda e, ws=ws, w=w, d0=d0, s0=s0: e.tensor_copy(out=ws[:, :, d0:d0 + 32], in_=w[:, :, s0:s0 + 32]), R=[wr], W=[wsr])
            st, str_ = st16.next()
            for (n0, nl) in subs:
                ba, bar = bk.next()
                bb, bbr = bk.next()
                for kc in range(KC):
                    p.op("pe", lambda e, ba=ba, w=w, kc=kc, n0=n0, nl=nl: e.matmul(ba[:, 0:nl], lhsT=w[:, kc, :], rhs=h[:, kc, n0:n0 + nl], start=(kc == 0), stop=(kc == KC - 1)),
                         R=[wr, h_res], W=[bar])
                for kc in range(KC):
                    p.op("pe", lambda e, bb=bb, ws=ws, kc=kc, n0=n0, nl=nl: e.matmul(bb[:, 0:nl], lhsT=ws[:, kc, :], rhs=h[:, kc, n0:n0 + nl], start=(kc == 0), stop=(kc == KC - 1)),
                         R=[wsr, h_res], W=[bbr])
                t1, t1r = tmp_ring.next()
                t2, t2r = tmp_ring.next()
                p.op("dve", lambda e, tu0=tu0, t1=t1, ba=ba, n0=n0, nl=nl: e.tensor_tensor(out=t1[:, 0:nl], in0=ba[:, 0:nl], in1=COS[0][:, tu0 + n0:tu0 + n0 + nl], op=ALU.mult),
                     R=[bar, COS[1]], W=[t1r])
                p.op("dve", lambda e, tu0=tu0, t2=t2, bb=bb, n0=n0, nl=nl: e.tensor_tensor(out=t2[:, 0:nl], in0=bb[:, 0:nl], in1=SIN[0][:, tu0 + n0:tu0 + n0 + nl], op=ALU.mult),
                     R=[bbr, SIN[1]], W=[t2r])
                p.op("pool", lambda e, st=st, t1=t1, t2=t2, n0=n0, nl=nl: e.tensor_tensor(out=st[:, n0:n0 + nl], in0=t1[:, 0:nl], in1=t2[:, 0:nl], op=ALU.add),
                     R=[t1r, t2r], W=[str_])
            dst = QU[j] if j < 8 else KU[j - 8]
            p.dma("sp", lambda e, tu0=tu0, st=st, dst=dst, T=T: e.dma_start(out=dst[:, tu0:tu0 + T], in_=st[:, 0:T]), R=[str_])
        for blk in range(T // 128):
            b, br = bk.next()
            for kc in range(KC):
                p.op("pe", lambda e, b=b, kc=kc, blk=blk: e.matmul(b[:, 0:256], lhsT=h[:, kc, blk * 128:(blk + 1) * 128], rhs=wv[:, kc, :], start=(kc == 0), stop=(kc == KC - 1)),
                     R=[wv_res, h_res], W=[br])
            vs, vsr = vst.next()
            p.op("act", lambda e, vs=vs, b=b: e.activation(out=vs, in_=b[:, 0:256], func=AF.Copy), R=[br], W=[vsr])
            p.dma("sp", lambda e, tu0=tu0, vs=vs, blk=blk: e.dma_start(out=VU[tu0 + blk * 128:tu0 + (blk + 1) * 128, :], in_=vs), R=[vsr])
        for cc in range(8):
            ws3 = []
            for base in (1536, 2560, 3584):
                w, wr = w_ring.next()
                p.dma("pool", lambda e, w=w, c0=base + cc * 128: e.dma_start(out=w, in_=w_v[:, :, c0:c0 + 128]), W=[wr])
                ws3.append((w, wr))
            zs, zsr = st32.next()
            bs, bsr = st32.next()
            for (n0, nl) in subs:
                bks = [bk.next() for _ in range(3)]
                for (w, wr), (b, br) in zip(ws3, bks):
                    for kc in range(KC):
                        p.op("pe", lambda e, b=b, w=w, kc=kc, n0=n0, nl=nl: e.matmul(b[:, 0:nl], lhsT=w[:, kc, :], rhs=h[:, kc, n0:n0 + nl], start=(kc == 0), stop=(kc == KC - 1)),
                             R=[wr, h_res], W=[br])
                (bgb, bgr), (cgb, cgr), (ub, ur) = bks
                t1, t1r = tmp_ring.next()
                p.op("act", lambda e, t1=t1, ub=ub, nl=nl: e.activation(out=t1[:, 0:nl], in_=ub[:, 0:nl], func=AF.Copy), R=[ur], W=[t1r])
                p.op("dve", lambda e, zs=zs, cgb=cgb, t1=t1, n0=n0, nl=nl: e.tensor_tensor(out=zs[:, n0:n0 + nl], in0=cgb[:, 0:nl], in1=t1[:, 0:nl], op=ALU.mult),
                     R=[cgr, t1r], W=[zsr])
                p.op("act", lambda e, bs=bs, bgb=bgb, n0=n0, nl=nl: e.activation(out=bs[:, n0:n0 + nl], in_=bgb[:, 0:nl], func=AF.Copy), R=[bgr], W=[bsr])
            p.dma("sp", lambda e, tu0=tu0, zs=zs, cc=cc, T=T: e.dma_start(out=ZU[cc, :, tu0:tu0 + T], in_=zs[:, 0:T]), R=[zsr])
            p.dma("sp", lambda e, tu0=tu0, bs=bs, cc=cc, T=T: e.dma_start(out=BGU[cc, :, tu0:tu0 + T], in_=bs[:, 0:T]), R=[bsr])
    p.barrier()
    a.reset(mk)


def phase_ab_mix(k, QU, KU, VU, ZU, BGU, MIXU):
    p, a = k.p, k.arena
    mk = a.mark()
    cs = load_consts(k, [("masks", [2, 128]), ("flags", [2]), ("sinkT", [8]), ("convT", [3, 8])])
    masks, flags, sinkT, convT = cs["masks"], cs["flags"], cs["sinkT"], cs["convT"]
    scale = 128.0 ** -0.5
    mres = Res()
    mP = a.alloc([4, 128], BF16)
    mN = a.alloc([4, 128], BF16)
    mP0 = a.alloc([4, 128], BF16)
    mNL = a.alloc([4, 128], BF16)
    for hh in range(4):
        p.op("dve", lambda e, hh=hh: e.tensor_copy(out=mP[:, hh, :], in_=masks[0][:, 0, :]), R=[masks[1]], W=[mres])
        p.op("dve", lambda e, hh=hh: e.tensor_copy(out=mN[:, hh, :], in_=masks[0][:, 1, :]), R=[masks[1]], W=[mres])
    p.op("dve", lambda e: e.tensor_scalar(out=mP0, in0=mP, scalar1=flags[0][:, 0:1], scalar2=None, op0=ALU.mult), R=[mres, flags[1]], W=[mres])
    p.op("dve", lambda e: e.tensor_scalar(out=mNL, in0=mN, scalar1=flags[0][:, 1:2], scalar2=None, op0=ALU.mult), R=[mres, flags[1]], W=[mres])
    esink = a.alloc([8], F32)
    p.op("act", lambda e: e.activation(out=esink, in_=sinkT[0], func=AF.Exp), R=[sinkT[1]], W=[mres])
    gsets = [(a.alloc([NU], BF16), a.alloc([NU // 128, 128], BF16), a.alloc([4, NU], BF16), Res()) for _ in range(2)]
    E_ring = Ring([a.alloc([512], BF16) for _ in range(6)])
    rd_ring = Ring([a.alloc([512], F32) for _ in range(2)])
    os_ring = Ring([a.alloc([4, 128], BF16) for _ in range(2)])
    sc_banks = k.bank_ring([0, 1, 2, 3])
    acc_banks = k.bank_ring([4, 5, 6, 7])
    flat = lambda t: t.rearrange("p a b -> p (a b)")
    for g in range(2):
        kU, vU, qU, gres = gsets[g]
        p.dma("sp", lambda e, g=g, kU=kU: e.dma_start(out=kU, in_=KU[g]), W=[gres])
        p.dma("sp", lambda e, g=g, vU=vU: e.dma_start(out=vU, in_=VU[:, g * 128:(g + 1) * 128].rearrange("(b p) d -> p b d", p=128)), W=[gres])
        p.dma("sp", lambda e, g=g, qU=qU: e.dma_start(out=qU, in_=QU[g * 4:(g + 1) * 4].rearrange("h p t -> p h t")), W=[gres])
    for g in range(2):
        kU, vU, qU, gres = gsets[g]
        qblocks = [(0, [(0, None), (1, None)]), (1, [(0, None), (1, None)])]
        for i in range(16):
            qblocks.append((i + 3, [(i + 2, mP0 if i == 0 else mP), (i + 3, None), (i + 4, mNL if i == 15 else mN), (0, None), (1, None)]))
        for ub, keys in qblocks:
            den, denr = acc_banks.next()
            ob, obr = acc_banks.next()
            nk = len(keys)
            for ki, (kb, mask) in enumerate(keys):
                sb, sbr = sc_banks.next()
                p.op("pe", lambda e, sb=sb, kb=kb, ub=ub, kU=kU, qU=qU: e.matmul(sb[:, 0:512], lhsT=kU[:, kb * 128:(kb + 1) * 128], rhs=qU[:, :, ub * 128:(ub + 1) * 128], start=True, stop=True),
                     R=[gres], W=[sbr])
                E, Er = E_ring.next()
                p.op("act", lambda e, E=E, sb=sb: e.activation(out=E, in_=sb[:, 0:512], func=AF.Exp, scale=scale), R=[sbr], W=[Er])
                if mask is not None:
                    p.op("dve", lambda e, E=E, mask=mask: e.tensor_tensor(out=E, in0=E, in1=flat(mask), op=ALU.mult), R=[Er, mres], W=[Er])
                p.op("pe", lambda e, den=den, E=E, ki=ki, nk=nk: e.matmul(den[:, 0:512], lhsT=k.ones16, rhs=E, start=(ki == 0), stop=(ki == nk - 1)),
                     R=[Er, k.cres], W=[denr])
                p.op("pe", lambda e, ob=ob, E=E, kb=kb, ki=ki, nk=nk, vU=vU: e.matmul(ob[:, 0:512], lhsT=vU[:, kb, :], rhs=E, start=(ki == 0), stop=(ki == nk - 1)),
                     R=[Er, gres], W=[obr])
            rd, rdr = rd_ring.next()
            for hh in range(4):
                p.op("dve", lambda e, rd=rd, den=den, hh=hh, g=g: e.tensor_scalar(out=rd[:, hh * 128:(hh + 1) * 128], in0=den[:, hh * 128:(hh + 1) * 128],
                                                                                   scalar1=esink[:, g * 4 + hh:g * 4 + hh + 1], scalar2=None, op0=ALU.add),
                     R=[denr, mres], W=[rdr])
            p.op("dve", lambda e, rd=rd: e.reciprocal(out=rd, in_=rd), R=[rdr], W=[rdr])
            os_, osr = os_ring.next()
            p.op("dve", lambda e, os_=os_, ob=ob, rd=rd: e.tensor_tensor(out=flat(os_), in0=ob[:, 0:512], in1=rd, op=ALU.mult), R=[obr, rdr], W=[osr])
            p.dma("sp", lambda e, os_=os_, g=g, ub=ub: e.dma_start(out=MIXU[g * 4:(g + 1) * 4, :, ub * 128:(ub + 1) * 128].rearrange("h p t -> p h t"), in_=os_), R=[osr])
    zb_ring = Ring([a.alloc([NU + 2], F32) for _ in range(2)])
    bg_ring = Ring([a.alloc([NU], F32) for _ in range(2)])
    acc = a.alloc([NOWN], F32)
    accr = Res()
    zc = a.alloc([NCTX + 2], F32)
    accc = a.alloc([NCTX], F32)
    cst_ring = Ring([a.alloc([NU], BF16) for _ in range(2)])
    p.op("pool", lambda e: e.memset(zc, 0.0), W=[accr])
    cpre = {}

    def conv_load(cc):
        zb, zbr = zb_ring.next()
        bg, bgr = bg_ring.next()
        p.dma("sp", lambda e, zb=zb, cc=cc: e.dma_start(out=zb[:, 1:NU + 1], in_=ZU[cc]), W=[zbr])
        p.dma("sp", lambda e, bg=bg, cc=cc: e.dma_start(out=bg, in_=BGU[cc]), W=[bgr])
        cpre[cc] = (zb, zbr, bg, bgr)

    conv_load(0)
    for cc in range(8):
        if cc + 1 < 8:
            conv_load(cc + 1)
        zb, zbr, bg, bgr = cpre.pop(cc)
        cst, cstr = cst_ring.next()
        w = lambda kk, cc=cc: convT[0][:, kk, cc:cc + 1]
        p.op("pool", lambda e, zb=zb: e.tensor_copy(out=zc[:, 1:NCTX + 1], in_=zb[:, 1:NCTX + 1]), R=[zbr], W=[accr])
        p.op("dve", lambda e, w=w: e.tensor_scalar(out=accc, in0=zc[:, 0:NCTX], scalar1=w(0), scalar2=None, op0=ALU.mult), R=[accr, convT[1]], W=[accr])
        for kk in (1, 2):
            p.op("dve", lambda e, w=w, kk=kk: e.scalar_tensor_tensor(out=accc, in0=zc[:, kk:kk + NCTX], scalar=w(kk), op0=ALU.mult, in1=accc, op1=ALU.add), R=[accr, convT[1]], W=[accr])
        p.op("dve", lambda e, cst=cst, bg=bg: e.tensor_tensor(out=cst[:, 0:NCTX], in0=accc, in1=bg[:, 0:NCTX], op=ALU.mult), R=[accr, bgr], W=[cstr])
        p.op("dve", lambda e, zb=zb: e.tensor_scalar(out=zb[:, U_OWN:U_OWN + 1], in0=zb[:, U_OWN:U_OWN + 1], scalar1=flags[0][:, 0:1], scalar2=None, op0=ALU.mult), R=[zbr, flags[1]], W=[zbr])
        p.op("dve", lambda e, zb=zb: e.tensor_scalar(out=zb[:, U_OWN + NOWN + 1:U_OWN + NOWN + 2], in0=zb[:, U_OWN + NOWN + 1:U_OWN + NOWN + 2], scalar1=flags[0][:, 1:2], scalar2=None, op0=ALU.mult), R=[zbr, flags[1]], W=[zbr])
        p.op("dve", lambda e, zb=zb, w=w: e.tensor_scalar(out=acc, in0=zb[:, U_OWN:U_OWN + NOWN], scalar1=w(0), scalar2=None, op0=ALU.mult), R=[zbr, convT[1]], W=[accr])
        for kk in (1, 2):
            p.op("dve", lambda e, zb=zb, w=w, kk=kk: e.scalar_tensor_tensor(out=acc, in0=zb[:, U_OWN + kk:U_OWN + kk + NOWN], scalar=w(kk), op0=ALU.mult, in1=acc, op1=ALU.add), R=[zbr, accr, convT[1]], W=[accr])
        p.op("dve", lambda e, cst=cst, bg=bg: e.tensor_tensor(out=cst[:, U_OWN:U_OWN + NOWN], in0=acc, in1=bg[:, U_OWN:U_OWN + NOWN], op=ALU.mult), R=[accr, bgr], W=[cstr])
        p.dma("sp", lambda e, cst=cst, cc=cc: e.dma_start(out=MIXU[8 + cc, :, 0:NCTX], in_=cst[:, 0:NCTX]), R=[cstr])
        p.dma("sp", lambda e, cst=cst, cc=cc: e.dma_start(out=MIXU[8 + cc, :, U_OWN:U_OWN + NOWN], in_=cst[:, U_OWN:U_OWN + NOWN]), R=[cstr])
    p.barrier()
    a.reset(mk)


def phase_outproj(k, XU, mix_of, ranges, w_out, site):
    p, a = k.p, k.arena
    mk = a.mark()
    tiles = make_tiles(ranges, TMAX_BIG)
    TM = max(sum(s[1] for s in t) for t in tiles)
    mix = a.alloc([KC, TM], BF16)
    mix_res = Res()
    w_ring = Ring([a.alloc([KC, 128], BF16) for _ in range(3)])
    xm_ring = Ring([a.alloc([TM], F32) for _ in range(2)])
    xo_ring = Ring([a.alloc([TM], F32) for _ in range(2)])
    bk = k.bank_ring(range(8))
    w_v = w_out.rearrange("(k p) n -> p k n", p=128)
    for tile in tiles:
        T = sum(s[1] for s in tile)
        subs = split_even(T, 512)
        for (u0, ln, off, r) in tile:
            for c in range(KC):
                p.dma("sp", lambda e, c=c, u0=u0, ln=ln, off=off: e.dma_start(out=mix[:, c, off:off + ln], in_=mix_of(c, u0, ln)), W=[mix_res])
        for m in range(KC):
            w, wr = w_ring.next()
            p.dma("pool", lambda e, w=w, m=m: e.dma_start(out=w, in_=w_v[:, :, m * 128:(m + 1) * 128]), W=[wr])
            ob = [bk.next() for _ in subs]
            for kc in range(KC):
                for (n0, nl), (b, br) in zip(subs, ob):
                    p.op("pe", lambda e, b=b, w=w, kc=kc, n0=n0, nl=nl: e.matmul(b[:, 0:nl], lhsT=w[:, kc, :], rhs=mix[:, kc, n0:n0 + nl], start=(kc == 0), stop=(kc == KC - 1)),
                         R=[wr, mix_res], W=[br])
            emit_residual(k, XU, tile, m, ob, subs, site, xm_ring, xo_ring)
    p.barrier()
    a.reset(mk)


def phase_rg_in(k, XU, w_in, site, g_of, u_of):
    p, a = k.p, k.arena
    mk = a.mark()
    tiles = make_tiles([(0, NCTX), (U_OWN, U_OWN + NOWN)], TMAX_BIG)
    TM = max(sum(s[1] for s in t) for t in tiles)
    h = a.alloc([KC, TM], BF16)
    h_res = Res()
    rstd = a.alloc([TM], F32)
    rstd_res = Res()
    xc_ring = Ring([a.alloc([TM], F32) for _ in range(3)])
    tmp_ring = Ring([a.alloc([TM], F32) for _ in range(3)])
    w_ring = Ring([a.alloc([KC, 128], BF16) for _ in range(4)])
    st16 = Ring([a.alloc([TM], BF16) for _ in range(2)])
    st32 = Ring([a.alloc([TM], F32) for _ in range(2)])
    bk = k.bank_ring(range(8))
    stat_banks = [(k.banks[i], k.bank_res[i]) for i in (5, 6, 7)]
    w_v = w_in.rearrange("(k p) n -> p k n", p=128)
    for tile in tiles:
        T = sum(s[1] for s in tile)
        subs = split_even(T, 512)
        ms = ModStream(k, XU, tile, site, xc_ring, tmp_ring, rstd, rstd_res, stat_banks)
        for kc in range(KC):
            ms.load_sq(kc)
            ms.mm(kc)
        ms.finish(h, h_res)
        for ch in range(32):
            w, wr = w_ring.next()
            p.dma("pool", lambda e, w=w, ch=ch: e.dma_start(out=w, in_=w_v[:, :, ch * 128:(ch + 1) * 128]), W=[wr])
            is_gate = ch < 16
            st, sr = (st16 if is_gate else st32).next()
            for (n0, nl) in subs:
                b, br = bk.next()
                for kc in range(KC):
                    p.op("pe", lambda e, b=b, w=w, kc=kc, n0=n0, nl=nl: e.matmul(b[:, 0:nl], lhsT=w[:, kc, :], rhs=h[:, kc, n0:n0 + nl], start=(kc == 0), stop=(kc == KC - 1)),
                         R=[wr, h_res], W=[br])
                if is_gate:
                    p.op("act", lambda e, st=st, b=b, n0=n0, nl=nl: e.activation(out=st[:, n0:n0 + nl], in_=b[:, 0:nl], func=AF.Gelu), R=[br], W=[sr])
                else:
                    p.op("dve", lambda e, st=st, b=b, n0=n0, nl=nl: e.tensor_copy(out=st[:, n0:n0 + nl], in_=b[:, 0:nl]), R=[br], W=[sr])
            for (u0, ln, off, r) in tile:
                dst = g_of(ch, u0, ln) if is_gate else u_of(ch - 16, u0, ln)
                if dst is None:
                    continue
                p.dma("sp", lambda e, st=st, dst=dst, off=off, ln=ln: e.dma_start(out=dst, in_=st[:, off:off + ln]), R=[sr])
    p.barrier()
    a.reset(mk)


NE = NOWN + 3
NS = NOWN + NCTX


def phase_rglru(k, mode, UE, UCTX, rg_w_a, rg_w_x, CAR=None, GOWN=None, MIX1=None, edges=None, car_tiles=None, UOWN=None, ABS=None, ctxfin=None):
    p, a = k.p, k.arena
    mk = a.mark()
    names = [("rgconvT", [KC, 5]), ("rgbaT", [2, KC]), ("rgbxT", [2, KC]), ("rglamT", [2, KC])]
    if mode == "full" and car_tiles is None:
        names += [("carF", [3, KC, 2]), ("carB", [3, KC, 2])]
    cs = load_consts(k, names)
    convT, baT, bxT, lamT = cs["rgconvT"], cs["rgbaT"], cs["rgbxT"], cs["rglamT"]
    cst = a.alloc([2, KC], F32)
    cres = Res()
    p.op("act", lambda e: e.activation(out=cst, in_=lamT[0], func=AF.Exp, scale=-1.0), R=[lamT[1]], W=[cres])
    p.op("act", lambda e: e.activation(out=cst, in_=cst, func=AF.Ln, bias=1.0, scale=1.0), R=[cres], W=[cres])
    p.op("dve", lambda e: e.tensor_scalar(out=cst, in0=cst, scalar1=-8.0, scalar2=None, op0=ALU.mult), R=[cres], W=[cres])
    wa = a.alloc([2, KC, 128], BF16)
    wx = a.alloc([2, KC, 128], BF16)
    wres = Res()
    p.dma("pool", lambda e: e.dma_start(out=wa, in_=rg_w_a.rearrange("d h i j -> i d h j")), W=[wres])
    p.dma("pool", lambda e: e.dma_start(out=wx, in_=rg_w_x.rearrange("d h i j -> i d h j")), W=[wres])
    ub_ring = Ring([a.alloc([NE], F32) for _ in range(2)])
    uc_ring = Ring([a.alloc([NCTX + 3], F32) for _ in range(2)])
    for (b_, r_) in uc_ring.bufs:
        p.op("pool", lambda e, b_=b_: e.memset(b_, 0.0), W=[r_])
    ucv = a.alloc([NS], F32)
    ucv_res = Res()
    u16 = a.alloc([NS], BF16)
    u16_res = Res()
    rbs = [a.alloc([NS], F32) for _ in range(2)]
    ibs = [a.alloc([NS], F32) for _ in range(2)]
    tbs = [a.alloc([NS], F32) for _ in range(2)]
    rres = [Res(), Res()]
    ires = [Res(), Res()]
    tress = [Res(), Res()]
    ab = [a.alloc([NS], F32) for _ in range(2)]
    bb = [a.alloc([NS], F32) for _ in range(2)]
    gres = [Res(), Res()]
    hb = [a.alloc([NS], F32) for _ in range(2)]
    hres = [Res(), Res()]
    sm = a.alloc([16], F32)
    smres = Res()
    if mode == "carry":
        car = a.alloc([KC, 4], F32)
        car_res = Res()
    else:
        g_ring = Ring([a.alloc([NOWN], BF16) for _ in range(2)])
        mo_ring = Ring([a.alloc([NOWN], BF16) for _ in range(2)])
        carF, carB = car_tiles if car_tiles is not None else (cs["carF"], cs["carB"])
        nstep = 4 if car_tiles is not None else 3
    bk = k.bank_ring(range(8))
    blocks = split_even(NS, 512)
    for c in range(KC):
        ub, ubr = ub_ring.next()
        ucx, ucxr = uc_ring.next()
        if edges is None:
            p.dma("sp", lambda e, ub=ub, c=c: e.dma_start(out=ub, in_=UE[c]), W=[ubr])
        else:
            p.dma("sp", lambda e, ub=ub, c=c: e.dma_start(out=ub[:, 2:2 + NOWN], in_=UOWN[c]), W=[ubr])
            for (dst_, src_) in ((0, 0), (1, 1), (2 + NOWN, 2)):
                p.op("pool", lambda e, ub=ub, c=c, dst_=dst_, src_=src_: e.tensor_copy(out=ub[:, dst_:dst_ + 1], in_=edges[0][:, src_, c:c + 1]), R=[edges[1]], W=[ubr])
        p.dma("sp", lambda e, ucx=ucx, c=c: e.dma_start(out=ucx[:, 2:2 + NCTX], in_=UCTX[c]), W=[ucxr])
        w = lambda kk, c=c: convT[0][:, c, kk:kk + 1]
        for (src, srcr, o0, n) in ((ub, ubr, 0, NOWN), (ucx, ucxr, NOWN, NCTX)):
            p.op("dve", lambda e, src=src, o0=o0, n=n, w=w: e.tensor_scalar(out=ucv[:, o0:o0 + n], in0=src[:, 0:n], scalar1=w(0), scalar2=w(4), op0=ALU.mult, op1=ALU.add),
                 R=[srcr, convT[1]], W=[ucv_res])
            for kk in (1, 2, 3):
                p.op("dve", lambda e, src=src, o0=o0, n=n, w=w, kk=kk: e.scalar_tensor_tensor(out=ucv[:, o0:o0 + n], in0=src[:, kk:kk + n], scalar=w(kk), op0=ALU.mult, in1=ucv[:, o0:o0 + n], op1=ALU.add),
                     R=[srcr, convT[1], ucv_res], W=[ucv_res])
        p.op("dve", lambda e: e.tensor_copy(out=u16, in_=ucv), R=[ucv_res], W=[u16_res])
        for d in range(2):
            rb, ib, tb = rbs[d], ibs[d], tbs[d]
            rr_, ir_, tres = rres[d], ires[d], tress[d]
            for (n0, nl) in blocks:
                rp, rpr = bk.next()
                ip, ipr = bk.next()
                p.op("pe", lambda e, rp=rp, d=d, c=c, n0=n0, nl=nl: e.matmul(rp[:, 0:nl], lhsT=wa[:, d, c, :], rhs=u16[:, n0:n0 + nl], start=True, stop=True), R=[wres, u16_res], W=[rpr])
                p.op("pe", lambda e, ip=ip, d=d, c=c, n0=n0, nl=nl: e.matmul(ip[:, 0:nl], lhsT=wx[:, d, c, :], rhs=u16[:, n0:n0 + nl], start=True, stop=True), R=[wres, u16_res], W=[ipr])
                p.op("act", lambda e, rp=rp, d=d, c=c, n0=n0, nl=nl, rb=rb: e.activation(out=rb[:, n0:n0 + nl], in_=rp[:, 0:nl], func=AF.Sigmoid, bias=baT[0][:, d, c:c + 1], scale=1.0), R=[rpr, baT[1]], W=[rr_])
                p.op("act", lambda e, ip=ip, d=d, c=c, n0=n0, nl=nl, ib=ib: e.activation(out=ib[:, n0:n0 + nl], in_=ip[:, 0:nl], func=AF.Sigmoid, bias=bxT[0][:, d, c:c + 1], scale=1.0), R=[ipr, bxT[1]], W=[ir_])
            A_, B_ = ab[d], bb[d]
            p.op("act", lambda e, A_=A_, d=d, c=c, rb=rb: e.activation(out=A_, in_=rb, func=AF.Exp, scale=cst[:, d, c:c + 1]), R=[rr_, cres], W=[gres[d]])
            p.op("dve", lambda e, A_=A_, tb=tb: e.tensor_tensor(out=tb, in0=A_, in1=A_, op=ALU.mult), R=[gres[d]], W=[tres])
            p.op("act", lambda e, tb=tb: e.activation(out=tb, in_=tb, func=AF.Sqrt, bias=1.0, scale=-1.0), R=[tres], W=[tres])
            p.op("dve", lambda e, tb=tb, ib=ib: e.tensor_tensor(out=tb, in0=tb, in1=ib, op=ALU.mult), R=[tres, ir_], W=[tres])
            p.op("dve", lambda e, B_=B_, tb=tb: e.tensor_tensor(out=B_, in0=tb, in1=ucv, op=ALU.mult), R=[tres, ucv_res], W=[gres[d]])
        H = hb
        p.op("dve", lambda e: e.tensor_tensor_scan(out=H[0][:, NOWN:NS], data0=ab[0][:, NOWN:NS], data1=bb[0][:, NOWN:NS], initial=0.0, op0=ALU.mult, op1=ALU.add),
             R=[gres[0]], W=[hres[0]])
        p.op("dve", lambda e: e.tensor_tensor_scan(out=H[1][:, NOWN:NS][:, ::-1], data0=ab[1][:, NOWN:NS][:, ::-1], data1=bb[1][:, NOWN:NS][:, ::-1], initial=0.0, op0=ALU.mult, op1=ALU.add),
             R=[gres[1]], W=[hres[1]])
        if mode == "carry":
            p.op("dve", lambda e: e.tensor_tensor_scan(out=H[0][:, 0:NOWN], data0=ab[0][:, 0:NOWN], data1=bb[0][:, 0:NOWN], initial=0.0, op0=ALU.mult, op1=ALU.add),
                 R=[gres[0]], W=[hres[0]])
            p.op("dve", lambda e: e.tensor_tensor_scan(out=H[1][:, 0:NOWN][:, ::-1], data0=ab[1][:, 0:NOWN][:, ::-1], data1=bb[1][:, 0:NOWN][:, ::-1], initial=0.0, op0=ALU.mult, op1=ALU.add),
                 R=[gres[1]], W=[hres[1]])
            p.op("dve", lambda e, c=c: e.tensor_reduce(out=car[:, c, 0:1], in_=ab[0][:, 0:NOWN], axis=AX.X, op=ALU.mult), R=[gres[0]], W=[car_res])
            p.op("dve", lambda e, c=c: e.tensor_copy(out=car[:, c, 1:2], in_=H[0][:, NOWN - 1:NOWN]), R=[hres[0]], W=[car_res])
            p.op("dve", lambda e, c=c: e.tensor_reduce(out=car[:, c, 2:3], in_=ab[1][:, 0:NOWN], axis=AX.X, op=ALU.mult), R=[gres[1]], W=[car_res])
            p.op("dve", lambda e, c=c: e.tensor_copy(out=car[:, c, 3:4], in_=H[1][:, 0:1]), R=[hres[1]], W=[car_res])
            if ABS is not None:
                for d in range(2):
                    p.dma("sp", lambda e, d=d, c=c: e.dma_start(out=ABS[2 * d, c], in_=ab[d][:, 0:NOWN]), R=[gres[d]])
                    p.dma("sp", lambda e, d=d, c=c: e.dma_start(out=ABS[2 * d + 1, c], in_=bb[d][:, 0:NOWN]), R=[gres[d]])
                p.op("dve", lambda e, c=c: e.tensor_copy(out=ctxfin[0][:, c, 0:1], in_=H[0][:, NS - 1:NS]), R=[hres[0]], W=[ctxfin[1]])
                p.op("dve", lambda e, c=c: e.tensor_copy(out=ctxfin[0][:, c, 1:2], in_=H[1][:, NOWN:NOWN + 1]), R=[hres[1]], W=[ctxfin[1]])
        else:
            p.op("dve", lambda e: e.tensor_copy(out=sm[:, 0:1], in_=H[0][:, NS - 1:NS]), R=[hres[0]], W=[smres])
            p.op("dve", lambda e: e.tensor_copy(out=sm[:, 1:2], in_=H[1][:, NOWN:NOWN + 1]), R=[hres[1]], W=[smres])
            for s_ in range(nstep):
                p.op("dve", lambda e, s_=s_, c=c: e.scalar_tensor_tensor(out=sm[:, 0:1], in0=sm[:, 0:1], scalar=carF[0][:, s_, c, 0:1], op0=ALU.mult, in1=carF[0][:, s_, c, 1:2], op1=ALU.add),
                     R=[smres, carF[1]], W=[smres])
                p.op("dve", lambda e, s_=s_, c=c: e.scalar_tensor_tensor(out=sm[:, 1:2], in0=sm[:, 1:2], scalar=carB[0][:, s_, c, 0:1], op0=ALU.mult, in1=carB[0][:, s_, c, 1:2], op1=ALU.add),
                     R=[smres, carB[1]], W=[smres])
            p.op("dve", lambda e: e.tensor_tensor_scan(out=H[0][:, 0:NOWN], data0=ab[0][:, 0:NOWN], data1=bb[0][:, 0:NOWN], initial=sm[:, 0:1], op0=ALU.mult, op1=ALU.add),
                 R=[gres[0], smres], W=[hres[0]])
            p.op("dve", lambda e: e.tensor_tensor_scan(out=H[1][:, 0:NOWN][:, ::-1], data0=ab[1][:, 0:NOWN][:, ::-1], data1=bb[1][:, 0:NOWN][:, ::-1], initial=sm[:, 1:2], op0=ALU.mult, op1=ALU.add),
                 R=[gres[1], smres], W=[hres[1]])
            g_, gr_ = g_ring.next()
            p.dma("sp", lambda e, g_=g_, c=c: e.dma_start(out=g_, in_=GOWN[c]), W=[gr_])
            p.op("pool", lambda e: e.tensor_tensor(out=H[0][:, 0:NOWN], in0=H[0][:, 0:NOWN], in1=H[1][:, 0:NOWN], op=ALU.add), R=[hres[0], hres[1]], W=[hres[0]])
            mo, mor = mo_ring.next()
            p.op("dve", lambda e, mo=mo, g_=g_: e.tensor_tensor(out=mo, in0=H[0][:, 0:NOWN], in1=g_, op=ALU.mult), R=[hres[0], gr_], W=[mor])
            p.dma("sp", lambda e, mo=mo, c=c: e.dma_start(out=MIX1[c], in_=mo), R=[mor])
    if mode == "carry":
        p.dma("sp", lambda e: e.dma_start(out=CAR, in_=car), R=[car_res])
    p.barrier()
    a.reset(mk)


def phase_rglru_apply(k, ABS, ctxfin, car_tiles, GOWN, MIX1):
    p, a = k.p, k.arena
    mk = a.mark()
    carF, carB = car_tiles
    sets = Ring([[a.alloc([NOWN], F32) for _ in range(4)] for _ in range(2)])
    h_ring = Ring([[a.alloc([NOWN], F32) for _ in range(2)] for _ in range(2)])
    g_ring = Ring([a.alloc([NOWN], BF16) for _ in range(2)])
    mo_ring = Ring([a.alloc([NOWN], BF16) for _ in range(2)])
    sm_ring = Ring([a.alloc([2], F32) for _ in range(2)])
    pre = {}

    def ap_load(c):
        (af, bf, ab_, bb_), sr = sets.next()
        for i_, t_ in enumerate((af, bf, ab_, bb_)):
            p.dma("sp", lambda e, i_=i_, t_=t_, c=c: e.dma_start(out=t_, in_=ABS[i_, c]), W=[sr])
        g_, gr_ = g_ring.next()
        p.dma("sp", lambda e, g_=g_, c=c: e.dma_start(out=g_, in_=GOWN[c]), W=[gr_])
        pre[c] = (af, bf, ab_, bb_, sr, g_, gr_)

    ap_load(0)
    for c in range(KC):
        if c + 1 < KC:
            ap_load(c + 1)
        af, bf, ab_, bb_, sr, g_, gr_ = pre.pop(c)
        sm, smr = sm_ring.next()
        p.op("dve", lambda e, sm=sm, c=c: e.tensor_copy(out=sm, in_=ctxfin[0][:, c, :]), R=[ctxfin[1]], W=[smr])
        for s_ in range(4):
            p.op("dve", lambda e, sm=sm, s_=s_, c=c: e.scalar_tensor_tensor(out=sm[:, 0:1], in0=sm[:, 0:1], scalar=carF[0][:, s_, c, 0:1], op0=ALU.mult, in1=carF[0][:, s_, c, 1:2], op1=ALU.add),
                 R=[smr, carF[1]], W=[smr])
            p.op("dve", lambda e, sm=sm, s_=s_, c=c: e.scalar_tensor_tensor(out=sm[:, 1:2], in0=sm[:, 1:2], scalar=carB[0][:, s_, c, 0:1], op0=ALU.mult, in1=carB[0][:, s_, c, 1:2], op1=ALU.add),
                 R=[smr, carB[1]], W=[smr])
        (hf, hb_), hr = h_ring.next()
        p.op("dve", lambda e, hf=hf, af=af, bf=bf, sm=sm: e.tensor_tensor_scan(out=hf, data0=af, data1=bf, initial=sm[:, 0:1], op0=ALU.mult, op1=ALU.add), R=[sr, smr], W=[hr])
        p.op("dve", lambda e, hb_=hb_, ab_=ab_, bb_=bb_, sm=sm: e.tensor_tensor_scan(out=hb_[:, ::-1], data0=ab_[:, ::-1], data1=bb_[:, ::-1], initial=sm[:, 1:2], op0=ALU.mult, op1=ALU.add), R=[sr, smr], W=[hr])
        p.op("dve", lambda e, hf=hf, hb_=hb_: e.tensor_tensor(out=hf, in0=hf, in1=hb_, op=ALU.add), R=[hr], W=[hr])
        mo, mor = mo_ring.next()
        p.op("pool", lambda e, mo=mo, hf=hf, g_=g_: e.tensor_tensor(out=mo, in0=hf, in1=g_, op=ALU.mult), R=[hr, gr_], W=[mor])
        p.dma("sp", lambda e, mo=mo, c=c: e.dma_start(out=MIX1[c], in_=mo), R=[mor])
    p.barrier()
    a.reset(mk)


RG4 = [[0, 1, 2, 3], [4, 5, 6, 7]]


def emit_allgather(k, src, dst):
    k.p.cc(lambda e: e.collective_compute("AllGather", ALU.bypass, replica_groups=RG4, ins=[src.opt()], outs=[dst.opt()]))


def phase_exchange_edges(k, UOWN, PUB1, GAT1, xfl, edges, edges_res):
    p, a = k.p, k.arena
    mk = a.mark()
    pub = a.alloc([3, KC], F32)
    r = Res()
    for (j, t) in ((0, 0), (1, NOWN - 2), (2, NOWN - 1)):
        p.dma("sp", lambda e, j=j, t=t: e.dma_start(out=pub[:, j, :], in_=UOWN[:, :, t:t + 1].rearrange("c p e -> p (c e)"), allow_slow_non_contiguous=True), W=[r])
    p.dma("sp", lambda e: e.dma_start(out=PUB1.rearrange("(p j) c -> p j c", j=3), in_=pub), R=[r])
    p.barrier()
    emit_allgather(k, PUB1, GAT1)
    p.barrier()
    g = a.alloc([4, 3, KC], F32)
    gr = Res()
    p.dma("sp", lambda e: e.dma_start(out=g, in_=GAT1.rearrange("(r p j) c -> p r j c", r=4, j=3)), W=[gr])
    for (dst_, col, kind) in ((0, 1, 0), (1, 2, 0), (2, 0, 1)):
        for rk in range(4):
            if rk == 0:
                p.op("dve", lambda e, dst_=dst_, col=col, kind=kind, rk=rk: e.tensor_scalar(out=edges[:, dst_, :], in0=g[:, rk, col, :], scalar1=xfl[0][:, kind, rk:rk + 1], scalar2=None, op0=ALU.mult),
                     R=[gr, xfl[1]], W=[edges_res])
            else:
                p.op("dve", lambda e, dst_=dst_, col=col, kind=kind, rk=rk: e.scalar_tensor_tensor(out=edges[:, dst_, :], in0=g[:, rk, col, :], scalar=xfl[0][:, kind, rk:rk + 1], op0=ALU.mult, in1=edges[:, dst_, :], op1=ALU.add),
                     R=[gr, xfl[1], edges_res], W=[edges_res])
    p.barrier()
    a.reset(mk)


def phase_exchange_carries(k, PUB2, GAT2, xfl, carF, carB, car_res):
    p, a = k.p, k.arena
    mk = a.mark()
    emit_allgather(k, PUB2, GAT2)
    p.barrier()
    g = a.alloc([4, KC, 4], F32)
    gr = Res()
    p.dma("sp", lambda e: e.dma_start(out=g, in_=GAT2.rearrange("(r c p) e -> p r c e", r=4, p=128)), W=[gr])
    for rk in range(4):
        for (dst, step, kind, ca, cb) in ((carF, rk, 2, 0, 1), (carB, 3 - rk, 3, 2, 3)):
            fl = xfl[0][:, kind, rk:rk + 1]
            p.op("dve", lambda e, dst=dst, step=step, fl=fl, ca=ca, rk=rk: e.tensor_scalar(out=dst[:, step, :, 0], in0=g[:, rk, :, ca], scalar1=-1.0, scalar2=fl, op0=ALU.add, op1=ALU.mult),
                 R=[gr, xfl[1]], W=[car_res])
            p.op("dve", lambda e, dst=dst, step=step: e.tensor_scalar(out=dst[:, step, :, 0], in0=dst[:, step, :, 0], scalar1=1.0, scalar2=None, op0=ALU.add),
                 R=[car_res], W=[car_res])
            p.op("dve", lambda e, dst=dst, step=step, fl=fl, cb=cb, rk=rk: e.tensor_scalar(out=dst[:, step, :, 1], in0=g[:, rk, :, cb], scalar1=fl, scalar2=None, op0=ALU.mult),
                 R=[gr, xfl[1]], W=[car_res])
    p.barrier()
    a.reset(mk)


def phase_final(k, XU, out, gfull_dram):
    p, a = k.p, k.arena
    mk = a.mark()
    gf = a.alloc([D], F32)
    gfr = Res()
    p.dma("sp", lambda e: e.dma_start(out=gf, in_=gfull_dram), W=[gfr])
    xb_ring = Ring([a.alloc([KC, 128], F32) for _ in range(3)])
    ob_ring = Ring([a.alloc([D], F32) for _ in range(2)])
    junk = a.alloc([512], F32)
    jres = Res()
    ss_ring = Ring([a.alloc([8], F32) for _ in range(2)])
    bk = k.bank_ring(range(8))
    XUv = XU.rearrange("c p t -> p c t")
    fl = {}

    def fin_load(blk):
        u0 = U_OWN + blk * 128
        xb, xbr = xb_ring.next()
        p.dma("sp", lambda e, xb=xb, u0=u0: e.dma_start(out=xb, in_=XUv[:, :, u0:u0 + 128]), W=[xbr])
        fl[blk] = (xb, xbr)

    fin_load(0)
    for blk in range(NOWN // 128):
        if blk + 1 < NOWN // 128:
            fin_load(blk + 1)
        xb, xbr = fl.pop(blk)
        ss, ssr = ss_ring.next()
        banks = [bk.next() for _ in range(4)]
        for g, (b, br) in enumerate(banks):
            for q in range(4):
                c = 4 * g + q
                p.op("pe", lambda e, b=b, xb=xb, c=c, q=q: e.transpose(b[:, q * 128:(q + 1) * 128], xb[:, c, :], k.ident), R=[xbr, k.cres], W=[br])
            p.op("act", lambda e, b=b, ss=ss, g=g: e.activation(out=junk, in_=b[:, :], func=AF.Square, accum_out=ss[:, g:g + 1]), R=[br], W=[ssr, jres])
        p.op("dve", lambda e, ss=ss: e.tensor_reduce(out=ss[:, 4:5], in_=ss[:, 0:4], axis=AX.X, op=ALU.add), R=[ssr], W=[ssr])
        p.op("dve", lambda e, ss=ss: e.tensor_scalar(out=ss[:, 5:6], in0=ss[:, 4:5], scalar1=1.0 / D, scalar2=EPS, op0=ALU.mult, op1=ALU.add), R=[ssr], W=[ssr])
        p.op("act", lambda e, ss=ss: e.activation(out=ss[:, 6:7], in_=ss[:, 5:6], func=AF.Sqrt), R=[ssr], W=[ssr])
        p.op("dve", lambda e, ss=ss: e.reciprocal(out=ss[:, 7:8], in_=ss[:, 6:7]), R=[ssr], W=[ssr])
        ob, obr = ob_ring.next()
        for g, (b, br) in enumerate(banks):
            p.op("dve", lambda e, b=b, ob=ob, ss=ss, g=g: e.scalar_tensor_tensor(out=ob[:, g * 512:(g + 1) * 512], in0=b[:, :], scalar=ss[:, 7:8], op0=ALU.mult,
                                                                                 in1=gf[:, g * 512:(g + 1) * 512], op1=ALU.mult), R=[br, ssr, gfr], W=[obr])
        p.dma("sp", lambda e, ob=ob, blk=blk: e.dma_start(out=out[blk * 128:(blk + 1) * 128, :], in_=ob), R=[obr])
    p.barrier()
    a.reset(mk)


def rg_small_inputs(inp):
    cw = np.concatenate([inp["rg_conv_w"][0], inp["rg_conv_b"][0][None, :]], axis=0)
    return {"rgconvT": np.ascontiguousarray(cw.reshape(5, KC, 128).transpose(2, 1, 0)),
            "rgbaT": fm(inp["rg_b_a"][0]), "rgbxT": fm(inp["rg_b_x"][0]), "rglamT": fm(inp["rg_lambda"][0]),
            "rg_w_a": inp["rg_w_a"][0], "rg_w_x": inp["rg_w_x"][0]}


def build_B():
    nc = bass.Bass("TRN2", target_bir_lowering=False)
    with ExitStack() as st:
        k = K(nc, st)
        UE = k.ext_in("UE", [KC, 128, NE])
        UCTX = k.ext_in("UCTX", [KC, 128, NCTX])
        k.ext_in("rgconvT", [128, KC, 5])
        k.ext_in("rgbaT", [128, 2, KC])
        k.ext_in("rgbxT", [128, 2, KC])
        k.ext_in("rglamT", [128, 2, KC])
        wa = k.ext_in("rg_w_a", [2, KC, 128, 128])
        wx = k.ext_in("rg_w_x", [2, KC, 128, 128])
        CAR = k.ext_out("CAR", [128, KC, 4])
        make_basic_consts(k)
        phase_rglru(k, "carry", UE, UCTX, wa, wx, CAR=CAR)
        k.p.final_wait()
        k.p.emit()
    return nc


def build_C():
    nc = bass.Bass("TRN2", target_bir_lowering=False)
    with ExitStack() as st:
        k = K(nc, st)
        p, a = k.p, k.arena
        UE = k.ext_in("UE", [KC, 128, NE])
        UCTX = k.ext_in("UCTX", [KC, 128, NCTX])
        GOWN = k.ext_in("GOWN", [KC, 128, NOWN], BF16)
        XOWN = k.ext_in("XOWN", [KC, 128, NOWN])
        k.ext_in("MODT", [128, 2, 144, 2])
        k.ext_in("normT", [128, 2, 3, KC])
        k.ext_in("rgconvT", [128, KC, 5])
        k.ext_in("rgbaT", [128, 2, KC])
        k.ext_in("rgbxT", [128, 2, KC])
        k.ext_in("rglamT", [128, 2, KC])
        k.ext_in("carF", [128, 3, KC, 2])
        k.ext_in("carB", [128, 3, KC, 2])
        wa = k.ext_in("rg_w_a", [2, KC, 128, 128])
        wx = k.ext_in("rg_w_x", [2, KC, 128, 128])
        rg_w_out = k.ext_in("rg_w_out", [D, D])
        f2i = k.ext_in("ffn2_w_in1", [D, 2 * DFF])
        f2o = k.ext_in("ffn2_w_out1", [DFF, D])
        gfull = k.ext_in("gfull", [128, D])
        out = k.ext_out("out", [NOWN, D])
        X1 = k.scratch("X1", [KC, 128, NU])
        MIX1 = k.scratch("MIX1", [KC, 128, NOWN], BF16)
        make_basic_consts(k)
        cs = load_consts(k, [("MODT", [2, 144, 2]), ("normT", [2, 3, KC])])
        modt = cs["MODT"][0]
        k.modt_res = cs["MODT"][1]
        for c in range(KC):
            p.dma("sp", lambda e, c=c: e.dma_start(out=X1[c, :, U_OWN:U_OWN + NOWN], in_=XOWN[c]))
        sites = {(1, s): derive_site(k, modt, cs["normT"], 1, s, half=(s != 1)) for s in (1, 2)}
        phase_rglru(k, "full", UE, UCTX, wa, wx, GOWN=GOWN, MIX1=MIX1)
        own = [(U_OWN, U_OWN + NOWN)]
        phase_outproj(k, X1, lambda c, u0, ln: MIX1[c, :, u0 - U_OWN:u0 - U_OWN + ln], own, rg_w_out, sites[(1, 1)])
        phase_ffn(k, X1, own, f2i, f2o, sites[(1, 2)])
        phase_final(k, X1, out, gfull)
        p.final_wait()
        p.emit()
    return nc


_PROGS = {}


def _prog(name, fn):
    if name not in _PROGS:
        _PROGS[name] = fn()
    return _PROGS[name]


def core_inputs_F(inp, core):
    m = core_inputs_A(inp, core, "all")
    m.update(rg_small_inputs(inp))
    rank = core % 4
    q = 9 * D // 4
    for l in range(2):
        m[f"w_mod{l}"] = np.ascontiguousarray(inp["w_mod"][l][:, rank * q:(rank + 1) * q])
    m["bmodT"] = np.ascontiguousarray(fm(inp["b_mod"])[:, :, rank * 36:(rank + 1) * 36])
    xfl = np.zeros((4, 4), np.float32)
    for r in range(4):
        xfl[0, r] = 1.0 if r == rank - 1 else 0.0
        xfl[1, r] = 1.0 if r == rank + 1 else 0.0
        xfl[2, r] = 1.0 if r < rank else 0.0
        xfl[3, r] = 1.0 if r > rank else 0.0
    m["xfl"] = np.ascontiguousarray(np.broadcast_to(xfl[None], (128, 4, 4)))
    m["rg_w_out"] = inp["rg_w_out"][0]
    m["ffn2_w_in1"] = inp["ffn2_w_in"][1]
    m["ffn2_w_out1"] = inp["ffn2_w_out"][1]
    m["gfull"] = np.ascontiguousarray(np.broadcast_to(inp["final_norm"][None, :], (128, D)).astype(np.float32))
    return m


def kernel(**inp):
    inp = {k_: np.asarray(v) for k_, v in inp.items()}
    cores = list(range(NCORES))
    nc = _prog("F", lambda: build_A("all", fused=True))
    res = run_bass_kernel_spmd(nc, [core_inputs_F(inp, c) for c in cores], core_ids=cores).results
    out = np.empty((2, 4 * NOWN, D), np.float32)
    for c in cores:
        out[c // 4, (c % 4) * NOWN:(c % 4 + 1) * NOWN] = res[c]["out"]
    return out


def kernel_unfused(**inp):
    inp = {k_: np.asarray(v) for k_, v in inp.items()}
    cores = list(range(NCORES))
    ncA = _prog("A", lambda: build_A("all"))
    resA = run_bass_kernel_spmd(ncA, [core_inputs_A(inp, c, "all") for c in cores], core_ids=cores).results
    small = rg_small_inputs(inp)
    UE = []
    for c in cores:
        ue = np.zeros((KC, 128, NE), np.float32)
        ue[:, :, 2:2 + NOWN] = resA[c]["UOWN"]
        if c % 4 > 0:
            ue[:, :, 0:2] = resA[c - 1]["UOWN"][:, :, NOWN - 2:NOWN]
        if c % 4 < 3:
            ue[:, :, 2 + NOWN] = resA[c + 1]["UOWN"][:, :, 0]
        UE.append(ue)
    ncB = _prog("B", build_B)
    mapsB = [dict(UE=UE[c], UCTX=resA[c]["UCTX"], **small) for c in cores]
    resB = run_bass_kernel_spmd(ncB, mapsB, core_ids=cores).results
    ident = np.zeros((128, KC, 2), np.float32)
    ident[:, :, 0] = 1.0
    mapsC = []
    normT = fm(np.stack([inp["norm_ffn1"], inp["norm_mix"], inp["norm_ffn2"]], axis=1))
    gfull = np.ascontiguousarray(np.broadcast_to(inp["final_norm"][None, :], (128, D)).astype(np.float32))
    for c in cores:
        b, ci = c // 4, c % 4
        prev = [resB[b * 4 + j]["CAR"][:, :, 0:2] for j in range(ci)]
        nxt = [resB[b * 4 + j]["CAR"][:, :, 2:4] for j in range(3, ci, -1)]
        carF = np.stack([ident] * (3 - len(prev)) + prev, axis=1)
        carB = np.stack([ident] * (3 - len(nxt)) + nxt, axis=1)
        m = dict(UE=UE[c], UCTX=resA[c]["UCTX"], GOWN=resA[c]["GOWN"], XOWN=resA[c]["XOWN"], MODT=resA[c]["MODT"],
                 normT=normT, carF=np.ascontiguousarray(carF), carB=np.ascontiguousarray(carB), rg_w_out=inp["rg_w_out"][0],
                 ffn2_w_in1=inp["ffn2_w_in"][1], ffn2_w_out1=inp["ffn2_w_out"][1], gfull=gfull, **small)
        mapsC.append(m)
    ncC = _prog("C", build_C)
    resC = run_bass_kernel_spmd(ncC, mapsC, core_ids=cores).results
    out = np.empty((2, 4 * NOWN, D), np.float32)
    for c in cores:
        out[c // 4, (c % 4) * NOWN:(c % 4 + 1) * NOWN] = resC[c]["out"]
    return out


def phase_rglru_carry(k, UOWN, UCTX, rg_w_a, rg_w_x, CAR, edges, ABS, ctxfin):
    p, a = k.p, k.arena
    mk = a.mark()
    cs = load_consts(k, [("rgconvT", [KC, 5]), ("rgbaT", [2, KC]), ("rgbxT", [2, KC]), ("rglamT", [2, KC])])
    convT, baT, bxT, lamT = cs["rgconvT"], cs["rgbaT"], cs["rgbxT"], cs["rglamT"]
    cst = a.alloc([2, KC], F32)
    cres = Res()
    p.op("act", lambda e: e.activation(out=cst, in_=lamT[0], func=AF.Exp, scale=-1.0), R=[lamT[1]], W=[cres])
    p.op("act", lambda e: e.activation(out=cst, in_=cst, func=AF.Ln, bias=1.0, scale=1.0), R=[cres], W=[cres])
    p.op("dve", lambda e: e.tensor_scalar(out=cst, in0=cst, scalar1=-8.0, scalar2=None, op0=ALU.mult), R=[cres], W=[cres])
    wa = a.alloc([2, KC, 128], BF16)
    wx = a.alloc([2, KC, 128], BF16)
    wres = Res()
    p.dma("pool", lambda e: e.dma_start(out=wa, in_=rg_w_a.rearrange("d h i j -> i d h j")), W=[wres])
    p.dma("pool", lambda e: e.dma_start(out=wx, in_=rg_w_x.rearrange("d h i j -> i d h j")), W=[wres])
    ub_ring = Ring([a.alloc([NE], F32) for _ in range(2)])
    uc_ring = Ring([a.alloc([NCTX + 3], F32) for _ in range(2)])
    for (b_, r_) in uc_ring.bufs:
        p.op("pool", lambda e, b_=b_: e.memset(b_, 0.0), W=[r_])
    ucv_ring = Ring([a.alloc([NS], F32) for _ in range(2)])
    u16_ring = Ring([a.alloc([NS], BF16) for _ in range(2)])
    rbs = [a.alloc([NS], F32) for _ in range(2)]
    ibs = [a.alloc([NS], F32) for _ in range(2)]
    tbs = [a.alloc([NS], F32) for _ in range(2)]
    rres, ires, tress = [Res(), Res()], [Res(), Res()], [Res(), Res()]
    ab = [a.alloc([NS], F32) for _ in range(2)]
    bb = [a.alloc([NS], F32) for _ in range(2)]
    gres = [Res(), Res()]
    hj = a.alloc([NS], F32)
    hjr = Res()
    car = a.alloc([KC, 4], F32)
    car_res = Res()
    bk = k.bank_ring(range(8))
    blocks = split_even(NS, 512)
    state = {}

    def stage1a(c):
        ub, ubr = ub_ring.next()
        ucx, ucxr = uc_ring.next()
        ucv, ucv_res = ucv_ring.next()
        u16, u16_res = u16_ring.next()
        state[c] = (ucv, ucv_res)
        p.dma("sp", lambda e: e.dma_start(out=ub[:, 2:2 + NOWN], in_=UOWN[c]), W=[ubr])
        for (dst_, src_) in ((0, 0), (1, 1), (2 + NOWN, 2)):
            p.op("pool", lambda e, dst_=dst_, src_=src_: e.tensor_copy(out=ub[:, dst_:dst_ + 1], in_=edges[0][:, src_, c:c + 1]), R=[edges[1]], W=[ubr])
        p.dma("sp", lambda e: e.dma_start(out=ucx[:, 2:2 + NCTX], in_=UCTX[c]), W=[ucxr])
        w = lambda kk: convT[0][:, c, kk:kk + 1]
        for (src, srcr, o0, n) in ((ub, ubr, 0, NOWN), (ucx, ucxr, NOWN, NCTX)):
            p.op("dve", lambda e, src=src, o0=o0, n=n: e.tensor_scalar(out=ucv[:, o0:o0 + n], in0=src[:, 0:n], scalar1=w(0), scalar2=w(4), op0=ALU.mult, op1=ALU.add),
                 R=[srcr, convT[1]], W=[ucv_res])
            for kk in (1, 2, 3):
                p.op("dve", lambda e, src=src, o0=o0, n=n, kk=kk: e.scalar_tensor_tensor(out=ucv[:, o0:o0 + n], in0=src[:, kk:kk + n], scalar=w(kk), op0=ALU.mult, in1=ucv[:, o0:o0 + n], op1=ALU.add),
                     R=[srcr, convT[1], ucv_res], W=[ucv_res])
        p.op("dve", lambda e: e.tensor_copy(out=u16, in_=ucv), R=[ucv_res], W=[u16_res])
        for d in range(2):
            for (n0, nl) in blocks:
                rp, rpr = bk.next()
                ip, ipr = bk.next()
                p.op("pe", lambda e, rp=rp, d=d, n0=n0, nl=nl: e.matmul(rp[:, 0:nl], lhsT=wa[:, d, c, :], rhs=u16[:, n0:n0 + nl], start=True, stop=True), R=[wres, u16_res], W=[rpr])
                p.op("pe", lambda e, ip=ip, d=d, n0=n0, nl=nl: e.matmul(ip[:, 0:nl], lhsT=wx[:, d, c, :], rhs=u16[:, n0:n0 + nl], start=True, stop=True), R=[wres, u16_res], W=[ipr])
                p.op("act", lambda e, rp=rp, d=d, n0=n0, nl=nl: e.activation(out=rbs[d][:, n0:n0 + nl], in_=rp[:, 0:nl], func=AF.Sigmoid, bias=baT[0][:, d, c:c + 1], scale=1.0), R=[rpr, baT[1]], W=[rres[d]])
                p.op("act", lambda e, ip=ip, d=d, n0=n0, nl=nl: e.activation(out=ibs[d][:, n0:n0 + nl], in_=ip[:, 0:nl], func=AF.Sigmoid, bias=bxT[0][:, d, c:c + 1], scale=1.0), R=[ipr, bxT[1]], W=[ires[d]])

    def stage1b(c):
        ucv, ucv_res = state.pop(c)
        for d in range(2):
            p.op("act", lambda e, d=d: e.activation(out=ab[d], in_=rbs[d], func=AF.Exp, scale=cst[:, d, c:c + 1]), R=[rres[d], cres], W=[gres[d]])
        for d in range(2):
            p.op("pool", lambda e, d=d: e.tensor_tensor(out=tbs[d], in0=ab[d], in1=ab[d], op=ALU.mult), R=[gres[d]], W=[tress[d]])
        for d in range(2):
            p.op("act", lambda e, d=d: e.activation(out=tbs[d], in_=tbs[d], func=AF.Sqrt, bias=1.0, scale=-1.0), R=[tress[d]], W=[tress[d]])
        for d in range(2):
            p.op("pool", lambda e, d=d: e.tensor_tensor(out=tbs[d], in0=tbs[d], in1=ibs[d], op=ALU.mult), R=[tress[d], ires[d]], W=[tress[d]])
        for d in range(2):
            p.op("dve", lambda e, d=d: e.tensor_tensor(out=bb[d], in0=tbs[d], in1=ucv, op=ALU.mult), R=[tress[d], ucv_res], W=[gres[d]])

    def stage2(c):
        p.op("dve", lambda e: e.tensor_tensor_scan(out=hj[:, NOWN:NS], data0=ab[0][:, NOWN:NS], data1=bb[0][:, NOWN:NS], initial=0.0, op0=ALU.mult, op1=ALU.add), R=[gres[0]], W=[hjr])
        p.op("dve", lambda e: e.tensor_copy(out=ctxfin[0][:, c, 0:1], in_=hj[:, NS - 1:NS]), R=[hjr], W=[ctxfin[1]])
        p.op("dve", lambda e: e.tensor_tensor_scan(out=hj[:, NOWN:NS][:, ::-1], data0=ab[1][:, NOWN:NS][:, ::-1], data1=bb[1][:, NOWN:NS][:, ::-1], initial=0.0, op0=ALU.mult, op1=ALU.add), R=[gres[1]], W=[hjr])
        p.op("dve", lambda e: e.tensor_copy(out=ctxfin[0][:, c, 1:2], in_=hj[:, NOWN:NOWN + 1]), R=[hjr], W=[ctxfin[1]])
        p.op("dve", lambda e: e.tensor_tensor_scan(out=hj[:, 0:NOWN], data0=ab[0][:, 0:NOWN], data1=bb[0][:, 0:NOWN], initial=0.0, op0=ALU.mult, op1=ALU.add), R=[gres[0]], W=[hjr])
        p.op("dve", lambda e: e.tensor_copy(out=car[:, c, 1:2], in_=hj[:, NOWN - 1:NOWN]), R=[hjr], W=[car_res])
        p.op("dve", lambda e: e.tensor_tensor_scan(out=hj[:, 0:NOWN][:, ::-1], data0=ab[1][:, 0:NOWN][:, ::-1], data1=bb[1][:, 0:NOWN][:, ::-1], initial=0.0, op0=ALU.mult, op1=ALU.add), R=[gres[1]], W=[hjr])
        p.op("dve", lambda e: e.tensor_copy(out=car[:, c, 3:4], in_=hj[:, 0:1]), R=[hjr], W=[car_res])
        p.op("dve", lambda e: e.tensor_reduce(out=car[:, c, 0:1], in_=ab[0][:, 0:NOWN], axis=AX.X, op=ALU.mult), R=[gres[0]], W=[car_res])
        p.op("dve", lambda e: e.tensor_reduce(out=car[:, c, 2:3], in_=ab[1][:, 0:NOWN], axis=AX.X, op=ALU.mult), R=[gres[1]], W=[car_res])
        for d in range(2):
            p.dma("sp", lambda e, d=d: e.dma_start(out=ABS[2 * d, c], in_=ab[d][:, 0:NOWN]), R=[gres[d]])
            p.dma("sp", lambda e, d=d: e.dma_start(out=ABS[2 * d + 1, c], in_=bb[d][:, 0:NOWN]), R=[gres[d]])

    stage1a(0)
    stage1b(0)
    for c in range(KC):
        if c + 1 < KC:
            stage1a(c + 1)
        stage2(c)
        if c + 1 < KC:
            stage1b(c + 1)
    p.dma("sp", lambda e: e.dma_start(out=CAR, in_=car), R=[car_res])
    p.barrier()
    a.reset(mk)
```
